# Optimizing a Trainium2 kernel written in Bass

```python
import jax
import jax.numpy as jnp
from jax import lax
import numpy as np

D_MODEL = 1024
BATCH = 8
SEQ = 4096
DEPTH = 4

MIX_WIDTH = 512
FOX_HEAD_DIM = 64
FOX_HEADS = MIX_WIDTH // FOX_HEAD_DIM
FOX_BLOCK = 128
HGRN_KEY_DIM = 64
HGRN_HEADS = MIX_WIDTH // HGRN_KEY_DIM
HGRN_VAL_DIM = MIX_WIDTH // HGRN_HEADS
HGRN_CHUNK = 64
RWKV_HEAD_DIM = 64
RWKV_HEADS = MIX_WIDTH // RWKV_HEAD_DIM
RWKV_DECAY_RANK = 64
RWKV_ICLR_RANK = 64
RWKV_VRES_RANK = 32
RWKV_GATE_RANK = 128
RWKV_LN_EPS = 64e-5
N_BRANCH = 3
D_FF = 2816
CONV_WIDTH = 3
NORM_EPS = 1e-6
MASK_VALUE = -1e30
LOG_FLOOR = 1e-30

FOX_SIZES = (MIX_WIDTH, MIX_WIDTH, MIX_WIDTH, FOX_HEADS)
HGRN_SIZES = (MIX_WIDTH, MIX_WIDTH, MIX_WIDTH, MIX_WIDTH)
RWKV_SIZES = (MIX_WIDTH, MIX_WIDTH, MIX_WIDTH, RWKV_DECAY_RANK, RWKV_ICLR_RANK, RWKV_GATE_RANK)
GATE_SIZES = (D_MODEL, D_MODEL, D_MODEL)
GROUP_SIZES = (sum(FOX_SIZES), sum(HGRN_SIZES), sum(RWKV_SIZES), sum(GATE_SIZES))
N_IN = sum(GROUP_SIZES)

kernel_name = 'hybrid_fox_hgrn2_rwkv7_block'


def split_cols(p, sizes):
    return jnp.split(p, np.cumsum(sizes)[:-1].tolist(), axis=-1)


def rms_norm(x, eps=NORM_EPS):
    xf = x.astype(jnp.float32)
    return (xf * lax.rsqrt(jnp.mean(xf * xf, axis=-1, keepdims=True) + eps)).astype(x.dtype)


def token_shift(p):
    return jnp.pad(p, ((0, 0), (1, 0), (0, 0)))[:, :-1]


def split_heads(t, n_heads):
    return t.reshape(t.shape[0], t.shape[1], n_heads, t.shape[2] // n_heads)


def merge_heads(t):
    return t.reshape(t.shape[0], t.shape[1], -1)


def causal_dwconv(u, w, b):
    k_w, s_len = w.shape[0], u.shape[1]
    up = jnp.pad(u, ((0, 0), (k_w - 1, 0), (0, 0)))
    out = b
    for j in range(k_w):
        out = out + up[:, j:j + s_len] * w[j]
    return out


def fox_attention(q, k, v, log_f, q_gain, k_gain):
    s_len, dh = q.shape[1], q.shape[3]
    q = rms_norm(q) * q_gain
    k = rms_norm(k) * k_gain
    cum = jnp.cumsum(log_f, axis=1).transpose(0, 2, 1)
    scale = dh ** -0.5
    outs = []
    for blk in range(s_len // FOX_BLOCK):
        lo, hi = blk * FOX_BLOCK, (blk + 1) * FOX_BLOCK
        logits = jnp.einsum('bqhd,bkhd->bhqk', q[:, lo:hi], k[:, :hi]).astype(jnp.float32) * scale
        logits = logits + cum[:, :, lo:hi, None] - cum[:, :, None, :hi]
        causal = (lo + jnp.arange(FOX_BLOCK))[:, None] >= jnp.arange(hi)[None, :]
        probs = jax.nn.softmax(jnp.where(causal, logits, MASK_VALUE), axis=-1).astype(v.dtype)
        outs.append(jnp.einsum('bhqk,bkhd->bqhd', probs, v[:, :hi]))
    return jnp.concatenate(outs, axis=1)


def hgrn2_chunked(q, k, v, log_g):
    f32 = jnp.float32
    bsz, s_len, n_h, d_k = q.shape
    d_v = v.shape[-1]
    n_c = s_len // HGRN_CHUNK

    def to_chunks(t):
        return t.astype(f32).reshape(bsz, n_c, HGRN_CHUNK, n_h, t.shape[-1]).transpose(1, 0, 3, 2, 4)

    mask = jnp.tril(jnp.ones((HGRN_CHUNK, HGRN_CHUNK), dtype=bool))[:, :, None]

    def step(state, inp):
        qb, kb, vb, gb = inp
        g_cum = jnp.cumsum(gb, axis=2)
        diff = g_cum[:, :, :, None, :] - g_cum[:, :, None, :, :]
        decay = jnp.where(mask, jnp.exp(jnp.where(mask, diff, 0.0)), 0.0)
        attn = jnp.einsum('bhtk,bhsk,bhtsk->bhts', qb, kb, decay)
        o = (jnp.einsum('bhts,bhsv->bhtv', attn, vb)
             + jnp.einsum('bhtk,bhkv->bhtv', qb * jnp.exp(g_cum), state))
        g_last = g_cum[:, :, -1]
        state = (state * jnp.exp(g_last)[..., None]
                 + jnp.einsum('bhsk,bhsv->bhkv', kb * jnp.exp(g_last[:, :, None, :] - g_cum), vb))
        return state, o

    state0 = jnp.zeros((bsz, n_h, d_k, d_v), f32)
    _, o = lax.scan(step, state0, (to_chunks(q), to_chunks(k), to_chunks(v), to_chunks(log_g)))
    return o.transpose(1, 0, 3, 2, 4).reshape(bsz, s_len, n_h, d_v)


def hgrn2_branch(p, lb, o_gain):
    q, f, i, og = split_cols(p, HGRN_SIZES)
    ff = f.astype(jnp.float32)
    gate = lb + (1.0 - lb) * jax.nn.sigmoid(ff)
    log_g = jnp.log(jnp.maximum(gate, LOG_FLOOR))
    k = 1.0 - gate
    o = hgrn2_chunked(split_heads(jax.nn.silu(q), HGRN_HEADS), split_heads(k, HGRN_HEADS),
                      split_heads(i, HGRN_HEADS), split_heads(log_g, HGRN_HEADS))
    o = rms_norm(o.astype(p.dtype)) * o_gain
    return merge_heads(o) * jax.nn.silu(og)


def rwkv7_scan(r, log_w, k, v, kk, a):
    f32 = jnp.float32
    bsz, _, n_h, n = r.shape

    def step(state, inp):
        r_t, lw_t, k_t, v_t, kk_t, a_t = inp
        removed = jnp.einsum('bhvk,bhk->bhv', state, kk_t)
        state = (state * jnp.exp(lw_t)[:, :, None, :]
                 - removed[..., None] * (kk_t * a_t)[:, :, None, :]
                 + v_t[..., None] * k_t[:, :, None, :])
        return state, jnp.einsum('bhvk,bhk->bhv', state, r_t)

    seq_first = lambda t: jnp.moveaxis(t.astype(f32), 1, 0)
    state0 = jnp.zeros((bsz, n_h, n, n), f32)
    _, out = lax.scan(step, state0, tuple(seq_first(t) for t in (r, log_w, k, v, kk, a)))
    return jnp.moveaxis(out, 0, 1)


def rwkv7_branch(p, v_first, vres, mu, w0, w2, a0, a2, g2, k_k, k_a, r_k, ln_w, ln_b):
    n_h = RWKV_HEADS
    p = p + (token_shift(p) - p) * mu
    r, k, v, w_lo, a_lo, g_lo = split_cols(p, RWKV_SIZES)
    w_raw = -jax.nn.softplus(-(w0 + jnp.tanh(w_lo) @ w2)) - 0.5
    log_w = -jnp.exp(w_raw.astype(jnp.float32))
    a = jax.nn.sigmoid(a0 + a_lo @ a2)
    g = jax.nn.sigmoid(g_lo) @ g2
    if v_first is None:
        v_first = v
    else:
        vres_lo, vres_mu, v0, v2 = vres
        vres_lo = vres_lo + (token_shift(vres_lo) - vres_lo) * vres_mu
        v = v + (v_first - v) * jax.nn.sigmoid(v0 + vres_lo @ v2)
    kk = split_heads(k * k_k, n_h).astype(jnp.float32)
    kk = kk * lax.rsqrt(jnp.maximum(jnp.sum(kk * kk, axis=-1, keepdims=True), 1e-24))
    k = k * (1.0 + (a - 1.0) * k_a)
    rh, kh, vh = split_heads(r, n_h), split_heads(k, n_h), split_heads(v, n_h)
    o = rwkv7_scan(rh, split_heads(log_w, n_h), kh, vh, kk, split_heads(a, n_h))
    mean = jnp.mean(o, axis=-1, keepdims=True)
    var = jnp.mean(jnp.square(o - mean), axis=-1, keepdims=True)
    o = merge_heads((o - mean) * lax.rsqrt(var + RWKV_LN_EPS)) * ln_w + ln_b
    bonus = jnp.sum(rh * kh * r_k, axis=-1, keepdims=True) * vh
    o = o + merge_heads(bonus)
    return (o * g).astype(p.dtype), v_first


def setup_inputs(seed: int = 0) -> dict:
    key = jax.random.key(seed)
    ks = iter(jax.random.split(key, 40))
    f32 = jnp.float32

    def nrm(shape, scale):
        return jax.random.normal(next(ks), shape, f32) * scale

    def uni(shape, lo, hi):
        return jax.random.uniform(next(ks), shape, f32, lo, hi)

    L, D, W = DEPTH, D_MODEL, MIX_WIDTH
    return {
        'x': nrm((BATCH, SEQ, D), 1.0),
        'c': nrm((BATCH, D), 1.0),
        'w_ada': nrm((L, D, 6 * D), 0.5 * D ** -0.5),
        'b_ada': nrm((L, 6 * D), 0.02),
        'norm1_g': 1.0 + nrm((L, D), 0.02),
        'norm2_g': 1.0 + nrm((L, D), 0.02),
        'w_in': nrm((L, D, N_IN), D ** -0.5),
        'fox_b_f': uni((L, FOX_HEADS), 1.0, 4.0),
        'fox_q_gain': 1.0 + nrm((L, FOX_HEAD_DIM), 0.02),
        'fox_k_gain': 1.0 + nrm((L, FOX_HEAD_DIM), 0.02),
        'hgrn_lb': nrm((L, W), 1.0),
        'hgrn_o_gain': 1.0 + nrm((L, HGRN_VAL_DIM), 0.02),
        'rwkv_mu': uni((L, sum(RWKV_SIZES)), 0.0, 1.0),
        'rwkv_w0': uni((L, W), -6.0, -1.0),
        'rwkv_w2': nrm((L, RWKV_DECAY_RANK, W), 0.1 * RWKV_DECAY_RANK ** -0.5),
        'rwkv_a0': nrm((L, W), 0.1),
        'rwkv_a2': nrm((L, RWKV_ICLR_RANK, W), RWKV_ICLR_RANK ** -0.5),
        'rwkv_g2': nrm((L, RWKV_GATE_RANK, W), RWKV_GATE_RANK ** -0.5),
        'rwkv_k_k': 0.85 + nrm((L, W), 0.02),
        'rwkv_k_a': 1.0 + nrm((L, W), 0.02),
        'rwkv_r_k': nrm((L, RWKV_HEADS, RWKV_HEAD_DIM), 0.1),
        'rwkv_ln_w': 1.0 + nrm((L, W), 0.02),
        'rwkv_ln_b': nrm((L, W), 0.02),
        'rwkv_vres_down': nrm((L - 1, D, RWKV_VRES_RANK), D ** -0.5),
        'rwkv_vres_mu': uni((L - 1, RWKV_VRES_RANK), 0.0, 1.0),
        'rwkv_v0': nrm((L - 1, W), 0.1),
        'rwkv_v2': nrm((L - 1, RWKV_VRES_RANK, W), RWKV_VRES_RANK ** -0.5),
        'w_branch': nrm((L, N_BRANCH, W, D), W ** -0.5),
        'w_out': nrm((L, D, D), D ** -0.5),
        'w_up': nrm((L, D, 2 * D_FF), D ** -0.5),
        'conv_w': nrm((L, CONV_WIDTH, 2 * D_FF), CONV_WIDTH ** -0.5),
        'conv_b': nrm((L, 2 * D_FF), 0.02),
        'w_down': nrm((L, D_FF, D), D_FF ** -0.5),
    }


def reference(x, c, w_ada, b_ada, norm1_g, norm2_g, w_in, fox_b_f, fox_q_gain, fox_k_gain,
              hgrn_lb, hgrn_o_gain, rwkv_mu, rwkv_w0, rwkv_w2, rwkv_a0, rwkv_a2, rwkv_g2,
              rwkv_k_k, rwkv_k_a, rwkv_r_k, rwkv_ln_w, rwkv_ln_b, rwkv_vres_down, rwkv_vres_mu,
              rwkv_v0, rwkv_v2, w_branch, w_out, w_up, conv_w, conv_b, w_down):
    lb_prob = jax.nn.softmax(hgrn_lb.astype(jnp.float32), axis=0)
    hgrn_lower = jnp.cumsum(lb_prob, axis=0) - lb_prob[0]
    cond = jax.nn.silu(c)
    v_first = None
    for l in range(DEPTH):
        mod = cond @ w_ada[l] + b_ada[l]
        sh1, sc1, gt1, sh2, sc2, gt2 = [m[:, None, :] for m in jnp.split(mod, 6, axis=-1)]

        h = rms_norm(x) * norm1_g[l] * (1.0 + sc1) + sh1
        if l == 0:
            proj = h @ w_in[l]
            vres = None
        else:
            proj = h @ jnp.concatenate([w_in[l], rwkv_vres_down[l - 1]], axis=1)
            proj, vres_lo = proj[..., :N_IN], proj[..., N_IN:]
            vres = (vres_lo, rwkv_vres_mu[l - 1], rwkv_v0[l - 1], rwkv_v2[l - 1])
        fox_p, hgrn_p, rwkv_p, gate_p = split_cols(proj, GROUP_SIZES)

        fq, fk, fv, ff = split_cols(fox_p, FOX_SIZES)
        log_f = jax.nn.log_sigmoid((ff + fox_b_f[l]).astype(jnp.float32))
        y_fox = merge_heads(fox_attention(split_heads(fq, FOX_HEADS), split_heads(fk, FOX_HEADS),
                                          split_heads(fv, FOX_HEADS), log_f,
                                          fox_q_gain[l], fox_k_gain[l]))
        y_hgrn = hgrn2_branch(hgrn_p, hgrn_lower[l], hgrn_o_gain[l])
        y_rwkv, v_first = rwkv7_branch(rwkv_p, v_first, vres, rwkv_mu[l], rwkv_w0[l], rwkv_w2[l],
                                       rwkv_a0[l], rwkv_a2[l], rwkv_g2[l], rwkv_k_k[l], rwkv_k_a[l],
                                       rwkv_r_k[l], rwkv_ln_w[l], rwkv_ln_b[l])

        g_fox, g_hgrn, g_rwkv = [jax.nn.sigmoid(g) for g in split_cols(gate_p, GATE_SIZES)]
        merged = (g_fox * (y_fox @ w_branch[l, 0])
                  + g_hgrn * (y_hgrn @ w_branch[l, 1])
                  + g_rwkv * (y_rwkv @ w_branch[l, 2]))
        x = x + gt1 * (merged @ w_out[l])

        h = rms_norm(x) * norm2_g[l] * (1.0 + sc2) + sh2
        u = causal_dwconv(h @ w_up[l], conv_w[l], conv_b[l])
        u_val, u_gate = jnp.split(u, 2, axis=-1)
        x = x + gt2 * ((jax.nn.silu(u_gate) * u_val) @ w_down[l])
    return x
```

```python
import contextlib
import os
HGCUT = int(os.environ.get('HGCUT', '9'))
HGSUB = int(os.environ.get('HGSUB', '9'))
import numpy as np
import ml_dtypes
import concourse.bass as bass
import concourse.mybir as mybir
from concourse.bass_utils import run_bass_kernel_spmd

F32 = mybir.dt.float32
BF16 = mybir.dt.bfloat16
AF = mybir.ActivationFunctionType
ALU = mybir.AluOpType

NDSEM = 24
COMPUTE = ('pe', 'act', 'dve', 'pool')


class T:
    __slots__ = ('ap', 'name', 'w', 'r', 'ut', 'xr')

    def __init__(self, ap, name='', ut=False, xr=False):
        self.ap = ap
        self.name = name
        self.w = None
        self.r = {}
        self.ut = ut
        self.xr = xr

    def __getitem__(self, k):
        return Vw(self, self.ap[k])

    @property
    def v(self):
        return Vw(self, self.ap)


class Vw:
    __slots__ = ('t', 'ap')

    def __init__(self, t, ap):
        self.t = t
        self.ap = ap

    def __getitem__(self, k):
        return Vw(self.t, self.ap[k])

    def re(self, pat, **kw):
        return Vw(self.t, self.ap.rearrange(pat, **kw))

    def bc(self, shape):
        return Vw(self.t, self.ap.to_broadcast(list(shape)))

    def cast(self, dt):
        return Vw(self.t, self.ap.bitcast(dt))

    def us(self, axis):
        return Vw(self.t, self.ap.unsqueeze(axis))


class Ins:
    __slots__ = ('eng', 'idx', 'fn', 'waits', 'signal', 'vc', 'key', 'val', 'isdma')


class Sched:
    def __init__(self):
        self.q = {e: [] for e in COMPUTE + ('sp',)}
        self.clock = {e: {} for e in COMPUTE + ('sp',)}
        self.dcount = [0] * NDSEM
        self.dlast = [None] * NDSEM
        self.dnext = 0

    def _add(self, eng, fn, reads, writes, isdma):
        ins = Ins()
        ins.eng = eng
        ins.idx = len(self.q[eng])
        ins.fn = fn
        ins.waits = []
        ins.signal = False
        ins.isdma = isdma
        clock = self.clock[eng]
        deps = []
        if isdma:
            j = self.dnext
            self.dnext = (j + 1) % NDSEM
            self.dcount[j] += 1
            ins.key = ('d', j)
            ins.val = 16 * self.dcount[j]
            if self.dlast[j] is not None:
                deps.append((self.dlast[j], 'sem'))
            self.dlast[j] = ins
            ins.signal = True
        else:
            ins.key = eng
            ins.val = ins.idx + 1
        for t in reads:
            if t.w is not None:
                deps.append((t.w, 'raw'))
            if t.xr:
                for r in t.r.values():
                    if r.eng != eng:
                        deps.append((r, 'rar'))
        for t in writes:
            if t.w is not None:
                deps.append((t.w, 'waw'))
            for r in t.r.values():
                deps.append((r, 'war'))
        for p, kind in deps:
            if (not p.isdma) and (not isdma) and p.eng == eng:
                if eng == 'pe' or kind != 'raw':
                    continue
            if clock.get(p.key, 0) >= p.val:
                continue
            p.signal = True
            ins.waits.append(p)
            for k, v in p.vc.items():
                if clock.get(k, 0) < v:
                    clock[k] = v
        vc = dict(clock)
        vc[ins.key] = ins.val
        ins.vc = vc
        for t in reads:
            t.r[ins.key] = ins
        for t in writes:
            t.w = ins
            t.r = {}
        self.q[eng].append(ins)
        return ins

    def op(self, eng, fn, reads=(), writes=()):
        return self._add(eng, fn, reads, writes, False)

    def dma(self, fn, reads=(), writes=(), queue='sp'):
        return self._add(queue, fn, reads, writes, True)

    def barrier(self):
        lasts = []
        for e in COMPUTE:
            for ins in reversed(self.q[e]):
                if ins.fn is not None and not ins.isdma:
                    lasts.append(ins)
                    break
        for j in range(NDSEM):
            if self.dlast[j] is not None:
                lasts.append(self.dlast[j])
        for e in COMPUTE + ('sp',):
            ins = Ins()
            ins.eng = e
            ins.idx = len(self.q[e])
            ins.fn = None
            ins.waits = []
            ins.signal = False
            ins.isdma = False
            ins.key = e
            ins.val = ins.idx
            clock = self.clock[e]
            for p in lasts:
                if (not p.isdma) and p.eng == e:
                    continue
                if clock.get(p.key, 0) >= p.val:
                    continue
                p.signal = True
                ins.waits.append(p)
                for k, v in p.vc.items():
                    if clock.get(k, 0) < v:
                        clock[k] = v
            ins.vc = dict(clock)
            self.q[e].append(ins)

    def emit(self, sems):
        sigcount = {}
        for e in COMPUTE:
            c = 0
            for ins in self.q[e]:
                if ins.fn is not None and ins.signal and not ins.isdma:
                    c += 1
                    sigcount[id(ins)] = c

        def run(eng_name, e):
            for ins in self.q[eng_name]:
                w = {}
                for p in ins.waits:
                    v = p.val if p.isdma else sigcount[id(p)]
                    if w.get(p.key, 0) < v:
                        w[p.key] = v
                for k, v in w.items():
                    e.wait_ge(sems[k], v)
                if ins.fn is None:
                    continue
                r = ins.fn(e)
                if ins.isdma:
                    r.then_inc(sems[ins.key], 16)
                elif ins.signal:
                    r.then_inc(sems[ins.key], 1)
        return run


def _ap(x):
    return x.ap if isinstance(x, Vw) else x


def _ts(*xs):
    return [x.t for x in xs if isinstance(x, Vw) and not x.t.ut]


class Ops:
    def __init__(self, S):
        self.S = S

    def tt(self, eng, out, a, b, op):
        self.S.op(eng, lambda e: e.tensor_tensor(out=out.ap, in0=a.ap, in1=b.ap, op=op), _ts(a, b), _ts(out))

    def ts(self, eng, out, a, s1, s2, op0, op1=None):
        if op1 is None:
            self.S.op(eng, lambda e: e.tensor_scalar(out=out.ap, in0=a.ap, scalar1=_ap(s1), scalar2=None, op0=op0),
                      _ts(a, s1), _ts(out))
        else:
            self.S.op(eng, lambda e: e.tensor_scalar(out=out.ap, in0=a.ap, scalar1=_ap(s1), scalar2=_ap(s2), op0=op0, op1=op1),
                      _ts(a, s1, s2), _ts(out))

    def stt(self, eng, out, a, s, b, op0, op1):
        self.S.op(eng, lambda e: e.scalar_tensor_tensor(out=out.ap, in0=a.ap, scalar=_ap(s), in1=b.ap, op0=op0, op1=op1),
                  _ts(a, s, b), _ts(out))

    def cp(self, eng, out, a):
        if eng == 'act':
            self.S.op(eng, lambda e: e.copy(out=out.ap, in_=a.ap), _ts(a), _ts(out))
        else:
            self.S.op(eng, lambda e: e.tensor_copy(out=out.ap, in_=a.ap), _ts(a), _ts(out))

    def act(self, out, a, func, bias=None, scale=None):
        kw = {}
        if bias is not None:
            kw['bias'] = _ap(bias)
        if scale is not None:
            kw['scale'] = _ap(scale)
        self.S.op('act', lambda e: e.activation(out=out.ap, in_=a.ap, func=func, **kw), _ts(a, bias, scale), _ts(out))

    def recip(self, out, a):
        self.S.op('dve', lambda e: e.reciprocal(out=out.ap, in_=a.ap), _ts(a), _ts(out))

    def rsqrt(self, out, a, add, tmp):
        self.act(tmp, a, AF.Sqrt, bias=add)
        self.recip(out, tmp)

    def scan(self, out, d0, d1, init, op0=ALU.mult, op1=ALU.add):
        self.S.op('dve', lambda e: e.tensor_tensor_scan(out=out.ap, data0=d0.ap, data1=d1.ap, initial=_ap(init), op0=op0, op1=op1),
                  _ts(d0, d1, init), _ts(out))

    def memset(self, eng, out, val):
        self.S.op(eng, lambda e: e.memset(out.ap, val), [], _ts(out))

    def mm(self, out, lhsT, rhs, start=True, stop=True):
        self.S.op('pe', lambda e: e.matmul(out.ap, lhsT.ap, rhs.ap, start=start, stop=stop), _ts(lhsT, rhs), _ts(out))

    def tr(self, out, a, ident):
        self.S.op('pe', lambda e: e.transpose(out=out.ap, in_=a.ap, identity=ident.ap), _ts(a, ident), _ts(out))

    def dma(self, out, a, queue='sp'):
        self.S.dma(lambda e: e.dma_start(out=out.ap, in_=a.ap), _ts(a), _ts(out), queue=queue)


D = 1024
KC = 8
W_MIX = 512
DFF = 2816
NFF = 22
EPS = 1e-6
LN_EPS = 64e-5
NCOLS = 243
C_N1G, C_N2G, C_FBF, C_QG, C_KG, C_OG, C_MU, C_W0, C_A0, C_KK, C_KA, C_RK, C_LNW, C_LNB, C_V0, C_VMU, C_CW, C_CB = \
    0, 8, 16, 17, 18, 19, 20, 34, 38, 42, 46, 50, 54, 58, 62, 66, 67, 199
J_FQ, J_FK, J_FV, J_HQ, J_HF, J_HI, J_HO, J_RR, J_RK, J_RV, J_LORA, J_GLO, J_GATE, J_MISC = 0, 4, 8, 12, 16, 20, 24, 28, 32, 36, 40, 41, 42, 66
NCH = 67
K_ID, K_OBD, K_ONE, K_BD, K_UI8, K_NSL8, K_ID8, K_NSU4, K_SU4, K_ME, K_MO, K_W = 0, 128, 256, 384, 512, 1024, 1536, 2048, 2304, 2560, 3072, 3584
KB_ID, KB_ONE, KB_TRI, KB_SEL, KB_W = 0, 128, 256, 384, 1408


def build(S_LEN, L, dbg=(), STOP=99):
    NT = S_LEN // 512
    NCK = S_LEN // 64
    nc = bass.Bass("TRN2", target_bir_lowering=False)
    S = Sched()
    o = Ops(S)

    def dram(name, shape, dt, kind="Internal"):
        if name in dbg:
            kind = "ExternalOutput"
        return T(nc.dram_tensor(name, list(shape), dt, kind=kind).ap(), name, ut=True)

    x_in = dram("x", [S_LEN, D], F32, "ExternalInput")
    cT_in = dram("cT", [128, KC], F32, "ExternalInput")
    cols_in = dram("cols", [L, 128, NCOLS], F32, "ExternalInput")
    lbT_in = dram("lbT", [128, 16], F32, "ExternalInput")
    wada_in = dram("wada", [L, 128, KC, 6 * D], F32, "ExternalInput")
    bada_in = dram("bada", [L, 1, 6 * D], F32, "ExternalInput")
    win_in = dram("win", [L, NCH, 128, KC * 128], F32, "ExternalInput")
    lora_in = dram("lora", [L, 128, 512], F32, "ExternalInput")
    g2_in = dram("g2", [L, 128, 512], F32, "ExternalInput")
    v2_in = dram("v2", [L, 64, 512], F32, "ExternalInput")
    wbr_in = dram("wbr", [L, 128, 12, D], F32, "ExternalInput")
    wout_in = dram("wout", [L, 128, KC, D], F32, "ExternalInput")
    wup_in = dram("wup", [L, NFF, 128, KC * 256], F32, "ExternalInput")
    wdn_in = dram("wdn", [L, 128, NFF, D], F32, "ExternalInput")
    cst_in = dram("cst", [128, K_W], F32, "ExternalInput")
    cstb_in = dram("cstb", [128, KB_W], F32, "ExternalInput")
    y_out = dram("y", [S_LEN, D], F32, "ExternalOutput")

    XT = dram("XT", [128, KC, S_LEN], F32)
    QKT = dram("QKT", [8, 128, S_LEN], BF16)
    FVT = dram("FVT", [4, 128, S_LEN], BF16)
    GT = dram("GT", [24, 128, S_LEN], BF16)
    HQ = dram("HQ", [4, 128, S_LEN], F32)
    HK = dram("HK", [4, 128, S_LEN], F32)
    HI = dram("HI", [4, 128, S_LEN], F32)
    HO = dram("HO", [4, 128, S_LEN], BF16)
    RKR = dram("RKR", [4, 128, NCK, 2, 64], F32)
    RKT = dram("RKT", [4, 128, S_LEN], F32)
    RBT = dram("RBT", [4, 128, S_LEN], F32)
    RVT = dram("RVT", [4, 128, S_LEN], F32)
    RBO = dram("RBO", [4, 128, S_LEN], F32)
    RG = dram("RG", [4, 128, S_LEN], BF16)
    FTD = dram("FTD", [8, S_LEN], F32)
    VF = dram("VF", [4, 128, S_LEN], F32)
    YB = dram("YB", [12, 128, S_LEN], BF16)
    H2 = dram("H2", [KC, 128, S_LEN], BF16)
    ACTT = dram("ACTT", [NFF, 128, S_LEN], BF16)

    arena = nc.alloc_sbuf_tensor("arena", [128, 53200], F32).ap()
    st = {'off': 0, 'mark': 0}

    def sb(n, name='', dt=F32):
        words = n if dt == F32 else (n + 1) // 2
        assert st['off'] + words <= 53200, ("SBUF arena overflow", name, st['off'], words)
        a = arena[:, st['off']: st['off'] + words]
        st['off'] += words
        assert st['off'] <= 53200, ("SBUF arena overflow", name, st['off'])
        if dt != F32:
            a = a.bitcast(dt)
        return T(a, name)

    psb = [T(nc.alloc_psum_tensor(f"ps{i}", [128, 512], F32).ap(), f"ps{i}", xr=True) for i in range(8)]
    pst = {'i': 0}

    pst['n'] = 8

    def ps():
        p = psb[pst['i'] % pst['n']]
        pst['i'] += 1
        return p

    cst = sb(K_W, 'cst')
    o.dma(cst.v, cst_in.v)
    ident = cst[:, K_ID:K_ID + 128]
    onesbd = cst[:, K_OBD:K_OBD + 128]
    ones = cst[:, K_ONE:K_ONE + 128]
    bd = cst[:, K_BD:K_BD + 128]
    ui8 = cst[:, K_UI8:K_UI8 + 512]
    ui4 = cst[:, K_UI8:K_UI8 + 256]
    nsl8 = cst[:, K_NSL8:K_NSL8 + 512]
    id8 = cst[:, K_ID8:K_ID8 + 512]
    nsu4 = cst[:, K_NSU4:K_NSU4 + 256]
    su4 = cst[:, K_SU4:K_SU4 + 256]
    me8 = cst[:, K_ME:K_ME + 512]
    mo8 = cst[:, K_MO:K_MO + 512]
    cstb = sb(KB_W, 'cstb', BF16)
    identb = cstb[:, KB_ID:KB_ID + 128]
    onesb = cstb[:, KB_ONE:KB_ONE + 128]
    trib = cstb[:, KB_TRI:KB_TRI + 128]
    selb = cstb[:, KB_SEL:KB_SEL + 1024]
    ones512 = sb(512, 'ones512')
    o.memset('dve', ones512.v, 1.0)
    colsT = [sb(NCOLS, f'cols{l}') for l in range(L)]
    for l in range(L):
        o.dma(colsT[l].v, cols_in[l])
    modT = [sb(48, f'mod{l}') for l in range(L)]
    derT = [sb(64, f'der{l}') for l in range(L)]
    lbT = sb(16, 'lbT')
    lowT = sb(16, 'lowT')
    o.dma(lbT.v, lbT_in.v)
    ex = sb(16, 'lbexp')
    sm = sb(4, 'lbsum')
    rs = sb(4, 'lbrs')
    oml = sb(16, 'oml')
    cT = sb(KC, 'cT')
    condT = sb(KC, 'condT')
    persist_mark = st['off']

    st['mark'] = st['off']
    cbt = sb(KB_W, 'cbtmp')
    o.dma(cbt.v, cstb_in.v)
    o.cp('dve', cstb.v, cbt.v)
    xin = [sb(D, f'xin{i}') for i in range(2)]
    xst = [sb(KC * 512, f'xst{i}') for i in range(2)]
    for i in range(NT):
        xs = xst[i % 2]
        for q in range(4):
            xi = xin[q % 2]
            t0 = i * 512 + q * 128
            o.dma(xi.v, x_in[t0:t0 + 128, :])
            for h in range(2):
                p = ps()
                for c in range(4):
                    kc = h * 4 + c
                    o.tr(p[:, c * 128:(c + 1) * 128], xi[:, kc * 128:(kc + 1) * 128], ident)
                dst = xs.v.re("p (k t) -> p k t", k=KC)[:, h * 4:(h + 1) * 4, q * 128:(q + 1) * 128]
                o.cp('act' if h == 0 else 'dve', dst, p.v.re("p (c t) -> p c t", c=4))
        o.dma(XT[:, :, i * 512:(i + 1) * 512], xs.v.re("p (k t) -> p k t", k=KC))

    o.dma(cT.v, cT_in.v)
    o.act(condT.v, cT.v, AF.Silu)
    wst = [sb(KC * 512, f'wst{i}') for i in range(2)]
    bada = sb(6 * D, 'bada')
    modrow = sb(6 * D, 'modrow')
    for l in range(L):
        o.dma(bada[0:1, :], bada_in[l])
        for blk in range(12):
            w = wst[blk % 2]
            o.dma(w.v.re("p (k n) -> p k n", k=KC), wada_in[l][:, :, blk * 512:(blk + 1) * 512])
            p = ps()
            for kc in range(KC):
                o.mm(p[0:1, :], condT[:, kc:kc + 1], w[:, kc * 512:(kc + 1) * 512], start=(kc == 0), stop=(kc == KC - 1))
            o.tt('dve', modrow[0:1, blk * 512:(blk + 1) * 512], p[0:1, :], bada[0:1, blk * 512:(blk + 1) * 512], ALU.add)
        p = ps()
        for j in range(48):
            o.mm(p[:, j:j + 1], modrow[0:1, j * 128:(j + 1) * 128], ones[0:1, 0:1])
        o.cp('dve', modT[l].v, p[:, 0:48])
        dr = derT[l]
        cl = colsT[l]
        o.ts('dve', dr[:, 0:8], modT[l][:, 8:16], 1.0, 32.0, ALU.add, ALU.mult)
        o.tt('dve', dr[:, 0:8], dr[:, 0:8], cl[:, C_N1G:C_N1G + 8], ALU.mult)
        o.ts('dve', dr[:, 8:16], modT[l][:, 32:40], 1.0, 32.0, ALU.add, ALU.mult)
        o.tt('dve', dr[:, 8:16], dr[:, 8:16], cl[:, C_N2G:C_N2G + 8], ALU.mult)
        o.ts('dve', dr[:, 17:21], cl[:, C_KA:C_KA + 4], -1.0, 1.0, ALU.mult, ALU.add)
        o.ts('dve', dr[:, 21:22], cl[:, C_QG:C_QG + 1], 8.0, None, ALU.mult)
        o.ts('dve', dr[:, 22:23], cl[:, C_KG:C_KG + 1], 8.0, None, ALU.mult)
        o.ts('dve', dr[:, 23:24], cl[:, C_OG:C_OG + 1], 8.0, None, ALU.mult)
        o.ts('dve', dr[:, 24:28], cl[:, C_MU + 0:C_MU + 4], -1.0, None, ALU.mult)
    o.act(ex.v, lbT.v, AF.Exp)
    o.cp('dve', sm.v, ex[:, 0:4])
    for l in range(1, L):
        o.tt('dve', sm.v, sm.v, ex[:, l * 4:(l + 1) * 4], ALU.add)
    o.recip(rs.v, sm.v)
    for l in range(L):
        o.tt('dve', ex[:, l * 4:(l + 1) * 4], ex[:, l * 4:(l + 1) * 4], rs.v, ALU.mult)
    o.memset('dve', lowT[:, 0:4], 0.0)
    for l in range(1, L):
        o.tt('dve', lowT[:, l * 4:(l + 1) * 4], lowT[:, (l - 1) * 4:l * 4], ex[:, l * 4:(l + 1) * 4], ALU.add)
    o.ts('dve', oml.v, lowT.v, -1.0, 1.0, ALU.mult, ALU.add)
    ctx = dict(nc=nc, S=S, o=o, sb=sb, ps=ps, st=st, L=L, NT=NT, NCK=NCK, S_LEN=S_LEN)
    NEG_C0 = -0.6065306597126334
    WK_N = 16
    sel_in = None

    def emit_layer(l):
        S.barrier()
        st['off'] = persist_mark
        cl = colsT[l]
        dr = derT[l]
        md = modT[l]
        Fcar = sb(2, 'Fcar')
        hsc = sb(3 * 4 * NCK, 'hsc')
        hscv = hsc.v.re("p (a c n) -> p a c n", a=3, c=4)
        gam = sb(4 * NCK, 'gam')
        gamv = gam.v.re("p (c n) -> p c n", c=4)
        ded = [sb(512, f'ded{i}') for i in range(7)]
        wkp = [sb(512, f'wk{i}') for i in range(WK_N)]
        wki = {'i': 0}

        def wk():
            t = wkp[wki['i'] % WK_N]
            wki['i'] += 1
            return t
        wkb = [sb(512, f'wkb{i}', BF16) for i in range(6)]
        wkbi = {'i': 0}

        def wkbf():
            t = wkb[wkbi['i'] % 6]
            wkbi['i'] += 1
            return t
        layer_mark = st['off']

        hts = [sb(KC * 512, f'ht{i}', BF16) for i in range(NT)]

        def norm_tile(xtile, Acol0, Bcol0, dst, sqt):
            sq = sqt
            o.act(sq.v, xtile.v, AF.Square)
            p = ps()
            for kc in range(KC):
                o.mm(p.v, onesb, sq[:, kc * 512:(kc + 1) * 512], start=(kc == 0), stop=(kc == KC - 1))
            rstd = wk()
            rtmp = wk()
            o.rsqrt(rstd[:, 0:512], p.v, 1024.0 * EPS, rtmp[:, 0:512])
            x3 = xtile.v.re("p (k t) -> p k t", k=KC)
            o.tt('dve', x3, x3, rstd[:, 0:512].us(1).bc([128, KC, 512]), ALU.mult)
            for kc in range(KC):
                o.act(dst[:, kc * 512:(kc + 1) * 512], xtile[:, kc * 512:(kc + 1) * 512], AF.Identity,
                      bias=md[:, Bcol0 + kc:Bcol0 + kc + 1], scale=dr[:, Acol0 + kc:Acol0 + kc + 1])

        nmark = st['off']
        xt = [sb(KC * 512, f'xt{i}') for i in range(2)]
        sqt = sb(KC * 512, 'sqt', BF16)
        for i in range(NT):
            x = xt[i % 2]
            o.dma(x.v.re("p (k t) -> p k t", k=KC), XT[:, :, i * 512:(i + 1) * 512])
            norm_tile(x, 0, 0, hts[i], sqt)
        S.barrier()
        st['off'] = nmark
        lorab = sb(512, 'lorab', BF16)
        g2b = sb(512, 'g2b', BF16)
        v2b = sb(512, 'v2b', BF16)
        wtmp = wk()
        o.dma(wtmp[:, 0:512], lora_in[l])
        o.cp('pool', lorab.v, wtmp[:, 0:512])
        wtmp = wk()
        o.dma(wtmp[:, 0:512], g2_in[l])
        o.cp('pool', g2b.v, wtmp[:, 0:512])
        wtmp = wk()
        o.dma(wtmp[0:64, 0:512], v2_in[l])
        o.cp('pool', v2b[0:64, :], wtmp[0:64, 0:512])
        twa = sb(S_LEN, 'twa', BF16)
        sgl = sb(S_LEN, 'sgl', BF16)
        vlo = sb(S_LEN, 'vlo', BF16)
        raws = [sb(514, f'raw{i}') for i in range(3)]
        Cx = sb(513, 'Cx')
        o.memset('dve', Cx[:, 0:1], 0.0)
        wf = [sb(KC * 128, 'wf0')] * 2
        wfi = {'i': 0}
        wbt = [[sb(KC * 128, f'wb{g}_{k}', BF16) for k in range(4)] for g in range(2)]
        grp = {'i': 0}

        def load_group(js):
            g = grp['i'] % 2
            grp['i'] += 1
            outs = []
            for k, j in enumerate(js):
                stg = wf[wfi['i'] % 2]
                wfi['i'] += 1
                o.dma(stg.v, win_in[l][j])
                o.cp('pool', wbt[g][k].v, stg.v)
                outs.append(wbt[g][k])
            return outs

        def proj(wt, i):
            p = ps()
            for kc in range(KC):
                o.mm(p.v, wt[:, kc * 128:(kc + 1) * 128], hts[i][:, kc * 512:(kc + 1) * 512], start=(kc == 0), stop=(kc == KC - 1))
            return p

        def shiftmix(p, raw, mu, r0=0, r1=128, dst=None):
            o.cp('act', raw[r0:r1, 1:513], p[r0:r1, :])
            d = wk()
            o.tt('dve', d[r0:r1, 0:512], raw[r0:r1, 0:512], raw[r0:r1, 1:513], ALU.subtract)
            m = dst if dst is not None else wk()
            o.stt('dve', m[r0:r1, 0:512], d[r0:r1, 0:512], mu, raw[r0:r1, 1:513], ALU.mult, ALU.add)
            o.cp('act', raw[r0:r1, 0:1], raw[r0:r1, 512:513])
            return m

        def reset_raws():
            for r in raws:
                o.memset('pool', r[:, 0:2], 0.0)

        def tsl(i):
            return slice(i * 512, (i + 1) * 512)

        (wm,) = load_group([J_MISC])
        reset_raws()
        for i in range(NT):
            p = proj(wm, i)
            t1 = wk()
            o.act(t1[0:8, 0:512], p[0:8, :], AF.Sigmoid, bias=cl[0:8, C_FBF:C_FBF + 1])
            o.act(t1[0:8, 0:512], t1[0:8, 0:512], AF.Ln)
            t2 = wk()
            o.scan(t2[0:8, 0:512], ones512[0:8, :], t1[0:8, 0:512], 0.0 if i == 0 else Fcar[0:8, 0:1])
            o.cp('dve', Fcar[0:8, 0:1], t2[0:8, 511:512])
            o.dma(FTD[0:8, tsl(i)], t2[0:8, 0:512])
            if l > 0:
                m = shiftmix(p, raws[0], cl[32:64, C_VMU:C_VMU + 1], 32, 64)
                o.cp('dve', vlo[32:64, tsl(i)], m[32:64, 0:512])
        wl, wg = load_group([J_LORA, J_GLO])
        reset_raws()
        for i in range(NT):
            p = proj(wl, i)
            m = shiftmix(p, raws[0], cl[:, C_MU + 12:C_MU + 13])
            o.act(twa[0:64, tsl(i)], m[0:64, 0:512], AF.Tanh)
            o.cp('dve', twa[64:128, tsl(i)], m[64:128, 0:512])
            p = proj(wg, i)
            m = shiftmix(p, raws[1], cl[:, C_MU + 13:C_MU + 14])
            o.act(sgl[:, tsl(i)], m[:, 0:512], AF.Sigmoid)
        for c in range(4):
            wr, wkk_, wv = load_group([J_RR + c, J_RK + c, J_RV + c])
            reset_raws()
            cs = slice(c * 128, (c + 1) * 128)
            for i in range(NT):
                pr = proj(wr, i)
                pk = proj(wkk_, i)
                pv = proj(wv, i)
                r_m = shiftmix(pr, raws[0], cl[:, C_MU + c:C_MU + c + 1], dst=ded[0])
                k_m = shiftmix(pk, raws[1], cl[:, C_MU + 4 + c:C_MU + 5 + c], dst=ded[1])
                v_m = shiftmix(pv, raws[2], cl[:, C_MU + 8 + c:C_MU + 9 + c], dst=ded[2])
                pa = ps()
                o.mm(pa.v, lorab[64:128, cs], twa[64:128, tsl(i)])
                a = ded[3]
                o.act(a[:, 0:512], pa.v, AF.Sigmoid, bias=cl[:, C_A0 + c:C_A0 + c + 1])
                pw = ps()
                o.mm(pw.v, lorab[0:64, cs], twa[0:64, tsl(i)])
                sgw = wk()
                o.act(sgw[:, 0:512], pw.v, AF.Sigmoid, bias=cl[:, C_W0 + c:C_W0 + c + 1])
                o.scan(Cx[:, 1:513], ones512.v, sgw[:, 0:512], 0.0)
                ref = Cx[:, 0:512].re("p (c t) -> p c t", t=64)[:, :, 0:1].bc([128, 8, 64])
                Dd = wk()
                o.tt('dve', Dd[:, 0:512].re("p (c t) -> p c t", t=64), Cx[:, 1:513].re("p (c t) -> p c t", t=64), ref, ALU.subtract)
                Dm = wk()
                o.tt('dve', Dm[:, 0:512].re("p (c t) -> p c t", t=64), Cx[:, 0:512].re("p (c t) -> p c t", t=64), ref, ALU.subtract)
                E1 = ded[4]
                E2 = wk()
                E3 = ded[5]
                o.act(E1[:, 0:512], Dd[:, 0:512], AF.Exp, scale=NEG_C0)
                o.act(E2[:, 0:512], Dm[:, 0:512], AF.Exp, scale=NEG_C0)
                o.act(E3[:, 0:512], Dd[:, 0:512], AF.Exp, scale=-NEG_C0)
                o.cp('pool', gamv[:, c, i * 8:(i + 1) * 8], E1[:, 0:512].re("p (c t) -> p c t", t=64)[:, :, 63])
                kk = wk()
                o.ts('pool', kk[:, 0:512], k_m[:, 0:512], cl[:, C_KK + c:C_KK + c + 1], None, ALU.mult)
                sq = wk()
                o.tt('pool', sq[:, 0:512], kk[:, 0:512], kk[:, 0:512], ALU.mult)
                pss = ps()
                o.mm(pss.v, onesbd, sq[:, 0:512])
                rinv = wk()
                rtmp = wk()
                o.act(rtmp[:, 0:512], pss.v, AF.Sqrt)
                o.ts('dve', rtmp[:, 0:512], rtmp[:, 0:512], 1e-12, None, ALU.max)
                o.recip(rinv[:, 0:512], rtmp[:, 0:512])
                kap = wk()
                o.tt('dve', kap[:, 0:512], kk[:, 0:512], rinv[:, 0:512], ALU.mult)
                tp = wk()
                o.ts('pool', tp[:, 0:512], a[:, 0:512], cl[:, C_KA + c:C_KA + c + 1], dr[:, 17 + c:18 + c], ALU.mult, ALU.add)
                kf = ded[6]
                o.tt('pool', kf[:, 0:512], k_m[:, 0:512], tp[:, 0:512], ALU.mult)
                bb = wk()
                o.tt('pool', bb[:, 0:512], a[:, 0:512], kap[:, 0:512], ALU.mult)
                ot = wk()
                o.tt('dve', ot[:, 0:512], r_m[:, 0:512], E1[:, 0:512], ALU.mult)
                o.dma(RKR[c][:, i * 8:(i + 1) * 8, 1, :], ot[:, 0:512].re("p (c t) -> p c t", t=64))
                ot = wk()
                o.tt('dve', ot[:, 0:512], kap[:, 0:512], E2[:, 0:512], ALU.mult)
                o.dma(RKR[c][:, i * 8:(i + 1) * 8, 0, :], ot[:, 0:512].re("p (c t) -> p c t", t=64))
                ot = wk()
                o.tt('pool', ot[:, 0:512], kf[:, 0:512], E3[:, 0:512], ALU.mult)
                o.dma(RKT[c][:, tsl(i)], ot[:, 0:512])
                ot = wk()
                o.tt('pool', ot[:, 0:512], bb[:, 0:512], E3[:, 0:512], ALU.mult)
                o.dma(RBT[c][:, tsl(i)], ot[:, 0:512])
                if l == 0:
                    o.dma(VF[c][:, tsl(i)], v_m[:, 0:512])
                    vv = v_m
                else:
                    vf = wk()
                    o.dma(vf[:, 0:512], VF[c][:, tsl(i)])
                    pg = ps()
                    o.mm(pg.v, v2b[32:64, cs], vlo[32:64, tsl(i)])
                    gt = wk()
                    o.act(gt[:, 0:512], pg.v, AF.Sigmoid, bias=cl[:, C_V0 + c:C_V0 + c + 1])
                    dv = wk()
                    o.tt('dve', dv[:, 0:512], vf[:, 0:512], v_m[:, 0:512], ALU.subtract)
                    o.tt('dve', dv[:, 0:512], dv[:, 0:512], gt[:, 0:512], ALU.mult)
                    vv = wk()
                    o.tt('dve', vv[:, 0:512], dv[:, 0:512], v_m[:, 0:512], ALU.add)
                o.dma(RVT[c][:, tsl(i)], vv[:, 0:512])
                rk = wk()
                o.stt('dve', rk[:, 0:512], r_m[:, 0:512], cl[:, C_RK + c:C_RK + c + 1], kf[:, 0:512], ALU.mult, ALU.mult)
                pb = ps()
                o.mm(pb.v, onesbd, rk[:, 0:512])
                bon = wk()
                o.tt('dve', bon[:, 0:512], pb.v, vv[:, 0:512], ALU.mult)
                o.dma(RBO[c][:, tsl(i)], bon[:, 0:512])
                pgg = ps()
                o.mm(pgg.v, g2b[:, cs], sgl[:, tsl(i)])
                gb = wkbf()
                o.cp('act', gb.v, pgg.v)
                o.dma(RG[c][:, tsl(i)], gb.v)
        for c in range(4):
            wq, wfg, wi, wo = load_group([J_HQ + c, J_HF + c, J_HI + c, J_HO + c])
            lcol = slice(l * 4 + c, l * 4 + c + 1)
            for i in range(NT):
                pq = proj(wq, i)
                pf = proj(wfg, i)
                pi_ = proj(wi, i)
                po = proj(wo, i)
                sq_ = wk()
                o.act(sq_[:, 0:512], pq.v, AF.Silu)
                sgf = wk()
                o.act(sgf[:, 0:512], pf.v, AF.Sigmoid)
                gate = wk()
                o.ts('dve', gate[:, 0:512], sgf[:, 0:512], oml[:, lcol], lowT[:, lcol], ALU.mult, ALU.add)
                lg = wk()
                o.act(lg[:, 0:512], gate[:, 0:512], AF.Ln)
                kx = wk()
                o.ts('pool', kx[:, 0:512], gate[:, 0:512], -1.0, 1.0, ALU.mult, ALU.add)
                o.scan(Cx[:, 1:513], ones512.v, lg[:, 0:512], 0.0)
                G3 = Cx[:, 1:513].re("p (c t) -> p c t", t=64)
                Dd = wk()
                o.tt('dve', Dd[:, 0:512].re("p (c t) -> p c t", t=64), G3, G3[:, :, 31:32].bc([128, 8, 64]), ALU.subtract)
                E1 = wk()
                E3 = wk()
                o.act(E1[:, 0:512], Dd[:, 0:512], AF.Exp)
                o.act(E3[:, 0:512], Dd[:, 0:512], AF.Exp, scale=-1.0)
                ot = wk()
                o.tt('dve', ot[:, 0:512], sq_[:, 0:512], E1[:, 0:512], ALU.mult)
                o.dma(HQ[c][:, tsl(i)], ot[:, 0:512])
                ot = wk()
                o.tt('pool', ot[:, 0:512], kx[:, 0:512], E3[:, 0:512], ALU.mult)
                o.dma(HK[c][:, tsl(i)], ot[:, 0:512])
                o.cp('pool', hscv[:, 0, c, i * 8:(i + 1) * 8], E1[:, 0:512].re("p (c t) -> p c t", t=64)[:, :, 63])
                G0 = Cx[:, 0:512].re("p (c t) -> p c t", t=64)
                d8 = wk()
                o.tt('dve', d8[:, 0:8], G3[:, :, 63], G0[:, :, 0], ALU.subtract)
                o.act(hscv[:, 1, c, i * 8:(i + 1) * 8], d8[:, 0:8], AF.Exp)
                d8 = wk()
                o.tt('dve', d8[:, 0:8], G3[:, :, 31], G0[:, :, 0], ALU.subtract)
                o.act(hscv[:, 2, c, i * 8:(i + 1) * 8], d8[:, 0:8], AF.Exp)
                it = wk()
                o.cp('act', it[:, 0:512], pi_.v)
                o.dma(HI[c][:, tsl(i)], it[:, 0:512])
                ob = wkbf()
                o.act(ob.v, po.v, AF.Silu)
                o.dma(HO[c][:, tsl(i)], ob.v)
        for c in range(4):
            wq, wk_ = load_group([J_FQ + c, J_FK + c])
            for i in range(NT):
                for which, wt in ((0, wq), (1, wk_)):
                    p = proj(wt, i)
                    sq = wk()
                    o.act(sq[:, 0:512], p.v, AF.Square)
                    pss = ps()
                    o.mm(pss.v, onesbd, sq[:, 0:512])
                    rstd = wk()
                    rtmp = wk()
                    o.rsqrt(rstd[:, 0:512], pss.v, 64.0 * EPS, rtmp[:, 0:512])
                    qb = wkbf()
                    o.stt('dve', qb.v, p.v, dr[:, 21 + which:22 + which], rstd[:, 0:512], ALU.mult, ALU.mult)
                    o.dma(QKT[which * 4 + c][:, tsl(i)], qb.v)
        wvs = load_group([J_FV + c for c in range(4)])
        for c in range(4):
            for i in range(NT):
                p = proj(wvs[c], i)
                vb = wkbf()
                o.cp('act', vb.v, p.v)
                o.dma(FVT[c][:, tsl(i)], vb.v)
        for c0 in range(0, 24, 4):
            wgs = load_group([J_GATE + c0 + k for k in range(4)])
            for k in range(4):
                for i in range(NT):
                    p = proj(wgs[k], i)
                    gb = wkbf()
                    o.act(gb.v, p.v, AF.Sigmoid)
                    o.dma(GT[c0 + k][:, tsl(i)], gb.v)
        if STOP <= 1:
            return
        NB = S_LEN // 128

        def r3(v, c):
            return v.re("p (c t) -> p c t", c=c)

        def par(v, bk):
            return v.re("p (c a t) -> p c a t", c=4, a=2)[:, :, bk, :]

        S.barrier()
        st['off'] = layer_mark
        Fsb = sb(S_LEN, 'Fsb')
        F8 = sb(S_LEN, 'F8', BF16)
        o.dma(Fsb[0:8, :], FTD.v)
        o.ts('dve', F8[0:8, :], Fsb[0:8, :], 8.0, None, ALU.mult)
        negF = sb(NB * 8, 'negF')
        for blk in range(NB):
            p = ps()
            o.tr(p[:, 0:8], Fsb[0:8, blk * 128:(blk + 1) * 128], ident[0:8, 0:8])
            o.ts('dve', negF[:, blk * 8:(blk + 1) * 8], p[:, 0:8], -1.0, None, ALU.mult)
        VTM = sb(NB * 512, 'VTM', BF16)
        vmark = st['off']
        fv = [sb(S_LEN, f'fv{c}', BF16) for c in range(4)]
        for c in range(4):
            o.dma(fv[c].v, FVT[c])
        for blk in range(NB):
            p = ps()
            pb = p.v.cast(BF16)
            for c in range(4):
                o.tr(pb[:, c * 128:(c + 1) * 128], fv[c][:, blk * 128:(blk + 1) * 128], identb)
            o.cp('act' if blk % 2 else 'dve', VTM[:, blk * 512:(blk + 1) * 512], pb[:, 0:512])
        S.barrier()
        st['off'] = vmark
        qk = [[sb(S_LEN, f'q{g}', BF16), sb(S_LEN, f'k{g}', BF16)] for g in range(2)]
        Pt = [sb(512, f'P{i}', BF16) for i in range(6)]
        yt = [sb(512, f'y{i}', BF16) for i in range(3)]
        rd = [sb(512, f'rd{i}') for i in range(2)]
        pcount = 0
        ycount = 0
        pst['n'] = 4
        for c in range(4):
            qT, kT = qk[c % 2]
            o.dma(qT.v, QKT[c])
            o.dma(kT.v, QKT[4 + c])
            for h2 in range(2):
                h = 2 * c + h2
                rows = slice(h2 * 64, (h2 + 1) * 64)
                for qc in range(NT):
                    pn = psb[4 + 2 * (ycount % 2)]
                    pd = psb[5 + 2 * (ycount % 2)]
                    nkb = 4 * qc + 4
                    for kb in range(nkb):
                        i_ = kb - 4 * qc
                        n0 = 128 * i_ if i_ > 0 else 0
                        qs = slice(qc * 512 + n0, (qc + 1) * 512)
                        psc = ps()
                        o.mm(psc[:, n0:512], kT[rows, kb * 128:(kb + 1) * 128], qT[rows, qs], start=True, stop=False)
                        if i_ >= 0:
                            o.mm(psc[:, n0:n0 + 128], identb, trib, start=False, stop=False)
                        o.mm(psc[:, n0:512], selb[0:8, h * 128:(h + 1) * 128], F8[0:8, qs], start=False, stop=True)
                        P = Pt[pcount % 6]
                        pcount += 1
                        o.act(P[:, n0:512], psc[:, n0:512], AF.Exp, bias=negF[:, kb * 8 + h:kb * 8 + h + 1], scale=0.125)
                        o.mm(pn[0:64, n0:512], VTM[:, kb * 512 + h * 64:kb * 512 + (h + 1) * 64], P[:, n0:512], start=(kb == 0), stop=(kb == nkb - 1))
                        o.mm(pd[0:64, n0:512], onesb[:, 0:64], P[:, n0:512], start=(kb == 0), stop=(kb == nkb - 1))
                    r = rd[ycount % 2]
                    y = yt[ycount % 3]
                    ycount += 1
                    o.recip(r[0:64, :], pd[0:64, :])
                    o.tt('dve', y[0:64, :], pn[0:64, :], r[0:64, :], ALU.mult)
                    o.dma(YB[c][rows, tsl(qc)], y[0:64, :])
        pst['n'] = 8
        if STOP <= 2:
            return

        S.barrier()
        st['off'] = layer_mark
        Mst = sb(512, 'Mst')
        M3 = r3(Mst.v, 4)
        o.memset('dve', Mst.v, 0.0)
        vpad = sb(1024, 'vpad')
        o.memset('pool', vpad.v, 0.0)
        ktm = [sb(512, f'ktm{i}') for i in range(2)]
        vtm = [sb(512, f'vtm{i}') for i in range(2)]
        hq = [sb(2048, f'hq{i}') for i in range(2)]
        hk = [sb(2048, f'hk{i}') for i in range(2)]
        hi = [sb(2048, f'hi{i}') for i in range(2)]
        ho = [sb(2048, f'ho{i}', BF16) for i in range(2)]
        ot = sb(2048, 'ot')
        bd4 = bd.us(1).bc([128, 4, 128])
        if HGCUT <= -1:
            return
        for i in range(NT):
            b = i % 2
            o.dma(r3(hq[b].v, 4), HQ.v.re("c p t -> p c t")[:, :, tsl(i)])
            o.dma(r3(hk[b].v, 4), HK.v.re("c p t -> p c t")[:, :, tsl(i)])
            o.dma(r3(hi[b].v, 4), HI.v.re("c p t -> p c t")[:, :, tsl(i)])
            o.dma(r3(ho[b].v, 4), HO.v.re("c p t -> p c t")[:, :, tsl(i)])
            if HGCUT <= 0:
                continue
            for ck in range(8):
                g = i * 8 + ck

                def cc(c):
                    return slice(c * 512 + ck * 64, c * 512 + (ck + 1) * 64)
                pk_ = ps()
                pv_ = ps()
                for c in range(4):
                    o.mm(pk_[0:64, c * 128:(c + 1) * 128], hk[b][:, cc(c)], ident)
                    o.mm(pv_[0:64, c * 128:(c + 1) * 128], hi[b][:, cc(c)], ident)
                kt = ktm[g % 2]
                vt = vtm[g % 2]
                if HGSUB <= 1:
                    continue
                o.cp('act', kt[0:64, :], pk_[0:64, :])
                o.cp('act', vt[0:64, :], pv_[0:64, :])
                if HGSUB <= 2:
                    continue
                o.tt('dve', vpad[0:64, 0:512], pv_[0:64, :], me8[0:64, :], ALU.mult)
                o.tt('dve', vpad[0:64, 512:1024], pv_[0:64, :], mo8[0:64, :], ALU.mult)
                if HGCUT <= 1:
                    continue
                M0s = wk()
                o.tt('dve', r3(M0s.v, 4), M3, hscv[:, 2, :, g:g + 1].bc([128, 4, 128]), ALU.mult)
                if HGSUB <= 3:
                    continue
                pA = [ps(), ps()]
                for h in range(8):
                    c = h // 2
                    rr = slice((h % 2) * 64, (h % 2) * 64 + 64)
                    b0 = c * 512 + ck * 64
                    o.mm(pA[h % 2][0:64, c * 64 + 32:(c + 1) * 64], hk[b][rr, b0:b0 + 64], hq[b][rr, b0 + 32:b0 + 64])
                    o.mm(pA[h % 2][0:32, c * 64:c * 64 + 32], hk[b][rr, b0:b0 + 32], hq[b][rr, b0:b0 + 32])
                Am = wk()
                o.memset('pool', Am[32:64, :], 0.0)
                for bk in range(2):
                    o.tt('dve', par(Am[0:64, :], bk)[:, :, 32:64], r3(pA[bk][0:64, 0:256], 4)[:, :, 32:64], r3(ui4[0:64, :], 4)[:, :, 32:64], ALU.mult)
                    o.tt('dve', par(Am[0:32, :], bk)[:, :, 0:32], r3(pA[bk][0:32, 0:256], 4)[:, :, 0:32], r3(ui4[0:32, :], 4)[:, :, 0:32], ALU.mult)
                if HGCUT <= 2:
                    continue
                pO = ps()
                for c in range(4):
                    oc_ = pO[:, c * 64:(c + 1) * 64]
                    o.mm(oc_, M0s[:, c * 128:(c + 1) * 128], hq[b][:, cc(c)], start=True, stop=False)
                    o.mm(oc_, vpad[0:64, c * 128:(c + 1) * 128], Am[0:64, (2 * c) * 64:(2 * c + 1) * 64], start=False, stop=False)
                    o.mm(oc_, vpad[0:64, 512 + c * 128:512 + (c + 1) * 128], Am[0:64, (2 * c + 1) * 64:(2 * c + 2) * 64], start=False, stop=True)
                o.cp('act', r3(ot.v, 4)[:, :, ck * 64:(ck + 1) * 64], r3(pO[:, 0:256], 4))
                if HGCUT <= 3:
                    continue
                pS = ps()
                for c in range(4):
                    o.mm(pS[:, c * 128:(c + 1) * 128], kt[0:64, c * 128:(c + 1) * 128], vt[0:64, c * 128:(c + 1) * 128])
                t1 = wk()
                o.tt('dve', r3(t1.v, 4), r3(pS.v, 4), bd4, ALU.mult)
                o.tt('dve', r3(t1.v, 4), r3(t1.v, 4), hscv[:, 0, :, g:g + 1].bc([128, 4, 128]), ALU.mult)
                o.tt('pool', M3, M3, hscv[:, 1, :, g:g + 1].bc([128, 4, 128]), ALU.mult)
                o.tt('pool', M3, M3, r3(t1.v, 4), ALU.add)
            for c in range(4):
                oc_ = ot[:, c * 512:(c + 1) * 512]
                sq = wk()
                o.act(sq.v, oc_, AF.Square)
                pss = ps()
                o.mm(pss.v, onesbd, sq.v)
                rstd = wk()
                rtmp = wk()
                o.rsqrt(rstd.v, pss.v, 64.0 * EPS, rtmp.v)
                y = wk()
                o.stt('dve', y.v, oc_, dr[:, 23:24], rstd.v, ALU.mult, ALU.mult)
                yb_ = wkbf()
                o.tt('dve', yb_.v, y.v, ho[b][:, c * 512:(c + 1) * 512], ALU.mult)
                o.dma(YB[4 + c][:, tsl(i)], yb_.v)
        if STOP <= 3:
            return

        S.barrier()
        st['off'] = layer_mark
        Mst = sb(512, 'MstR')
        M3 = r3(Mst.v, 4)
        o.memset('dve', Mst.v, 0.0)
        vpad = sb(1024, 'vpadR')
        nupad = sb(1024, 'nupad')
        o.memset('pool', vpad.v, 0.0)
        o.memset('pool', nupad.v, 0.0)
        sm64 = [sb(512, f'sm{i}') for i in range(14)]
        (ktm_, btm_, vtm_, RBt, Qt, RKt, Wsb, nut, Y0, Y1, Yt0, Yt1, Tt0, Tt1) = sm64
        rkr = [sb(4096, 'rkr0')] * 2
        rkk = [sb(2048, 'rkk0')] * 2
        rbb = [sb(2048, 'rbb0')] * 2
        rvv = [sb(2048, 'rvv0')] * 2
        bo = [sb(2048, 'bo0')] * 2
        rg = [sb(2048, 'rg0', BF16)] * 2
        ot = sb(2048, 'otR')
        hsl = [slice(h * 64, (h + 1) * 64) for h in range(8)]
        for i in range(NT):
            b = i % 2
            o.dma(rkr[b].v.re("p (c n a t) -> p c n a t", c=4, n=8, a=2), RKR.v.re("c p n a t -> p c n a t")[:, :, i * 8:(i + 1) * 8, :, :])
            o.dma(r3(rkk[b].v, 4), RKT.v.re("c p t -> p c t")[:, :, tsl(i)])
            o.dma(r3(rbb[b].v, 4), RBT.v.re("c p t -> p c t")[:, :, tsl(i)])
            o.dma(r3(rvv[b].v, 4), RVT.v.re("c p t -> p c t")[:, :, tsl(i)])
            o.dma(r3(bo[b].v, 4), RBO.v.re("c p t -> p c t")[:, :, tsl(i)])
            o.dma(r3(rg[b].v, 4), RG.v.re("c p t -> p c t")[:, :, tsl(i)])
            for ck in range(8):
                g = i * 8 + ck

                def cc(c):
                    return slice(c * 512 + ck * 64, c * 512 + (ck + 1) * 64)

                def KR(c, a0, a1):
                    off = (c * 8 + ck) * 128
                    return slice(off + a0 * 64, off + a1 * 64)
                pk_ = ps()
                pb_ = ps()
                pv_ = ps()
                for c in range(4):
                    o.mm(pk_[0:64, c * 128:(c + 1) * 128], rkk[b][:, cc(c)], ident)
                    o.mm(pb_[0:64, c * 128:(c + 1) * 128], rbb[b][:, cc(c)], ident)
                    o.mm(pv_[0:64, c * 128:(c + 1) * 128], rvv[b][:, cc(c)], ident)
                o.cp('act', ktm_[0:64, :], pk_[0:64, :])
                o.cp('act', btm_[0:64, :], pb_[0:64, :])
                o.cp('act', vtm_[0:64, :], pv_[0:64, :])
                o.tt('dve', vpad[0:64, 0:512], pv_[0:64, :], me8[0:64, :], ALU.mult)
                o.tt('dve', vpad[0:64, 512:1024], pv_[0:64, :], mo8[0:64, :], ALU.mult)
                pA1 = [ps(), ps()]
                pA2 = [ps(), ps()]
                for h in range(8):
                    c = h // 2
                    rr = slice((h % 2) * 64, (h % 2) * 64 + 64)
                    o.mm(pA1[h % 2][0:64, c * 128:(c + 1) * 128], rbb[b][rr, cc(c)], rkr[b][rr, KR(c, 0, 2)])
                    o.mm(pA2[h % 2][0:64, c * 128:(c + 1) * 128], rkk[b][rr, cc(c)], rkr[b][rr, KR(c, 0, 2)])
                for bk in range(2):
                    v1 = pA1[bk][0:64, :].re("p (h x) -> p h x", h=4)
                    v2 = pA2[bk][0:64, :].re("p (h x) -> p h x", h=4)
                    o.tt('dve', par(Yt0[0:64, :], bk), v1[:, :, 0:64], r3(nsu4[0:64, :], 4), ALU.mult)
                    o.tt('dve', par(RBt[0:64, :], bk), v1[:, :, 64:128], r3(ui4[0:64, :], 4), ALU.mult)
                    o.tt('dve', par(Qt[0:64, :], bk), v2[:, :, 0:64], r3(su4[0:64, :], 4), ALU.mult)
                    o.tt('dve', par(RKt[0:64, :], bk), v2[:, :, 64:128], r3(ui4[0:64, :], 4), ALU.mult)
                pA3 = [ps(), ps()]
                for h in range(8):
                    c = h // 2
                    rr = slice((h % 2) * 64, (h % 2) * 64 + 64)
                    o.mm(pA3[h % 2][0:64, c * 64:(c + 1) * 64], rkr[b][rr, KR(c, 0, 1)], rbb[b][rr, cc(c)])
                for bk in range(2):
                    o.tt('dve', par(Y0[0:64, :], bk), r3(pA3[bk][0:64, 0:256], 4), r3(nsl8[0:64, 0:256], 4), ALU.mult)
                o.tt('pool', Tt0[0:64, :], Yt0[0:64, :], id8[0:64, :], ALU.add)
                Ya, Yta, Tta = Y0, Yt0, Tt0
                Yb, Ytb, Ttb = Y1, Yt1, Tt1
                for j in range(5):
                    pY = ps()
                    pYt = ps()
                    for h in range(8):
                        o.mm(pY[0:64, hsl[h]], Yta[0:64, hsl[h]], Ya[0:64, hsl[h]])
                        o.mm(pYt[0:64, hsl[h]], Ya[0:64, hsl[h]], Yta[0:64, hsl[h]])
                    o.cp('act', Yb[0:64, :], pY[0:64, :])
                    o.cp('act', Ytb[0:64, :], pYt[0:64, :])
                    pT = ps()
                    for h in range(8):
                        o.mm(pT[0:64, hsl[h]], Yb[0:64, hsl[h]], Tta[0:64, hsl[h]])
                    o.tt('dve', Ttb[0:64, :], pT[0:64, :], Tta[0:64, :], ALU.add)
                    Ya, Yta, Tta, Yb, Ytb, Ttb = Yb, Ytb, Ttb, Ya, Yta, Tta
                pW = ps()
                for c in range(4):
                    o.mm(pW[0:64, c * 128:(c + 1) * 128], rkr[b][:, KR(c, 0, 1)], Mst[:, c * 128:(c + 1) * 128], start=True, stop=False)
                    for h2 in range(2):
                        h = 2 * c + h2
                        o.mm(pW[0:64, hsl[h]], Qt[0:64, hsl[h]], vtm_[0:64, hsl[h]], start=False, stop=True)
                o.cp('act', Wsb[0:64, :], pW[0:64, :])
                pU = ps()
                for h in range(8):
                    o.mm(pU[0:64, hsl[h]], Tta[0:64, hsl[h]], Wsb[0:64, hsl[h]])
                o.ts('dve', nut[0:64, :], pU[0:64, :], -1.0, None, ALU.mult)
                o.stt('dve', nupad[0:64, 0:512], pU[0:64, :], -1.0, me8[0:64, :], ALU.mult, ALU.mult)
                o.stt('dve', nupad[0:64, 512:1024], pU[0:64, :], -1.0, mo8[0:64, :], ALU.mult, ALU.mult)
                pO = ps()
                for c in range(4):
                    oc_ = pO[:, c * 64:(c + 1) * 64]
                    o.mm(oc_, Mst[:, c * 128:(c + 1) * 128], rkr[b][:, KR(c, 1, 2)], start=True, stop=False)
                    for h2 in range(2):
                        h = 2 * c + h2
                        o.mm(oc_, vpad[0:64, h2 * 512 + c * 128:h2 * 512 + (c + 1) * 128], RKt[0:64, hsl[h]], start=False, stop=False)
                        o.mm(oc_, nupad[0:64, h2 * 512 + c * 128:h2 * 512 + (c + 1) * 128], RBt[0:64, hsl[h]], start=False, stop=(h2 == 1))
                o.cp('act', r3(ot.v, 4)[:, :, ck * 64:(ck + 1) * 64], r3(pO[:, 0:256], 4))
                pS = ps()
                for c in range(4):
                    o.mm(pS[:, c * 128:(c + 1) * 128], ktm_[0:64, c * 128:(c + 1) * 128], vtm_[0:64, c * 128:(c + 1) * 128], start=True, stop=False)
                    o.mm(pS[:, c * 128:(c + 1) * 128], btm_[0:64, c * 128:(c + 1) * 128], nut[0:64, c * 128:(c + 1) * 128], start=False, stop=True)
                t1 = wk()
                o.tt('dve', r3(t1.v, 4), r3(pS.v, 4), bd4, ALU.mult)
                o.tt('dve', t1.v, t1.v, Mst.v, ALU.add)
                o.tt('pool', M3, r3(t1.v, 4), gamv[:, :, g:g + 1].bc([128, 4, 128]), ALU.mult)
            for c in range(4):
                oc_ = ot[:, c * 512:(c + 1) * 512]
                pm = ps()
                o.mm(pm.v, onesbd, oc_)
                sq = wk()
                o.act(sq.v, oc_, AF.Square)
                pvv = ps()
                o.mm(pvv.v, onesbd, sq.v)
                mean = wk()
                o.ts('dve', mean.v, pm.v, 1.0 / 64, None, ALU.mult)
                cen = wk()
                o.tt('dve', cen.v, oc_, mean.v, ALU.subtract)
                msq = wk()
                o.tt('pool', msq.v, mean.v, mean.v, ALU.mult)
                var = wk()
                o.stt('dve', var.v, pvv.v, 1.0 / 64, msq.v, ALU.mult, ALU.subtract)
                rstd = wk()
                rtmp = wk()
                o.rsqrt(rstd.v, var.v, LN_EPS, rtmp.v)
                y = wk()
                o.tt('dve', y.v, cen.v, rstd.v, ALU.mult)
                o.ts('dve', y.v, y.v, cl[:, C_LNW + c:C_LNW + c + 1], cl[:, C_LNB + c:C_LNB + c + 1], ALU.mult, ALU.add)
                o.tt('pool', y.v, y.v, bo[b][:, c * 512:(c + 1) * 512], ALU.add)
                yb_ = wkbf()
                o.tt('dve', yb_.v, y.v, rg[b][:, c * 512:(c + 1) * 512], ALU.mult)
                o.dma(YB[8 + c][:, tsl(i)], yb_.v)
        if STOP <= 4:
            return

        S.barrier()
        st['off'] = layer_mark
        wbrb = sb(12 * 1024, 'wbrb', BF16)
        woutb = sb(8 * 1024, 'woutb', BF16)
        stg = [sb(2048, 'stg0')] * 2
        for q in range(6):
            s_ = stg[q % 2]
            o.dma(r3(s_.v, 2), wbr_in[l][:, 2 * q:2 * q + 2, :])
            o.cp('pool', wbrb[:, 2 * q * 1024:(2 * q + 2) * 1024], s_.v)
        for q in range(4):
            s_ = stg[q % 2]
            o.dma(r3(s_.v, 2), wout_in[l][:, 2 * q:2 * q + 2, :])
            o.cp('pool', woutb[:, 2 * q * 1024:(2 * q + 2) * 1024], s_.v)
        ybt = [sb(12 * 512, 'ybt0', BF16)] * 2
        gtt = [sb(3 * 512, f'gtt{i}', BF16) for i in range(2)]
        mg = sb(8 * 512, 'mg', BF16)
        xt = [sb(KC * 512, 'xt30')] * 2
        sqt = sb(KC * 512, 'sqt3', BF16)
        h2o = [sb(KC * 512, 'h2o0', BF16)] * 2
        gn = 0
        for i in range(NT):
            yb_ = ybt[i % 2]
            o.dma(r3(yb_.v, 12), YB.v.re("c p t -> p c t")[:, :, tsl(i)])
            x = xt[i % 2]
            o.dma(r3(x.v, KC), XT[:, :, tsl(i)])
            for oc in range(8):
                g_ = gtt[gn % 2]
                gn += 1
                o.dma(r3(g_.v, 3), GT.v.re("(b o) p t -> o p b t", b=3)[oc][:, :, tsl(i)])
                acc = wk()
                for b_ in range(3):
                    p = ps()
                    for kc in range(4):
                        o.mm(p.v, wbrb[:, (b_ * 4 + kc) * 1024 + oc * 128:(b_ * 4 + kc) * 1024 + (oc + 1) * 128],
                             yb_[:, (b_ * 4 + kc) * 512:(b_ * 4 + kc + 1) * 512], start=(kc == 0), stop=(kc == 3))
                    if b_ == 0:
                        o.tt('dve', acc.v, p.v, g_[:, 0:512], ALU.mult)
                    else:
                        t = wk()
                        o.tt('dve', t.v, p.v, g_[:, b_ * 512:(b_ + 1) * 512], ALU.mult)
                        if b_ == 1:
                            o.tt('pool', acc.v, acc.v, t.v, ALU.add)
                        else:
                            o.tt('pool', mg[:, oc * 512:(oc + 1) * 512], acc.v, t.v, ALU.add)
            for oc in range(8):
                p = ps()
                for kc in range(KC):
                    o.mm(p.v, woutb[:, kc * 1024 + oc * 128:kc * 1024 + (oc + 1) * 128], mg[:, kc * 512:(kc + 1) * 512], start=(kc == 0), stop=(kc == KC - 1))
                xs_ = x[:, oc * 512:(oc + 1) * 512]
                o.stt('dve', xs_, p.v, md[:, 16 + oc:17 + oc], xs_, ALU.mult, ALU.add)
            o.dma(XT[:, :, tsl(i)], r3(x.v, KC))
            h2t = h2o[i % 2]
            norm_tile(x, 8, 24, h2t, sqt)
            o.dma(H2.v.re("k p t -> p k t")[:, :, tsl(i)], r3(h2t.v, KC))
        if STOP <= 5:
            return

        S.barrier()
        st['off'] = layer_mark
        hts = [sb(KC * 512, f'h2_{i}', BF16) for i in range(NT)]
        for i in range(NT):
            o.dma(r3(hts[i].v, KC), H2.v.re("k p t -> p k t")[:, :, tsl(i)])
        wst2 = [sb(2048, f'wst2_{i}') for i in range(2)]
        wpb = [sb(2048, f'wpb{i}', BF16) for i in range(2)]
        rawv = sb(514, 'rawv')
        rawg = sb(514, 'rawg')
        for j in range(NFF):
            s_ = wst2[j % 2]
            o.dma(s_.v, wup_in[l][j])
            wb_ = wpb[j % 2]
            o.cp('pool', wb_.v, s_.v)
            o.memset('pool', rawv[:, 0:2], 0.0)
            o.memset('pool', rawg[:, 0:2], 0.0)
            for i in range(NT):
                pv = ps()
                pg = ps()
                for kc in range(KC):
                    o.mm(pv.v, wb_[:, kc * 256:kc * 256 + 128], hts[i][:, kc * 512:(kc + 1) * 512], start=(kc == 0), stop=(kc == KC - 1))
                for kc in range(KC):
                    o.mm(pg.v, wb_[:, kc * 256 + 128:kc * 256 + 256], hts[i][:, kc * 512:(kc + 1) * 512], start=(kc == 0), stop=(kc == KC - 1))
                res = []
                for (p, raw, cc_) in ((pv, rawv, j), (pg, rawg, NFF + j)):
                    o.cp('act', raw[:, 2:514], p.v)
                    cv = wk()
                    o.ts('dve', cv.v, raw[:, 2:514], cl[:, C_CW + 88 + cc_:C_CW + 89 + cc_], cl[:, C_CB + cc_:C_CB + cc_ + 1], ALU.mult, ALU.add)
                    o.stt('dve', cv.v, raw[:, 1:513], cl[:, C_CW + 44 + cc_:C_CW + 45 + cc_], cv.v, ALU.mult, ALU.add)
                    o.stt('dve', cv.v, raw[:, 0:512], cl[:, C_CW + cc_:C_CW + cc_ + 1], cv.v, ALU.mult, ALU.add)
                    o.cp('act', raw[:, 0:2], raw[:, 512:514])
                    res.append(cv)
                sg = wk()
                o.act(sg.v, res[1].v, AF.Silu)
                ab = wkbf()
                o.tt('pool', ab.v, sg.v, res[0].v, ALU.mult)
                o.dma(ACTT[j][:, tsl(i)], ab.v)
        if STOP <= 6:
            return

        S.barrier()
        st['off'] = layer_mark
        wdnb = sb(NFF * 1024, 'wdnb', BF16)
        stg = [sb(2048, 'stgd0')] * 2
        for q in range(NFF // 2):
            s_ = stg[q % 2]
            o.dma(r3(s_.v, 2), wdn_in[l][:, 2 * q:2 * q + 2, :])
            o.cp('pool', wdnb[:, 2 * q * 1024:(2 * q + 2) * 1024], s_.v)
        att = [sb(NFF * 512, f'att{i}', BF16) for i in range(2)]
        xt = [sb(KC * 512, 'xt40')] * 2
        for i in range(NT):
            a_ = att[i % 2]
            o.dma(r3(a_.v, NFF), ACTT.v.re("c p t -> p c t")[:, :, tsl(i)])
            x = xt[i % 2]
            o.dma(r3(x.v, KC), XT[:, :, tsl(i)])
            for oc in range(8):
                p = ps()
                for kc in range(NFF):
                    o.mm(p.v, wdnb[:, kc * 1024 + oc * 128:kc * 1024 + (oc + 1) * 128], a_[:, kc * 512:(kc + 1) * 512], start=(kc == 0), stop=(kc == NFF - 1))
                xs_ = x[:, oc * 512:(oc + 1) * 512]
                o.stt('dve', xs_, p.v, md[:, 40 + oc:41 + oc], xs_, ALU.mult, ALU.add)
            o.dma(XT[:, :, tsl(i)], r3(x.v, KC))

    for l in range(L):
        emit_layer(l)

    S.barrier()
    st['off'] = persist_mark
    xl = [sb(KC * 512, f'xl{i}') for i in range(2)]
    yo = [sb(D, f'yo{i}') for i in range(2)]
    n = 0
    for i in range(NT):
        xs = xl[i % 2]
        o.dma(xs.v.re("p (k t) -> p k t", k=KC), XT[:, :, i * 512:(i + 1) * 512])
        for q in range(4):
            yy = yo[n % 2]
            n += 1
            for h in range(2):
                p = ps()
                for c in range(4):
                    kc = h * 4 + c
                    o.tr(p[:, c * 128:(c + 1) * 128], xs[:, kc * 512 + q * 128: kc * 512 + (q + 1) * 128], ident)
                o.cp('act' if h == 0 else 'dve', yy[:, h * 512:(h + 1) * 512], p.v)
            t0 = i * 512 + q * 128
            o.dma(y_out[t0:t0 + 128, :], yy.v)
    S.barrier()

    with contextlib.ExitStack() as es:
        sems = {}
        for e in COMPUTE:
            sems[e] = es.enter_context(nc.semaphore("s_" + e))
        for j in range(NDSEM):
            sems[('d', j)] = es.enter_context(nc.semaphore(f"d{j}"))
        block = es.enter_context(nc.Block())
        run = S.emit(sems)
        block.sync(lambda e: run('sp', e))
        block.tensor(lambda e: run('pe', e))
        block.scalar(lambda e: run('act', e))
        block.vector(lambda e: run('dve', e))
        block.gpsimd(lambda e: run('pool', e))
    return nc


def _fm(v):
    return np.ascontiguousarray(np.asarray(v, np.float32).reshape(-1, 128).T)


def make_consts():
    c = np.zeros((128, K_W), np.float32)
    c[:, K_ID:K_ID + 128] = np.eye(128)
    c[0:64, K_OBD:K_OBD + 64] = 1.0
    c[64:128, K_OBD + 64:K_OBD + 128] = 1.0
    c[:, K_ONE:K_ONE + 128] = 1.0
    r = np.arange(64)
    su = (r[:, None] < r[None, :]).astype(np.float32)
    ui = (r[:, None] <= r[None, :]).astype(np.float32)
    sl = (r[:, None] > r[None, :]).astype(np.float32)
    c[0:64, K_BD:K_BD + 64] = 1.0
    c[64:128, K_BD + 64:K_BD + 128] = 1.0
    c[0:64, K_UI8:K_UI8 + 512] = np.tile(ui, (1, 8))
    c[0:64, K_NSL8:K_NSL8 + 512] = -np.tile(sl, (1, 8))
    c[0:64, K_ID8:K_ID8 + 512] = np.tile(np.eye(64, dtype=np.float32), (1, 8))
    c[0:64, K_NSU4:K_NSU4 + 256] = -np.tile(su, (1, 4))
    c[0:64, K_SU4:K_SU4 + 256] = np.tile(su, (1, 4))
    me = np.concatenate([np.ones((64, 64), np.float32), np.zeros((64, 64), np.float32)], axis=1)
    c[0:64, K_ME:K_ME + 512] = np.tile(me, (1, 4))
    c[0:64, K_MO:K_MO + 512] = np.tile(1.0 - me, (1, 4))
    return c


def make_consts_b():
    c = np.zeros((128, KB_W), np.float32)
    c[:, KB_ID:KB_ID + 128] = np.eye(128)
    c[:, KB_ONE:KB_ONE + 128] = 1.0
    r = np.arange(128)
    c[:, KB_TRI:KB_TRI + 128] = np.where(r[:, None] <= r[None, :], 0.0, -98304.0)
    for h in range(8):
        c[h, KB_SEL + h * 128:KB_SEL + (h + 1) * 128] = 1.0
    return c


def prep_shared(inp, L):
    f = lambda k: np.asarray(inp[k], np.float32)
    cols = np.zeros((L, 128, NCOLS), np.float32)
    win = np.zeros((L, NCH, 128, KC * 128), np.float32)
    lora = np.zeros((L, 128, 512), np.float32)
    g2 = np.zeros((L, 128, 512), np.float32)
    v2 = np.zeros((L, 64, 512), np.float32)
    for l in range(L):
        cl = cols[l]
        cl[:, C_N1G:C_N1G + 8] = _fm(f('norm1_g')[l])
        cl[:, C_N2G:C_N2G + 8] = _fm(f('norm2_g')[l])
        cl[0:8, C_FBF] = f('fox_b_f')[l]
        cl[:, C_QG] = np.tile(f('fox_q_gain')[l], 2)
        cl[:, C_KG] = np.tile(f('fox_k_gain')[l], 2)
        cl[:, C_OG] = np.tile(f('hgrn_o_gain')[l], 2)
        cl[:, C_MU:C_MU + 14] = _fm(f('rwkv_mu')[l])
        cl[:, C_W0:C_W0 + 4] = _fm(f('rwkv_w0')[l])
        cl[:, C_A0:C_A0 + 4] = _fm(f('rwkv_a0')[l])
        cl[:, C_KK:C_KK + 4] = _fm(f('rwkv_k_k')[l])
        cl[:, C_KA:C_KA + 4] = _fm(f('rwkv_k_a')[l])
        cl[:, C_RK:C_RK + 4] = _fm(f('rwkv_r_k')[l].reshape(-1))
        cl[:, C_LNW:C_LNW + 4] = _fm(f('rwkv_ln_w')[l])
        cl[:, C_LNB:C_LNB + 4] = _fm(f('rwkv_ln_b')[l])
        if l > 0:
            cl[:, C_V0:C_V0 + 4] = _fm(f('rwkv_v0')[l - 1])
            cl[32:64, C_VMU] = f('rwkv_vres_mu')[l - 1]
            v2[l, 32:64] = f('rwkv_v2')[l - 1]
        cw = f('conv_w')[l]
        for j in range(3):
            cl[:, C_CW + j * 44:C_CW + (j + 1) * 44] = _fm(cw[j])
        cl[:, C_CB:C_CB + 44] = _fm(f('conv_b')[l])
        W = f('w_in')[l]
        main = np.concatenate([W[:, 0:1536], W[:, 1544:1544 + 2048 + 1792 + 3072]], axis=1)
        misc = np.zeros((D, 128), np.float32)
        misc[:, 0:8] = W[:, 1536:1544]
        if l > 0:
            misc[:, 32:64] = f('rwkv_vres_down')[l - 1]
        allc = np.concatenate([main, misc], axis=1)
        win[l] = allc.reshape(KC, 128, NCH, 128).transpose(2, 1, 0, 3).reshape(NCH, 128, KC * 128)
        lora[l, 0:64] = f('rwkv_w2')[l]
        lora[l, 64:128] = f('rwkv_a2')[l]
        g2[l] = f('rwkv_g2')[l]
    sh = {}
    sh['cols'] = cols
    sh['lbT'] = np.ascontiguousarray(f('hgrn_lb')[:4].reshape(-1, 4, 128).transpose(2, 0, 1).reshape(128, -1))
    if sh['lbT'].shape[1] < 16:
        sh['lbT'] = np.concatenate([sh['lbT'], np.zeros((128, 16 - sh['lbT'].shape[1]), np.float32)], axis=1)
    sh['wada'] = np.ascontiguousarray(f('w_ada')[:L].reshape(L, KC, 128, 6 * D).transpose(0, 2, 1, 3))
    sh['bada'] = np.ascontiguousarray(f('b_ada')[:L].reshape(L, 1, 6 * D))
    sh['win'] = win
    sh['lora'] = lora
    sh['g2'] = g2
    sh['v2'] = v2
    sh['wbr'] = np.ascontiguousarray(f('w_branch')[:L].reshape(L, 3, 4, 128, D).transpose(0, 3, 1, 2, 4).reshape(L, 128, 12, D))
    sh['wout'] = np.ascontiguousarray(f('w_out')[:L].reshape(L, KC, 128, D).transpose(0, 2, 1, 3))
    wu = f('w_up')[:L].reshape(L, KC, 128, 2, NFF, 128)
    sh['wup'] = np.ascontiguousarray(wu.transpose(0, 4, 2, 1, 3, 5).reshape(L, NFF, 128, KC * 256))
    sh['wdn'] = np.ascontiguousarray(f('w_down')[:L].reshape(L, NFF, 128, D).transpose(0, 2, 1, 3))
    sh['cst'] = make_consts()
    sh['cstb'] = make_consts_b()
    return sh


_CACHE = {}


def run(inp, S_LEN, L, dbg=(), STOP=99):
    key = (S_LEN, L, tuple(dbg), STOP)
    if key not in _CACHE:
        _CACHE[key] = build(S_LEN, L, dbg, STOP)
    nc = _CACHE[key]
    sh = prep_shared(inp, L)
    x = np.asarray(inp['x'], np.float32)
    c = np.asarray(inp['c'], np.float32)
    B = x.shape[0]
    in_maps = []
    for b in range(B):
        m = dict(sh)
        m['x'] = np.ascontiguousarray(x[b])
        m['cT'] = _fm(c[b])
        in_maps.append(m)
    res = run_bass_kernel_spmd(nc, in_maps, core_ids=list(range(B)))
    return res.results


def kernel(**inputs):
    res = run(inputs, 4096, 4)
    return np.stack([r['y'] for r in res]).astype(np.float32)
```

```python
import contextlib
import os
HGCUT = int(os.environ.get('HGCUT', '9'))
HGSUB = int(os.environ.get('HGSUB', '9'))
import numpy as np
import ml_dtypes
import concourse.bass as bass
import concourse.mybir as mybir
from concourse.bass_utils import run_bass_kernel_spmd

F32 = mybir.dt.float32
BF16 = mybir.dt.bfloat16
AF = mybir.ActivationFunctionType
ALU = mybir.AluOpType

NDSEM = 24
COMPUTE = ('pe', 'act', 'dve', 'pool')


class T:
    __slots__ = ('ap', 'name', 'w', 'r', 'ut', 'xr')

    def __init__(self, ap, name='', ut=False, xr=False):
        self.ap = ap
        self.name = name
        self.w = None
        self.r = {}
        self.ut = ut
        self.xr = xr

    def __getitem__(self, k):
        return Vw(self, self.ap[k])

    @property
    def v(self):
        return Vw(self, self.ap)


class Vw:
    __slots__ = ('t', 'ap')

    def __init__(self, t, ap):
        self.t = t
        self.ap = ap

    def __getitem__(self, k):
        return Vw(self.t, self.ap[k])

    def re(self, pat, **kw):
        return Vw(self.t, self.ap.rearrange(pat, **kw))

    def bc(self, shape):
        return Vw(self.t, self.ap.to_broadcast(list(shape)))

    def cast(self, dt):
        return Vw(self.t, self.ap.bitcast(dt))

    def us(self, axis):
        return Vw(self.t, self.ap.unsqueeze(axis))


class Ins:
    __slots__ = ('eng', 'idx', 'fn', 'waits', 'signal', 'vc', 'key', 'val', 'isdma')


class Sched:
    def __init__(self):
        self.q = {e: [] for e in COMPUTE + ('sp',)}
        self.clock = {e: {} for e in COMPUTE + ('sp',)}
        self.dcount = [0] * NDSEM
        self.dlast = [None] * NDSEM
        self.dnext = 0

    def _add(self, eng, fn, reads, writes, isdma):
        ins = Ins()
        ins.eng = eng
        ins.idx = len(self.q[eng])
        ins.fn = fn
        ins.waits = []
        ins.signal = False
        ins.isdma = isdma
        clock = self.clock[eng]
        deps = []
        if isdma:
            j = self.dnext
            self.dnext = (j + 1) % NDSEM
            self.dcount[j] += 1
            ins.key = ('d', j)
            ins.val = 16 * self.dcount[j]
            if self.dlast[j] is not None:
                deps.append((self.dlast[j], 'sem'))
            self.dlast[j] = ins
            ins.signal = True
        else:
            ins.key = eng
            ins.val = ins.idx + 1
        for t in reads:
            if t.w is not None:
                deps.append((t.w, 'raw'))
            if t.xr:
                for r in t.r.values():
                    if r.eng != eng:
                        deps.append((r, 'rar'))
        for t in writes:
            if t.w is not None:
                deps.append((t.w, 'waw'))
            for r in t.r.values():
                deps.append((r, 'war'))
        for p, kind in deps:
            if (not p.isdma) and (not isdma) and p.eng == eng:
                if eng == 'pe' or kind != 'raw':
                    continue
            if clock.get(p.key, 0) >= p.val:
                continue
            p.signal = True
            ins.waits.append(p)
            for k, v in p.vc.items():
                if clock.get(k, 0) < v:
                    clock[k] = v
        vc = dict(clock)
        vc[ins.key] = ins.val
        ins.vc = vc
        for t in reads:
            t.r[ins.key] = ins
        for t in writes:
            t.w = ins
            t.r = {}
        self.q[eng].append(ins)
        return ins

    def op(self, eng, fn, reads=(), writes=()):
        return self._add(eng, fn, reads, writes, False)

    def dma(self, fn, reads=(), writes=(), queue='sp'):
        return self._add(queue, fn, reads, writes, True)

    def barrier(self):
        lasts = []
        for e in COMPUTE:
            for ins in reversed(self.q[e]):
                if ins.fn is not None and not ins.isdma:
                    lasts.append(ins)
                    break
        for j in range(NDSEM):
            if self.dlast[j] is not None:
                lasts.append(self.dlast[j])
        for e in COMPUTE + ('sp',):
            ins = Ins()
            ins.eng = e
            ins.idx = len(self.q[e])
            ins.fn = None
            ins.waits = []
            ins.signal = False
            ins.isdma = False
            ins.key = e
            ins.val = ins.idx
            clock = self.clock[e]
            for p in lasts:
                if (not p.isdma) and p.eng == e:
                    continue
                if clock.get(p.key, 0) >= p.val:
                    continue
                p.signal = True
                ins.waits.append(p)
                for k, v in p.vc.items():
                    if clock.get(k, 0) < v:
                        clock[k] = v
            ins.vc = dict(clock)
            self.q[e].append(ins)

    def emit(self, sems):
        sigcount = {}
        for e in COMPUTE:
            c = 0
            for ins in self.q[e]:
                if ins.fn is not None and ins.signal and not ins.isdma:
                    c += 1
                    sigcount[id(ins)] = c

        def run(eng_name, e):
            for ins in self.q[eng_name]:
                w = {}
                for p in ins.waits:
                    v = p.val if p.isdma else sigcount[id(p)]
                    if w.get(p.key, 0) < v:
                        w[p.key] = v
                for k, v in w.items():
                    e.wait_ge(sems[k], v)
                if ins.fn is None:
                    continue
                r = ins.fn(e)
                if ins.isdma:
                    r.then_inc(sems[ins.key], 16)
                elif ins.signal:
                    r.then_inc(sems[ins.key], 1)
        return run


def _ap(x):
    return x.ap if isinstance(x, Vw) else x


def _ts(*xs):
    return [x.t for x in xs if isinstance(x, Vw) and not x.t.ut]


class Ops:
    def __init__(self, S):
        self.S = S

    def tt(self, eng, out, a, b, op):
        self.S.op(eng, lambda e: e.tensor_tensor(out=out.ap, in0=a.ap, in1=b.ap, op=op), _ts(a, b), _ts(out))

    def ts(self, eng, out, a, s1, s2, op0, op1=None):
        if op1 is None:
            self.S.op(eng, lambda e: e.tensor_scalar(out=out.ap, in0=a.ap, scalar1=_ap(s1), scalar2=None, op0=op0),
                      _ts(a, s1), _ts(out))
        else:
            self.S.op(eng, lambda e: e.tensor_scalar(out=out.ap, in0=a.ap, scalar1=_ap(s1), scalar2=_ap(s2), op0=op0, op1=op1),
                      _ts(a, s1, s2), _ts(out))

    def stt(self, eng, out, a, s, b, op0, op1):
        self.S.op(eng, lambda e: e.scalar_tensor_tensor(out=out.ap, in0=a.ap, scalar=_ap(s), in1=b.ap, op0=op0, op1=op1),
                  _ts(a, s, b), _ts(out))

    def cp(self, eng, out, a):
        if eng == 'act':
            self.S.op(eng, lambda e: e.copy(out=out.ap, in_=a.ap), _ts(a), _ts(out))
        else:
            self.S.op(eng, lambda e: e.tensor_copy(out=out.ap, in_=a.ap), _ts(a), _ts(out))

    def act(self, out, a, func, bias=None, scale=None):
        kw = {}
        if bias is not None:
            kw['bias'] = _ap(bias)
        if scale is not None:
            kw['scale'] = _ap(scale)
        self.S.op('act', lambda e: e.activation(out=out.ap, in_=a.ap, func=func, **kw), _ts(a, bias, scale), _ts(out))

    def recip(self, out, a):
        self.S.op('dve', lambda e: e.reciprocal(out=out.ap, in_=a.ap), _ts(a), _ts(out))

    def rsqrt(self, out, a, add, tmp):
        self.act(tmp, a, AF.Sqrt, bias=add)
        self.recip(out, tmp)

    def scan(self, out, d0, d1, init, op0=ALU.mult, op1=ALU.add):
        self.S.op('dve', lambda e: e.tensor_tensor_scan(out=out.ap, data0=d0.ap, data1=d1.ap, initial=_ap(init), op0=op0, op1=op1),
                  _ts(d0, d1, init), _ts(out))

    def memset(self, eng, out, val):
        self.S.op(eng, lambda e: e.memset(out.ap, val), [], _ts(out))

    def mm(self, out, lhsT, rhs, start=True, stop=True):
        self.S.op('pe', lambda e: e.matmul(out.ap, lhsT.ap, rhs.ap, start=start, stop=stop), _ts(lhsT, rhs), _ts(out))

    def tr(self, out, a, ident):
        self.S.op('pe', lambda e: e.transpose(out=out.ap, in_=a.ap, identity=ident.ap), _ts(a, ident), _ts(out))

    def dma(self, out, a, queue='sp'):
        self.S.dma(lambda e: e.dma_start(out=out.ap, in_=a.ap), _ts(a), _ts(out), queue=queue)


D = 1024
KC = 8
W_MIX = 512
DFF = 2816
NFF = 22
EPS = 1e-6
LN_EPS = 64e-5
NCOLS = 243
C_N1G, C_N2G, C_FBF, C_QG, C_KG, C_OG, C_MU, C_W0, C_A0, C_KK, C_KA, C_RK, C_LNW, C_LNB, C_V0, C_VMU, C_CW, C_CB = \
    0, 8, 16, 17, 18, 19, 20, 34, 38, 42, 46, 50, 54, 58, 62, 66, 67, 199
J_FQ, J_FK, J_FV, J_HQ, J_HF, J_HI, J_HO, J_RR, J_RK, J_RV, J_LORA, J_GLO, J_GATE, J_MISC = 0, 4, 8, 12, 16, 20, 24, 28, 32, 36, 40, 41, 42, 66
NCH = 67
K_ID, K_OBD, K_ONE, K_BD, K_UI8, K_NSL8, K_ID8, K_NSU4, K_SU4, K_ME, K_MO, K_W = 0, 128, 256, 384, 512, 1024, 1536, 2048, 2304, 2560, 3072, 3584
KB_ID, KB_ONE, KB_TRI, KB_SEL, KB_W = 0, 128, 256, 384, 1408


def build(S_LEN, L, dbg=(), STOP=99):
    NT = S_LEN // 512
    NCK = S_LEN // 64
    nc = bass.Bass("TRN2", target_bir_lowering=False)
    S = Sched()
    o = Ops(S)

    def dram(name, shape, dt, kind="Internal"):
        if name in dbg:
            kind = "ExternalOutput"
        return T(nc.dram_tensor(name, list(shape), dt, kind=kind).ap(), name, ut=True)

    x_in = dram("x", [S_LEN, D], F32, "ExternalInput")
    cT_in = dram("cT", [128, KC], F32, "ExternalInput")
    cols_in = dram("cols", [L, 128, NCOLS], F32, "ExternalInput")
    lbT_in = dram("lbT", [128, 16], F32, "ExternalInput")
    wada_in = dram("wada", [L, 128, KC, 6 * D], F32, "ExternalInput")
    bada_in = dram("bada", [L, 1, 6 * D], F32, "ExternalInput")
    win_in = dram("win", [L, NCH, 128, KC * 128], F32, "ExternalInput")
    lora_in = dram("lora", [L, 128, 512], F32, "ExternalInput")
    g2_in = dram("g2", [L, 128, 512], F32, "ExternalInput")
    v2_in = dram("v2", [L, 64, 512], F32, "ExternalInput")
    wbr_in = dram("wbr", [L, 128, 12, D], F32, "ExternalInput")
    wout_in = dram("wout", [L, 128, KC, D], F32, "ExternalInput")
    wup_in = dram("wup", [L, NFF, 128, KC * 256], F32, "ExternalInput")
    wdn_in = dram("wdn", [L, 128, NFF, D], F32, "ExternalInput")
    cst_in = dram("cst", [128, K_W], F32, "ExternalInput")
    cstb_in = dram("cstb", [128, KB_W], F32, "ExternalInput")
    y_out = dram("y", [S_LEN, D], F32, "ExternalOutput")

    XT = dram("XT", [128, KC, S_LEN], F32)
    QKT = dram("QKT", [8, 128, S_LEN], BF16)
    FVT = dram("FVT", [4, 128, S_LEN], BF16)
    GT = dram("GT", [24, 128, S_LEN], BF16)
    HQ = dram("HQ", [4, 128, S_LEN], F32)
    HK = dram("HK", [4, 128, S_LEN], F32)
    HI = dram("HI", [4, 128, S_LEN], F32)
    HO = dram("HO", [4, 128, S_LEN], BF16)
    RKR = dram("RKR", [4, 128, NCK, 2, 64], F32)
    RKT = dram("RKT", [4, 128, S_LEN], F32)
    RBT = dram("RBT", [4, 128, S_LEN], F32)
    RVT = dram("RVT", [4, 128, S_LEN], F32)
    RBO = dram("RBO", [4, 128, S_LEN], F32)
    RG = dram("RG", [4, 128, S_LEN], BF16)
    FTD = dram("FTD", [8, S_LEN], F32)
    VF = dram("VF", [4, 128, S_LEN], F32)
    YB = dram("YB", [12, 128, S_LEN], BF16)
    H2 = dram("H2", [KC, 128, S_LEN], BF16)
    ACTT = dram("ACTT", [NFF, 128, S_LEN], BF16)

    arena = nc.alloc_sbuf_tensor("arena", [128, 53200], F32).ap()
    st = {'off': 0, 'mark': 0}

    def sb(n, name='', dt=F32):
        words = n if dt == F32 else (n + 1) // 2
        assert st['off'] + words <= 53200, ("SBUF arena overflow", name, st['off'], words)
        a = arena[:, st['off']: st['off'] + words]
        st['off'] += words
        assert st['off'] <= 53200, ("SBUF arena overflow", name, st['off'])
        if dt != F32:
            a = a.bitcast(dt)
        return T(a, name)

    psb = [T(nc.alloc_psum_tensor(f"ps{i}", [128, 512], F32).ap(), f"ps{i}", xr=True) for i in range(8)]
    pst = {'i': 0}

    pst['n'] = 8

    def ps():
        p = psb[pst['i'] % pst['n']]
        pst['i'] += 1
        return p

    cst = sb(K_W, 'cst')
    o.dma(cst.v, cst_in.v)
    ident = cst[:, K_ID:K_ID + 128]
    onesbd = cst[:, K_OBD:K_OBD + 128]
    ones = cst[:, K_ONE:K_ONE + 128]
    bd = cst[:, K_BD:K_BD + 128]
    ui8 = cst[:, K_UI8:K_UI8 + 512]
    ui4 = cst[:, K_UI8:K_UI8 + 256]
    nsl8 = cst[:, K_NSL8:K_NSL8 + 512]
    id8 = cst[:, K_ID8:K_ID8 + 512]
    nsu4 = cst[:, K_NSU4:K_NSU4 + 256]
    su4 = cst[:, K_SU4:K_SU4 + 256]
    me8 = cst[:, K_ME:K_ME + 512]
    mo8 = cst[:, K_MO:K_MO + 512]
    cstb = sb(KB_W, 'cstb', BF16)
    identb = cstb[:, KB_ID:KB_ID + 128]
    onesb = cstb[:, KB_ONE:KB_ONE + 128]
    trib = cstb[:, KB_TRI:KB_TRI + 128]
    selb = cstb[:, KB_SEL:KB_SEL + 1024]
    ones512 = sb(512, 'ones512')
    o.memset('dve', ones512.v, 1.0)
    colsT = [sb(NCOLS, f'cols{l}') for l in range(L)]
    for l in range(L):
        o.dma(colsT[l].v, cols_in[l])
    modT = [sb(48, f'mod{l}') for l in range(L)]
    derT = [sb(64, f'der{l}') for l in range(L)]
    lbT = sb(16, 'lbT')
    lowT = sb(16, 'lowT')
    o.dma(lbT.v, lbT_in.v)
    ex = sb(16, 'lbexp')
    sm = sb(4, 'lbsum')
    rs = sb(4, 'lbrs')
    oml = sb(16, 'oml')
    cT = sb(KC, 'cT')
    condT = sb(KC, 'condT')
    persist_mark = st['off']

    st['mark'] = st['off']
    cbt = sb(KB_W, 'cbtmp')
    o.dma(cbt.v, cstb_in.v)
    o.cp('dve', cstb.v, cbt.v)
    xin = [sb(D, f'xin{i}') for i in range(2)]
    xst = [sb(KC * 512, f'xst{i}') for i in range(2)]
    for i in range(NT):
        xs = xst[i % 2]
        for q in range(4):
            xi = xin[q % 2]
            t0 = i * 512 + q * 128
            o.dma(xi.v, x_in[t0:t0 + 128, :])
            for h in range(2):
                p = ps()
                for c in range(4):
                    kc = h * 4 + c
                    o.tr(p[:, c * 128:(c + 1) * 128], xi[:, kc * 128:(kc + 1) * 128], ident)
                dst = xs.v.re("p (k t) -> p k t", k=KC)[:, h * 4:(h + 1) * 4, q * 128:(q + 1) * 128]
                o.cp('act' if h == 0 else 'dve', dst, p.v.re("p (c t) -> p c t", c=4))
        o.dma(XT[:, :, i * 512:(i + 1) * 512], xs.v.re("p (k t) -> p k t", k=KC))

    o.dma(cT.v, cT_in.v)
    o.act(condT.v, cT.v, AF.Silu)
    wst = [sb(KC * 512, f'wst{i}') for i in range(2)]
    bada = sb(6 * D, 'bada')
    modrow = sb(6 * D, 'modrow')
    for l in range(L):
        o.dma(bada[0:1, :], bada_in[l])
        for blk in range(12):
            w = wst[blk % 2]
            o.dma(w.v.re("p (k n) -> p k n", k=KC), wada_in[l][:, :, blk * 512:(blk + 1) * 512])
            p = ps()
            for kc in range(KC):
                o.mm(p[0:1, :], condT[:, kc:kc + 1], w[:, kc * 512:(kc + 1) * 512], start=(kc == 0), stop=(kc == KC - 1))
            o.tt('dve', modrow[0:1, blk * 512:(blk + 1) * 512], p[0:1, :], bada[0:1, blk * 512:(blk + 1) * 512], ALU.add)
        p = ps()
        for j in range(48):
            o.mm(p[:, j:j + 1], modrow[0:1, j * 128:(j + 1) * 128], ones[0:1, 0:1])
        o.cp('dve', modT[l].v, p[:, 0:48])
        dr = derT[l]
        cl = colsT[l]
        o.ts('dve', dr[:, 0:8], modT[l][:, 8:16], 1.0, 32.0, ALU.add, ALU.mult)
        o.tt('dve', dr[:, 0:8], dr[:, 0:8], cl[:, C_N1G:C_N1G + 8], ALU.mult)
        o.ts('dve', dr[:, 8:16], modT[l][:, 32:40], 1.0, 32.0, ALU.add, ALU.mult)
        o.tt('dve', dr[:, 8:16], dr[:, 8:16], cl[:, C_N2G:C_N2G + 8], ALU.mult)
        o.ts('dve', dr[:, 17:21], cl[:, C_KA:C_KA + 4], -1.0, 1.0, ALU.mult, ALU.add)
        o.ts('dve', dr[:, 21:22], cl[:, C_QG:C_QG + 1], 8.0, None, ALU.mult)
        o.ts('dve', dr[:, 22:23], cl[:, C_KG:C_KG + 1], 8.0, None, ALU.mult)
        o.ts('dve', dr[:, 23:24], cl[:, C_OG:C_OG + 1], 8.0, None, ALU.mult)
        o.ts('dve', dr[:, 24:28], cl[:, C_MU + 0:C_MU + 4], -1.0, None, ALU.mult)
    o.act(ex.v, lbT.v, AF.Exp)
    o.cp('dve', sm.v, ex[:, 0:4])
    for l in range(1, L):
        o.tt('dve', sm.v, sm.v, ex[:, l * 4:(l + 1) * 4], ALU.add)
    o.recip(rs.v, sm.v)
    for l in range(L):
        o.tt('dve', ex[:, l * 4:(l + 1) * 4], ex[:, l * 4:(l + 1) * 4], rs.v, ALU.mult)
    o.memset('dve', lowT[:, 0:4], 0.0)
    for l in range(1, L):
        o.tt('dve', lowT[:, l * 4:(l + 1) * 4], lowT[:, (l - 1) * 4:l * 4], ex[:, l * 4:(l + 1) * 4], ALU.add)
    o.ts('dve', oml.v, lowT.v, -1.0, 1.0, ALU.mult, ALU.add)
    ctx = dict(nc=nc, S=S, o=o, sb=sb, ps=ps, st=st, L=L, NT=NT, NCK=NCK, S_LEN=S_LEN)
    NEG_C0 = -0.6065306597126334
    WK_N = 16
    sel_in = None

    def emit_layer(l):
        S.barrier()
        st['off'] = persist_mark
        cl = colsT[l]
        dr = derT[l]
        md = modT[l]
        Fcar = sb(2, 'Fcar')
        hsc = sb(3 * 4 * NCK, 'hsc')
        hscv = hsc.v.re("p (a c n) -> p a c n", a=3, c=4)
        gam = sb(4 * NCK, 'gam')
        gamv = gam.v.re("p (c n) -> p c n", c=4)
        ded = [sb(512, f'ded{i}') for i in range(7)]
        wkp = [sb(512, f'wk{i}') for i in range(WK_N)]
        wki = {'i': 0}

        def wk():
            t = wkp[wki['i'] % WK_N]
            wki['i'] += 1
            return t
        wkb = [sb(512, f'wkb{i}', BF16) for i in range(6)]
        wkbi = {'i': 0}

        def wkbf():
            t = wkb[wkbi['i'] % 6]
            wkbi['i'] += 1
            return t
        layer_mark = st['off']

        hts = [sb(KC * 512, f'ht{i}', BF16) for i in range(NT)]

        def norm_tile(xtile, Acol0, Bcol0, dst, sqt):
            sq = sqt
            o.act(sq.v, xtile.v, AF.Square)
            p = ps()
            for kc in range(KC):
                o.mm(p.v, onesb, sq[:, kc * 512:(kc + 1) * 512], start=(kc == 0), stop=(kc == KC - 1))
            rstd = wk()
            rtmp = wk()
            o.rsqrt(rstd[:, 0:512], p.v, 1024.0 * EPS, rtmp[:, 0:512])
            x3 = xtile.v.re("p (k t) -> p k t", k=KC)
            o.tt('dve', x3, x3, rstd[:, 0:512].us(1).bc([128, KC, 512]), ALU.mult)
            for kc in range(KC):
                o.act(dst[:, kc * 512:(kc + 1) * 512], xtile[:, kc * 512:(kc + 1) * 512], AF.Identity,
                      bias=md[:, Bcol0 + kc:Bcol0 + kc + 1], scale=dr[:, Acol0 + kc:Acol0 + kc + 1])

        nmark = st['off']
        xt = [sb(KC * 512, f'xt{i}') for i in range(2)]
        sqt = sb(KC * 512, 'sqt', BF16)
        for i in range(NT):
            x = xt[i % 2]
            o.dma(x.v.re("p (k t) -> p k t", k=KC), XT[:, :, i * 512:(i + 1) * 512])
            norm_tile(x, 0, 0, hts[i], sqt)
        S.barrier()
        st['off'] = nmark
        lorab = sb(512, 'lorab', BF16)
        g2b = sb(512, 'g2b', BF16)
        v2b = sb(512, 'v2b', BF16)
        wtmp = wk()
        o.dma(wtmp[:, 0:512], lora_in[l])
        o.cp('pool', lorab.v, wtmp[:, 0:512])
        wtmp = wk()
        o.dma(wtmp[:, 0:512], g2_in[l])
        o.cp('pool', g2b.v, wtmp[:, 0:512])
        wtmp = wk()
        o.dma(wtmp[0:64, 0:512], v2_in[l])
        o.cp('pool', v2b[0:64, :], wtmp[0:64, 0:512])
        twa = sb(S_LEN, 'twa', BF16)
        sgl = sb(S_LEN, 'sgl', BF16)
        vlo = sb(S_LEN, 'vlo', BF16)
        raws = [sb(514, f'raw{i}') for i in range(3)]
        Cx = sb(513, 'Cx')
        o.memset('dve', Cx[:, 0:1], 0.0)
        wf = [sb(KC * 128, 'wf0')] * 2
        wfi = {'i': 0}
        wbt = [[sb(KC * 128, f'wb{g}_{k}', BF16) for k in range(4)] for g in range(2)]
        grp = {'i': 0}

        def load_group(js):
            g = grp['i'] % 2
            grp['i'] += 1
            outs = []
            for k, j in enumerate(js):
                stg = wf[wfi['i'] % 2]
                wfi['i'] += 1
                o.dma(stg.v, win_in[l][j])
                o.cp('pool', wbt[g][k].v, stg.v)
                outs.append(wbt[g][k])
            return outs

        def proj(wt, i):
            p = ps()
            for kc in range(KC):
                o.mm(p.v, wt[:, kc * 128:(kc + 1) * 128], hts[i][:, kc * 512:(kc + 1) * 512], start=(kc == 0), stop=(kc == KC - 1))
            return p

        def shiftmix(p, raw, mu, r0=0, r1=128, dst=None):
            o.cp('act', raw[r0:r1, 1:513], p[r0:r1, :])
            d = wk()
            o.tt('dve', d[r0:r1, 0:512], raw[r0:r1, 0:512], raw[r0:r1, 1:513], ALU.subtract)
            m = dst if dst is not None else wk()
            o.stt('dve', m[r0:r1, 0:512], d[r0:r1, 0:512], mu, raw[r0:r1, 1:513], ALU.mult, ALU.add)
            o.cp('act', raw[r0:r1, 0:1], raw[r0:r1, 512:513])
            return m

        def reset_raws():
            for r in raws:
                o.memset('pool', r[:, 0:2], 0.0)

        def tsl(i):
            return slice(i * 512, (i + 1) * 512)

        (wm,) = load_group([J_MISC])
        reset_raws()
        for i in range(NT):
            p = proj(wm, i)
            t1 = wk()
            o.act(t1[0:8, 0:512], p[0:8, :], AF.Sigmoid, bias=cl[0:8, C_FBF:C_FBF + 1])
            o.act(t1[0:8, 0:512], t1[0:8, 0:512], AF.Ln)
            t2 = wk()
            o.scan(t2[0:8, 0:512], ones512[0:8, :], t1[0:8, 0:512], 0.0 if i == 0 else Fcar[0:8, 0:1])
            o.cp('dve', Fcar[0:8, 0:1], t2[0:8, 511:512])
            o.dma(FTD[0:8, tsl(i)], t2[0:8, 0:512])
            if l > 0:
                m = shiftmix(p, raws[0], cl[32:64, C_VMU:C_VMU + 1], 32, 64)
                o.cp('dve', vlo[32:64, tsl(i)], m[32:64, 0:512])
        wl, wg = load_group([J_LORA, J_GLO])
        reset_raws()
        for i in range(NT):
            p = proj(wl, i)
            m = shiftmix(p, raws[0], cl[:, C_MU + 12:C_MU + 13])
            o.act(twa[0:64, tsl(i)], m[0:64, 0:512], AF.Tanh)
            o.cp('dve', twa[64:128, tsl(i)], m[64:128, 0:512])
            p = proj(wg, i)
            m = shiftmix(p, raws[1], cl[:, C_MU + 13:C_MU + 14])
            o.act(sgl[:, tsl(i)], m[:, 0:512], AF.Sigmoid)
        for c in range(4):
            wr, wkk_, wv = load_group([J_RR + c, J_RK + c, J_RV + c])
            reset_raws()
            cs = slice(c * 128, (c + 1) * 128)
            for i in range(NT):
                pr = proj(wr, i)
                pk = proj(wkk_, i)
                pv = proj(wv, i)
                r_m = shiftmix(pr, raws[0], cl[:, C_MU + c:C_MU + c + 1], dst=ded[0])
                k_m = shiftmix(pk, raws[1], cl[:, C_MU + 4 + c:C_MU + 5 + c], dst=ded[1])
                v_m = shiftmix(pv, raws[2], cl[:, C_MU + 8 + c:C_MU + 9 + c], dst=ded[2])
                pa = ps()
                o.mm(pa.v, lorab[64:128, cs], twa[64:128, tsl(i)])
                a = ded[3]
                o.act(a[:, 0:512], pa.v, AF.Sigmoid, bias=cl[:, C_A0 + c:C_A0 + c + 1])
                pw = ps()
                o.mm(pw.v, lorab[0:64, cs], twa[0:64, tsl(i)])
                sgw = wk()
                o.act(sgw[:, 0:512], pw.v, AF.Sigmoid, bias=cl[:, C_W0 + c:C_W0 + c + 1])
                o.scan(Cx[:, 1:513], ones512.v, sgw[:, 0:512], 0.0)
                ref = Cx[:, 0:512].re("p (c t) -> p c t", t=64)[:, :, 0:1].bc([128, 8, 64])
                Dd = wk()
                o.tt('dve', Dd[:, 0:512].re("p (c t) -> p c t", t=64), Cx[:, 1:513].re("p (c t) -> p c t", t=64), ref, ALU.subtract)
                Dm = wk()
                o.tt('dve', Dm[:, 0:512].re("p (c t) -> p c t", t=64), Cx[:, 0:512].re("p (c t) -> p c t", t=64), ref, ALU.subtract)
                E1 = ded[4]
                E2 = wk()
                E3 = ded[5]
                o.act(E1[:, 0:512], Dd[:, 0:512], AF.Exp, scale=NEG_C0)
                o.act(E2[:, 0:512], Dm[:, 0:512], AF.Exp, scale=NEG_C0)
                o.act(E3[:, 0:512], Dd[:, 0:512], AF.Exp, scale=-NEG_C0)
                o.cp('pool', gamv[:, c, i * 8:(i + 1) * 8], E1[:, 0:512].re("p (c t) -> p c t", t=64)[:, :, 63])
                kk = wk()
                o.ts('pool', kk[:, 0:512], k_m[:, 0:512], cl[:, C_KK + c:C_KK + c + 1], None, ALU.mult)
                sq = wk()
                o.tt('pool', sq[:, 0:512], kk[:, 0:512], kk[:, 0:512], ALU.mult)
                pss = ps()
                o.mm(pss.v, onesbd, sq[:, 0:512])
                rinv = wk()
                rtmp = wk()
                o.act(rtmp[:, 0:512], pss.v, AF.Sqrt)
                o.ts('dve', rtmp[:, 0:512], rtmp[:, 0:512], 1e-12, None, ALU.max)
                o.recip(rinv[:, 0:512], rtmp[:, 0:512])
                kap = wk()
                o.tt('dve', kap[:, 0:512], kk[:, 0:512], rinv[:, 0:512], ALU.mult)
                tp = wk()
                o.ts('pool', tp[:, 0:512], a[:, 0:512], cl[:, C_KA + c:C_KA + c + 1], dr[:, 17 + c:18 + c], ALU.mult, ALU.add)
                kf = ded[6]
                o.tt('pool', kf[:, 0:512], k_m[:, 0:512], tp[:, 0:512], ALU.mult)
                bb = wk()
                o.tt('pool', bb[:, 0:512], a[:, 0:512], kap[:, 0:512], ALU.mult)
                ot = wk()
                o.tt('dve', ot[:, 0:512], r_m[:, 0:512], E1[:, 0:512], ALU.mult)
                o.dma(RKR[c][:, i * 8:(i + 1) * 8, 1, :], ot[:, 0:512].re("p (c t) -> p c t", t=64))
                ot = wk()
                o.tt('dve', ot[:, 0:512], kap[:, 0:512], E2[:, 0:512], ALU.mult)
                o.dma(RKR[c][:, i * 8:(i + 1) * 8, 0, :], ot[:, 0:512].re("p (c t) -> p c t", t=64))
                ot = wk()
                o.tt('pool', ot[:, 0:512], kf[:, 0:512], E3[:, 0:512], ALU.mult)
                o.dma(RKT[c][:, tsl(i)], ot[:, 0:512])
                ot = wk()
                o.tt('pool', ot[:, 0:512], bb[:, 0:512], E3[:, 0:512], ALU.mult)
                o.dma(RBT[c][:, tsl(i)], ot[:, 0:512])
                if l == 0:
                    o.dma(VF[c][:, tsl(i)], v_m[:, 0:512])
                    vv = v_m
                else:
                    vf = wk()
                    o.dma(vf[:, 0:512], VF[c][:, tsl(i)])
                    pg = ps()
                    o.mm(pg.v, v2b[32:64, cs], vlo[32:64, tsl(i)])
                    gt = wk()
                    o.act(gt[:, 0:512], pg.v, AF.Sigmoid, bias=cl[:, C_V0 + c:C_V0 + c + 1])
                    dv = wk()
                    o.tt('dve', dv[:, 0:512], vf[:, 0:512], v_m[:, 0:512], ALU.subtract)
                    o.tt('dve', dv[:, 0:512], dv[:, 0:512], gt[:, 0:512], ALU.mult)
                    vv = wk()
                    o.tt('dve', vv[:, 0:512], dv[:, 0:512], v_m[:, 0:512], ALU.add)
                o.dma(RVT[c][:, tsl(i)], vv[:, 0:512])
                rk = wk()
                o.stt('dve', rk[:, 0:512], r_m[:, 0:512], cl[:, C_RK + c:C_RK + c + 1], kf[:, 0:512], ALU.mult, ALU.mult)
                pb = ps()
                o.mm(pb.v, onesbd, rk[:, 0:512])
                bon = wk()
                o.tt('dve', bon[:, 0:512], pb.v, vv[:, 0:512], ALU.mult)
                o.dma(RBO[c][:, tsl(i)], bon[:, 0:512])
                pgg = ps()
                o.mm(pgg.v, g2b[:, cs], sgl[:, tsl(i)])
                gb = wkbf()
                o.cp('act', gb.v, pgg.v)
                o.dma(RG[c][:, tsl(i)], gb.v)
        for c in range(4):
            wq, wfg, wi, wo = load_group([J_HQ + c, J_HF + c, J_HI + c, J_HO + c])
            lcol = slice(l * 4 + c, l * 4 + c + 1)
            for i in range(NT):
                pq = proj(wq, i)
                pf = proj(wfg, i)
                pi_ = proj(wi, i)
                po = proj(wo, i)
                sq_ = wk()
                o.act(sq_[:, 0:512], pq.v, AF.Silu)
                sgf = wk()
                o.act(sgf[:, 0:512], pf.v, AF.Sigmoid)
                gate = wk()
                o.ts('dve', gate[:, 0:512], sgf[:, 0:512], oml[:, lcol], lowT[:, lcol], ALU.mult, ALU.add)
                lg = wk()
                o.act(lg[:, 0:512], gate[:, 0:512], AF.Ln)
                kx = wk()
                o.ts('pool', kx[:, 0:512], gate[:, 0:512], -1.0, 1.0, ALU.mult, ALU.add)
                o.scan(Cx[:, 1:513], ones512.v, lg[:, 0:512], 0.0)
                G3 = Cx[:, 1:513].re("p (c t) -> p c t", t=64)
                Dd = wk()
                o.tt('dve', Dd[:, 0:512].re("p (c t) -> p c t", t=64), G3, G3[:, :, 31:32].bc([128, 8, 64]), ALU.subtract)
                E1 = wk()
                E3 = wk()
                o.act(E1[:, 0:512], Dd[:, 0:512], AF.Exp)
                o.act(E3[:, 0:512], Dd[:, 0:512], AF.Exp, scale=-1.0)
                ot = wk()
                o.tt('dve', ot[:, 0:512], sq_[:, 0:512], E1[:, 0:512], ALU.mult)
                o.dma(HQ[c][:, tsl(i)], ot[:, 0:512])
                ot = wk()
                o.tt('pool', ot[:, 0:512], kx[:, 0:512], E3[:, 0:512], ALU.mult)
                o.dma(HK[c][:, tsl(i)], ot[:, 0:512])
                o.cp('pool', hscv[:, 0, c, i * 8:(i + 1) * 8], E1[:, 0:512].re("p (c t) -> p c t", t=64)[:, :, 63])
                G0 = Cx[:, 0:512].re("p (c t) -> p c t", t=64)
                d8 = wk()
                o.tt('dve', d8[:, 0:8], G3[:, :, 63], G0[:, :, 0], ALU.subtract)
                o.act(hscv[:, 1, c, i * 8:(i + 1) * 8], d8[:, 0:8], AF.Exp)
                d8 = wk()
                o.tt('dve', d8[:, 0:8], G3[:, :, 31], G0[:, :, 0], ALU.subtract)
                o.act(hscv[:, 2, c, i * 8:(i + 1) * 8], d8[:, 0:8], AF.Exp)
                it = wk()
                o.cp('act', it[:, 0:512], pi_.v)
                o.dma(HI[c][:, tsl(i)], it[:, 0:512])
                ob = wkbf()
                o.act(ob.v, po.v, AF.Silu)
                o.dma(HO[c][:, tsl(i)], ob.v)
        for c in range(4):
            wq, wk_ = load_group([J_FQ + c, J_FK + c])
            for i in range(NT):
                for which, wt in ((0, wq), (1, wk_)):
                    p = proj(wt, i)
                    sq = wk()
                    o.act(sq[:, 0:512], p.v, AF.Square)
                    pss = ps()
                    o.mm(pss.v, onesbd, sq[:, 0:512])
                    rstd = wk()
                    rtmp = wk()
                    o.rsqrt(rstd[:, 0:512], pss.v, 64.0 * EPS, rtmp[:, 0:512])
                    qb = wkbf()
                    o.stt('dve', qb.v, p.v, dr[:, 21 + which:22 + which], rstd[:, 0:512], ALU.mult, ALU.mult)
                    o.dma(QKT[which * 4 + c][:, tsl(i)], qb.v)
        wvs = load_group([J_FV + c for c in range(4)])
        for c in range(4):
            for i in range(NT):
                p = proj(wvs[c], i)
                vb = wkbf()
                o.cp('act', vb.v, p.v)
                o.dma(FVT[c][:, tsl(i)], vb.v)
        for c0 in range(0, 24, 4):
            wgs = load_group([J_GATE + c0 + k for k in range(4)])
            for k in range(4):
                for i in range(NT):
                    p = proj(wgs[k], i)
                    gb = wkbf()
                    o.act(gb.v, p.v, AF.Sigmoid)
                    o.dma(GT[c0 + k][:, tsl(i)], gb.v)
        if STOP <= 1:
            return
        NB = S_LEN // 128

        def r3(v, c):
            return v.re("p (c t) -> p c t", c=c)

        def par(v, bk):
            return v.re("p (c a t) -> p c a t", c=4, a=2)[:, :, bk, :]

        S.barrier()
        st['off'] = layer_mark
        Fsb = sb(S_LEN, 'Fsb')
        F8 = sb(S_LEN, 'F8', BF16)
        o.dma(Fsb[0:8, :], FTD.v)
        o.ts('dve', F8[0:8, :], Fsb[0:8, :], 8.0, None, ALU.mult)
        negF = sb(NB * 8, 'negF')
        for blk in range(NB):
            p = ps()
            o.tr(p[:, 0:8], Fsb[0:8, blk * 128:(blk + 1) * 128], ident[0:8, 0:8])
            o.ts('dve', negF[:, blk * 8:(blk + 1) * 8], p[:, 0:8], -1.0, None, ALU.mult)
        VTM = sb(NB * 512, 'VTM', BF16)
        vmark = st['off']
        fv = [sb(S_LEN, f'fv{c}', BF16) for c in range(4)]
        for c in range(4):
            o.dma(fv[c].v, FVT[c])
        for blk in range(NB):
            p = ps()
            pb = p.v.cast(BF16)
            for c in range(4):
                o.tr(pb[:, c * 128:(c + 1) * 128], fv[c][:, blk * 128:(blk + 1) * 128], identb)
            o.cp('act' if blk % 2 else 'dve', VTM[:, blk * 512:(blk + 1) * 512], pb[:, 0:512])
        S.barrier()
        st['off'] = vmark
        qk = [[sb(S_LEN, f'q{g}', BF16), sb(S_LEN, f'k{g}', BF16)] for g in range(2)]
        Pt = [sb(512, f'P{i}', BF16) for i in range(6)]
        yt = [sb(512, f'y{i}', BF16) for i in range(3)]
        rd = [sb(512, f'rd{i}') for i in range(2)]
        pst['pc'] = 0
        ycount = 0
        pst['n'] = 4
        for c in range(4):
            qT, kT = qk[c % 2]
            o.dma(qT.v, QKT[c])
            o.dma(kT.v, QKT[4 + c])
            for h2 in range(2):
                h = 2 * c + h2
                rows = slice(h2 * 64, (h2 + 1) * 64)
                for qc in range(NT):
                    pn = psb[4 + 2 * (ycount % 2)]
                    pd = psb[5 + 2 * (ycount % 2)]
                    nkb = 4 * qc + 4
                    pend = []

                    def stage_a(kb):
                        i_ = kb - 4 * qc
                        n0 = 128 * i_ if i_ > 0 else 0
                        qs = slice(qc * 512 + n0, (qc + 1) * 512)
                        psc = ps()
                        o.mm(psc[:, n0:512], kT[rows, kb * 128:(kb + 1) * 128], qT[rows, qs], start=True, stop=False)
                        if i_ >= 0:
                            o.mm(psc[:, n0:n0 + 128], identb, trib, start=False, stop=False)
                        o.mm(psc[:, n0:512], selb[0:8, h * 128:(h + 1) * 128], F8[0:8, qs], start=False, stop=True)
                        P = Pt[pst['pc'] % 6]
                        pst['pc'] += 1
                        o.act(P[:, n0:512], psc[:, n0:512], AF.Exp, bias=negF[:, kb * 8 + h:kb * 8 + h + 1], scale=0.125)
                        return (kb, n0, P)

                    def stage_b(kb, n0, P):
                        o.mm(pn[0:64, n0:512], VTM[:, kb * 512 + h * 64:kb * 512 + (h + 1) * 64], P[:, n0:512], start=(kb == 0), stop=(kb == nkb - 1))
                        o.mm(pd[0:64, n0:512], onesb[:, 0:64], P[:, n0:512], start=(kb == 0), stop=(kb == nkb - 1))
                    for kb in range(nkb):
                        pend.append(stage_a(kb))
                        if len(pend) > 2:
                            stage_b(*pend.pop(0))
                    while pend:
                        stage_b(*pend.pop(0))
                    r = rd[ycount % 2]
                    y = yt[ycount % 3]
                    ycount += 1
                    o.recip(r[0:64, :], pd[0:64, :])
                    o.tt('dve', y[0:64, :], pn[0:64, :], r[0:64, :], ALU.mult)
                    o.dma(YB[c][rows, tsl(qc)], y[0:64, :])
        pst['n'] = 8
        if STOP <= 2:
            return

        S.barrier()
        st['off'] = layer_mark
        Mst = sb(512, 'Mst')
        M3 = r3(Mst.v, 4)
        o.memset('dve', Mst.v, 0.0)
        vpad = sb(1024, 'vpad')
        o.memset('pool', vpad.v, 0.0)
        ktm = [sb(512, f'ktm{i}') for i in range(2)]
        vtm = [sb(512, f'vtm{i}') for i in range(2)]
        hq = [sb(2048, f'hq{i}') for i in range(2)]
        hk = [sb(2048, f'hk{i}') for i in range(2)]
        hi = [sb(2048, f'hi{i}') for i in range(2)]
        ho = [sb(2048, f'ho{i}', BF16) for i in range(2)]
        ot = sb(2048, 'ot')
        bd4 = bd.us(1).bc([128, 4, 128])
        if HGCUT <= -1:
            return
        for i in range(NT):
            b = i % 2
            o.dma(r3(hq[b].v, 4), HQ.v.re("c p t -> p c t")[:, :, tsl(i)])
            o.dma(r3(hk[b].v, 4), HK.v.re("c p t -> p c t")[:, :, tsl(i)])
            o.dma(r3(hi[b].v, 4), HI.v.re("c p t -> p c t")[:, :, tsl(i)])
            o.dma(r3(ho[b].v, 4), HO.v.re("c p t -> p c t")[:, :, tsl(i)])
            if HGCUT <= 0:
                continue
            for ck in range(8):
                g = i * 8 + ck

                def cc(c):
                    return slice(c * 512 + ck * 64, c * 512 + (ck + 1) * 64)
                pk_ = ps()
                pv_ = ps()
                for c in range(4):
                    o.mm(pk_[0:64, c * 128:(c + 1) * 128], hk[b][:, cc(c)], ident)
                    o.mm(pv_[0:64, c * 128:(c + 1) * 128], hi[b][:, cc(c)], ident)
                kt = ktm[g % 2]
                vt = vtm[g % 2]
                if HGSUB <= 1:
                    continue
                o.cp('act', kt[0:64, :], pk_[0:64, :])
                o.cp('act', vt[0:64, :], pv_[0:64, :])
                if HGSUB <= 2:
                    continue
                o.tt('dve', vpad[0:64, 0:512], pv_[0:64, :], me8[0:64, :], ALU.mult)
                o.tt('dve', vpad[0:64, 512:1024], pv_[0:64, :], mo8[0:64, :], ALU.mult)
                if HGCUT <= 1:
                    continue
                M0s = wk()
                o.tt('dve', r3(M0s.v, 4), M3, hscv[:, 2, :, g:g + 1].bc([128, 4, 128]), ALU.mult)
                if HGSUB <= 3:
                    continue
                pA = [ps(), ps()]
                for h in range(8):
                    c = h // 2
                    rr = slice((h % 2) * 64, (h % 2) * 64 + 64)
                    b0 = c * 512 + ck * 64
                    o.mm(pA[h % 2][0:64, c * 64 + 32:(c + 1) * 64], hk[b][rr, b0:b0 + 64], hq[b][rr, b0 + 32:b0 + 64])
                    o.mm(pA[h % 2][0:32, c * 64:c * 64 + 32], hk[b][rr, b0:b0 + 32], hq[b][rr, b0:b0 + 32])
                Am = wk()
                o.memset('pool', Am[32:64, :], 0.0)
                for bk in range(2):
                    o.tt('dve', par(Am[0:64, :], bk)[:, :, 32:64], r3(pA[bk][0:64, 0:256], 4)[:, :, 32:64], r3(ui4[0:64, :], 4)[:, :, 32:64], ALU.mult)
                    o.tt('dve', par(Am[0:32, :], bk)[:, :, 0:32], r3(pA[bk][0:32, 0:256], 4)[:, :, 0:32], r3(ui4[0:32, :], 4)[:, :, 0:32], ALU.mult)
                if HGCUT <= 2:
                    continue
                pO = ps()
                for c in range(4):
                    oc_ = pO[:, c * 64:(c + 1) * 64]
                    o.mm(oc_, M0s[:, c * 128:(c + 1) * 128], hq[b][:, cc(c)], start=True, stop=False)
                    o.mm(oc_, vpad[0:64, c * 128:(c + 1) * 128], Am[0:64, (2 * c) * 64:(2 * c + 1) * 64], start=False, stop=False)
                    o.mm(oc_, vpad[0:64, 512 + c * 128:512 + (c + 1) * 128], Am[0:64, (2 * c + 1) * 64:(2 * c + 2) * 64], start=False, stop=True)
                o.cp('act', r3(ot.v, 4)[:, :, ck * 64:(ck + 1) * 64], r3(pO[:, 0:256], 4))
                if HGCUT <= 3:
                    continue
                pS = ps()
                for c in range(4):
                    o.mm(pS[:, c * 128:(c + 1) * 128], kt[0:64, c * 128:(c + 1) * 128], vt[0:64, c * 128:(c + 1) * 128])
                t1 = wk()
                o.tt('dve', r3(t1.v, 4), r3(pS.v, 4), bd4, ALU.mult)
                o.tt('dve', r3(t1.v, 4), r3(t1.v, 4), hscv[:, 0, :, g:g + 1].bc([128, 4, 128]), ALU.mult)
                o.tt('pool', M3, M3, hscv[:, 1, :, g:g + 1].bc([128, 4, 128]), ALU.mult)
                o.tt('pool', M3, M3, r3(t1.v, 4), ALU.add)
            for c in range(4):
                oc_ = ot[:, c * 512:(c + 1) * 512]
                sq = wk()
                o.act(sq.v, oc_, AF.Square)
                pss = ps()
                o.mm(pss.v, onesbd, sq.v)
                rstd = wk()
                rtmp = wk()
                o.rsqrt(rstd.v, pss.v, 64.0 * EPS, rtmp.v)
                y = wk()
                o.stt('dve', y.v, oc_, dr[:, 23:24], rstd.v, ALU.mult, ALU.mult)
                yb_ = wkbf()
                o.tt('dve', yb_.v, y.v, ho[b][:, c * 512:(c + 1) * 512], ALU.mult)
                o.dma(YB[4 + c][:, tsl(i)], yb_.v)
        if STOP <= 3:
            return

        S.barrier()
        st['off'] = layer_mark
        Mst = sb(512, 'MstR')
        M3 = r3(Mst.v, 4)
        o.memset('dve', Mst.v, 0.0)
        vpad = sb(1024, 'vpadR')
        nupad = sb(1024, 'nupad')
        o.memset('pool', vpad.v, 0.0)
        o.memset('pool', nupad.v, 0.0)
        sm64 = [sb(512, f'sm{i}') for i in range(14)]
        (ktm_, btm_, vtm_, RBt, Qt, RKt, Wsb, nut, Y0, Y1, Yt0, Yt1, Tt0, Tt1) = sm64
        rkr = [sb(4096, 'rkr0')] * 2
        rkk = [sb(2048, 'rkk0')] * 2
        rbb = [sb(2048, 'rbb0')] * 2
        rvv = [sb(2048, 'rvv0')] * 2
        bo = [sb(2048, 'bo0')] * 2
        rg = [sb(2048, 'rg0', BF16)] * 2
        ot = sb(2048, 'otR')
        hsl = [slice(h * 64, (h + 1) * 64) for h in range(8)]
        for i in range(NT):
            b = i % 2
            o.dma(rkr[b].v.re("p (c n a t) -> p c n a t", c=4, n=8, a=2), RKR.v.re("c p n a t -> p c n a t")[:, :, i * 8:(i + 1) * 8, :, :])
            o.dma(r3(rkk[b].v, 4), RKT.v.re("c p t -> p c t")[:, :, tsl(i)])
            o.dma(r3(rbb[b].v, 4), RBT.v.re("c p t -> p c t")[:, :, tsl(i)])
            o.dma(r3(rvv[b].v, 4), RVT.v.re("c p t -> p c t")[:, :, tsl(i)])
            o.dma(r3(bo[b].v, 4), RBO.v.re("c p t -> p c t")[:, :, tsl(i)])
            o.dma(r3(rg[b].v, 4), RG.v.re("c p t -> p c t")[:, :, tsl(i)])
            for ck in range(8):
                g = i * 8 + ck

                def cc(c):
                    return slice(c * 512 + ck * 64, c * 512 + (ck + 1) * 64)

                def KR(c, a0, a1):
                    off = (c * 8 + ck) * 128
                    return slice(off + a0 * 64, off + a1 * 64)
                pk_ = ps()
                pb_ = ps()
                pv_ = ps()
                for c in range(4):
                    o.mm(pk_[0:64, c * 128:(c + 1) * 128], rkk[b][:, cc(c)], ident)
                    o.mm(pb_[0:64, c * 128:(c + 1) * 128], rbb[b][:, cc(c)], ident)
                    o.mm(pv_[0:64, c * 128:(c + 1) * 128], rvv[b][:, cc(c)], ident)
                o.cp('act', ktm_[0:64, :], pk_[0:64, :])
                o.cp('act', btm_[0:64, :], pb_[0:64, :])
                o.cp('act', vtm_[0:64, :], pv_[0:64, :])
                o.tt('dve', vpad[0:64, 0:512], pv_[0:64, :], me8[0:64, :], ALU.mult)
                o.tt('dve', vpad[0:64, 512:1024], pv_[0:64, :], mo8[0:64, :], ALU.mult)
                pA1 = [ps(), ps()]
                pA2 = [ps(), ps()]
                for h in range(8):
                    c = h // 2
                    rr = slice((h % 2) * 64, (h % 2) * 64 + 64)
                    o.mm(pA1[h % 2][0:64, c * 128:(c + 1) * 128], rbb[b][rr, cc(c)], rkr[b][rr, KR(c, 0, 2)])
                    o.mm(pA2[h % 2][0:64, c * 128:(c + 1) * 128], rkk[b][rr, cc(c)], rkr[b][rr, KR(c, 0, 2)])
                for bk in range(2):
                    v1 = pA1[bk][0:64, :].re("p (h x) -> p h x", h=4)
                    v2 = pA2[bk][0:64, :].re("p (h x) -> p h x", h=4)
                    o.tt('dve', par(Yt0[0:64, :], bk), v1[:, :, 0:64], r3(nsu4[0:64, :], 4), ALU.mult)
                    o.tt('dve', par(RBt[0:64, :], bk), v1[:, :, 64:128], r3(ui4[0:64, :], 4), ALU.mult)
                    o.tt('dve', par(Qt[0:64, :], bk), v2[:, :, 0:64], r3(su4[0:64, :], 4), ALU.mult)
                    o.tt('dve', par(RKt[0:64, :], bk), v2[:, :, 64:128], r3(ui4[0:64, :], 4), ALU.mult)
                pA3 = [ps(), ps()]
                for h in range(8):
                    c = h // 2
                    rr = slice((h % 2) * 64, (h % 2) * 64 + 64)
                    o.mm(pA3[h % 2][0:64, c * 64:(c + 1) * 64], rkr[b][rr, KR(c, 0, 1)], rbb[b][rr, cc(c)])
                for bk in range(2):
                    o.tt('dve', par(Y0[0:64, :], bk), r3(pA3[bk][0:64, 0:256], 4), r3(nsl8[0:64, 0:256], 4), ALU.mult)
                o.tt('pool', Tt0[0:64, :], Yt0[0:64, :], id8[0:64, :], ALU.add)
                Ya, Yta, Tta = Y0, Yt0, Tt0
                Yb, Ytb, Ttb = Y1, Yt1, Tt1
                for j in range(5):
                    pY = ps()
                    pYt = ps()
                    for h in range(8):
                        o.mm(pY[0:64, hsl[h]], Yta[0:64, hsl[h]], Ya[0:64, hsl[h]])
                        o.mm(pYt[0:64, hsl[h]], Ya[0:64, hsl[h]], Yta[0:64, hsl[h]])
                    o.cp('act', Yb[0:64, :], pY[0:64, :])
                    o.cp('act', Ytb[0:64, :], pYt[0:64, :])
                    pT = ps()
                    for h in range(8):
                        o.mm(pT[0:64, hsl[h]], Yb[0:64, hsl[h]], Tta[0:64, hsl[h]])
                    o.tt('dve', Ttb[0:64, :], pT[0:64, :], Tta[0:64, :], ALU.add)
                    Ya, Yta, Tta, Yb, Ytb, Ttb = Yb, Ytb, Ttb, Ya, Yta, Tta
                pW = ps()
                for c in range(4):
                    o.mm(pW[0:64, c * 128:(c + 1) * 128], rkr[b][:, KR(c, 0, 1)], Mst[:, c * 128:(c + 1) * 128], start=True, stop=False)
                    for h2 in range(2):
                        h = 2 * c + h2
                        o.mm(pW[0:64, hsl[h]], Qt[0:64, hsl[h]], vtm_[0:64, hsl[h]], start=False, stop=True)
                o.cp('act', Wsb[0:64, :], pW[0:64, :])
                pU = ps()
                for h in range(8):
                    o.mm(pU[0:64, hsl[h]], Tta[0:64, hsl[h]], Wsb[0:64, hsl[h]])
                o.ts('dve', nut[0:64, :], pU[0:64, :], -1.0, None, ALU.mult)
                o.stt('dve', nupad[0:64, 0:512], pU[0:64, :], -1.0, me8[0:64, :], ALU.mult, ALU.mult)
                o.stt('dve', nupad[0:64, 512:1024], pU[0:64, :], -1.0, mo8[0:64, :], ALU.mult, ALU.mult)
                pO = ps()
                for c in range(4):
                    oc_ = pO[:, c * 64:(c + 1) * 64]
                    o.mm(oc_, Mst[:, c * 128:(c + 1) * 128], rkr[b][:, KR(c, 1, 2)], start=True, stop=False)
                    for h2 in range(2):
                        h = 2 * c + h2
                        o.mm(oc_, vpad[0:64, h2 * 512 + c * 128:h2 * 512 + (c + 1) * 128], RKt[0:64, hsl[h]], start=False, stop=False)
                        o.mm(oc_, nupad[0:64, h2 * 512 + c * 128:h2 * 512 + (c + 1) * 128], RBt[0:64, hsl[h]], start=False, stop=(h2 == 1))
                o.cp('act', r3(ot.v, 4)[:, :, ck * 64:(ck + 1) * 64], r3(pO[:, 0:256], 4))
                pS = ps()
                for c in range(4):
                    o.mm(pS[:, c * 128:(c + 1) * 128], ktm_[0:64, c * 128:(c + 1) * 128], vtm_[0:64, c * 128:(c + 1) * 128], start=True, stop=False)
                    o.mm(pS[:, c * 128:(c + 1) * 128], btm_[0:64, c * 128:(c + 1) * 128], nut[0:64, c * 128:(c + 1) * 128], start=False, stop=True)
                t1 = wk()
                o.tt('dve', r3(t1.v, 4), r3(pS.v, 4), bd4, ALU.mult)
                o.tt('dve', t1.v, t1.v, Mst.v, ALU.add)
                o.tt('pool', M3, r3(t1.v, 4), gamv[:, :, g:g + 1].bc([128, 4, 128]), ALU.mult)
            for c in range(4):
                oc_ = ot[:, c * 512:(c + 1) * 512]
                pm = ps()
                o.mm(pm.v, onesbd, oc_)
                sq = wk()
                o.act(sq.v, oc_, AF.Square)
                pvv = ps()
                o.mm(pvv.v, onesbd, sq.v)
                mean = wk()
                o.ts('dve', mean.v, pm.v, 1.0 / 64, None, ALU.mult)
                cen = wk()
                o.tt('dve', cen.v, oc_, mean.v, ALU.subtract)
                msq = wk()
                o.tt('pool', msq.v, mean.v, mean.v, ALU.mult)
                var = wk()
                o.stt('dve', var.v, pvv.v, 1.0 / 64, msq.v, ALU.mult, ALU.subtract)
                rstd = wk()
                rtmp = wk()
                o.rsqrt(rstd.v, var.v, LN_EPS, rtmp.v)
                y = wk()
                o.tt('dve', y.v, cen.v, rstd.v, ALU.mult)
                o.ts('dve', y.v, y.v, cl[:, C_LNW + c:C_LNW + c + 1], cl[:, C_LNB + c:C_LNB + c + 1], ALU.mult, ALU.add)
                o.tt('pool', y.v, y.v, bo[b][:, c * 512:(c + 1) * 512], ALU.add)
                yb_ = wkbf()
                o.tt('dve', yb_.v, y.v, rg[b][:, c * 512:(c + 1) * 512], ALU.mult)
                o.dma(YB[8 + c][:, tsl(i)], yb_.v)
        if STOP <= 4:
            return

        S.barrier()
        st['off'] = layer_mark
        wbrb = sb(12 * 1024, 'wbrb', BF16)
        woutb = sb(8 * 1024, 'woutb', BF16)
        stg = [sb(2048, 'stg0')] * 2
        for q in range(6):
            s_ = stg[q % 2]
            o.dma(r3(s_.v, 2), wbr_in[l][:, 2 * q:2 * q + 2, :])
            o.cp('pool', wbrb[:, 2 * q * 1024:(2 * q + 2) * 1024], s_.v)
        for q in range(4):
            s_ = stg[q % 2]
            o.dma(r3(s_.v, 2), wout_in[l][:, 2 * q:2 * q + 2, :])
            o.cp('pool', woutb[:, 2 * q * 1024:(2 * q + 2) * 1024], s_.v)
        ybt = [sb(12 * 512, 'ybt0', BF16)] * 2
        gtt = [sb(3 * 512, f'gtt{i}', BF16) for i in range(2)]
        mg = sb(8 * 512, 'mg', BF16)
        xt = [sb(KC * 512, 'xt30')] * 2
        sqt = sb(KC * 512, 'sqt3', BF16)
        h2o = [sb(KC * 512, 'h2o0', BF16)] * 2
        gn = 0
        for i in range(NT):
            yb_ = ybt[i % 2]
            o.dma(r3(yb_.v, 12), YB.v.re("c p t -> p c t")[:, :, tsl(i)])
            x = xt[i % 2]
            o.dma(r3(x.v, KC), XT[:, :, tsl(i)])
            for oc in range(8):
                g_ = gtt[gn % 2]
                gn += 1
                o.dma(r3(g_.v, 3), GT.v.re("(b o) p t -> o p b t", b=3)[oc][:, :, tsl(i)])
                acc = wk()
                for b_ in range(3):
                    p = ps()
                    for kc in range(4):
                        o.mm(p.v, wbrb[:, (b_ * 4 + kc) * 1024 + oc * 128:(b_ * 4 + kc) * 1024 + (oc + 1) * 128],
                             yb_[:, (b_ * 4 + kc) * 512:(b_ * 4 + kc + 1) * 512], start=(kc == 0), stop=(kc == 3))
                    if b_ == 0:
                        o.tt('dve', acc.v, p.v, g_[:, 0:512], ALU.mult)
                    else:
                        t = wk()
                        o.tt('dve', t.v, p.v, g_[:, b_ * 512:(b_ + 1) * 512], ALU.mult)
                        if b_ == 1:
                            o.tt('pool', acc.v, acc.v, t.v, ALU.add)
                        else:
                            o.tt('pool', mg[:, oc * 512:(oc + 1) * 512], acc.v, t.v, ALU.add)
            for oc in range(8):
                p = ps()
                for kc in range(KC):
                    o.mm(p.v, woutb[:, kc * 1024 + oc * 128:kc * 1024 + (oc + 1) * 128], mg[:, kc * 512:(kc + 1) * 512], start=(kc == 0), stop=(kc == KC - 1))
                xs_ = x[:, oc * 512:(oc + 1) * 512]
                o.stt('dve', xs_, p.v, md[:, 16 + oc:17 + oc], xs_, ALU.mult, ALU.add)
            o.dma(XT[:, :, tsl(i)], r3(x.v, KC))
            h2t = h2o[i % 2]
            norm_tile(x, 8, 24, h2t, sqt)
            o.dma(H2.v.re("k p t -> p k t")[:, :, tsl(i)], r3(h2t.v, KC))
        if STOP <= 5:
            return

        S.barrier()
        st['off'] = layer_mark
        hts = [sb(KC * 512, f'h2_{i}', BF16) for i in range(NT)]
        for i in range(NT):
            o.dma(r3(hts[i].v, KC), H2.v.re("k p t -> p k t")[:, :, tsl(i)])
        wst2 = [sb(2048, f'wst2_{i}') for i in range(2)]
        wpb = [sb(2048, f'wpb{i}', BF16) for i in range(2)]
        rawv = sb(514, 'rawv')
        rawg = sb(514, 'rawg')
        for j in range(NFF):
            s_ = wst2[j % 2]
            o.dma(s_.v, wup_in[l][j])
            wb_ = wpb[j % 2]
            o.cp('pool', wb_.v, s_.v)
            o.memset('pool', rawv[:, 0:2], 0.0)
            o.memset('pool', rawg[:, 0:2], 0.0)
            for i in range(NT):
                pv = ps()
                pg = ps()
                for kc in range(KC):
                    o.mm(pv.v, wb_[:, kc * 256:kc * 256 + 128], hts[i][:, kc * 512:(kc + 1) * 512], start=(kc == 0), stop=(kc == KC - 1))
                for kc in range(KC):
                    o.mm(pg.v, wb_[:, kc * 256 + 128:kc * 256 + 256], hts[i][:, kc * 512:(kc + 1) * 512], start=(kc == 0), stop=(kc == KC - 1))
                res = []
                for (p, raw, cc_) in ((pv, rawv, j), (pg, rawg, NFF + j)):
                    o.cp('act', raw[:, 2:514], p.v)
                    cv = wk()
                    o.ts('dve', cv.v, raw[:, 2:514], cl[:, C_CW + 88 + cc_:C_CW + 89 + cc_], cl[:, C_CB + cc_:C_CB + cc_ + 1], ALU.mult, ALU.add)
                    o.stt('dve', cv.v, raw[:, 1:513], cl[:, C_CW + 44 + cc_:C_CW + 45 + cc_], cv.v, ALU.mult, ALU.add)
                    o.stt('dve', cv.v, raw[:, 0:512], cl[:, C_CW + cc_:C_CW + cc_ + 1], cv.v, ALU.mult, ALU.add)
                    o.cp('act', raw[:, 0:2], raw[:, 512:514])
                    res.append(cv)
                sg = wk()
                o.act(sg.v, res[1].v, AF.Silu)
                ab = wkbf()
                o.tt('pool', ab.v, sg.v, res[0].v, ALU.mult)
                o.dma(ACTT[j][:, tsl(i)], ab.v)
        if STOP <= 6:
            return

        S.barrier()
        st['off'] = layer_mark
        wdnb = sb(NFF * 1024, 'wdnb', BF16)
        stg = [sb(2048, 'stgd0')] * 2
        for q in range(NFF // 2):
            s_ = stg[q % 2]
            o.dma(r3(s_.v, 2), wdn_in[l][:, 2 * q:2 * q + 2, :])
            o.cp('pool', wdnb[:, 2 * q * 1024:(2 * q + 2) * 1024], s_.v)
        att = [sb(NFF * 512, f'att{i}', BF16) for i in range(2)]
        xt = [sb(KC * 512, 'xt40')] * 2
        for i in range(NT):
            a_ = att[i % 2]
            o.dma(r3(a_.v, NFF), ACTT.v.re("c p t -> p c t")[:, :, tsl(i)])
            x = xt[i % 2]
            o.dma(r3(x.v, KC), XT[:, :, tsl(i)])
            for oc in range(8):
                p = ps()
                for kc in range(NFF):
                    o.mm(p.v, wdnb[:, kc * 1024 + oc * 128:kc * 1024 + (oc + 1) * 128], a_[:, kc * 512:(kc + 1) * 512], start=(kc == 0), stop=(kc == NFF - 1))
                xs_ = x[:, oc * 512:(oc + 1) * 512]
                o.stt('dve', xs_, p.v, md[:, 40 + oc:41 + oc], xs_, ALU.mult, ALU.add)
            o.dma(XT[:, :, tsl(i)], r3(x.v, KC))

    for l in range(L):
        emit_layer(l)

    S.barrier()
    st['off'] = persist_mark
    xl = [sb(KC * 512, f'xl{i}') for i in range(2)]
    yo = [sb(D, f'yo{i}') for i in range(2)]
    n = 0
    for i in range(NT):
        xs = xl[i % 2]
        o.dma(xs.v.re("p (k t) -> p k t", k=KC), XT[:, :, i * 512:(i + 1) * 512])
        for q in range(4):
            yy = yo[n % 2]
            n += 1
            for h in range(2):
                p = ps()
                for c in range(4):
                    kc = h * 4 + c
                    o.tr(p[:, c * 128:(c + 1) * 128], xs[:, kc * 512 + q * 128: kc * 512 + (q + 1) * 128], ident)
                o.cp('act' if h == 0 else 'dve', yy[:, h * 512:(h + 1) * 512], p.v)
            t0 = i * 512 + q * 128
            o.dma(y_out[t0:t0 + 128, :], yy.v)
    S.barrier()

    with contextlib.ExitStack() as es:
        sems = {}
        for e in COMPUTE:
            sems[e] = es.enter_context(nc.semaphore("s_" + e))
        for j in range(NDSEM):
            sems[('d', j)] = es.enter_context(nc.semaphore(f"d{j}"))
        block = es.enter_context(nc.Block())
        run = S.emit(sems)
        block.sync(lambda e: run('sp', e))
        block.tensor(lambda e: run('pe', e))
        block.scalar(lambda e: run('act', e))
        block.vector(lambda e: run('dve', e))
        block.gpsimd(lambda e: run('pool', e))
    return nc


def _fm(v):
    return np.ascontiguousarray(np.asarray(v, np.float32).reshape(-1, 128).T)


def make_consts():
    c = np.zeros((128, K_W), np.float32)
    c[:, K_ID:K_ID + 128] = np.eye(128)
    c[0:64, K_OBD:K_OBD + 64] = 1.0
    c[64:128, K_OBD + 64:K_OBD + 128] = 1.0
    c[:, K_ONE:K_ONE + 128] = 1.0
    r = np.arange(64)
    su = (r[:, None] < r[None, :]).astype(np.float32)
    ui = (r[:, None] <= r[None, :]).astype(np.float32)
    sl = (r[:, None] > r[None, :]).astype(np.float32)
    c[0:64, K_BD:K_BD + 64] = 1.0
    c[64:128, K_BD + 64:K_BD + 128] = 1.0
    c[0:64, K_UI8:K_UI8 + 512] = np.tile(ui, (1, 8))
    c[0:64, K_NSL8:K_NSL8 + 512] = -np.tile(sl, (1, 8))
    c[0:64, K_ID8:K_ID8 + 512] = np.tile(np.eye(64, dtype=np.float32), (1, 8))
    c[0:64, K_NSU4:K_NSU4 + 256] = -np.tile(su, (1, 4))
    c[0:64, K_SU4:K_SU4 + 256] = np.tile(su, (1, 4))
    me = np.concatenate([np.ones((64, 64), np.float32), np.zeros((64, 64), np.float32)], axis=1)
    c[0:64, K_ME:K_ME + 512] = np.tile(me, (1, 4))
    c[0:64, K_MO:K_MO + 512] = np.tile(1.0 - me, (1, 4))
    return c


def make_consts_b():
    c = np.zeros((128, KB_W), np.float32)
    c[:, KB_ID:KB_ID + 128] = np.eye(128)
    c[:, KB_ONE:KB_ONE + 128] = 1.0
    r = np.arange(128)
    c[:, KB_TRI:KB_TRI + 128] = np.where(r[:, None] <= r[None, :], 0.0, -98304.0)
    for h in range(8):
        c[h, KB_SEL + h * 128:KB_SEL + (h + 1) * 128] = 1.0
    return c


def prep_shared(inp, L):
    f = lambda k: np.asarray(inp[k], np.float32)
    cols = np.zeros((L, 128, NCOLS), np.float32)
    win = np.zeros((L, NCH, 128, KC * 128), np.float32)
    lora = np.zeros((L, 128, 512), np.float32)
    g2 = np.zeros((L, 128, 512), np.float32)
    v2 = np.zeros((L, 64, 512), np.float32)
    for l in range(L):
        cl = cols[l]
        cl[:, C_N1G:C_N1G + 8] = _fm(f('norm1_g')[l])
        cl[:, C_N2G:C_N2G + 8] = _fm(f('norm2_g')[l])
        cl[0:8, C_FBF] = f('fox_b_f')[l]
        cl[:, C_QG] = np.tile(f('fox_q_gain')[l], 2)
        cl[:, C_KG] = np.tile(f('fox_k_gain')[l], 2)
        cl[:, C_OG] = np.tile(f('hgrn_o_gain')[l], 2)
        cl[:, C_MU:C_MU + 14] = _fm(f('rwkv_mu')[l])
        cl[:, C_W0:C_W0 + 4] = _fm(f('rwkv_w0')[l])
        cl[:, C_A0:C_A0 + 4] = _fm(f('rwkv_a0')[l])
        cl[:, C_KK:C_KK + 4] = _fm(f('rwkv_k_k')[l])
        cl[:, C_KA:C_KA + 4] = _fm(f('rwkv_k_a')[l])
        cl[:, C_RK:C_RK + 4] = _fm(f('rwkv_r_k')[l].reshape(-1))
        cl[:, C_LNW:C_LNW + 4] = _fm(f('rwkv_ln_w')[l])
        cl[:, C_LNB:C_LNB + 4] = _fm(f('rwkv_ln_b')[l])
        if l > 0:
            cl[:, C_V0:C_V0 + 4] = _fm(f('rwkv_v0')[l - 1])
            cl[32:64, C_VMU] = f('rwkv_vres_mu')[l - 1]
            v2[l, 32:64] = f('rwkv_v2')[l - 1]
        cw = f('conv_w')[l]
        for j in range(3):
            cl[:, C_CW + j * 44:C_CW + (j + 1) * 44] = _fm(cw[j])
        cl[:, C_CB:C_CB + 44] = _fm(f('conv_b')[l])
        W = f('w_in')[l]
        main = np.concatenate([W[:, 0:1536], W[:, 1544:1544 + 2048 + 1792 + 3072]], axis=1)
        misc = np.zeros((D, 128), np.float32)
        misc[:, 0:8] = W[:, 1536:1544]
        if l > 0:
            misc[:, 32:64] = f('rwkv_vres_down')[l - 1]
        allc = np.concatenate([main, misc], axis=1)
        win[l] = allc.reshape(KC, 128, NCH, 128).transpose(2, 1, 0, 3).reshape(NCH, 128, KC * 128)
        lora[l, 0:64] = f('rwkv_w2')[l]
        lora[l, 64:128] = f('rwkv_a2')[l]
        g2[l] = f('rwkv_g2')[l]
    sh = {}
    sh['cols'] = cols
    sh['lbT'] = np.ascontiguousarray(f('hgrn_lb')[:4].reshape(-1, 4, 128).transpose(2, 0, 1).reshape(128, -1))
    if sh['lbT'].shape[1] < 16:
        sh['lbT'] = np.concatenate([sh['lbT'], np.zeros((128, 16 - sh['lbT'].shape[1]), np.float32)], axis=1)
    sh['wada'] = np.ascontiguousarray(f('w_ada')[:L].reshape(L, KC, 128, 6 * D).transpose(0, 2, 1, 3))
    sh['bada'] = np.ascontiguousarray(f('b_ada')[:L].reshape(L, 1, 6 * D))
    sh['win'] = win
    sh['lora'] = lora
    sh['g2'] = g2
    sh['v2'] = v2
    sh['wbr'] = np.ascontiguousarray(f('w_branch')[:L].reshape(L, 3, 4, 128, D).transpose(0, 3, 1, 2, 4).reshape(L, 128, 12, D))
    sh['wout'] = np.ascontiguousarray(f('w_out')[:L].reshape(L, KC, 128, D).transpose(0, 2, 1, 3))
    wu = f('w_up')[:L].reshape(L, KC, 128, 2, NFF, 128)
    sh['wup'] = np.ascontiguousarray(wu.transpose(0, 4, 2, 1, 3, 5).reshape(L, NFF, 128, KC * 256))
    sh['wdn'] = np.ascontiguousarray(f('w_down')[:L].reshape(L, NFF, 128, D).transpose(0, 2, 1, 3))
    sh['cst'] = make_consts()
    sh['cstb'] = make_consts_b()
    return sh


_CACHE = {}


def run(inp, S_LEN, L, dbg=(), STOP=99):
    key = (S_LEN, L, tuple(dbg), STOP)
    if key not in _CACHE:
        _CACHE[key] = build(S_LEN, L, dbg, STOP)
    nc = _CACHE[key]
    sh = prep_shared(inp, L)
    x = np.asarray(inp['x'], np.float32)
    c = np.asarray(inp['c'], np.float32)
    B = x.shape[0]
    in_maps = []
    for b in range(B):
        m = dict(sh)
        m['x'] = np.ascontiguousarray(x[b])
        m['cT'] = _fm(c[b])
        in_maps.append(m)
    res = run_bass_kernel_spmd(nc, in_maps, core_ids=list(range(B)))
    return res.results


def kernel(**inputs):
    res = run(inputs, 4096, 4)
    return np.stack([r['y'] for r in res]).astype(np.float32)
```

```python
import contextlib
import os
HGCUT = int(os.environ.get('HGCUT', '9'))
HGSUB = int(os.environ.get('HGSUB', '9'))
import numpy as np
import ml_dtypes
import concourse.bass as bass
import concourse.mybir as mybir
from concourse.bass_utils import run_bass_kernel_spmd

F32 = mybir.dt.float32
BF16 = mybir.dt.bfloat16
AF = mybir.ActivationFunctionType
ALU = mybir.AluOpType

NDSEM = 24
COMPUTE = ('pe', 'act', 'dve', 'pool')


class T:
    __slots__ = ('ap', 'name', 'w', 'r', 'ut', 'xr')

    def __init__(self, ap, name='', ut=False, xr=False):
        self.ap = ap
        self.name = name
        self.w = None
        self.r = {}
        self.ut = ut
        self.xr = xr

    def __getitem__(self, k):
        return Vw(self, self.ap[k])

    @property
    def v(self):
        return Vw(self, self.ap)


class Vw:
    __slots__ = ('t', 'ap')

    def __init__(self, t, ap):
        self.t = t
        self.ap = ap

    def __getitem__(self, k):
        return Vw(self.t, self.ap[k])

    def re(self, pat, **kw):
        return Vw(self.t, self.ap.rearrange(pat, **kw))

    def bc(self, shape):
        return Vw(self.t, self.ap.to_broadcast(list(shape)))

    def cast(self, dt):
        return Vw(self.t, self.ap.bitcast(dt))

    def us(self, axis):
        return Vw(self.t, self.ap.unsqueeze(axis))


class Ins:
    __slots__ = ('eng', 'idx', 'fn', 'waits', 'signal', 'vc', 'key', 'val', 'isdma')


class Sched:
    def __init__(self):
        self.q = {e: [] for e in COMPUTE + ('sp',)}
        self.clock = {e: {} for e in COMPUTE + ('sp',)}
        self.dcount = [0] * NDSEM
        self.dlast = [None] * NDSEM
        self.dnext = 0

    def _add(self, eng, fn, reads, writes, isdma):
        ins = Ins()
        ins.eng = eng
        ins.idx = len(self.q[eng])
        ins.fn = fn
        ins.waits = []
        ins.signal = False
        ins.isdma = isdma
        clock = self.clock[eng]
        deps = []
        if isdma:
            j = self.dnext
            self.dnext = (j + 1) % NDSEM
            self.dcount[j] += 1
            ins.key = ('d', j)
            ins.val = 16 * self.dcount[j]
            if self.dlast[j] is not None:
                deps.append((self.dlast[j], 'sem'))
            self.dlast[j] = ins
            ins.signal = True
        else:
            ins.key = eng
            ins.val = ins.idx + 1
        for t in reads:
            if t.w is not None:
                deps.append((t.w, 'raw'))
            if t.xr:
                for r in t.r.values():
                    if r.eng != eng:
                        deps.append((r, 'rar'))
        for t in writes:
            if t.w is not None:
                deps.append((t.w, 'waw'))
            for r in t.r.values():
                deps.append((r, 'war'))
        for p, kind in deps:
            if (not p.isdma) and (not isdma) and p.eng == eng:
                if eng == 'pe' or kind != 'raw':
                    continue
            if clock.get(p.key, 0) >= p.val:
                continue
            p.signal = True
            ins.waits.append(p)
            for k, v in p.vc.items():
                if clock.get(k, 0) < v:
                    clock[k] = v
        vc = dict(clock)
        vc[ins.key] = ins.val
        ins.vc = vc
        for t in reads:
            t.r[ins.key] = ins
        for t in writes:
            t.w = ins
            t.r = {}
        self.q[eng].append(ins)
        return ins

    def op(self, eng, fn, reads=(), writes=()):
        return self._add(eng, fn, reads, writes, False)

    def dma(self, fn, reads=(), writes=(), queue='sp'):
        return self._add(queue, fn, reads, writes, True)

    def barrier(self):
        lasts = []
        for e in COMPUTE:
            for ins in reversed(self.q[e]):
                if ins.fn is not None and not ins.isdma:
                    lasts.append(ins)
                    break
        for j in range(NDSEM):
            if self.dlast[j] is not None:
                lasts.append(self.dlast[j])
        for e in COMPUTE + ('sp',):
            ins = Ins()
            ins.eng = e
            ins.idx = len(self.q[e])
            ins.fn = None
            ins.waits = []
            ins.signal = False
            ins.isdma = False
            ins.key = e
            ins.val = ins.idx
            clock = self.clock[e]
            for p in lasts:
                if (not p.isdma) and p.eng == e:
                    continue
                if clock.get(p.key, 0) >= p.val:
                    continue
                p.signal = True
                ins.waits.append(p)
                for k, v in p.vc.items():
                    if clock.get(k, 0) < v:
                        clock[k] = v
            ins.vc = dict(clock)
            self.q[e].append(ins)

    def emit(self, sems):
        sigcount = {}
        for e in COMPUTE:
            c = 0
            for ins in self.q[e]:
                if ins.fn is not None and ins.signal and not ins.isdma:
                    c += 1
                    sigcount[id(ins)] = c

        def run(eng_name, e):
            for ins in self.q[eng_name]:
                w = {}
                for p in ins.waits:
                    v = p.val if p.isdma else sigcount[id(p)]
                    if w.get(p.key, 0) < v:
                        w[p.key] = v
                for k, v in w.items():
                    e.wait_ge(sems[k], v)
                if ins.fn is None:
                    continue
                r = ins.fn(e)
                if ins.isdma:
                    r.then_inc(sems[ins.key], 16)
                elif ins.signal:
                    r.then_inc(sems[ins.key], 1)
        return run


def _ap(x):
    return x.ap if isinstance(x, Vw) else x


def _ts(*xs):
    return [x.t for x in xs if isinstance(x, Vw) and not x.t.ut]


class Ops:
    def __init__(self, S):
        self.S = S

    def tt(self, eng, out, a, b, op):
        self.S.op(eng, lambda e: e.tensor_tensor(out=out.ap, in0=a.ap, in1=b.ap, op=op), _ts(a, b), _ts(out))

    def ts(self, eng, out, a, s1, s2, op0, op1=None):
        if op1 is None:
            self.S.op(eng, lambda e: e.tensor_scalar(out=out.ap, in0=a.ap, scalar1=_ap(s1), scalar2=None, op0=op0),
                      _ts(a, s1), _ts(out))
        else:
            self.S.op(eng, lambda e: e.tensor_scalar(out=out.ap, in0=a.ap, scalar1=_ap(s1), scalar2=_ap(s2), op0=op0, op1=op1),
                      _ts(a, s1, s2), _ts(out))

    def stt(self, eng, out, a, s, b, op0, op1):
        self.S.op(eng, lambda e: e.scalar_tensor_tensor(out=out.ap, in0=a.ap, scalar=_ap(s), in1=b.ap, op0=op0, op1=op1),
                  _ts(a, s, b), _ts(out))

    def cp(self, eng, out, a):
        if eng == 'act':
            self.S.op(eng, lambda e: e.copy(out=out.ap, in_=a.ap), _ts(a), _ts(out))
        else:
            self.S.op(eng, lambda e: e.tensor_copy(out=out.ap, in_=a.ap), _ts(a), _ts(out))

    def act(self, out, a, func, bias=None, scale=None):
        kw = {}
        if bias is not None:
            kw['bias'] = _ap(bias)
        if scale is not None:
            kw['scale'] = _ap(scale)
        self.S.op('act', lambda e: e.activation(out=out.ap, in_=a.ap, func=func, **kw), _ts(a, bias, scale), _ts(out))

    def recip(self, out, a):
        self.S.op('dve', lambda e: e.reciprocal(out=out.ap, in_=a.ap), _ts(a), _ts(out))

    def rsqrt(self, out, a, add, tmp):
        self.act(tmp, a, AF.Sqrt, bias=add)
        self.recip(out, tmp)

    def scan(self, out, d0, d1, init, op0=ALU.mult, op1=ALU.add):
        self.S.op('dve', lambda e: e.tensor_tensor_scan(out=out.ap, data0=d0.ap, data1=d1.ap, initial=_ap(init), op0=op0, op1=op1),
                  _ts(d0, d1, init), _ts(out))

    def memset(self, eng, out, val):
        self.S.op(eng, lambda e: e.memset(out.ap, val), [], _ts(out))

    def mm(self, out, lhsT, rhs, start=True, stop=True):
        self.S.op('pe', lambda e: e.matmul(out.ap, lhsT.ap, rhs.ap, start=start, stop=stop), _ts(lhsT, rhs), _ts(out))

    def tr(self, out, a, ident):
        self.S.op('pe', lambda e: e.transpose(out=out.ap, in_=a.ap, identity=ident.ap), _ts(a, ident), _ts(out))

    def dma(self, out, a, queue='sp'):
        self.S.dma(lambda e: e.dma_start(out=out.ap, in_=a.ap), _ts(a), _ts(out), queue=queue)


D = 1024
KC = 8
W_MIX = 512
DFF = 2816
NFF = 22
EPS = 1e-6
LN_EPS = 64e-5
NCOLS = 243
C_N1G, C_N2G, C_FBF, C_QG, C_KG, C_OG, C_MU, C_W0, C_A0, C_KK, C_KA, C_RK, C_LNW, C_LNB, C_V0, C_VMU, C_CW, C_CB = \
    0, 8, 16, 17, 18, 19, 20, 34, 38, 42, 46, 50, 54, 58, 62, 66, 67, 199
J_FQ, J_FK, J_FV, J_HQ, J_HF, J_HI, J_HO, J_RR, J_RK, J_RV, J_LORA, J_GLO, J_GATE, J_MISC = 0, 4, 8, 12, 16, 20, 24, 28, 32, 36, 40, 41, 42, 66
NCH = 67
K_ID, K_OBD, K_ONE, K_BD, K_UI8, K_NSL8, K_ID8, K_NSU4, K_SU4, K_ME, K_MO, K_W = 0, 128, 256, 384, 512, 1024, 1536, 2048, 2304, 2560, 3072, 3584
KB_ID, KB_ONE, KB_TRI, KB_SEL, KB_W = 0, 128, 256, 384, 1408


def build(S_LEN, L, dbg=(), STOP=99):
    NT = S_LEN // 512
    NCK = S_LEN // 64
    nc = bass.Bass("TRN2", target_bir_lowering=False)
    S = Sched()
    o = Ops(S)

    def dram(name, shape, dt, kind="Internal"):
        if name in dbg:
            kind = "ExternalOutput"
        return T(nc.dram_tensor(name, list(shape), dt, kind=kind).ap(), name, ut=True)

    x_in = dram("x", [S_LEN, D], F32, "ExternalInput")
    cT_in = dram("cT", [128, KC], F32, "ExternalInput")
    cols_in = dram("cols", [L, 128, NCOLS], F32, "ExternalInput")
    lbT_in = dram("lbT", [128, 16], F32, "ExternalInput")
    wada_in = dram("wada", [L, 128, KC, 6 * D], F32, "ExternalInput")
    bada_in = dram("bada", [L, 1, 6 * D], F32, "ExternalInput")
    win_in = dram("win", [L, NCH, 128, KC * 128], F32, "ExternalInput")
    lora_in = dram("lora", [L, 128, 512], F32, "ExternalInput")
    g2_in = dram("g2", [L, 128, 512], F32, "ExternalInput")
    v2_in = dram("v2", [L, 64, 512], F32, "ExternalInput")
    wbr_in = dram("wbr", [L, 128, 12, D], F32, "ExternalInput")
    wout_in = dram("wout", [L, 128, KC, D], F32, "ExternalInput")
    wup_in = dram("wup", [L, NFF, 128, KC * 256], F32, "ExternalInput")
    wdn_in = dram("wdn", [L, 128, NFF, D], F32, "ExternalInput")
    cst_in = dram("cst", [128, K_W], F32, "ExternalInput")
    cstb_in = dram("cstb", [128, KB_W], F32, "ExternalInput")
    y_out = dram("y", [S_LEN, D], F32, "ExternalOutput")

    XT = dram("XT", [128, KC, S_LEN], F32)
    QKT = dram("QKT", [8, 128, S_LEN], BF16)
    FVT = dram("FVT", [4, 128, S_LEN], BF16)
    GT = dram("GT", [24, 128, S_LEN], BF16)
    HQ = dram("HQ", [4, 128, S_LEN], F32)
    HK = dram("HK", [4, 128, S_LEN], F32)
    HI = dram("HI", [4, 128, S_LEN], F32)
    HO = dram("HO", [4, 128, S_LEN], BF16)
    RKR = dram("RKR", [4, 128, NCK, 2, 64], F32)
    RKT = dram("RKT", [4, 128, S_LEN], F32)
    RBT = dram("RBT", [4, 128, S_LEN], F32)
    RVT = dram("RVT", [4, 128, S_LEN], F32)
    RBO = dram("RBO", [4, 128, S_LEN], F32)
    RG = dram("RG", [4, 128, S_LEN], BF16)
    FTD = dram("FTD", [8, S_LEN], F32)
    VF = dram("VF", [4, 128, S_LEN], F32)
    YB = dram("YB", [12, 128, S_LEN], BF16)
    H2 = dram("H2", [KC, 128, S_LEN], BF16)
    ACTT = dram("ACTT", [NFF, 128, S_LEN], BF16)

    arena = nc.alloc_sbuf_tensor("arena", [128, 53200], F32).ap()
    st = {'off': 0, 'mark': 0}

    def sb(n, name='', dt=F32):
        words = n if dt == F32 else (n + 1) // 2
        assert st['off'] + words <= 53200, ("SBUF arena overflow", name, st['off'], words)
        a = arena[:, st['off']: st['off'] + words]
        st['off'] += words
        assert st['off'] <= 53200, ("SBUF arena overflow", name, st['off'])
        if dt != F32:
            a = a.bitcast(dt)
        return T(a, name)

    psb = [T(nc.alloc_psum_tensor(f"ps{i}", [128, 512], F32).ap(), f"ps{i}", xr=True) for i in range(8)]
    pst = {'i': 0}

    pst['n'] = 8

    def ps():
        p = psb[pst['i'] % pst['n']]
        pst['i'] += 1
        return p

    cst = sb(K_W, 'cst')
    o.dma(cst.v, cst_in.v)
    ident = cst[:, K_ID:K_ID + 128]
    onesbd = cst[:, K_OBD:K_OBD + 128]
    ones = cst[:, K_ONE:K_ONE + 128]
    bd = cst[:, K_BD:K_BD + 128]
    ui8 = cst[:, K_UI8:K_UI8 + 512]
    ui4 = cst[:, K_UI8:K_UI8 + 256]
    nsl8 = cst[:, K_NSL8:K_NSL8 + 512]
    id8 = cst[:, K_ID8:K_ID8 + 512]
    nsu4 = cst[:, K_NSU4:K_NSU4 + 256]
    su4 = cst[:, K_SU4:K_SU4 + 256]
    me8 = cst[:, K_ME:K_ME + 512]
    mo8 = cst[:, K_MO:K_MO + 512]
    cstb = sb(KB_W, 'cstb', BF16)
    identb = cstb[:, KB_ID:KB_ID + 128]
    onesb = cstb[:, KB_ONE:KB_ONE + 128]
    trib = cstb[:, KB_TRI:KB_TRI + 128]
    selb = cstb[:, KB_SEL:KB_SEL + 1024]
    ones512 = sb(512, 'ones512')
    o.memset('dve', ones512.v, 1.0)
    colsT = [sb(NCOLS, f'cols{l}') for l in range(L)]
    for l in range(L):
        o.dma(colsT[l].v, cols_in[l])
    modT = [sb(48, f'mod{l}') for l in range(L)]
    derT = [sb(64, f'der{l}') for l in range(L)]
    lbT = sb(16, 'lbT')
    lowT = sb(16, 'lowT')
    o.dma(lbT.v, lbT_in.v)
    ex = sb(16, 'lbexp')
    sm = sb(4, 'lbsum')
    rs = sb(4, 'lbrs')
    oml = sb(16, 'oml')
    cT = sb(KC, 'cT')
    condT = sb(KC, 'condT')
    persist_mark = st['off']

    st['mark'] = st['off']
    cbt = sb(KB_W, 'cbtmp')
    o.dma(cbt.v, cstb_in.v)
    o.cp('dve', cstb.v, cbt.v)
    xin = [sb(D, f'xin{i}') for i in range(2)]
    xst = [sb(KC * 512, f'xst{i}') for i in range(2)]
    for i in range(NT):
        xs = xst[i % 2]
        for q in range(4):
            xi = xin[q % 2]
            t0 = i * 512 + q * 128
            o.dma(xi.v, x_in[t0:t0 + 128, :])
            for h in range(2):
                p = ps()
                for c in range(4):
                    kc = h * 4 + c
                    o.tr(p[:, c * 128:(c + 1) * 128], xi[:, kc * 128:(kc + 1) * 128], ident)
                dst = xs.v.re("p (k t) -> p k t", k=KC)[:, h * 4:(h + 1) * 4, q * 128:(q + 1) * 128]
                o.cp('act' if h == 0 else 'dve', dst, p.v.re("p (c t) -> p c t", c=4))
        o.dma(XT[:, :, i * 512:(i + 1) * 512], xs.v.re("p (k t) -> p k t", k=KC))

    o.dma(cT.v, cT_in.v)
    o.act(condT.v, cT.v, AF.Silu)
    wst = [sb(KC * 512, f'wst{i}') for i in range(2)]
    bada = sb(6 * D, 'bada')
    modrow = sb(6 * D, 'modrow')
    for l in range(L):
        o.dma(bada[0:1, :], bada_in[l])
        for blk in range(12):
            w = wst[blk % 2]
            o.dma(w.v.re("p (k n) -> p k n", k=KC), wada_in[l][:, :, blk * 512:(blk + 1) * 512])
            p = ps()
            for kc in range(KC):
                o.mm(p[0:1, :], condT[:, kc:kc + 1], w[:, kc * 512:(kc + 1) * 512], start=(kc == 0), stop=(kc == KC - 1))
            o.tt('dve', modrow[0:1, blk * 512:(blk + 1) * 512], p[0:1, :], bada[0:1, blk * 512:(blk + 1) * 512], ALU.add)
        p = ps()
        for j in range(48):
            o.mm(p[:, j:j + 1], modrow[0:1, j * 128:(j + 1) * 128], ones[0:1, 0:1])
        o.cp('dve', modT[l].v, p[:, 0:48])
        dr = derT[l]
        cl = colsT[l]
        o.ts('dve', dr[:, 0:8], modT[l][:, 8:16], 1.0, 32.0, ALU.add, ALU.mult)
        o.tt('dve', dr[:, 0:8], dr[:, 0:8], cl[:, C_N1G:C_N1G + 8], ALU.mult)
        o.ts('dve', dr[:, 8:16], modT[l][:, 32:40], 1.0, 32.0, ALU.add, ALU.mult)
        o.tt('dve', dr[:, 8:16], dr[:, 8:16], cl[:, C_N2G:C_N2G + 8], ALU.mult)
        o.ts('dve', dr[:, 17:21], cl[:, C_KA:C_KA + 4], -1.0, 1.0, ALU.mult, ALU.add)
        o.ts('dve', dr[:, 21:22], cl[:, C_QG:C_QG + 1], 8.0, None, ALU.mult)
        o.ts('dve', dr[:, 22:23], cl[:, C_KG:C_KG + 1], 8.0, None, ALU.mult)
        o.ts('dve', dr[:, 23:24], cl[:, C_OG:C_OG + 1], 8.0, None, ALU.mult)
        o.ts('dve', dr[:, 24:28], cl[:, C_MU + 0:C_MU + 4], -1.0, None, ALU.mult)
    o.act(ex.v, lbT.v, AF.Exp)
    o.cp('dve', sm.v, ex[:, 0:4])
    for l in range(1, L):
        o.tt('dve', sm.v, sm.v, ex[:, l * 4:(l + 1) * 4], ALU.add)
    o.recip(rs.v, sm.v)
    for l in range(L):
        o.tt('dve', ex[:, l * 4:(l + 1) * 4], ex[:, l * 4:(l + 1) * 4], rs.v, ALU.mult)
    o.memset('dve', lowT[:, 0:4], 0.0)
    for l in range(1, L):
        o.tt('dve', lowT[:, l * 4:(l + 1) * 4], lowT[:, (l - 1) * 4:l * 4], ex[:, l * 4:(l + 1) * 4], ALU.add)
    o.ts('dve', oml.v, lowT.v, -1.0, 1.0, ALU.mult, ALU.add)
    ctx = dict(nc=nc, S=S, o=o, sb=sb, ps=ps, st=st, L=L, NT=NT, NCK=NCK, S_LEN=S_LEN)
    NEG_C0 = -0.6065306597126334
    WK_N = 16
    sel_in = None

    def emit_layer(l):
        S.barrier()
        st['off'] = persist_mark
        cl = colsT[l]
        dr = derT[l]
        md = modT[l]
        Fcar = sb(2, 'Fcar')
        hsc = sb(3 * 4 * NCK, 'hsc')
        hscv = hsc.v.re("p (a c n) -> p a c n", a=3, c=4)
        gam = sb(4 * NCK, 'gam')
        gamv = gam.v.re("p (c n) -> p c n", c=4)
        ded = [sb(512, f'ded{i}') for i in range(7)]
        wkp = [sb(512, f'wk{i}') for i in range(WK_N)]
        wki = {'i': 0}

        def wk():
            t = wkp[wki['i'] % WK_N]
            wki['i'] += 1
            return t
        wkb = [sb(512, f'wkb{i}', BF16) for i in range(6)]
        wkbi = {'i': 0}

        def wkbf():
            t = wkb[wkbi['i'] % 6]
            wkbi['i'] += 1
            return t
        layer_mark = st['off']

        hts = [sb(KC * 512, f'ht{i}', BF16) for i in range(NT)]

        def norm_tile(xtile, Acol0, Bcol0, dst, sqt):
            sq = sqt
            o.act(sq.v, xtile.v, AF.Square)
            p = ps()
            for kc in range(KC):
                o.mm(p.v, onesb, sq[:, kc * 512:(kc + 1) * 512], start=(kc == 0), stop=(kc == KC - 1))
            rstd = wk()
            rtmp = wk()
            o.rsqrt(rstd[:, 0:512], p.v, 1024.0 * EPS, rtmp[:, 0:512])
            x3 = xtile.v.re("p (k t) -> p k t", k=KC)
            o.tt('dve', x3, x3, rstd[:, 0:512].us(1).bc([128, KC, 512]), ALU.mult)
            for kc in range(KC):
                o.act(dst[:, kc * 512:(kc + 1) * 512], xtile[:, kc * 512:(kc + 1) * 512], AF.Identity,
                      bias=md[:, Bcol0 + kc:Bcol0 + kc + 1], scale=dr[:, Acol0 + kc:Acol0 + kc + 1])

        nmark = st['off']
        xt = [sb(KC * 512, f'xt{i}') for i in range(2)]
        sqt = sb(KC * 512, 'sqt', BF16)
        for i in range(NT):
            x = xt[i % 2]
            o.dma(x.v.re("p (k t) -> p k t", k=KC), XT[:, :, i * 512:(i + 1) * 512])
            norm_tile(x, 0, 0, hts[i], sqt)
        S.barrier()
        st['off'] = nmark
        lorab = sb(512, 'lorab', BF16)
        g2b = sb(512, 'g2b', BF16)
        v2b = sb(512, 'v2b', BF16)
        wtmp = wk()
        o.dma(wtmp[:, 0:512], lora_in[l])
        o.cp('pool', lorab.v, wtmp[:, 0:512])
        wtmp = wk()
        o.dma(wtmp[:, 0:512], g2_in[l])
        o.cp('pool', g2b.v, wtmp[:, 0:512])
        wtmp = wk()
        o.dma(wtmp[0:64, 0:512], v2_in[l])
        o.cp('pool', v2b[0:64, :], wtmp[0:64, 0:512])
        twa = sb(S_LEN, 'twa', BF16)
        sgl = sb(S_LEN, 'sgl', BF16)
        vlo = sb(S_LEN, 'vlo', BF16)
        raws = [sb(514, f'raw{i}') for i in range(3)]
        Cx = sb(513, 'Cx')
        o.memset('dve', Cx[:, 0:1], 0.0)
        wf = [sb(KC * 128, 'wf0')] * 2
        wfi = {'i': 0}
        wbt = [[sb(KC * 128, f'wb{g}_{k}', BF16) for k in range(4)] for g in range(2)]
        grp = {'i': 0}

        def load_group(js):
            g = grp['i'] % 2
            grp['i'] += 1
            outs = []
            for k, j in enumerate(js):
                stg = wf[wfi['i'] % 2]
                wfi['i'] += 1
                o.dma(stg.v, win_in[l][j])
                o.cp('pool', wbt[g][k].v, stg.v)
                outs.append(wbt[g][k])
            return outs

        def proj(wt, i):
            p = ps()
            for kc in range(KC):
                o.mm(p.v, wt[:, kc * 128:(kc + 1) * 128], hts[i][:, kc * 512:(kc + 1) * 512], start=(kc == 0), stop=(kc == KC - 1))
            return p

        def shiftmix(p, raw, mu, r0=0, r1=128, dst=None):
            o.cp('act', raw[r0:r1, 1:513], p[r0:r1, :])
            d = wk()
            o.tt('dve', d[r0:r1, 0:512], raw[r0:r1, 0:512], raw[r0:r1, 1:513], ALU.subtract)
            m = dst if dst is not None else wk()
            o.stt('dve', m[r0:r1, 0:512], d[r0:r1, 0:512], mu, raw[r0:r1, 1:513], ALU.mult, ALU.add)
            o.cp('act', raw[r0:r1, 0:1], raw[r0:r1, 512:513])
            return m

        def reset_raws():
            for r in raws:
                o.memset('pool', r[:, 0:2], 0.0)

        def tsl(i):
            return slice(i * 512, (i + 1) * 512)

        (wm,) = load_group([J_MISC])
        reset_raws()
        for i in range(NT):
            p = proj(wm, i)
            t1 = wk()
            o.act(t1[0:8, 0:512], p[0:8, :], AF.Sigmoid, bias=cl[0:8, C_FBF:C_FBF + 1])
            o.act(t1[0:8, 0:512], t1[0:8, 0:512], AF.Ln)
            t2 = wk()
            o.scan(t2[0:8, 0:512], ones512[0:8, :], t1[0:8, 0:512], 0.0 if i == 0 else Fcar[0:8, 0:1])
            o.cp('dve', Fcar[0:8, 0:1], t2[0:8, 511:512])
            o.dma(FTD[0:8, tsl(i)], t2[0:8, 0:512])
            if l > 0:
                m = shiftmix(p, raws[0], cl[32:64, C_VMU:C_VMU + 1], 32, 64)
                o.cp('dve', vlo[32:64, tsl(i)], m[32:64, 0:512])
        wl, wg = load_group([J_LORA, J_GLO])
        reset_raws()
        for i in range(NT):
            p = proj(wl, i)
            m = shiftmix(p, raws[0], cl[:, C_MU + 12:C_MU + 13])
            o.act(twa[0:64, tsl(i)], m[0:64, 0:512], AF.Tanh)
            o.cp('dve', twa[64:128, tsl(i)], m[64:128, 0:512])
            p = proj(wg, i)
            m = shiftmix(p, raws[1], cl[:, C_MU + 13:C_MU + 14])
            o.act(sgl[:, tsl(i)], m[:, 0:512], AF.Sigmoid)
        for c in range(4):
            wr, wkk_, wv = load_group([J_RR + c, J_RK + c, J_RV + c])
            reset_raws()
            cs = slice(c * 128, (c + 1) * 128)
            for i in range(NT):
                pr = proj(wr, i)
                pk = proj(wkk_, i)
                pv = proj(wv, i)
                r_m = shiftmix(pr, raws[0], cl[:, C_MU + c:C_MU + c + 1], dst=ded[0])
                k_m = shiftmix(pk, raws[1], cl[:, C_MU + 4 + c:C_MU + 5 + c], dst=ded[1])
                v_m = shiftmix(pv, raws[2], cl[:, C_MU + 8 + c:C_MU + 9 + c], dst=ded[2])
                pa = ps()
                o.mm(pa.v, lorab[64:128, cs], twa[64:128, tsl(i)])
                a = ded[3]
                o.act(a[:, 0:512], pa.v, AF.Sigmoid, bias=cl[:, C_A0 + c:C_A0 + c + 1])
                pw = ps()
                o.mm(pw.v, lorab[0:64, cs], twa[0:64, tsl(i)])
                sgw = wk()
                o.act(sgw[:, 0:512], pw.v, AF.Sigmoid, bias=cl[:, C_W0 + c:C_W0 + c + 1])
                o.scan(Cx[:, 1:513], ones512.v, sgw[:, 0:512], 0.0)
                ref = Cx[:, 0:512].re("p (c t) -> p c t", t=64)[:, :, 0:1].bc([128, 8, 64])
                Dd = wk()
                o.tt('dve', Dd[:, 0:512].re("p (c t) -> p c t", t=64), Cx[:, 1:513].re("p (c t) -> p c t", t=64), ref, ALU.subtract)
                Dm = wk()
                o.tt('dve', Dm[:, 0:512].re("p (c t) -> p c t", t=64), Cx[:, 0:512].re("p (c t) -> p c t", t=64), ref, ALU.subtract)
                E1 = ded[4]
                E2 = wk()
                E3 = ded[5]
                o.act(E1[:, 0:512], Dd[:, 0:512], AF.Exp, scale=NEG_C0)
                o.act(E2[:, 0:512], Dm[:, 0:512], AF.Exp, scale=NEG_C0)
                o.act(E3[:, 0:512], Dd[:, 0:512], AF.Exp, scale=-NEG_C0)
                o.cp('pool', gamv[:, c, i * 8:(i + 1) * 8], E1[:, 0:512].re("p (c t) -> p c t", t=64)[:, :, 63])
                kk = wk()
                o.ts('pool', kk[:, 0:512], k_m[:, 0:512], cl[:, C_KK + c:C_KK + c + 1], None, ALU.mult)
                sq = wk()
                o.tt('pool', sq[:, 0:512], kk[:, 0:512], kk[:, 0:512], ALU.mult)
                pss = ps()
                o.mm(pss.v, onesbd, sq[:, 0:512])
                rinv = wk()
                rtmp = wk()
                o.act(rtmp[:, 0:512], pss.v, AF.Sqrt)
                o.ts('dve', rtmp[:, 0:512], rtmp[:, 0:512], 1e-12, None, ALU.max)
                o.recip(rinv[:, 0:512], rtmp[:, 0:512])
                kap = wk()
                o.tt('dve', kap[:, 0:512], kk[:, 0:512], rinv[:, 0:512], ALU.mult)
                tp = wk()
                o.ts('pool', tp[:, 0:512], a[:, 0:512], cl[:, C_KA + c:C_KA + c + 1], dr[:, 17 + c:18 + c], ALU.mult, ALU.add)
                kf = ded[6]
                o.tt('pool', kf[:, 0:512], k_m[:, 0:512], tp[:, 0:512], ALU.mult)
                bb = wk()
                o.tt('pool', bb[:, 0:512], a[:, 0:512], kap[:, 0:512], ALU.mult)
                ot = wk()
                o.tt('dve', ot[:, 0:512], r_m[:, 0:512], E1[:, 0:512], ALU.mult)
                o.dma(RKR[c][:, i * 8:(i + 1) * 8, 1, :], ot[:, 0:512].re("p (c t) -> p c t", t=64))
                ot = wk()
                o.tt('dve', ot[:, 0:512], kap[:, 0:512], E2[:, 0:512], ALU.mult)
                o.dma(RKR[c][:, i * 8:(i + 1) * 8, 0, :], ot[:, 0:512].re("p (c t) -> p c t", t=64))
                ot = wk()
                o.tt('pool', ot[:, 0:512], kf[:, 0:512], E3[:, 0:512], ALU.mult)
                o.dma(RKT[c][:, tsl(i)], ot[:, 0:512])
                ot = wk()
                o.tt('pool', ot[:, 0:512], bb[:, 0:512], E3[:, 0:512], ALU.mult)
                o.dma(RBT[c][:, tsl(i)], ot[:, 0:512])
                if l == 0:
                    o.dma(VF[c][:, tsl(i)], v_m[:, 0:512])
                    vv = v_m
                else:
                    vf = wk()
                    o.dma(vf[:, 0:512], VF[c][:, tsl(i)])
                    pg = ps()
                    o.mm(pg.v, v2b[32:64, cs], vlo[32:64, tsl(i)])
                    gt = wk()
                    o.act(gt[:, 0:512], pg.v, AF.Sigmoid, bias=cl[:, C_V0 + c:C_V0 + c + 1])
                    dv = wk()
                    o.tt('dve', dv[:, 0:512], vf[:, 0:512], v_m[:, 0:512], ALU.subtract)
                    o.tt('dve', dv[:, 0:512], dv[:, 0:512], gt[:, 0:512], ALU.mult)
                    vv = wk()
                    o.tt('dve', vv[:, 0:512], dv[:, 0:512], v_m[:, 0:512], ALU.add)
                o.dma(RVT[c][:, tsl(i)], vv[:, 0:512])
                rk = wk()
                o.stt('dve', rk[:, 0:512], r_m[:, 0:512], cl[:, C_RK + c:C_RK + c + 1], kf[:, 0:512], ALU.mult, ALU.mult)
                pb = ps()
                o.mm(pb.v, onesbd, rk[:, 0:512])
                bon = wk()
                o.tt('dve', bon[:, 0:512], pb.v, vv[:, 0:512], ALU.mult)
                o.dma(RBO[c][:, tsl(i)], bon[:, 0:512])
                pgg = ps()
                o.mm(pgg.v, g2b[:, cs], sgl[:, tsl(i)])
                gb = wkbf()
                o.cp('act', gb.v, pgg.v)
                o.dma(RG[c][:, tsl(i)], gb.v)
        for c in range(4):
            wq, wfg, wi, wo = load_group([J_HQ + c, J_HF + c, J_HI + c, J_HO + c])
            lcol = slice(l * 4 + c, l * 4 + c + 1)
            for i in range(NT):
                pq = proj(wq, i)
                pf = proj(wfg, i)
                pi_ = proj(wi, i)
                po = proj(wo, i)
                sq_ = wk()
                o.act(sq_[:, 0:512], pq.v, AF.Silu)
                sgf = wk()
                o.act(sgf[:, 0:512], pf.v, AF.Sigmoid)
                gate = wk()
                o.ts('dve', gate[:, 0:512], sgf[:, 0:512], oml[:, lcol], lowT[:, lcol], ALU.mult, ALU.add)
                lg = wk()
                o.act(lg[:, 0:512], gate[:, 0:512], AF.Ln)
                kx = wk()
                o.ts('pool', kx[:, 0:512], gate[:, 0:512], -1.0, 1.0, ALU.mult, ALU.add)
                o.scan(Cx[:, 1:513], ones512.v, lg[:, 0:512], 0.0)
                G3 = Cx[:, 1:513].re("p (c t) -> p c t", t=64)
                Dd = wk()
                o.tt('dve', Dd[:, 0:512].re("p (c t) -> p c t", t=64), G3, G3[:, :, 31:32].bc([128, 8, 64]), ALU.subtract)
                E1 = wk()
                E3 = wk()
                o.act(E1[:, 0:512], Dd[:, 0:512], AF.Exp)
                o.act(E3[:, 0:512], Dd[:, 0:512], AF.Exp, scale=-1.0)
                ot = wk()
                o.tt('dve', ot[:, 0:512], sq_[:, 0:512], E1[:, 0:512], ALU.mult)
                o.dma(HQ[c][:, tsl(i)], ot[:, 0:512])
                ot = wk()
                o.tt('pool', ot[:, 0:512], kx[:, 0:512], E3[:, 0:512], ALU.mult)
                o.dma(HK[c][:, tsl(i)], ot[:, 0:512])
                o.cp('pool', hscv[:, 0, c, i * 8:(i + 1) * 8], E1[:, 0:512].re("p (c t) -> p c t", t=64)[:, :, 63])
                G0 = Cx[:, 0:512].re("p (c t) -> p c t", t=64)
                d8 = wk()
                o.tt('dve', d8[:, 0:8], G3[:, :, 63], G0[:, :, 0], ALU.subtract)
                o.act(hscv[:, 1, c, i * 8:(i + 1) * 8], d8[:, 0:8], AF.Exp)
                d8 = wk()
                o.tt('dve', d8[:, 0:8], G3[:, :, 31], G0[:, :, 0], ALU.subtract)
                o.act(hscv[:, 2, c, i * 8:(i + 1) * 8], d8[:, 0:8], AF.Exp)
                it = wk()
                o.cp('act', it[:, 0:512], pi_.v)
                o.dma(HI[c][:, tsl(i)], it[:, 0:512])
                ob = wkbf()
                o.act(ob.v, po.v, AF.Silu)
                o.dma(HO[c][:, tsl(i)], ob.v)
        for c in range(4):
            wq, wk_ = load_group([J_FQ + c, J_FK + c])
            for i in range(NT):
                for which, wt in ((0, wq), (1, wk_)):
                    p = proj(wt, i)
                    sq = wk()
                    o.act(sq[:, 0:512], p.v, AF.Square)
                    pss = ps()
                    o.mm(pss.v, onesbd, sq[:, 0:512])
                    rstd = wk()
                    rtmp = wk()
                    o.rsqrt(rstd[:, 0:512], pss.v, 64.0 * EPS, rtmp[:, 0:512])
                    qb = wkbf()
                    o.stt('dve', qb.v, p.v, dr[:, 21 + which:22 + which], rstd[:, 0:512], ALU.mult, ALU.mult)
                    o.dma(QKT[which * 4 + c][:, tsl(i)], qb.v)
        wvs = load_group([J_FV + c for c in range(4)])
        for c in range(4):
            for i in range(NT):
                p = proj(wvs[c], i)
                vb = wkbf()
                o.cp('act', vb.v, p.v)
                o.dma(FVT[c][:, tsl(i)], vb.v)
        for c0 in range(0, 24, 4):
            wgs = load_group([J_GATE + c0 + k for k in range(4)])
            for k in range(4):
                for i in range(NT):
                    p = proj(wgs[k], i)
                    gb = wkbf()
                    o.act(gb.v, p.v, AF.Sigmoid)
                    o.dma(GT[c0 + k][:, tsl(i)], gb.v)
        if STOP <= 1:
            return
        NB = S_LEN // 128

        def r3(v, c):
            return v.re("p (c t) -> p c t", c=c)

        def par(v, bk):
            return v.re("p (c a t) -> p c a t", c=4, a=2)[:, :, bk, :]

        S.barrier()
        st['off'] = layer_mark
        Fsb = sb(S_LEN, 'Fsb')
        F8 = sb(S_LEN, 'F8', BF16)
        o.dma(Fsb[0:8, :], FTD.v)
        o.ts('dve', F8[0:8, :], Fsb[0:8, :], 8.0, None, ALU.mult)
        negF = sb(NB * 8, 'negF')
        for blk in range(NB):
            p = ps()
            o.tr(p[:, 0:8], Fsb[0:8, blk * 128:(blk + 1) * 128], ident[0:8, 0:8])
            o.ts('dve', negF[:, blk * 8:(blk + 1) * 8], p[:, 0:8], -1.0, None, ALU.mult)
        VTM = sb(NB * 512, 'VTM', BF16)
        vmark = st['off']
        fv = [sb(S_LEN, f'fv{c}', BF16) for c in range(4)]
        for c in range(4):
            o.dma(fv[c].v, FVT[c])
        for blk in range(NB):
            p = ps()
            pb = p.v.cast(BF16)
            for c in range(4):
                o.tr(pb[:, c * 128:(c + 1) * 128], fv[c][:, blk * 128:(blk + 1) * 128], identb)
            o.cp('act' if blk % 2 else 'dve', VTM[:, blk * 512:(blk + 1) * 512], pb[:, 0:512])
        S.barrier()
        st['off'] = vmark
        qk = [[sb(S_LEN, f'q{g}', BF16), sb(S_LEN, f'k{g}', BF16)] for g in range(2)]
        Pt = [sb(512, f'P{i}', BF16) for i in range(6)]
        yt = [sb(512, f'y{i}', BF16) for i in range(3)]
        rd = [sb(512, f'rd{i}') for i in range(2)]
        pst['pc'] = 0
        ycount = 0
        pst['n'] = 4
        for c in range(4):
            qT, kT = qk[c % 2]
            o.dma(qT.v, QKT[c])
            o.dma(kT.v, QKT[4 + c])
            for h2 in range(2):
                h = 2 * c + h2
                rows = slice(h2 * 64, (h2 + 1) * 64)
                for qc in range(NT):
                    pn = psb[4 + 2 * (ycount % 2)]
                    pd = psb[5 + 2 * (ycount % 2)]
                    nkb = 4 * qc + 4
                    pend = []

                    def stage_a(kb):
                        i_ = kb - 4 * qc
                        n0 = 128 * i_ if i_ > 0 else 0
                        qs = slice(qc * 512 + n0, (qc + 1) * 512)
                        psc = ps()
                        o.mm(psc[:, n0:512], kT[rows, kb * 128:(kb + 1) * 128], qT[rows, qs], start=True, stop=False)
                        if i_ >= 0:
                            o.mm(psc[:, n0:n0 + 128], identb, trib, start=False, stop=False)
                        o.mm(psc[:, n0:512], selb[0:8, h * 128:(h + 1) * 128], F8[0:8, qs], start=False, stop=True)
                        P = Pt[pst['pc'] % 6]
                        pst['pc'] += 1
                        o.act(P[:, n0:512], psc[:, n0:512], AF.Exp, bias=negF[:, kb * 8 + h:kb * 8 + h + 1], scale=0.125)
                        return (kb, n0, P)

                    def stage_b(kb, n0, P):
                        o.mm(pn[0:64, n0:512], VTM[:, kb * 512 + h * 64:kb * 512 + (h + 1) * 64], P[:, n0:512], start=(kb == 0), stop=(kb == nkb - 1))
                        o.mm(pd[0:64, n0:512], onesb[:, 0:64], P[:, n0:512], start=(kb == 0), stop=(kb == nkb - 1))
                    for kb in range(nkb):
                        pend.append(stage_a(kb))
                        if len(pend) > 2:
                            stage_b(*pend.pop(0))
                    while pend:
                        stage_b(*pend.pop(0))
                    r = rd[ycount % 2]
                    y = yt[ycount % 3]
                    ycount += 1
                    o.recip(r[0:64, :], pd[0:64, :])
                    o.tt('dve', y[0:64, :], pn[0:64, :], r[0:64, :], ALU.mult)
                    o.dma(YB[c][rows, tsl(qc)], y[0:64, :])
        pst['n'] = 8
        if STOP <= 2:
            return

        S.barrier()
        st['off'] = layer_mark
        Mst = sb(512, 'Mst')
        M3 = r3(Mst.v, 4)
        o.memset('dve', Mst.v, 0.0)
        vpad = sb(1024, 'vpad', BF16)
        o.memset('pool', vpad.v, 0.0)
        ktm = [sb(512, f'ktm{i}', BF16) for i in range(2)]
        vtm = [sb(512, f'vtm{i}', BF16) for i in range(2)]
        amt = [sb(512, f'amt{i}', BF16) for i in range(2)]
        m0t = [sb(512, f'm0t{i}', BF16) for i in range(2)]
        hqb = sb(2048, 'hqb', BF16)
        hkb = sb(2048, 'hkb', BF16)
        hib = sb(2048, 'hib', BF16)
        hq = [sb(2048, f'hq{i}') for i in range(2)]
        hk = [sb(2048, f'hk{i}') for i in range(2)]
        hi = [sb(2048, f'hi{i}') for i in range(2)]
        ho = [sb(2048, f'ho{i}', BF16) for i in range(2)]
        ot = sb(2048, 'ot')
        bd4 = bd.us(1).bc([128, 4, 128])
        if HGCUT <= -1:
            return
        for i in range(NT):
            b = i % 2
            o.dma(r3(hq[b].v, 4), HQ.v.re("c p t -> p c t")[:, :, tsl(i)])
            o.dma(r3(hk[b].v, 4), HK.v.re("c p t -> p c t")[:, :, tsl(i)])
            o.dma(r3(hi[b].v, 4), HI.v.re("c p t -> p c t")[:, :, tsl(i)])
            o.dma(r3(ho[b].v, 4), HO.v.re("c p t -> p c t")[:, :, tsl(i)])
            o.cp('pool', hqb.v, hq[b].v)
            o.cp('pool', hkb.v, hk[b].v)
            o.cp('pool', hib.v, hi[b].v)
            if HGCUT <= 0:
                continue
            for ck in range(8):
                g = i * 8 + ck

                def cc(c):
                    return slice(c * 512 + ck * 64, c * 512 + (ck + 1) * 64)
                pk_ = ps()
                pv_ = ps()
                for c in range(4):
                    o.mm(pk_[0:64, c * 128:(c + 1) * 128], hkb[:, cc(c)], identb)
                    o.mm(pv_[0:64, c * 128:(c + 1) * 128], hib[:, cc(c)], identb)
                kt = ktm[g % 2]
                vt = vtm[g % 2]
                if HGSUB <= 1:
                    continue
                o.cp('act', kt[0:64, :], pk_[0:64, :])
                o.cp('act', vt[0:64, :], pv_[0:64, :])
                if HGSUB <= 2:
                    continue
                o.tt('dve', vpad[0:64, 0:512], pv_[0:64, :], me8[0:64, :], ALU.mult)
                o.tt('dve', vpad[0:64, 512:1024], pv_[0:64, :], mo8[0:64, :], ALU.mult)
                if HGCUT <= 1:
                    continue
                M0s = m0t[g % 2]
                o.tt('dve', r3(M0s.v, 4), M3, hscv[:, 2, :, g:g + 1].bc([128, 4, 128]), ALU.mult)
                if HGSUB <= 3:
                    continue
                pA = [ps(), ps()]
                for h in range(8):
                    c = h // 2
                    rr = slice((h % 2) * 64, (h % 2) * 64 + 64)
                    b0 = c * 512 + ck * 64
                    o.mm(pA[h % 2][0:64, c * 64 + 32:(c + 1) * 64], hkb[rr, b0:b0 + 64], hqb[rr, b0 + 32:b0 + 64])
                    o.mm(pA[h % 2][0:32, c * 64:c * 64 + 32], hkb[rr, b0:b0 + 32], hqb[rr, b0:b0 + 32])
                Am = amt[g % 2]
                o.memset('pool', Am[32:64, :], 0.0)
                for bk in range(2):
                    o.tt('dve', par(Am[0:64, :], bk)[:, :, 32:64], r3(pA[bk][0:64, 0:256], 4)[:, :, 32:64], r3(ui4[0:64, :], 4)[:, :, 32:64], ALU.mult)
                    o.tt('dve', par(Am[0:32, :], bk)[:, :, 0:32], r3(pA[bk][0:32, 0:256], 4)[:, :, 0:32], r3(ui4[0:32, :], 4)[:, :, 0:32], ALU.mult)
                if HGCUT <= 2:
                    continue
                pO = ps()
                for c in range(4):
                    oc_ = pO[:, c * 64:(c + 1) * 64]
                    o.mm(oc_, M0s[:, c * 128:(c + 1) * 128], hqb[:, cc(c)], start=True, stop=False)
                    o.mm(oc_, vpad[0:64, c * 128:(c + 1) * 128], Am[0:64, (2 * c) * 64:(2 * c + 1) * 64], start=False, stop=False)
                    o.mm(oc_, vpad[0:64, 512 + c * 128:512 + (c + 1) * 128], Am[0:64, (2 * c + 1) * 64:(2 * c + 2) * 64], start=False, stop=True)
                o.cp('act', r3(ot.v, 4)[:, :, ck * 64:(ck + 1) * 64], r3(pO[:, 0:256], 4))
                if HGCUT <= 3:
                    continue
                pS = ps()
                for c in range(4):
                    o.mm(pS[:, c * 128:(c + 1) * 128], kt[0:64, c * 128:(c + 1) * 128], vt[0:64, c * 128:(c + 1) * 128])
                t1 = wk()
                o.tt('dve', r3(t1.v, 4), r3(pS.v, 4), bd4, ALU.mult)
                o.tt('dve', r3(t1.v, 4), r3(t1.v, 4), hscv[:, 0, :, g:g + 1].bc([128, 4, 128]), ALU.mult)
                o.tt('pool', M3, M3, hscv[:, 1, :, g:g + 1].bc([128, 4, 128]), ALU.mult)
                o.tt('pool', M3, M3, r3(t1.v, 4), ALU.add)
            for c in range(4):
                oc_ = ot[:, c * 512:(c + 1) * 512]
                sq = wk()
                o.act(sq.v, oc_, AF.Square)
                pss = ps()
                o.mm(pss.v, onesbd, sq.v)
                rstd = wk()
                rtmp = wk()
                o.rsqrt(rstd.v, pss.v, 64.0 * EPS, rtmp.v)
                y = wk()
                o.stt('dve', y.v, oc_, dr[:, 23:24], rstd.v, ALU.mult, ALU.mult)
                yb_ = wkbf()
                o.tt('dve', yb_.v, y.v, ho[b][:, c * 512:(c + 1) * 512], ALU.mult)
                o.dma(YB[4 + c][:, tsl(i)], yb_.v)
        if STOP <= 3:
            return

        S.barrier()
        st['off'] = layer_mark
        Mst = sb(512, 'MstR')
        M3 = r3(Mst.v, 4)
        o.memset('dve', Mst.v, 0.0)
        Mstb = sb(512, 'MstRb', BF16)
        o.memset('pool', Mstb.v, 0.0)
        vpad = sb(1024, 'vpadR', BF16)
        nupad = sb(1024, 'nupad', BF16)
        o.memset('pool', vpad.v, 0.0)
        o.memset('pool', nupad.v, 0.0)
        sm64 = [sb(512, f'sm{i}', BF16) for i in range(14)]
        (ktm_, btm_, vtm_, RBt, Qt, RKt, Wsb, nut, Y0, Y1, Yt0, Yt1, Tt0, Tt1) = sm64
        rkr = [sb(4096, 'rkr0')] * 2
        rkk = [sb(2048, 'rkk0')] * 2
        rbb = [sb(2048, 'rbb0')] * 2
        rvv = [sb(2048, 'rvv0')] * 2
        bo = [sb(2048, 'bo0')] * 2
        rg = [sb(2048, 'rg0', BF16)] * 2
        rkrb = sb(4096, 'rkrb', BF16)
        rkkb = sb(2048, 'rkkb', BF16)
        rbbb = sb(2048, 'rbbb', BF16)
        rvvb = sb(2048, 'rvvb', BF16)
        ot = sb(2048, 'otR')
        hsl = [slice(h * 64, (h + 1) * 64) for h in range(8)]
        for i in range(NT):
            b = i % 2
            o.dma(rkr[b].v.re("p (c n a t) -> p c n a t", c=4, n=8, a=2), RKR.v.re("c p n a t -> p c n a t")[:, :, i * 8:(i + 1) * 8, :, :])
            o.dma(r3(rkk[b].v, 4), RKT.v.re("c p t -> p c t")[:, :, tsl(i)])
            o.dma(r3(rbb[b].v, 4), RBT.v.re("c p t -> p c t")[:, :, tsl(i)])
            o.dma(r3(rvv[b].v, 4), RVT.v.re("c p t -> p c t")[:, :, tsl(i)])
            o.dma(r3(bo[b].v, 4), RBO.v.re("c p t -> p c t")[:, :, tsl(i)])
            o.dma(r3(rg[b].v, 4), RG.v.re("c p t -> p c t")[:, :, tsl(i)])
            o.cp('pool', rkrb.v, rkr[b].v)
            o.cp('pool', rkkb.v, rkk[b].v)
            o.cp('pool', rbbb.v, rbb[b].v)
            o.cp('pool', rvvb.v, rvv[b].v)
            for ck in range(8):
                g = i * 8 + ck

                def cc(c):
                    return slice(c * 512 + ck * 64, c * 512 + (ck + 1) * 64)

                def KR(c, a0, a1):
                    off = (c * 8 + ck) * 128
                    return slice(off + a0 * 64, off + a1 * 64)
                pk_ = ps()
                pb_ = ps()
                pv_ = ps()
                for c in range(4):
                    o.mm(pk_[0:64, c * 128:(c + 1) * 128], rkkb[:, cc(c)], identb)
                    o.mm(pb_[0:64, c * 128:(c + 1) * 128], rbbb[:, cc(c)], identb)
                    o.mm(pv_[0:64, c * 128:(c + 1) * 128], rvvb[:, cc(c)], identb)
                o.cp('act', ktm_[0:64, :], pk_[0:64, :])
                o.cp('act', btm_[0:64, :], pb_[0:64, :])
                o.cp('act', vtm_[0:64, :], pv_[0:64, :])
                o.tt('dve', vpad[0:64, 0:512], pv_[0:64, :], me8[0:64, :], ALU.mult)
                o.tt('dve', vpad[0:64, 512:1024], pv_[0:64, :], mo8[0:64, :], ALU.mult)
                pA1 = [ps(), ps()]
                pA2 = [ps(), ps()]
                for h in range(8):
                    c = h // 2
                    rr = slice((h % 2) * 64, (h % 2) * 64 + 64)
                    o.mm(pA1[h % 2][0:64, c * 128:(c + 1) * 128], rbbb[rr, cc(c)], rkrb[rr, KR(c, 0, 2)])
                    o.mm(pA2[h % 2][0:64, c * 128:(c + 1) * 128], rkkb[rr, cc(c)], rkrb[rr, KR(c, 0, 2)])
                for bk in range(2):
                    v1 = pA1[bk][0:64, :].re("p (h x) -> p h x", h=4)
                    v2 = pA2[bk][0:64, :].re("p (h x) -> p h x", h=4)
                    o.tt('dve', par(Yt0[0:64, :], bk), v1[:, :, 0:64], r3(nsu4[0:64, :], 4), ALU.mult)
                    o.tt('dve', par(RBt[0:64, :], bk), v1[:, :, 64:128], r3(ui4[0:64, :], 4), ALU.mult)
                    o.tt('dve', par(Qt[0:64, :], bk), v2[:, :, 0:64], r3(su4[0:64, :], 4), ALU.mult)
                    o.tt('dve', par(RKt[0:64, :], bk), v2[:, :, 64:128], r3(ui4[0:64, :], 4), ALU.mult)
                pA3 = [ps(), ps()]
                for h in range(8):
                    c = h // 2
                    rr = slice((h % 2) * 64, (h % 2) * 64 + 64)
                    o.mm(pA3[h % 2][0:64, c * 64:(c + 1) * 64], rkrb[rr, KR(c, 0, 1)], rbbb[rr, cc(c)])
                for bk in range(2):
                    o.tt('dve', par(Y0[0:64, :], bk), r3(pA3[bk][0:64, 0:256], 4), r3(nsl8[0:64, 0:256], 4), ALU.mult)
                o.tt('pool', Tt0[0:64, :], Yt0[0:64, :], id8[0:64, :], ALU.add)
                Ya, Yta, Tta = Y0, Yt0, Tt0
                Yb, Ytb, Ttb = Y1, Yt1, Tt1
                for j in range(5):
                    pY = ps()
                    pYt = ps()
                    for h in range(8):
                        o.mm(pY[0:64, hsl[h]], Yta[0:64, hsl[h]], Ya[0:64, hsl[h]])
                        o.mm(pYt[0:64, hsl[h]], Ya[0:64, hsl[h]], Yta[0:64, hsl[h]])
                    o.cp('act', Yb[0:64, :], pY[0:64, :])
                    o.cp('act', Ytb[0:64, :], pYt[0:64, :])
                    pT = ps()
                    for h in range(8):
                        o.mm(pT[0:64, hsl[h]], Yb[0:64, hsl[h]], Tta[0:64, hsl[h]])
                    o.tt('dve', Ttb[0:64, :], pT[0:64, :], Tta[0:64, :], ALU.add)
                    Ya, Yta, Tta, Yb, Ytb, Ttb = Yb, Ytb, Ttb, Ya, Yta, Tta
                pW = ps()
                for c in range(4):
                    o.mm(pW[0:64, c * 128:(c + 1) * 128], rkrb[:, KR(c, 0, 1)], Mstb[:, c * 128:(c + 1) * 128], start=True, stop=False)
                    for h2 in range(2):
                        h = 2 * c + h2
                        o.mm(pW[0:64, hsl[h]], Qt[0:64, hsl[h]], vtm_[0:64, hsl[h]], start=False, stop=True)
                o.cp('act', Wsb[0:64, :], pW[0:64, :])
                pU = ps()
                for h in range(8):
                    o.mm(pU[0:64, hsl[h]], Tta[0:64, hsl[h]], Wsb[0:64, hsl[h]])
                o.ts('dve', nut[0:64, :], pU[0:64, :], -1.0, None, ALU.mult)
                o.stt('dve', nupad[0:64, 0:512], pU[0:64, :], -1.0, me8[0:64, :], ALU.mult, ALU.mult)
                o.stt('dve', nupad[0:64, 512:1024], pU[0:64, :], -1.0, mo8[0:64, :], ALU.mult, ALU.mult)
                pO = ps()
                for c in range(4):
                    oc_ = pO[:, c * 64:(c + 1) * 64]
                    o.mm(oc_, Mstb[:, c * 128:(c + 1) * 128], rkrb[:, KR(c, 1, 2)], start=True, stop=False)
                    for h2 in range(2):
                        h = 2 * c + h2
                        o.mm(oc_, vpad[0:64, h2 * 512 + c * 128:h2 * 512 + (c + 1) * 128], RKt[0:64, hsl[h]], start=False, stop=False)
                        o.mm(oc_, nupad[0:64, h2 * 512 + c * 128:h2 * 512 + (c + 1) * 128], RBt[0:64, hsl[h]], start=False, stop=(h2 == 1))
                o.cp('act', r3(ot.v, 4)[:, :, ck * 64:(ck + 1) * 64], r3(pO[:, 0:256], 4))
                pS = ps()
                for c in range(4):
                    o.mm(pS[:, c * 128:(c + 1) * 128], ktm_[0:64, c * 128:(c + 1) * 128], vtm_[0:64, c * 128:(c + 1) * 128], start=True, stop=False)
                    o.mm(pS[:, c * 128:(c + 1) * 128], btm_[0:64, c * 128:(c + 1) * 128], nut[0:64, c * 128:(c + 1) * 128], start=False, stop=True)
                t1 = wk()
                o.tt('dve', r3(t1.v, 4), r3(pS.v, 4), bd4, ALU.mult)
                o.tt('dve', t1.v, t1.v, Mst.v, ALU.add)
                o.tt('pool', M3, r3(t1.v, 4), gamv[:, :, g:g + 1].bc([128, 4, 128]), ALU.mult)
                o.cp('act', Mstb.v, Mst.v)
            for c in range(4):
                oc_ = ot[:, c * 512:(c + 1) * 512]
                pm = ps()
                o.mm(pm.v, onesbd, oc_)
                sq = wk()
                o.act(sq.v, oc_, AF.Square)
                pvv = ps()
                o.mm(pvv.v, onesbd, sq.v)
                mean = wk()
                o.ts('dve', mean.v, pm.v, 1.0 / 64, None, ALU.mult)
                cen = wk()
                o.tt('dve', cen.v, oc_, mean.v, ALU.subtract)
                msq = wk()
                o.tt('pool', msq.v, mean.v, mean.v, ALU.mult)
                var = wk()
                o.stt('dve', var.v, pvv.v, 1.0 / 64, msq.v, ALU.mult, ALU.subtract)
                rstd = wk()
                rtmp = wk()
                o.rsqrt(rstd.v, var.v, LN_EPS, rtmp.v)
                y = wk()
                o.tt('dve', y.v, cen.v, rstd.v, ALU.mult)
                o.ts('dve', y.v, y.v, cl[:, C_LNW + c:C_LNW + c + 1], cl[:, C_LNB + c:C_LNB + c + 1], ALU.mult, ALU.add)
                o.tt('pool', y.v, y.v, bo[b][:, c * 512:(c + 1) * 512], ALU.add)
                yb_ = wkbf()
                o.tt('dve', yb_.v, y.v, rg[b][:, c * 512:(c + 1) * 512], ALU.mult)
                o.dma(YB[8 + c][:, tsl(i)], yb_.v)
        if STOP <= 4:
            return

        S.barrier()
        st['off'] = layer_mark
        wbrb = sb(12 * 1024, 'wbrb', BF16)
        woutb = sb(8 * 1024, 'woutb', BF16)
        stg = [sb(2048, 'stg0')] * 2
        for q in range(6):
            s_ = stg[q % 2]
            o.dma(r3(s_.v, 2), wbr_in[l][:, 2 * q:2 * q + 2, :])
            o.cp('pool', wbrb[:, 2 * q * 1024:(2 * q + 2) * 1024], s_.v)
        for q in range(4):
            s_ = stg[q % 2]
            o.dma(r3(s_.v, 2), wout_in[l][:, 2 * q:2 * q + 2, :])
            o.cp('pool', woutb[:, 2 * q * 1024:(2 * q + 2) * 1024], s_.v)
        ybt = [sb(12 * 512, 'ybt0', BF16)] * 2
        gtt = [sb(3 * 512, f'gtt{i}', BF16) for i in range(2)]
        mg = sb(8 * 512, 'mg', BF16)
        xt = [sb(KC * 512, 'xt30')] * 2
        sqt = sb(KC * 512, 'sqt3', BF16)
        h2o = [sb(KC * 512, 'h2o0', BF16)] * 2
        gn = 0
        for i in range(NT):
            yb_ = ybt[i % 2]
            o.dma(r3(yb_.v, 12), YB.v.re("c p t -> p c t")[:, :, tsl(i)])
            x = xt[i % 2]
            o.dma(r3(x.v, KC), XT[:, :, tsl(i)])
            for oc in range(8):
                g_ = gtt[gn % 2]
                gn += 1
                o.dma(r3(g_.v, 3), GT.v.re("(b o) p t -> o p b t", b=3)[oc][:, :, tsl(i)])
                acc = wk()
                for b_ in range(3):
                    p = ps()
                    for kc in range(4):
                        o.mm(p.v, wbrb[:, (b_ * 4 + kc) * 1024 + oc * 128:(b_ * 4 + kc) * 1024 + (oc + 1) * 128],
                             yb_[:, (b_ * 4 + kc) * 512:(b_ * 4 + kc + 1) * 512], start=(kc == 0), stop=(kc == 3))
                    if b_ == 0:
                        o.tt('dve', acc.v, p.v, g_[:, 0:512], ALU.mult)
                    else:
                        t = wk()
                        o.tt('dve', t.v, p.v, g_[:, b_ * 512:(b_ + 1) * 512], ALU.mult)
                        if b_ == 1:
                            o.tt('pool', acc.v, acc.v, t.v, ALU.add)
                        else:
                            o.tt('pool', mg[:, oc * 512:(oc + 1) * 512], acc.v, t.v, ALU.add)
            for oc in range(8):
                p = ps()
                for kc in range(KC):
                    o.mm(p.v, woutb[:, kc * 1024 + oc * 128:kc * 1024 + (oc + 1) * 128], mg[:, kc * 512:(kc + 1) * 512], start=(kc == 0), stop=(kc == KC - 1))
                xs_ = x[:, oc * 512:(oc + 1) * 512]
                o.stt('dve', xs_, p.v, md[:, 16 + oc:17 + oc], xs_, ALU.mult, ALU.add)
            o.dma(XT[:, :, tsl(i)], r3(x.v, KC))
            h2t = h2o[i % 2]
            norm_tile(x, 8, 24, h2t, sqt)
            o.dma(H2.v.re("k p t -> p k t")[:, :, tsl(i)], r3(h2t.v, KC))
        if STOP <= 5:
            return

        S.barrier()
        st['off'] = layer_mark
        hts = [sb(KC * 512, f'h2_{i}', BF16) for i in range(NT)]
        for i in range(NT):
            o.dma(r3(hts[i].v, KC), H2.v.re("k p t -> p k t")[:, :, tsl(i)])
        wst2 = [sb(2048, f'wst2_{i}') for i in range(2)]
        wpb = [sb(2048, f'wpb{i}', BF16) for i in range(2)]
        rawv = sb(514, 'rawv')
        rawg = sb(514, 'rawg')
        for j in range(NFF):
            s_ = wst2[j % 2]
            o.dma(s_.v, wup_in[l][j])
            wb_ = wpb[j % 2]
            o.cp('pool', wb_.v, s_.v)
            o.memset('pool', rawv[:, 0:2], 0.0)
            o.memset('pool', rawg[:, 0:2], 0.0)
            for i in range(NT):
                pv = ps()
                pg = ps()
                for kc in range(KC):
                    o.mm(pv.v, wb_[:, kc * 256:kc * 256 + 128], hts[i][:, kc * 512:(kc + 1) * 512], start=(kc == 0), stop=(kc == KC - 1))
                for kc in range(KC):
                    o.mm(pg.v, wb_[:, kc * 256 + 128:kc * 256 + 256], hts[i][:, kc * 512:(kc + 1) * 512], start=(kc == 0), stop=(kc == KC - 1))
                res = []
                for (p, raw, cc_) in ((pv, rawv, j), (pg, rawg, NFF + j)):
                    o.cp('act', raw[:, 2:514], p.v)
                    cv = wk()
                    o.ts('dve', cv.v, raw[:, 2:514], cl[:, C_CW + 88 + cc_:C_CW + 89 + cc_], cl[:, C_CB + cc_:C_CB + cc_ + 1], ALU.mult, ALU.add)
                    o.stt('dve', cv.v, raw[:, 1:513], cl[:, C_CW + 44 + cc_:C_CW + 45 + cc_], cv.v, ALU.mult, ALU.add)
                    o.stt('dve', cv.v, raw[:, 0:512], cl[:, C_CW + cc_:C_CW + cc_ + 1], cv.v, ALU.mult, ALU.add)
                    o.cp('act', raw[:, 0:2], raw[:, 512:514])
                    res.append(cv)
                sg = wk()
                o.act(sg.v, res[1].v, AF.Silu)
                ab = wkbf()
                o.tt('pool', ab.v, sg.v, res[0].v, ALU.mult)
                o.dma(ACTT[j][:, tsl(i)], ab.v)
        if STOP <= 6:
            return

        S.barrier()
        st['off'] = layer_mark
        wdnb = sb(NFF * 1024, 'wdnb', BF16)
        stg = [sb(2048, 'stgd0')] * 2
        for q in range(NFF // 2):
            s_ = stg[q % 2]
            o.dma(r3(s_.v, 2), wdn_in[l][:, 2 * q:2 * q + 2, :])
            o.cp('pool', wdnb[:, 2 * q * 1024:(2 * q + 2) * 1024], s_.v)
        att = [sb(NFF * 512, f'att{i}', BF16) for i in range(2)]
        xt = [sb(KC * 512, 'xt40')] * 2
        for i in range(NT):
            a_ = att[i % 2]
            o.dma(r3(a_.v, NFF), ACTT.v.re("c p t -> p c t")[:, :, tsl(i)])
            x = xt[i % 2]
            o.dma(r3(x.v, KC), XT[:, :, tsl(i)])
            for oc in range(8):
                p = ps()
                for kc in range(NFF):
                    o.mm(p.v, wdnb[:, kc * 1024 + oc * 128:kc * 1024 + (oc + 1) * 128], a_[:, kc * 512:(kc + 1) * 512], start=(kc == 0), stop=(kc == NFF - 1))
                xs_ = x[:, oc * 512:(oc + 1) * 512]
                o.stt('dve', xs_, p.v, md[:, 40 + oc:41 + oc], xs_, ALU.mult, ALU.add)
            o.dma(XT[:, :, tsl(i)], r3(x.v, KC))

    for l in range(L):
        emit_layer(l)

    S.barrier()
    st['off'] = persist_mark
    xl = [sb(KC * 512, f'xl{i}') for i in range(2)]
    yo = [sb(D, f'yo{i}') for i in range(2)]
    n = 0
    for i in range(NT):
        xs = xl[i % 2]
        o.dma(xs.v.re("p (k t) -> p k t", k=KC), XT[:, :, i * 512:(i + 1) * 512])
        for q in range(4):
            yy = yo[n % 2]
            n += 1
            for h in range(2):
                p = ps()
                for c in range(4):
                    kc = h * 4 + c
                    o.tr(p[:, c * 128:(c + 1) * 128], xs[:, kc * 512 + q * 128: kc * 512 + (q + 1) * 128], ident)
                o.cp('act' if h == 0 else 'dve', yy[:, h * 512:(h + 1) * 512], p.v)
            t0 = i * 512 + q * 128
            o.dma(y_out[t0:t0 + 128, :], yy.v)
    S.barrier()

    with contextlib.ExitStack() as es:
        sems = {}
        for e in COMPUTE:
            sems[e] = es.enter_context(nc.semaphore("s_" + e))
        for j in range(NDSEM):
            sems[('d', j)] = es.enter_context(nc.semaphore(f"d{j}"))
        block = es.enter_context(nc.Block())
        run = S.emit(sems)
        block.sync(lambda e: run('sp', e))
        block.tensor(lambda e: run('pe', e))
        block.scalar(lambda e: run('act', e))
        block.vector(lambda e: run('dve', e))
        block.gpsimd(lambda e: run('pool', e))
    return nc


def _fm(v):
    return np.ascontiguousarray(np.asarray(v, np.float32).reshape(-1, 128).T)


def make_consts():
    c = np.zeros((128, K_W), np.float32)
    c[:, K_ID:K_ID + 128] = np.eye(128)
    c[0:64, K_OBD:K_OBD + 64] = 1.0
    c[64:128, K_OBD + 64:K_OBD + 128] = 1.0
    c[:, K_ONE:K_ONE + 128] = 1.0
    r = np.arange(64)
    su = (r[:, None] < r[None, :]).astype(np.float32)
    ui = (r[:, None] <= r[None, :]).astype(np.float32)
    sl = (r[:, None] > r[None, :]).astype(np.float32)
    c[0:64, K_BD:K_BD + 64] = 1.0
    c[64:128, K_BD + 64:K_BD + 128] = 1.0
    c[0:64, K_UI8:K_UI8 + 512] = np.tile(ui, (1, 8))
    c[0:64, K_NSL8:K_NSL8 + 512] = -np.tile(sl, (1, 8))
    c[0:64, K_ID8:K_ID8 + 512] = np.tile(np.eye(64, dtype=np.float32), (1, 8))
    c[0:64, K_NSU4:K_NSU4 + 256] = -np.tile(su, (1, 4))
    c[0:64, K_SU4:K_SU4 + 256] = np.tile(su, (1, 4))
    me = np.concatenate([np.ones((64, 64), np.float32), np.zeros((64, 64), np.float32)], axis=1)
    c[0:64, K_ME:K_ME + 512] = np.tile(me, (1, 4))
    c[0:64, K_MO:K_MO + 512] = np.tile(1.0 - me, (1, 4))
    return c


def make_consts_b():
    c = np.zeros((128, KB_W), np.float32)
    c[:, KB_ID:KB_ID + 128] = np.eye(128)
    c[:, KB_ONE:KB_ONE + 128] = 1.0
    r = np.arange(128)
    c[:, KB_TRI:KB_TRI + 128] = np.where(r[:, None] <= r[None, :], 0.0, -98304.0)
    for h in range(8):
        c[h, KB_SEL + h * 128:KB_SEL + (h + 1) * 128] = 1.0
    return c


def prep_shared(inp, L):
    f = lambda k: np.asarray(inp[k], np.float32)
    cols = np.zeros((L, 128, NCOLS), np.float32)
    win = np.zeros((L, NCH, 128, KC * 128), np.float32)
    lora = np.zeros((L, 128, 512), np.float32)
    g2 = np.zeros((L, 128, 512), np.float32)
    v2 = np.zeros((L, 64, 512), np.float32)
    for l in range(L):
        cl = cols[l]
        cl[:, C_N1G:C_N1G + 8] = _fm(f('norm1_g')[l])
        cl[:, C_N2G:C_N2G + 8] = _fm(f('norm2_g')[l])
        cl[0:8, C_FBF] = f('fox_b_f')[l]
        cl[:, C_QG] = np.tile(f('fox_q_gain')[l], 2)
        cl[:, C_KG] = np.tile(f('fox_k_gain')[l], 2)
        cl[:, C_OG] = np.tile(f('hgrn_o_gain')[l], 2)
        cl[:, C_MU:C_MU + 14] = _fm(f('rwkv_mu')[l])
        cl[:, C_W0:C_W0 + 4] = _fm(f('rwkv_w0')[l])
        cl[:, C_A0:C_A0 + 4] = _fm(f('rwkv_a0')[l])
        cl[:, C_KK:C_KK + 4] = _fm(f('rwkv_k_k')[l])
        cl[:, C_KA:C_KA + 4] = _fm(f('rwkv_k_a')[l])
        cl[:, C_RK:C_RK + 4] = _fm(f('rwkv_r_k')[l].reshape(-1))
        cl[:, C_LNW:C_LNW + 4] = _fm(f('rwkv_ln_w')[l])
        cl[:, C_LNB:C_LNB + 4] = _fm(f('rwkv_ln_b')[l])
        if l > 0:
            cl[:, C_V0:C_V0 + 4] = _fm(f('rwkv_v0')[l - 1])
            cl[32:64, C_VMU] = f('rwkv_vres_mu')[l - 1]
            v2[l, 32:64] = f('rwkv_v2')[l - 1]
        cw = f('conv_w')[l]
        for j in range(3):
            cl[:, C_CW + j * 44:C_CW + (j + 1) * 44] = _fm(cw[j])
        cl[:, C_CB:C_CB + 44] = _fm(f('conv_b')[l])
        W = f('w_in')[l]
        main = np.concatenate([W[:, 0:1536], W[:, 1544:1544 + 2048 + 1792 + 3072]], axis=1)
        misc = np.zeros((D, 128), np.float32)
        misc[:, 0:8] = W[:, 1536:1544]
        if l > 0:
            misc[:, 32:64] = f('rwkv_vres_down')[l - 1]
        allc = np.concatenate([main, misc], axis=1)
        win[l] = allc.reshape(KC, 128, NCH, 128).transpose(2, 1, 0, 3).reshape(NCH, 128, KC * 128)
        lora[l, 0:64] = f('rwkv_w2')[l]
        lora[l, 64:128] = f('rwkv_a2')[l]
        g2[l] = f('rwkv_g2')[l]
    sh = {}
    sh['cols'] = cols
    sh['lbT'] = np.ascontiguousarray(f('hgrn_lb')[:4].reshape(-1, 4, 128).transpose(2, 0, 1).reshape(128, -1))
    if sh['lbT'].shape[1] < 16:
        sh['lbT'] = np.concatenate([sh['lbT'], np.zeros((128, 16 - sh['lbT'].shape[1]), np.float32)], axis=1)
    sh['wada'] = np.ascontiguousarray(f('w_ada')[:L].reshape(L, KC, 128, 6 * D).transpose(0, 2, 1, 3))
    sh['bada'] = np.ascontiguousarray(f('b_ada')[:L].reshape(L, 1, 6 * D))
    sh['win'] = win
    sh['lora'] = lora
    sh['g2'] = g2
    sh['v2'] = v2
    sh['wbr'] = np.ascontiguousarray(f('w_branch')[:L].reshape(L, 3, 4, 128, D).transpose(0, 3, 1, 2, 4).reshape(L, 128, 12, D))
    sh['wout'] = np.ascontiguousarray(f('w_out')[:L].reshape(L, KC, 128, D).transpose(0, 2, 1, 3))
    wu = f('w_up')[:L].reshape(L, KC, 128, 2, NFF, 128)
    sh['wup'] = np.ascontiguousarray(wu.transpose(0, 4, 2, 1, 3, 5).reshape(L, NFF, 128, KC * 256))
    sh['wdn'] = np.ascontiguousarray(f('w_down')[:L].reshape(L, NFF, 128, D).transpose(0, 2, 1, 3))
    sh['cst'] = make_consts()
    sh['cstb'] = make_consts_b()
    return sh


_CACHE = {}


def run(inp, S_LEN, L, dbg=(), STOP=99):
    key = (S_LEN, L, tuple(dbg), STOP)
    if key not in _CACHE:
        _CACHE[key] = build(S_LEN, L, dbg, STOP)
    nc = _CACHE[key]
    sh = prep_shared(inp, L)
    x = np.asarray(inp['x'], np.float32)
    c = np.asarray(inp['c'], np.float32)
    B = x.shape[0]
    in_maps = []
    for b in range(B):
        m = dict(sh)
        m['x'] = np.ascontiguousarray(x[b])
        m['cT'] = _fm(c[b])
        in_maps.append(m)
    res = run_bass_kernel_spmd(nc, in_maps, core_ids=list(range(B)))
    return res.results


def kernel(**inputs):
    res = run(inputs, 4096, 4)
    return np.stack([r['y'] for r in res]).astype(np.float32)
```

```python
import contextlib
import os
HGCUT = int(os.environ.get('HGCUT', '9'))
HGSUB = int(os.environ.get('HGSUB', '9'))
import numpy as np
import ml_dtypes
import concourse.bass as bass
import concourse.mybir as mybir
from concourse.bass_utils import run_bass_kernel_spmd

F32 = mybir.dt.float32
BF16 = mybir.dt.bfloat16
AF = mybir.ActivationFunctionType
ALU = mybir.AluOpType

NDSEM = 24
COMPUTE = ('pe', 'act', 'dve', 'pool')


class T:
    __slots__ = ('ap', 'name', 'w', 'r', 'ut', 'xr')

    def __init__(self, ap, name='', ut=False, xr=False):
        self.ap = ap
        self.name = name
        self.w = None
        self.r = {}
        self.ut = ut
        self.xr = xr

    def __getitem__(self, k):
        return Vw(self, self.ap[k])

    @property
    def v(self):
        return Vw(self, self.ap)


class Vw:
    __slots__ = ('t', 'ap')

    def __init__(self, t, ap):
        self.t = t
        self.ap = ap

    def __getitem__(self, k):
        return Vw(self.t, self.ap[k])

    def re(self, pat, **kw):
        return Vw(self.t, self.ap.rearrange(pat, **kw))

    def bc(self, shape):
        return Vw(self.t, self.ap.to_broadcast(list(shape)))

    def cast(self, dt):
        return Vw(self.t, self.ap.bitcast(dt))

    def us(self, axis):
        return Vw(self.t, self.ap.unsqueeze(axis))


class Ins:
    __slots__ = ('eng', 'idx', 'fn', 'waits', 'signal', 'vc', 'key', 'val', 'isdma')


class Sched:
    def __init__(self):
        self.q = {e: [] for e in COMPUTE + ('sp',)}
        self.clock = {e: {} for e in COMPUTE + ('sp',)}
        self.dcount = [0] * NDSEM
        self.dlast = [None] * NDSEM
        self.dnext = 0

    def _add(self, eng, fn, reads, writes, isdma):
        ins = Ins()
        ins.eng = eng
        ins.idx = len(self.q[eng])
        ins.fn = fn
        ins.waits = []
        ins.signal = False
        ins.isdma = isdma
        clock = self.clock[eng]
        deps = []
        if isdma:
            j = self.dnext
            self.dnext = (j + 1) % NDSEM
            self.dcount[j] += 1
            ins.key = ('d', j)
            ins.val = 16 * self.dcount[j]
            if self.dlast[j] is not None:
                deps.append((self.dlast[j], 'sem'))
            self.dlast[j] = ins
            ins.signal = True
        else:
            ins.key = eng
            ins.val = ins.idx + 1
        for t in reads:
            if t.w is not None:
                deps.append((t.w, 'raw'))
            if t.xr:
                for r in t.r.values():
                    if r.eng != eng:
                        deps.append((r, 'rar'))
        for t in writes:
            if t.w is not None:
                deps.append((t.w, 'waw'))
            for r in t.r.values():
                deps.append((r, 'war'))
        for p, kind in deps:
            if (not p.isdma) and (not isdma) and p.eng == eng:
                if eng == 'pe' or kind != 'raw':
                    continue
            if clock.get(p.key, 0) >= p.val:
                continue
            p.signal = True
            ins.waits.append(p)
            for k, v in p.vc.items():
                if clock.get(k, 0) < v:
                    clock[k] = v
        vc = dict(clock)
        vc[ins.key] = ins.val
        ins.vc = vc
        for t in reads:
            t.r[ins.key] = ins
        for t in writes:
            t.w = ins
            t.r = {}
        self.q[eng].append(ins)
        return ins

    def op(self, eng, fn, reads=(), writes=()):
        return self._add(eng, fn, reads, writes, False)

    def dma(self, fn, reads=(), writes=(), queue='sp'):
        return self._add(queue, fn, reads, writes, True)

    def barrier(self):
        lasts = []
        for e in COMPUTE:
            for ins in reversed(self.q[e]):
                if ins.fn is not None and not ins.isdma:
                    lasts.append(ins)
                    break
        for j in range(NDSEM):
            if self.dlast[j] is not None:
                lasts.append(self.dlast[j])
        for e in COMPUTE + ('sp',):
            ins = Ins()
            ins.eng = e
            ins.idx = len(self.q[e])
            ins.fn = None
            ins.waits = []
            ins.signal = False
            ins.isdma = False
            ins.key = e
            ins.val = ins.idx
            clock = self.clock[e]
            for p in lasts:
                if (not p.isdma) and p.eng == e:
                    continue
                if clock.get(p.key, 0) >= p.val:
                    continue
                p.signal = True
                ins.waits.append(p)
                for k, v in p.vc.items():
                    if clock.get(k, 0) < v:
                        clock[k] = v
            ins.vc = dict(clock)
            self.q[e].append(ins)

    def emit(self, sems):
        sigcount = {}
        for e in COMPUTE:
            c = 0
            for ins in self.q[e]:
                if ins.fn is not None and ins.signal and not ins.isdma:
                    c += 1
                    sigcount[id(ins)] = c

        def run(eng_name, e):
            for ins in self.q[eng_name]:
                w = {}
                for p in ins.waits:
                    v = p.val if p.isdma else sigcount[id(p)]
                    if w.get(p.key, 0) < v:
                        w[p.key] = v
                for k, v in w.items():
                    e.wait_ge(sems[k], v)
                if ins.fn is None:
                    continue
                r = ins.fn(e)
                if ins.isdma:
                    r.then_inc(sems[ins.key], 16)
                elif ins.signal:
                    r.then_inc(sems[ins.key], 1)
        return run


def _ap(x):
    return x.ap if isinstance(x, Vw) else x


def _ts(*xs):
    return [x.t for x in xs if isinstance(x, Vw) and not x.t.ut]


class Ops:
    def __init__(self, S):
        self.S = S

    def tt(self, eng, out, a, b, op):
        self.S.op(eng, lambda e: e.tensor_tensor(out=out.ap, in0=a.ap, in1=b.ap, op=op), _ts(a, b), _ts(out))

    def ts(self, eng, out, a, s1, s2, op0, op1=None):
        if op1 is None:
            self.S.op(eng, lambda e: e.tensor_scalar(out=out.ap, in0=a.ap, scalar1=_ap(s1), scalar2=None, op0=op0),
                      _ts(a, s1), _ts(out))
        else:
            self.S.op(eng, lambda e: e.tensor_scalar(out=out.ap, in0=a.ap, scalar1=_ap(s1), scalar2=_ap(s2), op0=op0, op1=op1),
                      _ts(a, s1, s2), _ts(out))

    def stt(self, eng, out, a, s, b, op0, op1):
        self.S.op(eng, lambda e: e.scalar_tensor_tensor(out=out.ap, in0=a.ap, scalar=_ap(s), in1=b.ap, op0=op0, op1=op1),
                  _ts(a, s, b), _ts(out))

    def cp(self, eng, out, a):
        if eng == 'act':
            self.S.op(eng, lambda e: e.copy(out=out.ap, in_=a.ap), _ts(a), _ts(out))
        else:
            self.S.op(eng, lambda e: e.tensor_copy(out=out.ap, in_=a.ap), _ts(a), _ts(out))

    def act(self, out, a, func, bias=None, scale=None):
        kw = {}
        if bias is not None:
            kw['bias'] = _ap(bias)
        if scale is not None:
            kw['scale'] = _ap(scale)
        self.S.op('act', lambda e: e.activation(out=out.ap, in_=a.ap, func=func, **kw), _ts(a, bias, scale), _ts(out))

    def recip(self, out, a):
        self.S.op('dve', lambda e: e.reciprocal(out=out.ap, in_=a.ap), _ts(a), _ts(out))

    def rsqrt(self, out, a, add, tmp):
        self.act(tmp, a, AF.Sqrt, bias=add)
        self.recip(out, tmp)

    def scan(self, out, d0, d1, init, op0=ALU.mult, op1=ALU.add):
        self.S.op('dve', lambda e: e.tensor_tensor_scan(out=out.ap, data0=d0.ap, data1=d1.ap, initial=_ap(init), op0=op0, op1=op1),
                  _ts(d0, d1, init), _ts(out))

    def memset(self, eng, out, val):
        self.S.op(eng, lambda e: e.memset(out.ap, val), [], _ts(out))

    def mm(self, out, lhsT, rhs, start=True, stop=True):
        self.S.op('pe', lambda e: e.matmul(out.ap, lhsT.ap, rhs.ap, start=start, stop=stop), _ts(lhsT, rhs), _ts(out))

    def tr(self, out, a, ident):
        self.S.op('pe', lambda e: e.transpose(out=out.ap, in_=a.ap, identity=ident.ap), _ts(a, ident), _ts(out))

    def dma(self, out, a, queue='sp'):
        self.S.dma(lambda e: e.dma_start(out=out.ap, in_=a.ap), _ts(a), _ts(out), queue=queue)


D = 1024
KC = 8
W_MIX = 512
DFF = 2816
NFF = 22
EPS = 1e-6
LN_EPS = 64e-5
NCOLS = 243
C_N1G, C_N2G, C_FBF, C_QG, C_KG, C_OG, C_MU, C_W0, C_A0, C_KK, C_KA, C_RK, C_LNW, C_LNB, C_V0, C_VMU, C_CW, C_CB = \
    0, 8, 16, 17, 18, 19, 20, 34, 38, 42, 46, 50, 54, 58, 62, 66, 67, 199
J_FQ, J_FK, J_FV, J_HQ, J_HF, J_HI, J_HO, J_RR, J_RK, J_RV, J_LORA, J_GLO, J_GATE, J_MISC = 0, 4, 8, 12, 16, 20, 24, 28, 32, 36, 40, 41, 42, 66
NCH = 67
K_ID, K_OBD, K_ONE, K_BD, K_UI8, K_NSL8, K_ID8, K_NSU4, K_SU4, K_ME, K_MO, K_W = 0, 128, 256, 384, 512, 1024, 1536, 2048, 2304, 2560, 3072, 3584
KB_ID, KB_ONE, KB_TRI, KB_SEL, KB_W = 0, 128, 256, 384, 1408


def build(S_LEN, L, dbg=(), STOP=99):
    NT = S_LEN // 512
    NCK = S_LEN // 64
    nc = bass.Bass("TRN2", target_bir_lowering=False)
    S = Sched()
    o = Ops(S)

    def dram(name, shape, dt, kind="Internal"):
        if name in dbg:
            kind = "ExternalOutput"
        return T(nc.dram_tensor(name, list(shape), dt, kind=kind).ap(), name, ut=True)

    x_in = dram("x", [S_LEN, D], F32, "ExternalInput")
    cT_in = dram("cT", [128, KC], F32, "ExternalInput")
    cols_in = dram("cols", [L, 128, NCOLS], F32, "ExternalInput")
    lbT_in = dram("lbT", [128, 16], F32, "ExternalInput")
    wada_in = dram("wada", [L, 128, KC, 6 * D], F32, "ExternalInput")
    bada_in = dram("bada", [L, 1, 6 * D], F32, "ExternalInput")
    win_in = dram("win", [L, NCH, 128, KC * 128], F32, "ExternalInput")
    lora_in = dram("lora", [L, 128, 512], F32, "ExternalInput")
    g2_in = dram("g2", [L, 128, 512], F32, "ExternalInput")
    v2_in = dram("v2", [L, 64, 512], F32, "ExternalInput")
    wbr_in = dram("wbr", [L, 128, 12, D], F32, "ExternalInput")
    wout_in = dram("wout", [L, 128, KC, D], F32, "ExternalInput")
    wup_in = dram("wup", [L, NFF, 128, KC * 256], F32, "ExternalInput")
    wdn_in = dram("wdn", [L, 128, NFF, D], F32, "ExternalInput")
    cst_in = dram("cst", [128, K_W], F32, "ExternalInput")
    cstb_in = dram("cstb", [128, KB_W], F32, "ExternalInput")
    y_out = dram("y", [S_LEN, D], F32, "ExternalOutput")

    XT = dram("XT", [128, KC, S_LEN], F32)
    QKT = dram("QKT", [8, 128, S_LEN], BF16)
    FVT = dram("FVT", [4, 128, S_LEN], BF16)
    GT = dram("GT", [24, 128, S_LEN], BF16)
    HQ = dram("HQ", [4, 128, S_LEN], F32)
    HK = dram("HK", [4, 128, S_LEN], F32)
    HI = dram("HI", [4, 128, S_LEN], F32)
    HO = dram("HO", [4, 128, S_LEN], BF16)
    RKR = dram("RKR", [4, 128, NCK, 2, 64], F32)
    RKT = dram("RKT", [4, 128, S_LEN], F32)
    RBT = dram("RBT", [4, 128, S_LEN], F32)
    RVT = dram("RVT", [4, 128, S_LEN], F32)
    RBO = dram("RBO", [4, 128, S_LEN], F32)
    RG = dram("RG", [4, 128, S_LEN], BF16)
    FTD = dram("FTD", [8, S_LEN], F32)
    VF = dram("VF", [4, 128, S_LEN], F32)
    YB = dram("YB", [12, 128, S_LEN], BF16)
    H2 = dram("H2", [KC, 128, S_LEN], BF16)
    ACTT = dram("ACTT", [NFF, 128, S_LEN], BF16)

    arena = nc.alloc_sbuf_tensor("arena", [128, 53200], F32).ap()
    st = {'off': 0, 'mark': 0}

    def sb(n, name='', dt=F32):
        words = n if dt == F32 else (n + 1) // 2
        assert st['off'] + words <= 53200, ("SBUF arena overflow", name, st['off'], words)
        a = arena[:, st['off']: st['off'] + words]
        st['off'] += words
        assert st['off'] <= 53200, ("SBUF arena overflow", name, st['off'])
        if dt != F32:
            a = a.bitcast(dt)
        return T(a, name)

    psb = [T(nc.alloc_psum_tensor(f"ps{i}", [128, 512], F32).ap(), f"ps{i}", xr=True) for i in range(8)]
    pst = {'i': 0}

    pst['n'] = 8

    def ps():
        p = psb[pst['i'] % pst['n']]
        pst['i'] += 1
        return p

    cst = sb(K_W, 'cst')
    o.dma(cst.v, cst_in.v)
    ident = cst[:, K_ID:K_ID + 128]
    onesbd = cst[:, K_OBD:K_OBD + 128]
    ones = cst[:, K_ONE:K_ONE + 128]
    bd = cst[:, K_BD:K_BD + 128]
    ui8 = cst[:, K_UI8:K_UI8 + 512]
    ui4 = cst[:, K_UI8:K_UI8 + 256]
    nsl8 = cst[:, K_NSL8:K_NSL8 + 512]
    id8 = cst[:, K_ID8:K_ID8 + 512]
    nsu4 = cst[:, K_NSU4:K_NSU4 + 256]
    su4 = cst[:, K_SU4:K_SU4 + 256]
    me8 = cst[:, K_ME:K_ME + 512]
    mo8 = cst[:, K_MO:K_MO + 512]
    cstb = sb(KB_W, 'cstb', BF16)
    identb = cstb[:, KB_ID:KB_ID + 128]
    onesb = cstb[:, KB_ONE:KB_ONE + 128]
    trib = cstb[:, KB_TRI:KB_TRI + 128]
    selb = cstb[:, KB_SEL:KB_SEL + 1024]
    ones512 = sb(512, 'ones512')
    o.memset('dve', ones512.v, 1.0)
    colsT = [sb(NCOLS, f'cols{l}') for l in range(L)]
    for l in range(L):
        o.dma(colsT[l].v, cols_in[l])
    modT = [sb(48, f'mod{l}') for l in range(L)]
    derT = [sb(64, f'der{l}') for l in range(L)]
    lbT = sb(16, 'lbT')
    lowT = sb(16, 'lowT')
    o.dma(lbT.v, lbT_in.v)
    ex = sb(16, 'lbexp')
    sm = sb(4, 'lbsum')
    rs = sb(4, 'lbrs')
    oml = sb(16, 'oml')
    cT = sb(KC, 'cT')
    condT = sb(KC, 'condT')
    persist_mark = st['off']

    st['mark'] = st['off']
    cbt = sb(KB_W, 'cbtmp')
    o.dma(cbt.v, cstb_in.v)
    o.cp('dve', cstb.v, cbt.v)
    xin = [sb(D, f'xin{i}') for i in range(2)]
    xst = [sb(KC * 512, f'xst{i}') for i in range(2)]
    for i in range(NT):
        xs = xst[i % 2]
        for q in range(4):
            xi = xin[q % 2]
            t0 = i * 512 + q * 128
            o.dma(xi.v, x_in[t0:t0 + 128, :])
            for h in range(2):
                p = ps()
                for c in range(4):
                    kc = h * 4 + c
                    o.tr(p[:, c * 128:(c + 1) * 128], xi[:, kc * 128:(kc + 1) * 128], ident)
                dst = xs.v.re("p (k t) -> p k t", k=KC)[:, h * 4:(h + 1) * 4, q * 128:(q + 1) * 128]
                o.cp('act' if h == 0 else 'dve', dst, p.v.re("p (c t) -> p c t", c=4))
        o.dma(XT[:, :, i * 512:(i + 1) * 512], xs.v.re("p (k t) -> p k t", k=KC))

    o.dma(cT.v, cT_in.v)
    o.act(condT.v, cT.v, AF.Silu)
    wst = [sb(KC * 512, f'wst{i}') for i in range(2)]
    bada = sb(6 * D, 'bada')
    modrow = sb(6 * D, 'modrow')
    for l in range(L):
        o.dma(bada[0:1, :], bada_in[l])
        for blk in range(12):
            w = wst[blk % 2]
            o.dma(w.v.re("p (k n) -> p k n", k=KC), wada_in[l][:, :, blk * 512:(blk + 1) * 512])
            p = ps()
            for kc in range(KC):
                o.mm(p[0:1, :], condT[:, kc:kc + 1], w[:, kc * 512:(kc + 1) * 512], start=(kc == 0), stop=(kc == KC - 1))
            o.tt('dve', modrow[0:1, blk * 512:(blk + 1) * 512], p[0:1, :], bada[0:1, blk * 512:(blk + 1) * 512], ALU.add)
        p = ps()
        for j in range(48):
            o.mm(p[:, j:j + 1], modrow[0:1, j * 128:(j + 1) * 128], ones[0:1, 0:1])
        o.cp('dve', modT[l].v, p[:, 0:48])
        dr = derT[l]
        cl = colsT[l]
        o.ts('dve', dr[:, 0:8], modT[l][:, 8:16], 1.0, 32.0, ALU.add, ALU.mult)
        o.tt('dve', dr[:, 0:8], dr[:, 0:8], cl[:, C_N1G:C_N1G + 8], ALU.mult)
        o.ts('dve', dr[:, 8:16], modT[l][:, 32:40], 1.0, 32.0, ALU.add, ALU.mult)
        o.tt('dve', dr[:, 8:16], dr[:, 8:16], cl[:, C_N2G:C_N2G + 8], ALU.mult)
        o.ts('dve', dr[:, 17:21], cl[:, C_KA:C_KA + 4], -1.0, 1.0, ALU.mult, ALU.add)
        o.ts('dve', dr[:, 21:22], cl[:, C_QG:C_QG + 1], 8.0, None, ALU.mult)
        o.ts('dve', dr[:, 22:23], cl[:, C_KG:C_KG + 1], 8.0, None, ALU.mult)
        o.ts('dve', dr[:, 23:24], cl[:, C_OG:C_OG + 1], 8.0, None, ALU.mult)
        o.ts('dve', dr[:, 24:28], cl[:, C_MU + 0:C_MU + 4], -1.0, None, ALU.mult)
    o.act(ex.v, lbT.v, AF.Exp)
    o.cp('dve', sm.v, ex[:, 0:4])
    for l in range(1, L):
        o.tt('dve', sm.v, sm.v, ex[:, l * 4:(l + 1) * 4], ALU.add)
    o.recip(rs.v, sm.v)
    for l in range(L):
        o.tt('dve', ex[:, l * 4:(l + 1) * 4], ex[:, l * 4:(l + 1) * 4], rs.v, ALU.mult)
    o.memset('dve', lowT[:, 0:4], 0.0)
    for l in range(1, L):
        o.tt('dve', lowT[:, l * 4:(l + 1) * 4], lowT[:, (l - 1) * 4:l * 4], ex[:, l * 4:(l + 1) * 4], ALU.add)
    o.ts('dve', oml.v, lowT.v, -1.0, 1.0, ALU.mult, ALU.add)
    ctx = dict(nc=nc, S=S, o=o, sb=sb, ps=ps, st=st, L=L, NT=NT, NCK=NCK, S_LEN=S_LEN)
    NEG_C0 = -0.6065306597126334
    WK_N = 16
    sel_in = None

    def emit_layer(l):
        S.barrier()
        st['off'] = persist_mark
        cl = colsT[l]
        dr = derT[l]
        md = modT[l]
        Fcar = sb(2, 'Fcar')
        hsc = sb(3 * 4 * NCK, 'hsc')
        hscv = hsc.v.re("p (a c n) -> p a c n", a=3, c=4)
        gam = sb(4 * NCK, 'gam')
        gamv = gam.v.re("p (c n) -> p c n", c=4)
        ded = [sb(512, f'ded{i}') for i in range(7)]
        wkp = [sb(512, f'wk{i}') for i in range(WK_N)]
        wki = {'i': 0}

        def wk():
            t = wkp[wki['i'] % WK_N]
            wki['i'] += 1
            return t
        wkb = [sb(512, f'wkb{i}', BF16) for i in range(6)]
        wkbi = {'i': 0}

        def wkbf():
            t = wkb[wkbi['i'] % 6]
            wkbi['i'] += 1
            return t
        layer_mark = st['off']

        hts = [sb(KC * 512, f'ht{i}', BF16) for i in range(NT)]

        def norm_tile(xtile, Acol0, Bcol0, dst, sqt):
            sq = sqt
            o.act(sq.v, xtile.v, AF.Square)
            p = ps()
            for kc in range(KC):
                o.mm(p.v, onesb, sq[:, kc * 512:(kc + 1) * 512], start=(kc == 0), stop=(kc == KC - 1))
            rstd = wk()
            rtmp = wk()
            o.rsqrt(rstd[:, 0:512], p.v, 1024.0 * EPS, rtmp[:, 0:512])
            x3 = xtile.v.re("p (k t) -> p k t", k=KC)
            o.tt('dve', x3, x3, rstd[:, 0:512].us(1).bc([128, KC, 512]), ALU.mult)
            for kc in range(KC):
                o.act(dst[:, kc * 512:(kc + 1) * 512], xtile[:, kc * 512:(kc + 1) * 512], AF.Identity,
                      bias=md[:, Bcol0 + kc:Bcol0 + kc + 1], scale=dr[:, Acol0 + kc:Acol0 + kc + 1])

        nmark = st['off']
        xt = [sb(KC * 512, f'xt{i}') for i in range(2)]
        sqt = sb(KC * 512, 'sqt', BF16)
        for i in range(NT):
            x = xt[i % 2]
            o.dma(x.v.re("p (k t) -> p k t", k=KC), XT[:, :, i * 512:(i + 1) * 512])
            norm_tile(x, 0, 0, hts[i], sqt)
        S.barrier()
        st['off'] = nmark
        lorab = sb(512, 'lorab', BF16)
        g2b = sb(512, 'g2b', BF16)
        v2b = sb(512, 'v2b', BF16)
        wtmp = wk()
        o.dma(wtmp[:, 0:512], lora_in[l])
        o.cp('pool', lorab.v, wtmp[:, 0:512])
        wtmp = wk()
        o.dma(wtmp[:, 0:512], g2_in[l])
        o.cp('pool', g2b.v, wtmp[:, 0:512])
        wtmp = wk()
        o.dma(wtmp[0:64, 0:512], v2_in[l])
        o.cp('pool', v2b[0:64, :], wtmp[0:64, 0:512])
        twa = sb(S_LEN, 'twa', BF16)
        sgl = sb(S_LEN, 'sgl', BF16)
        vlo = sb(S_LEN, 'vlo', BF16)
        raws = [sb(514, f'raw{i}') for i in range(3)]
        Cx = sb(513, 'Cx')
        o.memset('dve', Cx[:, 0:1], 0.0)
        wf = [sb(KC * 128, 'wf0')] * 2
        wfi = {'i': 0}
        wbt = [[sb(KC * 128, f'wb{g}_{k}', BF16) for k in range(4)] for g in range(2)]
        grp = {'i': 0}

        def load_group(js):
            g = grp['i'] % 2
            grp['i'] += 1
            outs = []
            for k, j in enumerate(js):
                stg = wf[wfi['i'] % 2]
                wfi['i'] += 1
                o.dma(stg.v, win_in[l][j])
                o.cp('pool', wbt[g][k].v, stg.v)
                outs.append(wbt[g][k])
            return outs

        def proj(wt, i):
            p = ps()
            for kc in range(KC):
                o.mm(p.v, wt[:, kc * 128:(kc + 1) * 128], hts[i][:, kc * 512:(kc + 1) * 512], start=(kc == 0), stop=(kc == KC - 1))
            return p

        def shiftmix(p, raw, mu, r0=0, r1=128, dst=None):
            o.cp('act', raw[r0:r1, 1:513], p[r0:r1, :])
            d = wk()
            o.tt('dve', d[r0:r1, 0:512], raw[r0:r1, 0:512], raw[r0:r1, 1:513], ALU.subtract)
            m = dst if dst is not None else wk()
            o.stt('dve', m[r0:r1, 0:512], d[r0:r1, 0:512], mu, raw[r0:r1, 1:513], ALU.mult, ALU.add)
            o.cp('act', raw[r0:r1, 0:1], raw[r0:r1, 512:513])
            return m

        def reset_raws():
            for r in raws:
                o.memset('pool', r[:, 0:2], 0.0)

        def tsl(i):
            return slice(i * 512, (i + 1) * 512)

        (wm,) = load_group([J_MISC])
        reset_raws()
        for i in range(NT):
            p = proj(wm, i)
            t1 = wk()
            o.act(t1[0:8, 0:512], p[0:8, :], AF.Sigmoid, bias=cl[0:8, C_FBF:C_FBF + 1])
            o.act(t1[0:8, 0:512], t1[0:8, 0:512], AF.Ln)
            t2 = wk()
            o.scan(t2[0:8, 0:512], ones512[0:8, :], t1[0:8, 0:512], 0.0 if i == 0 else Fcar[0:8, 0:1])
            o.cp('dve', Fcar[0:8, 0:1], t2[0:8, 511:512])
            o.dma(FTD[0:8, tsl(i)], t2[0:8, 0:512])
            if l > 0:
                m = shiftmix(p, raws[0], cl[32:64, C_VMU:C_VMU + 1], 32, 64)
                o.cp('dve', vlo[32:64, tsl(i)], m[32:64, 0:512])
        wl, wg = load_group([J_LORA, J_GLO])
        reset_raws()
        for i in range(NT):
            p = proj(wl, i)
            m = shiftmix(p, raws[0], cl[:, C_MU + 12:C_MU + 13])
            o.act(twa[0:64, tsl(i)], m[0:64, 0:512], AF.Tanh)
            o.cp('dve', twa[64:128, tsl(i)], m[64:128, 0:512])
            p = proj(wg, i)
            m = shiftmix(p, raws[1], cl[:, C_MU + 13:C_MU + 14])
            o.act(sgl[:, tsl(i)], m[:, 0:512], AF.Sigmoid)
        for c in range(4):
            wr, wkk_, wv = load_group([J_RR + c, J_RK + c, J_RV + c])
            reset_raws()
            cs = slice(c * 128, (c + 1) * 128)
            for i in range(NT):
                pr = proj(wr, i)
                pk = proj(wkk_, i)
                pv = proj(wv, i)
                r_m = shiftmix(pr, raws[0], cl[:, C_MU + c:C_MU + c + 1], dst=ded[0])
                k_m = shiftmix(pk, raws[1], cl[:, C_MU + 4 + c:C_MU + 5 + c], dst=ded[1])
                v_m = shiftmix(pv, raws[2], cl[:, C_MU + 8 + c:C_MU + 9 + c], dst=ded[2])
                pa = ps()
                o.mm(pa.v, lorab[64:128, cs], twa[64:128, tsl(i)])
                a = ded[3]
                o.act(a[:, 0:512], pa.v, AF.Sigmoid, bias=cl[:, C_A0 + c:C_A0 + c + 1])
                pw = ps()
                o.mm(pw.v, lorab[0:64, cs], twa[0:64, tsl(i)])
                sgw = wk()
                o.act(sgw[:, 0:512], pw.v, AF.Sigmoid, bias=cl[:, C_W0 + c:C_W0 + c + 1])
                o.scan(Cx[:, 1:513], ones512.v, sgw[:, 0:512], 0.0)
                ref = Cx[:, 0:512].re("p (c t) -> p c t", t=64)[:, :, 0:1].bc([128, 8, 64])
                Dd = wk()
                o.tt('dve', Dd[:, 0:512].re("p (c t) -> p c t", t=64), Cx[:, 1:513].re("p (c t) -> p c t", t=64), ref, ALU.subtract)
                Dm = wk()
                o.tt('dve', Dm[:, 0:512].re("p (c t) -> p c t", t=64), Cx[:, 0:512].re("p (c t) -> p c t", t=64), ref, ALU.subtract)
                E1 = ded[4]
                E2 = wk()
                E3 = ded[5]
                o.act(E1[:, 0:512], Dd[:, 0:512], AF.Exp, scale=NEG_C0)
                o.act(E2[:, 0:512], Dm[:, 0:512], AF.Exp, scale=NEG_C0)
                o.act(E3[:, 0:512], Dd[:, 0:512], AF.Exp, scale=-NEG_C0)
                o.cp('pool', gamv[:, c, i * 8:(i + 1) * 8], E1[:, 0:512].re("p (c t) -> p c t", t=64)[:, :, 63])
                kk = wk()
                o.ts('pool', kk[:, 0:512], k_m[:, 0:512], cl[:, C_KK + c:C_KK + c + 1], None, ALU.mult)
                sq = wk()
                o.tt('pool', sq[:, 0:512], kk[:, 0:512], kk[:, 0:512], ALU.mult)
                pss = ps()
                o.mm(pss.v, onesbd, sq[:, 0:512])
                rinv = wk()
                rtmp = wk()
                o.act(rtmp[:, 0:512], pss.v, AF.Sqrt)
                o.ts('dve', rtmp[:, 0:512], rtmp[:, 0:512], 1e-12, None, ALU.max)
                o.recip(rinv[:, 0:512], rtmp[:, 0:512])
                kap = wk()
                o.tt('dve', kap[:, 0:512], kk[:, 0:512], rinv[:, 0:512], ALU.mult)
                tp = wk()
                o.ts('pool', tp[:, 0:512], a[:, 0:512], cl[:, C_KA + c:C_KA + c + 1], dr[:, 17 + c:18 + c], ALU.mult, ALU.add)
                kf = ded[6]
                o.tt('pool', kf[:, 0:512], k_m[:, 0:512], tp[:, 0:512], ALU.mult)
                bb = wk()
                o.tt('pool', bb[:, 0:512], a[:, 0:512], kap[:, 0:512], ALU.mult)
                ot = wk()
                o.tt('dve', ot[:, 0:512], r_m[:, 0:512], E1[:, 0:512], ALU.mult)
                o.dma(RKR[c][:, i * 8:(i + 1) * 8, 1, :], ot[:, 0:512].re("p (c t) -> p c t", t=64))
                ot = wk()
                o.tt('dve', ot[:, 0:512], kap[:, 0:512], E2[:, 0:512], ALU.mult)
                o.dma(RKR[c][:, i * 8:(i + 1) * 8, 0, :], ot[:, 0:512].re("p (c t) -> p c t", t=64))
                ot = wk()
                o.tt('pool', ot[:, 0:512], kf[:, 0:512], E3[:, 0:512], ALU.mult)
                o.dma(RKT[c][:, tsl(i)], ot[:, 0:512])
                ot = wk()
                o.tt('pool', ot[:, 0:512], bb[:, 0:512], E3[:, 0:512], ALU.mult)
                o.dma(RBT[c][:, tsl(i)], ot[:, 0:512])
                if l == 0:
                    o.dma(VF[c][:, tsl(i)], v_m[:, 0:512])
                    vv = v_m
                else:
                    vf = wk()
                    o.dma(vf[:, 0:512], VF[c][:, tsl(i)])
                    pg = ps()
                    o.mm(pg.v, v2b[32:64, cs], vlo[32:64, tsl(i)])
                    gt = wk()
                    o.act(gt[:, 0:512], pg.v, AF.Sigmoid, bias=cl[:, C_V0 + c:C_V0 + c + 1])
                    dv = wk()
                    o.tt('dve', dv[:, 0:512], vf[:, 0:512], v_m[:, 0:512], ALU.subtract)
                    o.tt('dve', dv[:, 0:512], dv[:, 0:512], gt[:, 0:512], ALU.mult)
                    vv = wk()
                    o.tt('dve', vv[:, 0:512], dv[:, 0:512], v_m[:, 0:512], ALU.add)
                o.dma(RVT[c][:, tsl(i)], vv[:, 0:512])
                rk = wk()
                o.stt('dve', rk[:, 0:512], r_m[:, 0:512], cl[:, C_RK + c:C_RK + c + 1], kf[:, 0:512], ALU.mult, ALU.mult)
                pb = ps()
                o.mm(pb.v, onesbd, rk[:, 0:512])
                bon = wk()
                o.tt('dve', bon[:, 0:512], pb.v, vv[:, 0:512], ALU.mult)
                o.dma(RBO[c][:, tsl(i)], bon[:, 0:512])
                pgg = ps()
                o.mm(pgg.v, g2b[:, cs], sgl[:, tsl(i)])
                gb = wkbf()
                o.cp('act', gb.v, pgg.v)
                o.dma(RG[c][:, tsl(i)], gb.v)
        for c in range(4):
            wq, wfg, wi, wo = load_group([J_HQ + c, J_HF + c, J_HI + c, J_HO + c])
            lcol = slice(l * 4 + c, l * 4 + c + 1)
            for i in range(NT):
                pq = proj(wq, i)
                pf = proj(wfg, i)
                pi_ = proj(wi, i)
                po = proj(wo, i)
                sq_ = wk()
                o.act(sq_[:, 0:512], pq.v, AF.Silu)
                sgf = wk()
                o.act(sgf[:, 0:512], pf.v, AF.Sigmoid)
                gate = wk()
                o.ts('dve', gate[:, 0:512], sgf[:, 0:512], oml[:, lcol], lowT[:, lcol], ALU.mult, ALU.add)
                lg = wk()
                o.act(lg[:, 0:512], gate[:, 0:512], AF.Ln)
                kx = wk()
                o.ts('pool', kx[:, 0:512], gate[:, 0:512], -1.0, 1.0, ALU.mult, ALU.add)
                o.scan(Cx[:, 1:513], ones512.v, lg[:, 0:512], 0.0)
                G3 = Cx[:, 1:513].re("p (c t) -> p c t", t=64)
                Dd = wk()
                o.tt('dve', Dd[:, 0:512].re("p (c t) -> p c t", t=64), G3, G3[:, :, 31:32].bc([128, 8, 64]), ALU.subtract)
                E1 = wk()
                E3 = wk()
                o.act(E1[:, 0:512], Dd[:, 0:512], AF.Exp)
                o.act(E3[:, 0:512], Dd[:, 0:512], AF.Exp, scale=-1.0)
                ot = wk()
                o.tt('dve', ot[:, 0:512], sq_[:, 0:512], E1[:, 0:512], ALU.mult)
                o.dma(HQ[c][:, tsl(i)], ot[:, 0:512])
                ot = wk()
                o.tt('pool', ot[:, 0:512], kx[:, 0:512], E3[:, 0:512], ALU.mult)
                o.dma(HK[c][:, tsl(i)], ot[:, 0:512])
                o.cp('pool', hscv[:, 0, c, i * 8:(i + 1) * 8], E1[:, 0:512].re("p (c t) -> p c t", t=64)[:, :, 63])
                G0 = Cx[:, 0:512].re("p (c t) -> p c t", t=64)
                d8 = wk()
                o.tt('dve', d8[:, 0:8], G3[:, :, 63], G0[:, :, 0], ALU.subtract)
                o.act(hscv[:, 1, c, i * 8:(i + 1) * 8], d8[:, 0:8], AF.Exp)
                d8 = wk()
                o.tt('dve', d8[:, 0:8], G3[:, :, 31], G0[:, :, 0], ALU.subtract)
                o.act(hscv[:, 2, c, i * 8:(i + 1) * 8], d8[:, 0:8], AF.Exp)
                it = wk()
                o.cp('act', it[:, 0:512], pi_.v)
                o.dma(HI[c][:, tsl(i)], it[:, 0:512])
                ob = wkbf()
                o.act(ob.v, po.v, AF.Silu)
                o.dma(HO[c][:, tsl(i)], ob.v)
        for c in range(4):
            wq, wk_ = load_group([J_FQ + c, J_FK + c])
            for i in range(NT):
                for which, wt in ((0, wq), (1, wk_)):
                    p = proj(wt, i)
                    sq = wk()
                    o.act(sq[:, 0:512], p.v, AF.Square)
                    pss = ps()
                    o.mm(pss.v, onesbd, sq[:, 0:512])
                    rstd = wk()
                    rtmp = wk()
                    o.rsqrt(rstd[:, 0:512], pss.v, 64.0 * EPS, rtmp[:, 0:512])
                    qb = wkbf()
                    o.stt('dve', qb.v, p.v, dr[:, 21 + which:22 + which], rstd[:, 0:512], ALU.mult, ALU.mult)
                    o.dma(QKT[which * 4 + c][:, tsl(i)], qb.v)
        wvs = load_group([J_FV + c for c in range(4)])
        for c in range(4):
            for i in range(NT):
                p = proj(wvs[c], i)
                vb = wkbf()
                o.cp('act', vb.v, p.v)
                o.dma(FVT[c][:, tsl(i)], vb.v)
        for c0 in range(0, 24, 4):
            wgs = load_group([J_GATE + c0 + k for k in range(4)])
            for k in range(4):
                for i in range(NT):
                    p = proj(wgs[k], i)
                    gb = wkbf()
                    o.act(gb.v, p.v, AF.Sigmoid)
                    o.dma(GT[c0 + k][:, tsl(i)], gb.v)
        if STOP <= 1:
            return
        NB = S_LEN // 128

        def r3(v, c):
            return v.re("p (c t) -> p c t", c=c)

        def par(v, bk):
            return v.re("p (c a t) -> p c a t", c=4, a=2)[:, :, bk, :]

        S.barrier()
        st['off'] = layer_mark
        Fsb = sb(S_LEN, 'Fsb')
        F8 = sb(S_LEN, 'F8', BF16)
        o.memset('pool', F8.v, 0.0)
        o.dma(Fsb[0:8, :], FTD.v)
        o.ts('dve', F8[0:8, :], Fsb[0:8, :], 8.0, None, ALU.mult)
        negF = sb(NB * 8, 'negF')
        for blk in range(NB):
            p = ps()
            o.tr(p[:, 0:8], Fsb[0:8, blk * 128:(blk + 1) * 128], ident[0:8, 0:8])
            o.ts('dve', negF[:, blk * 8:(blk + 1) * 8], p[:, 0:8], -1.0, None, ALU.mult)
        VTM = sb(NB * 512, 'VTM', BF16)
        V4 = VTM.v.re("p (n h x) -> p n h x", n=NB, h=4)
        o.memset('pool', VTM.v, 1.0)
        qz = [sb(S_LEN, f'qz{g}', BF16) for g in range(2)]
        kT = sb(S_LEN, 'kT', BF16)
        fv2 = [sb(S_LEN, f'fv{c}', BF16) for c in range(2)]
        Pt = [sb(512, f'P{i}', BF16) for i in range(6)]
        yt = [sb(512, f'y{i}', BF16) for i in range(3)]
        rd = [sb(512, f'rd{i}') for i in range(2)]
        pst['pc'] = 0
        ycount = 0
        for half in range(2):
            pst['n'] = 8
            for cc_ in range(2):
                o.dma(fv2[cc_].v, FVT[2 * half + cc_])
            for blk in range(NB):
                p = ps()
                pb = p.v.cast(BF16)
                for cc_ in range(2):
                    o.tr(pb[:, cc_ * 128:(cc_ + 1) * 128], fv2[cc_][:, blk * 128:(blk + 1) * 128], identb)
                o.cp('act' if blk % 2 else 'dve', V4[:, blk, :, 0:64], pb[:, 0:256].re("p (h x) -> p h x", h=4))
            pst['n'] = 4
            for cc_ in range(2):
                c = 2 * half + cc_
                o.dma(kT.v, QKT[4 + c])
                o.memset('pool', qz[0][64:128, :], 0.0)
                o.memset('pool', qz[1][0:64, :], 0.0)
                o.dma(qz[0][0:64, :], QKT[c][0:64, :])
                o.dma(qz[1][64:128, :], QKT[c][64:128, :])
                for h2 in range(2):
                    h = 2 * c + h2
                    hl = 2 * cc_ + h2
                    rows = slice(h2 * 64, (h2 + 1) * 64)
                    for qc in range(NT):
                        pn = psb[4 + ycount % 4]
                        nkb = 4 * qc + 4
                        pend = []

                        def stage_a(kb):
                            i_ = kb - 4 * qc
                            n0 = 128 * i_ if i_ > 0 else 0
                            qs = slice(qc * 512 + n0, (qc + 1) * 512)
                            psc = ps()
                            o.mm(psc[:, n0:512], kT[:, kb * 128:(kb + 1) * 128], qz[h2][:, qs], start=True, stop=False)
                            if i_ >= 0:
                                o.mm(psc[:, n0:n0 + 128], identb, trib, start=False, stop=False)
                            o.mm(psc[:, n0:512], selb[:, h * 128:(h + 1) * 128], F8[:, qs], start=False, stop=True)
                            P = Pt[pst['pc'] % 6]
                            pst['pc'] += 1
                            o.act(P[:, n0:512], psc[:, n0:512], AF.Exp, bias=negF[:, kb * 8 + h:kb * 8 + h + 1], scale=0.125)
                            return (kb, n0, P)

                        def stage_b(kb, n0, P):
                            o.mm(pn[:, n0:512], V4[:, kb, hl, :], P[:, n0:512], start=(kb == 0), stop=(kb == nkb - 1))
                        for kb in range(nkb):
                            pend.append(stage_a(kb))
                            if len(pend) > 2:
                                stage_b(*pend.pop(0))
                        while pend:
                            stage_b(*pend.pop(0))
                        r = rd[ycount % 2]
                        y = yt[ycount % 3]
                        ycount += 1
                        o.cp('act', r[0:64, :], pn[64:128, :])
                        o.recip(r[0:64, :], r[0:64, :])
                        o.tt('dve', y[0:64, :], pn[0:64, :], r[0:64, :], ALU.mult)
                        o.dma(YB[c][rows, tsl(qc)], y[0:64, :])
        pst['n'] = 8
        if STOP <= 2:
            return

        S.barrier()
        st['off'] = layer_mark
        Mst = sb(512, 'Mst')
        M3 = r3(Mst.v, 4)
        o.memset('dve', Mst.v, 0.0)
        vpad = sb(1024, 'vpad', BF16)
        o.memset('pool', vpad.v, 0.0)
        ktm = [sb(512, f'ktm{i}', BF16) for i in range(2)]
        vtm = [sb(512, f'vtm{i}', BF16) for i in range(2)]
        amt = [sb(512, f'amt{i}', BF16) for i in range(2)]
        m0t = [sb(512, f'm0t{i}', BF16) for i in range(2)]
        hqb = sb(2048, 'hqb', BF16)
        hkb = sb(2048, 'hkb', BF16)
        hib = sb(2048, 'hib', BF16)
        hq = [sb(2048, f'hq{i}') for i in range(2)]
        hk = [sb(2048, f'hk{i}') for i in range(2)]
        hi = [sb(2048, f'hi{i}') for i in range(2)]
        ho = [sb(2048, f'ho{i}', BF16) for i in range(2)]
        ot = sb(2048, 'ot')
        bd4 = bd.us(1).bc([128, 4, 128])
        if HGCUT <= -1:
            return
        for i in range(NT):
            b = i % 2
            o.dma(r3(hq[b].v, 4), HQ.v.re("c p t -> p c t")[:, :, tsl(i)])
            o.dma(r3(hk[b].v, 4), HK.v.re("c p t -> p c t")[:, :, tsl(i)])
            o.dma(r3(hi[b].v, 4), HI.v.re("c p t -> p c t")[:, :, tsl(i)])
            o.dma(r3(ho[b].v, 4), HO.v.re("c p t -> p c t")[:, :, tsl(i)])
            o.cp('pool', hqb.v, hq[b].v)
            o.cp('pool', hkb.v, hk[b].v)
            o.cp('pool', hib.v, hi[b].v)
            if HGCUT <= 0:
                continue
            for ck in range(8):
                g = i * 8 + ck

                def cc(c):
                    return slice(c * 512 + ck * 64, c * 512 + (ck + 1) * 64)
                pk_ = ps()
                pv_ = ps()
                for c in range(4):
                    o.mm(pk_[0:64, c * 128:(c + 1) * 128], hkb[:, cc(c)], identb)
                    o.mm(pv_[0:64, c * 128:(c + 1) * 128], hib[:, cc(c)], identb)
                kt = ktm[g % 2]
                vt = vtm[g % 2]
                if HGSUB <= 1:
                    continue
                o.cp('act', kt[0:64, :], pk_[0:64, :])
                o.cp('act', vt[0:64, :], pv_[0:64, :])
                if HGSUB <= 2:
                    continue
                o.tt('dve', vpad[0:64, 0:512], pv_[0:64, :], me8[0:64, :], ALU.mult)
                o.tt('dve', vpad[0:64, 512:1024], pv_[0:64, :], mo8[0:64, :], ALU.mult)
                if HGCUT <= 1:
                    continue
                M0s = m0t[g % 2]
                o.tt('dve', r3(M0s.v, 4), M3, hscv[:, 2, :, g:g + 1].bc([128, 4, 128]), ALU.mult)
                if HGSUB <= 3:
                    continue
                pA = [ps(), ps()]
                for h in range(8):
                    c = h // 2
                    rr = slice((h % 2) * 64, (h % 2) * 64 + 64)
                    b0 = c * 512 + ck * 64
                    o.mm(pA[h % 2][0:64, c * 64 + 32:(c + 1) * 64], hkb[rr, b0:b0 + 64], hqb[rr, b0 + 32:b0 + 64])
                    o.mm(pA[h % 2][0:32, c * 64:c * 64 + 32], hkb[rr, b0:b0 + 32], hqb[rr, b0:b0 + 32])
                Am = amt[g % 2]
                o.memset('pool', Am[32:64, :], 0.0)
                for bk in range(2):
                    o.tt('dve', par(Am[0:64, :], bk)[:, :, 32:64], r3(pA[bk][0:64, 0:256], 4)[:, :, 32:64], r3(ui4[0:64, :], 4)[:, :, 32:64], ALU.mult)
                    o.tt('dve', par(Am[0:32, :], bk)[:, :, 0:32], r3(pA[bk][0:32, 0:256], 4)[:, :, 0:32], r3(ui4[0:32, :], 4)[:, :, 0:32], ALU.mult)
                if HGCUT <= 2:
                    continue
                pO = ps()
                for c in range(4):
                    oc_ = pO[:, c * 64:(c + 1) * 64]
                    o.mm(oc_, M0s[:, c * 128:(c + 1) * 128], hqb[:, cc(c)], start=True, stop=False)
                    o.mm(oc_, vpad[0:64, c * 128:(c + 1) * 128], Am[0:64, (2 * c) * 64:(2 * c + 1) * 64], start=False, stop=False)
                    o.mm(oc_, vpad[0:64, 512 + c * 128:512 + (c + 1) * 128], Am[0:64, (2 * c + 1) * 64:(2 * c + 2) * 64], start=False, stop=True)
                o.cp('act', r3(ot.v, 4)[:, :, ck * 64:(ck + 1) * 64], r3(pO[:, 0:256], 4))
                if HGCUT <= 3:
                    continue
                pS = ps()
                for c in range(4):
                    o.mm(pS[:, c * 128:(c + 1) * 128], kt[0:64, c * 128:(c + 1) * 128], vt[0:64, c * 128:(c + 1) * 128])
                t1 = wk()
                o.tt('dve', r3(t1.v, 4), r3(pS.v, 4), bd4, ALU.mult)
                o.tt('dve', r3(t1.v, 4), r3(t1.v, 4), hscv[:, 0, :, g:g + 1].bc([128, 4, 128]), ALU.mult)
                o.tt('pool', M3, M3, hscv[:, 1, :, g:g + 1].bc([128, 4, 128]), ALU.mult)
                o.tt('pool', M3, M3, r3(t1.v, 4), ALU.add)
            for c in range(4):
                oc_ = ot[:, c * 512:(c + 1) * 512]
                sq = wk()
                o.act(sq.v, oc_, AF.Square)
                pss = ps()
                o.mm(pss.v, onesbd, sq.v)
                rstd = wk()
                rtmp = wk()
                o.rsqrt(rstd.v, pss.v, 64.0 * EPS, rtmp.v)
                y = wk()
                o.stt('dve', y.v, oc_, dr[:, 23:24], rstd.v, ALU.mult, ALU.mult)
                yb_ = wkbf()
                o.tt('dve', yb_.v, y.v, ho[b][:, c * 512:(c + 1) * 512], ALU.mult)
                o.dma(YB[4 + c][:, tsl(i)], yb_.v)
        if STOP <= 3:
            return

        S.barrier()
        st['off'] = layer_mark
        Mst = sb(512, 'MstR')
        M3 = r3(Mst.v, 4)
        o.memset('dve', Mst.v, 0.0)
        Mstb = sb(512, 'MstRb', BF16)
        o.memset('pool', Mstb.v, 0.0)
        vpad = sb(1024, 'vpadR', BF16)
        nupad = sb(1024, 'nupad', BF16)
        o.memset('pool', vpad.v, 0.0)
        o.memset('pool', nupad.v, 0.0)
        sm64 = [sb(512, f'sm{i}', BF16) for i in range(14)]
        (ktm_, btm_, vtm_, RBt, Qt, RKt, Wsb, nut, Y0, Y1, Yt0, Yt1, Tt0, Tt1) = sm64
        rkr = [sb(4096, 'rkr0')] * 2
        rkk = [sb(2048, 'rkk0')] * 2
        rbb = [sb(2048, 'rbb0')] * 2
        rvv = [sb(2048, 'rvv0')] * 2
        bo = [sb(2048, 'bo0')] * 2
        rg = [sb(2048, 'rg0', BF16)] * 2
        rkrb = sb(4096, 'rkrb', BF16)
        rkkb = sb(2048, 'rkkb', BF16)
        rbbb = sb(2048, 'rbbb', BF16)
        rvvb = sb(2048, 'rvvb', BF16)
        ot = sb(2048, 'otR')
        hsl = [slice(h * 64, (h + 1) * 64) for h in range(8)]
        for i in range(NT):
            b = i % 2
            o.dma(rkr[b].v.re("p (c n a t) -> p c n a t", c=4, n=8, a=2), RKR.v.re("c p n a t -> p c n a t")[:, :, i * 8:(i + 1) * 8, :, :])
            o.dma(r3(rkk[b].v, 4), RKT.v.re("c p t -> p c t")[:, :, tsl(i)])
            o.dma(r3(rbb[b].v, 4), RBT.v.re("c p t -> p c t")[:, :, tsl(i)])
            o.dma(r3(rvv[b].v, 4), RVT.v.re("c p t -> p c t")[:, :, tsl(i)])
            o.dma(r3(bo[b].v, 4), RBO.v.re("c p t -> p c t")[:, :, tsl(i)])
            o.dma(r3(rg[b].v, 4), RG.v.re("c p t -> p c t")[:, :, tsl(i)])
            o.cp('pool', rkrb.v, rkr[b].v)
            o.cp('pool', rkkb.v, rkk[b].v)
            o.cp('pool', rbbb.v, rbb[b].v)
            o.cp('pool', rvvb.v, rvv[b].v)
            for ck in range(8):
                g = i * 8 + ck

                def cc(c):
                    return slice(c * 512 + ck * 64, c * 512 + (ck + 1) * 64)

                def KR(c, a0, a1):
                    off = (c * 8 + ck) * 128
                    return slice(off + a0 * 64, off + a1 * 64)
                pk_ = ps()
                pb_ = ps()
                pv_ = ps()
                for c in range(4):
                    o.mm(pk_[0:64, c * 128:(c + 1) * 128], rkkb[:, cc(c)], identb)
                    o.mm(pb_[0:64, c * 128:(c + 1) * 128], rbbb[:, cc(c)], identb)
                    o.mm(pv_[0:64, c * 128:(c + 1) * 128], rvvb[:, cc(c)], identb)
                o.cp('act', ktm_[0:64, :], pk_[0:64, :])
                o.cp('act', btm_[0:64, :], pb_[0:64, :])
                o.cp('act', vtm_[0:64, :], pv_[0:64, :])
                o.tt('dve', vpad[0:64, 0:512], pv_[0:64, :], me8[0:64, :], ALU.mult)
                o.tt('dve', vpad[0:64, 512:1024], pv_[0:64, :], mo8[0:64, :], ALU.mult)
                pA1 = [ps(), ps()]
                pA2 = [ps(), ps()]
                for h in range(8):
                    c = h // 2
                    rr = slice((h % 2) * 64, (h % 2) * 64 + 64)
                    o.mm(pA1[h % 2][0:64, c * 128:(c + 1) * 128], rbbb[rr, cc(c)], rkrb[rr, KR(c, 0, 2)])
                    o.mm(pA2[h % 2][0:64, c * 128:(c + 1) * 128], rkkb[rr, cc(c)], rkrb[rr, KR(c, 0, 2)])
                for bk in range(2):
                    v1 = pA1[bk][0:64, :].re("p (h x) -> p h x", h=4)
                    v2 = pA2[bk][0:64, :].re("p (h x) -> p h x", h=4)
                    o.tt('dve', par(Yt0[0:64, :], bk), v1[:, :, 0:64], r3(nsu4[0:64, :], 4), ALU.mult)
                    o.tt('dve', par(RBt[0:64, :], bk), v1[:, :, 64:128], r3(ui4[0:64, :], 4), ALU.mult)
                    o.tt('dve', par(Qt[0:64, :], bk), v2[:, :, 0:64], r3(su4[0:64, :], 4), ALU.mult)
                    o.tt('dve', par(RKt[0:64, :], bk), v2[:, :, 64:128], r3(ui4[0:64, :], 4), ALU.mult)
                pA3 = [ps(), ps()]
                for h in range(8):
                    c = h // 2
                    rr = slice((h % 2) * 64, (h % 2) * 64 + 64)
                    o.mm(pA3[h % 2][0:64, c * 64:(c + 1) * 64], rkrb[rr, KR(c, 0, 1)], rbbb[rr, cc(c)])
                for bk in range(2):
                    o.tt('dve', par(Y0[0:64, :], bk), r3(pA3[bk][0:64, 0:256], 4), r3(nsl8[0:64, 0:256], 4), ALU.mult)
                o.tt('pool', Tt0[0:64, :], Yt0[0:64, :], id8[0:64, :], ALU.add)
                Ya, Yta, Tta = Y0, Yt0, Tt0
                Yb, Ytb, Ttb = Y1, Yt1, Tt1
                for j in range(5):
                    pY = ps()
                    pYt = ps()
                    for h in range(8):
                        o.mm(pY[0:64, hsl[h]], Yta[0:64, hsl[h]], Ya[0:64, hsl[h]])
                        o.mm(pYt[0:64, hsl[h]], Ya[0:64, hsl[h]], Yta[0:64, hsl[h]])
                    o.cp('act', Yb[0:64, :], pY[0:64, :])
                    o.cp('act', Ytb[0:64, :], pYt[0:64, :])
                    pT = ps()
                    for h in range(8):
                        o.mm(pT[0:64, hsl[h]], Yb[0:64, hsl[h]], Tta[0:64, hsl[h]])
                    o.tt('dve', Ttb[0:64, :], pT[0:64, :], Tta[0:64, :], ALU.add)
                    Ya, Yta, Tta, Yb, Ytb, Ttb = Yb, Ytb, Ttb, Ya, Yta, Tta
                pW = ps()
                for c in range(4):
                    o.mm(pW[0:64, c * 128:(c + 1) * 128], rkrb[:, KR(c, 0, 1)], Mstb[:, c * 128:(c + 1) * 128], start=True, stop=False)
                    for h2 in range(2):
                        h = 2 * c + h2
                        o.mm(pW[0:64, hsl[h]], Qt[0:64, hsl[h]], vtm_[0:64, hsl[h]], start=False, stop=True)
                o.cp('act', Wsb[0:64, :], pW[0:64, :])
                pU = ps()
                for h in range(8):
                    o.mm(pU[0:64, hsl[h]], Tta[0:64, hsl[h]], Wsb[0:64, hsl[h]])
                o.ts('dve', nut[0:64, :], pU[0:64, :], -1.0, None, ALU.mult)
                o.stt('dve', nupad[0:64, 0:512], pU[0:64, :], -1.0, me8[0:64, :], ALU.mult, ALU.mult)
                o.stt('dve', nupad[0:64, 512:1024], pU[0:64, :], -1.0, mo8[0:64, :], ALU.mult, ALU.mult)
                pO = ps()
                for c in range(4):
                    oc_ = pO[:, c * 64:(c + 1) * 64]
                    o.mm(oc_, Mstb[:, c * 128:(c + 1) * 128], rkrb[:, KR(c, 1, 2)], start=True, stop=False)
                    for h2 in range(2):
                        h = 2 * c + h2
                        o.mm(oc_, vpad[0:64, h2 * 512 + c * 128:h2 * 512 + (c + 1) * 128], RKt[0:64, hsl[h]], start=False, stop=False)
                        o.mm(oc_, nupad[0:64, h2 * 512 + c * 128:h2 * 512 + (c + 1) * 128], RBt[0:64, hsl[h]], start=False, stop=(h2 == 1))
                o.cp('act', r3(ot.v, 4)[:, :, ck * 64:(ck + 1) * 64], r3(pO[:, 0:256], 4))
                pS = ps()
                for c in range(4):
                    o.mm(pS[:, c * 128:(c + 1) * 128], ktm_[0:64, c * 128:(c + 1) * 128], vtm_[0:64, c * 128:(c + 1) * 128], start=True, stop=False)
                    o.mm(pS[:, c * 128:(c + 1) * 128], btm_[0:64, c * 128:(c + 1) * 128], nut[0:64, c * 128:(c + 1) * 128], start=False, stop=True)
                t1 = wk()
                o.tt('dve', r3(t1.v, 4), r3(pS.v, 4), bd4, ALU.mult)
                o.tt('dve', t1.v, t1.v, Mst.v, ALU.add)
                o.tt('pool', M3, r3(t1.v, 4), gamv[:, :, g:g + 1].bc([128, 4, 128]), ALU.mult)
                o.cp('act', Mstb.v, Mst.v)
            for c in range(4):
                oc_ = ot[:, c * 512:(c + 1) * 512]
                pm = ps()
                o.mm(pm.v, onesbd, oc_)
                sq = wk()
                o.act(sq.v, oc_, AF.Square)
                pvv = ps()
                o.mm(pvv.v, onesbd, sq.v)
                mean = wk()
                o.ts('dve', mean.v, pm.v, 1.0 / 64, None, ALU.mult)
                cen = wk()
                o.tt('dve', cen.v, oc_, mean.v, ALU.subtract)
                msq = wk()
                o.tt('pool', msq.v, mean.v, mean.v, ALU.mult)
                var = wk()
                o.stt('dve', var.v, pvv.v, 1.0 / 64, msq.v, ALU.mult, ALU.subtract)
                rstd = wk()
                rtmp = wk()
                o.rsqrt(rstd.v, var.v, LN_EPS, rtmp.v)
                y = wk()
                o.tt('dve', y.v, cen.v, rstd.v, ALU.mult)
                o.ts('dve', y.v, y.v, cl[:, C_LNW + c:C_LNW + c + 1], cl[:, C_LNB + c:C_LNB + c + 1], ALU.mult, ALU.add)
                o.tt('pool', y.v, y.v, bo[b][:, c * 512:(c + 1) * 512], ALU.add)
                yb_ = wkbf()
                o.tt('dve', yb_.v, y.v, rg[b][:, c * 512:(c + 1) * 512], ALU.mult)
                o.dma(YB[8 + c][:, tsl(i)], yb_.v)
        if STOP <= 4:
            return

        S.barrier()
        st['off'] = layer_mark
        wbrb = sb(12 * 1024, 'wbrb', BF16)
        woutb = sb(8 * 1024, 'woutb', BF16)
        stg = [sb(2048, 'stg0')] * 2
        for q in range(6):
            s_ = stg[q % 2]
            o.dma(r3(s_.v, 2), wbr_in[l][:, 2 * q:2 * q + 2, :])
            o.cp('pool', wbrb[:, 2 * q * 1024:(2 * q + 2) * 1024], s_.v)
        for q in range(4):
            s_ = stg[q % 2]
            o.dma(r3(s_.v, 2), wout_in[l][:, 2 * q:2 * q + 2, :])
            o.cp('pool', woutb[:, 2 * q * 1024:(2 * q + 2) * 1024], s_.v)
        ybt = [sb(12 * 512, 'ybt0', BF16)] * 2
        gtt = [sb(3 * 512, f'gtt{i}', BF16) for i in range(2)]
        mg = sb(8 * 512, 'mg', BF16)
        xt = [sb(KC * 512, 'xt30')] * 2
        sqt = sb(KC * 512, 'sqt3', BF16)
        h2o = [sb(KC * 512, 'h2o0', BF16)] * 2
        gn = 0
        for i in range(NT):
            yb_ = ybt[i % 2]
            o.dma(r3(yb_.v, 12), YB.v.re("c p t -> p c t")[:, :, tsl(i)])
            x = xt[i % 2]
            o.dma(r3(x.v, KC), XT[:, :, tsl(i)])
            for oc in range(8):
                g_ = gtt[gn % 2]
                gn += 1
                o.dma(r3(g_.v, 3), GT.v.re("(b o) p t -> o p b t", b=3)[oc][:, :, tsl(i)])
                acc = wk()
                for b_ in range(3):
                    p = ps()
                    for kc in range(4):
                        o.mm(p.v, wbrb[:, (b_ * 4 + kc) * 1024 + oc * 128:(b_ * 4 + kc) * 1024 + (oc + 1) * 128],
                             yb_[:, (b_ * 4 + kc) * 512:(b_ * 4 + kc + 1) * 512], start=(kc == 0), stop=(kc == 3))
                    if b_ == 0:
                        o.tt('dve', acc.v, p.v, g_[:, 0:512], ALU.mult)
                    else:
                        t = wk()
                        o.tt('dve', t.v, p.v, g_[:, b_ * 512:(b_ + 1) * 512], ALU.mult)
                        if b_ == 1:
                            o.tt('pool', acc.v, acc.v, t.v, ALU.add)
                        else:
                            o.tt('pool', mg[:, oc * 512:(oc + 1) * 512], acc.v, t.v, ALU.add)
            for oc in range(8):
                p = ps()
                for kc in range(KC):
                    o.mm(p.v, woutb[:, kc * 1024 + oc * 128:kc * 1024 + (oc + 1) * 128], mg[:, kc * 512:(kc + 1) * 512], start=(kc == 0), stop=(kc == KC - 1))
                xs_ = x[:, oc * 512:(oc + 1) * 512]
                o.stt('dve', xs_, p.v, md[:, 16 + oc:17 + oc], xs_, ALU.mult, ALU.add)
            o.dma(XT[:, :, tsl(i)], r3(x.v, KC))
            h2t = h2o[i % 2]
            norm_tile(x, 8, 24, h2t, sqt)
            o.dma(H2.v.re("k p t -> p k t")[:, :, tsl(i)], r3(h2t.v, KC))
        if STOP <= 5:
            return

        S.barrier()
        st['off'] = layer_mark
        hts = [sb(KC * 512, f'h2_{i}', BF16) for i in range(NT)]
        for i in range(NT):
            o.dma(r3(hts[i].v, KC), H2.v.re("k p t -> p k t")[:, :, tsl(i)])
        wst2 = [sb(2048, f'wst2_{i}') for i in range(2)]
        wpb = [sb(2048, f'wpb{i}', BF16) for i in range(2)]
        rawv = sb(514, 'rawv')
        rawg = sb(514, 'rawg')
        for j in range(NFF):
            s_ = wst2[j % 2]
            o.dma(s_.v, wup_in[l][j])
            wb_ = wpb[j % 2]
            o.cp('pool', wb_.v, s_.v)
            o.memset('pool', rawv[:, 0:2], 0.0)
            o.memset('pool', rawg[:, 0:2], 0.0)
            for i in range(NT):
                pv = ps()
                pg = ps()
                for kc in range(KC):
                    o.mm(pv.v, wb_[:, kc * 256:kc * 256 + 128], hts[i][:, kc * 512:(kc + 1) * 512], start=(kc == 0), stop=(kc == KC - 1))
                for kc in range(KC):
                    o.mm(pg.v, wb_[:, kc * 256 + 128:kc * 256 + 256], hts[i][:, kc * 512:(kc + 1) * 512], start=(kc == 0), stop=(kc == KC - 1))
                res = []
                for (p, raw, cc_) in ((pv, rawv, j), (pg, rawg, NFF + j)):
                    o.cp('act', raw[:, 2:514], p.v)
                    cv = wk()
                    o.ts('dve', cv.v, raw[:, 2:514], cl[:, C_CW + 88 + cc_:C_CW + 89 + cc_], cl[:, C_CB + cc_:C_CB + cc_ + 1], ALU.mult, ALU.add)
                    o.stt('dve', cv.v, raw[:, 1:513], cl[:, C_CW + 44 + cc_:C_CW + 45 + cc_], cv.v, ALU.mult, ALU.add)
                    o.stt('dve', cv.v, raw[:, 0:512], cl[:, C_CW + cc_:C_CW + cc_ + 1], cv.v, ALU.mult, ALU.add)
                    o.cp('act', raw[:, 0:2], raw[:, 512:514])
                    res.append(cv)
                sg = wk()
                o.act(sg.v, res[1].v, AF.Silu)
                ab = wkbf()
                o.tt('pool', ab.v, sg.v, res[0].v, ALU.mult)
                o.dma(ACTT[j][:, tsl(i)], ab.v)
        if STOP <= 6:
            return

        S.barrier()
        st['off'] = layer_mark
        wdnb = sb(NFF * 1024, 'wdnb', BF16)
        stg = [sb(2048, 'stgd0')] * 2
        for q in range(NFF // 2):
            s_ = stg[q % 2]
            o.dma(r3(s_.v, 2), wdn_in[l][:, 2 * q:2 * q + 2, :])
            o.cp('pool', wdnb[:, 2 * q * 1024:(2 * q + 2) * 1024], s_.v)
        att = [sb(NFF * 512, f'att{i}', BF16) for i in range(2)]
        xt = [sb(KC * 512, 'xt40')] * 2
        for i in range(NT):
            a_ = att[i % 2]
            o.dma(r3(a_.v, NFF), ACTT.v.re("c p t -> p c t")[:, :, tsl(i)])
            x = xt[i % 2]
            o.dma(r3(x.v, KC), XT[:, :, tsl(i)])
            for oc in range(8):
                p = ps()
                for kc in range(NFF):
                    o.mm(p.v, wdnb[:, kc * 1024 + oc * 128:kc * 1024 + (oc + 1) * 128], a_[:, kc * 512:(kc + 1) * 512], start=(kc == 0), stop=(kc == NFF - 1))
                xs_ = x[:, oc * 512:(oc + 1) * 512]
                o.stt('dve', xs_, p.v, md[:, 40 + oc:41 + oc], xs_, ALU.mult, ALU.add)
            o.dma(XT[:, :, tsl(i)], r3(x.v, KC))

    for l in range(L):
        emit_layer(l)

    S.barrier()
    st['off'] = persist_mark
    xl = [sb(KC * 512, f'xl{i}') for i in range(2)]
    yo = [sb(D, f'yo{i}') for i in range(2)]
    n = 0
    for i in range(NT):
        xs = xl[i % 2]
        o.dma(xs.v.re("p (k t) -> p k t", k=KC), XT[:, :, i * 512:(i + 1) * 512])
        for q in range(4):
            yy = yo[n % 2]
            n += 1
            for h in range(2):
                p = ps()
                for c in range(4):
                    kc = h * 4 + c
                    o.tr(p[:, c * 128:(c + 1) * 128], xs[:, kc * 512 + q * 128: kc * 512 + (q + 1) * 128], ident)
                o.cp('act' if h == 0 else 'dve', yy[:, h * 512:(h + 1) * 512], p.v)
            t0 = i * 512 + q * 128
            o.dma(y_out[t0:t0 + 128, :], yy.v)
    S.barrier()

    with contextlib.ExitStack() as es:
        sems = {}
        for e in COMPUTE:
            sems[e] = es.enter_context(nc.semaphore("s_" + e))
        for j in range(NDSEM):
            sems[('d', j)] = es.enter_context(nc.semaphore(f"d{j}"))
        block = es.enter_context(nc.Block())
        run = S.emit(sems)
        block.sync(lambda e: run('sp', e))
        block.tensor(lambda e: run('pe', e))
        block.scalar(lambda e: run('act', e))
        block.vector(lambda e: run('dve', e))
        block.gpsimd(lambda e: run('pool', e))
    return nc


def _fm(v):
    return np.ascontiguousarray(np.asarray(v, np.float32).reshape(-1, 128).T)


def make_consts():
    c = np.zeros((128, K_W), np.float32)
    c[:, K_ID:K_ID + 128] = np.eye(128)
    c[0:64, K_OBD:K_OBD + 64] = 1.0
    c[64:128, K_OBD + 64:K_OBD + 128] = 1.0
    c[:, K_ONE:K_ONE + 128] = 1.0
    r = np.arange(64)
    su = (r[:, None] < r[None, :]).astype(np.float32)
    ui = (r[:, None] <= r[None, :]).astype(np.float32)
    sl = (r[:, None] > r[None, :]).astype(np.float32)
    c[0:64, K_BD:K_BD + 64] = 1.0
    c[64:128, K_BD + 64:K_BD + 128] = 1.0
    c[0:64, K_UI8:K_UI8 + 512] = np.tile(ui, (1, 8))
    c[0:64, K_NSL8:K_NSL8 + 512] = -np.tile(sl, (1, 8))
    c[0:64, K_ID8:K_ID8 + 512] = np.tile(np.eye(64, dtype=np.float32), (1, 8))
    c[0:64, K_NSU4:K_NSU4 + 256] = -np.tile(su, (1, 4))
    c[0:64, K_SU4:K_SU4 + 256] = np.tile(su, (1, 4))
    me = np.concatenate([np.ones((64, 64), np.float32), np.zeros((64, 64), np.float32)], axis=1)
    c[0:64, K_ME:K_ME + 512] = np.tile(me, (1, 4))
    c[0:64, K_MO:K_MO + 512] = np.tile(1.0 - me, (1, 4))
    return c


def make_consts_b():
    c = np.zeros((128, KB_W), np.float32)
    c[:, KB_ID:KB_ID + 128] = np.eye(128)
    c[:, KB_ONE:KB_ONE + 128] = 1.0
    r = np.arange(128)
    c[:, KB_TRI:KB_TRI + 128] = np.where(r[:, None] <= r[None, :], 0.0, -98304.0)
    for h in range(8):
        c[h, KB_SEL + h * 128:KB_SEL + (h + 1) * 128] = 1.0
    return c


def prep_shared(inp, L):
    f = lambda k: np.asarray(inp[k], np.float32)
    cols = np.zeros((L, 128, NCOLS), np.float32)
    win = np.zeros((L, NCH, 128, KC * 128), np.float32)
    lora = np.zeros((L, 128, 512), np.float32)
    g2 = np.zeros((L, 128, 512), np.float32)
    v2 = np.zeros((L, 64, 512), np.float32)
    for l in range(L):
        cl = cols[l]
        cl[:, C_N1G:C_N1G + 8] = _fm(f('norm1_g')[l])
        cl[:, C_N2G:C_N2G + 8] = _fm(f('norm2_g')[l])
        cl[0:8, C_FBF] = f('fox_b_f')[l]
        cl[:, C_QG] = np.tile(f('fox_q_gain')[l], 2)
        cl[:, C_KG] = np.tile(f('fox_k_gain')[l], 2)
        cl[:, C_OG] = np.tile(f('hgrn_o_gain')[l], 2)
        cl[:, C_MU:C_MU + 14] = _fm(f('rwkv_mu')[l])
        cl[:, C_W0:C_W0 + 4] = _fm(f('rwkv_w0')[l])
        cl[:, C_A0:C_A0 + 4] = _fm(f('rwkv_a0')[l])
        cl[:, C_KK:C_KK + 4] = _fm(f('rwkv_k_k')[l])
        cl[:, C_KA:C_KA + 4] = _fm(f('rwkv_k_a')[l])
        cl[:, C_RK:C_RK + 4] = _fm(f('rwkv_r_k')[l].reshape(-1))
        cl[:, C_LNW:C_LNW + 4] = _fm(f('rwkv_ln_w')[l])
        cl[:, C_LNB:C_LNB + 4] = _fm(f('rwkv_ln_b')[l])
        if l > 0:
            cl[:, C_V0:C_V0 + 4] = _fm(f('rwkv_v0')[l - 1])
            cl[32:64, C_VMU] = f('rwkv_vres_mu')[l - 1]
            v2[l, 32:64] = f('rwkv_v2')[l - 1]
        cw = f('conv_w')[l]
        for j in range(3):
            cl[:, C_CW + j * 44:C_CW + (j + 1) * 44] = _fm(cw[j])
        cl[:, C_CB:C_CB + 44] = _fm(f('conv_b')[l])
        W = f('w_in')[l]
        main = np.concatenate([W[:, 0:1536], W[:, 1544:1544 + 2048 + 1792 + 3072]], axis=1)
        misc = np.zeros((D, 128), np.float32)
        misc[:, 0:8] = W[:, 1536:1544]
        if l > 0:
            misc[:, 32:64] = f('rwkv_vres_down')[l - 1]
        allc = np.concatenate([main, misc], axis=1)
        win[l] = allc.reshape(KC, 128, NCH, 128).transpose(2, 1, 0, 3).reshape(NCH, 128, KC * 128)
        lora[l, 0:64] = f('rwkv_w2')[l]
        lora[l, 64:128] = f('rwkv_a2')[l]
        g2[l] = f('rwkv_g2')[l]
    sh = {}
    sh['cols'] = cols
    sh['lbT'] = np.ascontiguousarray(f('hgrn_lb')[:4].reshape(-1, 4, 128).transpose(2, 0, 1).reshape(128, -1))
    if sh['lbT'].shape[1] < 16:
        sh['lbT'] = np.concatenate([sh['lbT'], np.zeros((128, 16 - sh['lbT'].shape[1]), np.float32)], axis=1)
    sh['wada'] = np.ascontiguousarray(f('w_ada')[:L].reshape(L, KC, 128, 6 * D).transpose(0, 2, 1, 3))
    sh['bada'] = np.ascontiguousarray(f('b_ada')[:L].reshape(L, 1, 6 * D))
    sh['win'] = win
    sh['lora'] = lora
    sh['g2'] = g2
    sh['v2'] = v2
    sh['wbr'] = np.ascontiguousarray(f('w_branch')[:L].reshape(L, 3, 4, 128, D).transpose(0, 3, 1, 2, 4).reshape(L, 128, 12, D))
    sh['wout'] = np.ascontiguousarray(f('w_out')[:L].reshape(L, KC, 128, D).transpose(0, 2, 1, 3))
    wu = f('w_up')[:L].reshape(L, KC, 128, 2, NFF, 128)
    sh['wup'] = np.ascontiguousarray(wu.transpose(0, 4, 2, 1, 3, 5).reshape(L, NFF, 128, KC * 256))
    sh['wdn'] = np.ascontiguousarray(f('w_down')[:L].reshape(L, NFF, 128, D).transpose(0, 2, 1, 3))
    sh['cst'] = make_consts()
    sh['cstb'] = make_consts_b()
    return sh


_CACHE = {}


def run(inp, S_LEN, L, dbg=(), STOP=99):
    key = (S_LEN, L, tuple(dbg), STOP)
    if key not in _CACHE:
        _CACHE[key] = build(S_LEN, L, dbg, STOP)
    nc = _CACHE[key]
    sh = prep_shared(inp, L)
    x = np.asarray(inp['x'], np.float32)
    c = np.asarray(inp['c'], np.float32)
    B = x.shape[0]
    in_maps = []
    for b in range(B):
        m = dict(sh)
        m['x'] = np.ascontiguousarray(x[b])
        m['cT'] = _fm(c[b])
        in_maps.append(m)
    res = run_bass_kernel_spmd(nc, in_maps, core_ids=list(range(B)))
    return res.results


def kernel(**inputs):
    res = run(inputs, 4096, 4)
    return np.stack([r['y'] for r in res]).astype(np.float32)
```

```python
import contextlib
import os
HGCUT = int(os.environ.get('HGCUT', '9'))
HGSUB = int(os.environ.get('HGSUB', '9'))
import numpy as np
import ml_dtypes
import concourse.bass as bass
import concourse.mybir as mybir
from concourse.bass_utils import run_bass_kernel_spmd

F32 = mybir.dt.float32
BF16 = mybir.dt.bfloat16
AF = mybir.ActivationFunctionType
ALU = mybir.AluOpType

NDSEM = 24
COMPUTE = ('pe', 'act', 'dve', 'pool')


class T:
    __slots__ = ('ap', 'name', 'w', 'r', 'ut', 'xr')

    def __init__(self, ap, name='', ut=False, xr=False):
        self.ap = ap
        self.name = name
        self.w = None
        self.r = {}
        self.ut = ut
        self.xr = xr

    def __getitem__(self, k):
        return Vw(self, self.ap[k])

    @property
    def v(self):
        return Vw(self, self.ap)


class Vw:
    __slots__ = ('t', 'ap')

    def __init__(self, t, ap):
        self.t = t
        self.ap = ap

    def __getitem__(self, k):
        return Vw(self.t, self.ap[k])

    def re(self, pat, **kw):
        return Vw(self.t, self.ap.rearrange(pat, **kw))

    def bc(self, shape):
        return Vw(self.t, self.ap.to_broadcast(list(shape)))

    def cast(self, dt):
        return Vw(self.t, self.ap.bitcast(dt))

    def us(self, axis):
        return Vw(self.t, self.ap.unsqueeze(axis))


class Ins:
    __slots__ = ('eng', 'idx', 'fn', 'waits', 'signal', 'vc', 'key', 'val', 'isdma')


class Sched:
    def __init__(self):
        self.q = {e: [] for e in COMPUTE + ('sp',)}
        self.clock = {e: {} for e in COMPUTE + ('sp',)}
        self.dcount = [0] * NDSEM
        self.dlast = [None] * NDSEM
        self.dnext = 0

    def _add(self, eng, fn, reads, writes, isdma):
        ins = Ins()
        ins.eng = eng
        ins.idx = len(self.q[eng])
        ins.fn = fn
        ins.waits = []
        ins.signal = False
        ins.isdma = isdma
        clock = self.clock[eng]
        deps = []
        if isdma:
            j = self.dnext
            self.dnext = (j + 1) % NDSEM
            self.dcount[j] += 1
            ins.key = ('d', j)
            ins.val = 16 * self.dcount[j]
            if self.dlast[j] is not None:
                deps.append((self.dlast[j], 'sem'))
            self.dlast[j] = ins
            ins.signal = True
        else:
            ins.key = eng
            ins.val = ins.idx + 1
        for t in reads:
            if t.w is not None:
                deps.append((t.w, 'raw'))
            if t.xr:
                for r in t.r.values():
                    if r.eng != eng:
                        deps.append((r, 'rar'))
        for t in writes:
            if t.w is not None:
                deps.append((t.w, 'waw'))
            for r in t.r.values():
                deps.append((r, 'war'))
        for p, kind in deps:
            if (not p.isdma) and (not isdma) and p.eng == eng:
                if eng == 'pe' or kind != 'raw':
                    continue
            if clock.get(p.key, 0) >= p.val:
                continue
            p.signal = True
            ins.waits.append(p)
            for k, v in p.vc.items():
                if clock.get(k, 0) < v:
                    clock[k] = v
        vc = dict(clock)
        vc[ins.key] = ins.val
        ins.vc = vc
        for t in reads:
            t.r[ins.key] = ins
        for t in writes:
            t.w = ins
            t.r = {}
        self.q[eng].append(ins)
        return ins

    def op(self, eng, fn, reads=(), writes=()):
        return self._add(eng, fn, reads, writes, False)

    def dma(self, fn, reads=(), writes=(), queue='sp'):
        return self._add(queue, fn, reads, writes, True)

    def barrier(self):
        lasts = []
        for e in COMPUTE:
            for ins in reversed(self.q[e]):
                if ins.fn is not None and not ins.isdma:
                    lasts.append(ins)
                    break
        for j in range(NDSEM):
            if self.dlast[j] is not None:
                lasts.append(self.dlast[j])
        for e in COMPUTE + ('sp',):
            ins = Ins()
            ins.eng = e
            ins.idx = len(self.q[e])
            ins.fn = None
            ins.waits = []
            ins.signal = False
            ins.isdma = False
            ins.key = e
            ins.val = ins.idx
            clock = self.clock[e]
            for p in lasts:
                if (not p.isdma) and p.eng == e:
                    continue
                if clock.get(p.key, 0) >= p.val:
                    continue
                p.signal = True
                ins.waits.append(p)
                for k, v in p.vc.items():
                    if clock.get(k, 0) < v:
                        clock[k] = v
            ins.vc = dict(clock)
            self.q[e].append(ins)

    def emit(self, sems):
        sigcount = {}
        for e in COMPUTE:
            c = 0
            for ins in self.q[e]:
                if ins.fn is not None and ins.signal and not ins.isdma:
                    c += 1
                    sigcount[id(ins)] = c

        def run(eng_name, e):
            for ins in self.q[eng_name]:
                w = {}
                for p in ins.waits:
                    v = p.val if p.isdma else sigcount[id(p)]
                    if w.get(p.key, 0) < v:
                        w[p.key] = v
                for k, v in w.items():
                    e.wait_ge(sems[k], v)
                if ins.fn is None:
                    continue
                r = ins.fn(e)
                if ins.isdma:
                    r.then_inc(sems[ins.key], 16)
                elif ins.signal:
                    r.then_inc(sems[ins.key], 1)
        return run


def _ap(x):
    return x.ap if isinstance(x, Vw) else x


def _ts(*xs):
    return [x.t for x in xs if isinstance(x, Vw) and not x.t.ut]


class Ops:
    def __init__(self, S):
        self.S = S

    def tt(self, eng, out, a, b, op):
        self.S.op(eng, lambda e: e.tensor_tensor(out=out.ap, in0=a.ap, in1=b.ap, op=op), _ts(a, b), _ts(out))

    def ts(self, eng, out, a, s1, s2, op0, op1=None):
        if op1 is None:
            self.S.op(eng, lambda e: e.tensor_scalar(out=out.ap, in0=a.ap, scalar1=_ap(s1), scalar2=None, op0=op0),
                      _ts(a, s1), _ts(out))
        else:
            self.S.op(eng, lambda e: e.tensor_scalar(out=out.ap, in0=a.ap, scalar1=_ap(s1), scalar2=_ap(s2), op0=op0, op1=op1),
                      _ts(a, s1, s2), _ts(out))

    def stt(self, eng, out, a, s, b, op0, op1):
        self.S.op(eng, lambda e: e.scalar_tensor_tensor(out=out.ap, in0=a.ap, scalar=_ap(s), in1=b.ap, op0=op0, op1=op1),
                  _ts(a, s, b), _ts(out))

    def cp(self, eng, out, a):
        if eng == 'act':
            self.S.op(eng, lambda e: e.copy(out=out.ap, in_=a.ap), _ts(a), _ts(out))
        else:
            self.S.op(eng, lambda e: e.tensor_copy(out=out.ap, in_=a.ap), _ts(a), _ts(out))

    def act(self, out, a, func, bias=None, scale=None):
        kw = {}
        if bias is not None:
            kw['bias'] = _ap(bias)
        if scale is not None:
            kw['scale'] = _ap(scale)
        self.S.op('act', lambda e: e.activation(out=out.ap, in_=a.ap, func=func, **kw), _ts(a, bias, scale), _ts(out))

    def recip(self, out, a):
        self.S.op('dve', lambda e: e.reciprocal(out=out.ap, in_=a.ap), _ts(a), _ts(out))

    def rsqrt(self, out, a, add, tmp):
        self.act(tmp, a, AF.Sqrt, bias=add)
        self.recip(out, tmp)

    def scan(self, out, d0, d1, init, op0=ALU.mult, op1=ALU.add):
        self.S.op('dve', lambda e: e.tensor_tensor_scan(out=out.ap, data0=d0.ap, data1=d1.ap, initial=_ap(init), op0=op0, op1=op1),
                  _ts(d0, d1, init), _ts(out))

    def memset(self, eng, out, val):
        self.S.op(eng, lambda e: e.memset(out.ap, val), [], _ts(out))

    def mm(self, out, lhsT, rhs, start=True, stop=True):
        self.S.op('pe', lambda e: e.matmul(out.ap, lhsT.ap, rhs.ap, start=start, stop=stop), _ts(lhsT, rhs), _ts(out))

    def tr(self, out, a, ident):
        self.S.op('pe', lambda e: e.transpose(out=out.ap, in_=a.ap, identity=ident.ap), _ts(a, ident), _ts(out))

    def dma(self, out, a, queue='sp'):
        self.S.dma(lambda e: e.dma_start(out=out.ap, in_=a.ap), _ts(a), _ts(out), queue=queue)


D = 1024
KC = 8
W_MIX = 512
DFF = 2816
NFF = 22
EPS = 1e-6
LN_EPS = 64e-5
NCOLS = 243
C_N1G, C_N2G, C_FBF, C_QG, C_KG, C_OG, C_MU, C_W0, C_A0, C_KK, C_KA, C_RK, C_LNW, C_LNB, C_V0, C_VMU, C_CW, C_CB = \
    0, 8, 16, 17, 18, 19, 20, 34, 38, 42, 46, 50, 54, 58, 62, 66, 67, 199
J_FQ, J_FK, J_FV, J_HQ, J_HF, J_HI, J_HO, J_RR, J_RK, J_RV, J_LORA, J_GLO, J_GATE, J_MISC = 0, 4, 8, 12, 16, 20, 24, 28, 32, 36, 40, 41, 42, 66
NCH = 67
K_ID, K_OBD, K_ONE, K_BD, K_UI8, K_NSL8, K_ID8, K_NSU4, K_SU4, K_ME, K_MO, K_W = 0, 128, 256, 384, 512, 1024, 1536, 2048, 2304, 2560, 3072, 3584
KB_ID, KB_ONE, KB_TRI, KB_SEL, KB_W = 0, 128, 256, 384, 1408


def build(S_LEN, L, dbg=(), STOP=99):
    NT = S_LEN // 512
    NCK = S_LEN // 64
    nc = bass.Bass("TRN2", target_bir_lowering=False)
    S = Sched()
    o = Ops(S)

    def dram(name, shape, dt, kind="Internal"):
        if name in dbg:
            kind = "ExternalOutput"
        return T(nc.dram_tensor(name, list(shape), dt, kind=kind).ap(), name, ut=True)

    x_in = dram("x", [S_LEN, D], F32, "ExternalInput")
    cT_in = dram("cT", [128, KC], F32, "ExternalInput")
    cols_in = dram("cols", [L, 128, NCOLS], F32, "ExternalInput")
    lbT_in = dram("lbT", [128, 16], F32, "ExternalInput")
    wada_in = dram("wada", [L, 128, KC, 6 * D], F32, "ExternalInput")
    bada_in = dram("bada", [L, 1, 6 * D], F32, "ExternalInput")
    win_in = dram("win", [L, NCH, 128, KC * 128], F32, "ExternalInput")
    lora_in = dram("lora", [L, 128, 512], F32, "ExternalInput")
    g2_in = dram("g2", [L, 128, 512], F32, "ExternalInput")
    v2_in = dram("v2", [L, 64, 512], F32, "ExternalInput")
    wbr_in = dram("wbr", [L, 128, 12, D], F32, "ExternalInput")
    wout_in = dram("wout", [L, 128, KC, D], F32, "ExternalInput")
    wup_in = dram("wup", [L, NFF, 128, KC * 256], F32, "ExternalInput")
    wdn_in = dram("wdn", [L, 128, NFF, D], F32, "ExternalInput")
    cst_in = dram("cst", [128, K_W], F32, "ExternalInput")
    cstb_in = dram("cstb", [128, KB_W], F32, "ExternalInput")
    y_out = dram("y", [S_LEN, D], F32, "ExternalOutput")

    XT = dram("XT", [128, KC, S_LEN], F32)
    QKT = dram("QKT", [8, 128, S_LEN], BF16)
    FVT = dram("FVT", [4, 128, S_LEN], BF16)
    GT = dram("GT", [24, 128, S_LEN], BF16)
    HQ = dram("HQ", [4, 128, S_LEN], F32)
    HK = dram("HK", [4, 128, S_LEN], F32)
    HI = dram("HI", [4, 128, S_LEN], F32)
    HO = dram("HO", [4, 128, S_LEN], BF16)
    RKR = dram("RKR", [4, 128, NCK, 2, 64], F32)
    RKT = dram("RKT", [4, 128, S_LEN], F32)
    RBT = dram("RBT", [4, 128, S_LEN], F32)
    RVT = dram("RVT", [4, 128, S_LEN], F32)
    RBO = dram("RBO", [4, 128, S_LEN], F32)
    RG = dram("RG", [4, 128, S_LEN], BF16)
    FTD = dram("FTD", [8, S_LEN], F32)
    VF = dram("VF", [4, 128, S_LEN], F32)
    YB = dram("YB", [12, 128, S_LEN], BF16)
    H2 = dram("H2", [KC, 128, S_LEN], BF16)
    ACTT = dram("ACTT", [NFF, 128, S_LEN], BF16)

    arena = nc.alloc_sbuf_tensor("arena", [128, 53200], F32).ap()
    st = {'off': 0, 'mark': 0}

    def sb(n, name='', dt=F32):
        words = n if dt == F32 else (n + 1) // 2
        assert st['off'] + words <= 53200, ("SBUF arena overflow", name, st['off'], words)
        a = arena[:, st['off']: st['off'] + words]
        st['off'] += words
        assert st['off'] <= 53200, ("SBUF arena overflow", name, st['off'])
        if dt != F32:
            a = a.bitcast(dt)
        return T(a, name)

    psb = [T(nc.alloc_psum_tensor(f"ps{i}", [128, 512], F32).ap(), f"ps{i}", xr=True) for i in range(8)]
    pst = {'i': 0}

    pst['n'] = 8

    def ps():
        p = psb[pst['i'] % pst['n']]
        pst['i'] += 1
        return p

    cst = sb(K_W, 'cst')
    o.dma(cst.v, cst_in.v)
    ident = cst[:, K_ID:K_ID + 128]
    onesbd = cst[:, K_OBD:K_OBD + 128]
    ones = cst[:, K_ONE:K_ONE + 128]
    bd = cst[:, K_BD:K_BD + 128]
    ui8 = cst[:, K_UI8:K_UI8 + 512]
    ui4 = cst[:, K_UI8:K_UI8 + 256]
    nsl8 = cst[:, K_NSL8:K_NSL8 + 512]
    id8 = cst[:, K_ID8:K_ID8 + 512]
    nsu4 = cst[:, K_NSU4:K_NSU4 + 256]
    su4 = cst[:, K_SU4:K_SU4 + 256]
    me8 = cst[:, K_ME:K_ME + 512]
    mo8 = cst[:, K_MO:K_MO + 512]
    cstb = sb(KB_W, 'cstb', BF16)
    identb = cstb[:, KB_ID:KB_ID + 128]
    onesb = cstb[:, KB_ONE:KB_ONE + 128]
    trib = cstb[:, KB_TRI:KB_TRI + 128]
    selb = cstb[:, KB_SEL:KB_SEL + 1024]
    ones512 = sb(512, 'ones512')
    o.memset('dve', ones512.v, 1.0)
    colsT = [sb(NCOLS, f'cols{l}') for l in range(L)]
    for l in range(L):
        o.dma(colsT[l].v, cols_in[l])
    modT = [sb(48, f'mod{l}') for l in range(L)]
    derT = [sb(64, f'der{l}') for l in range(L)]
    lbT = sb(16, 'lbT')
    lowT = sb(16, 'lowT')
    o.dma(lbT.v, lbT_in.v)
    ex = sb(16, 'lbexp')
    sm = sb(4, 'lbsum')
    rs = sb(4, 'lbrs')
    oml = sb(16, 'oml')
    cT = sb(KC, 'cT')
    condT = sb(KC, 'condT')
    persist_mark = st['off']

    st['mark'] = st['off']
    cbt = sb(KB_W, 'cbtmp')
    o.dma(cbt.v, cstb_in.v)
    o.cp('dve', cstb.v, cbt.v)
    xin = [sb(D, f'xin{i}') for i in range(2)]
    xst = [sb(KC * 512, f'xst{i}') for i in range(2)]
    for i in range(NT):
        xs = xst[i % 2]
        for q in range(4):
            xi = xin[q % 2]
            t0 = i * 512 + q * 128
            o.dma(xi.v, x_in[t0:t0 + 128, :])
            for h in range(2):
                p = ps()
                for c in range(4):
                    kc = h * 4 + c
                    o.tr(p[:, c * 128:(c + 1) * 128], xi[:, kc * 128:(kc + 1) * 128], ident)
                dst = xs.v.re("p (k t) -> p k t", k=KC)[:, h * 4:(h + 1) * 4, q * 128:(q + 1) * 128]
                o.cp('act' if h == 0 else 'dve', dst, p.v.re("p (c t) -> p c t", c=4))
        o.dma(XT[:, :, i * 512:(i + 1) * 512], xs.v.re("p (k t) -> p k t", k=KC))

    o.dma(cT.v, cT_in.v)
    o.act(condT.v, cT.v, AF.Silu)
    wst = [sb(KC * 512, f'wst{i}') for i in range(2)]
    bada = sb(6 * D, 'bada')
    modrow = sb(6 * D, 'modrow')
    for l in range(L):
        o.dma(bada[0:1, :], bada_in[l])
        for blk in range(12):
            w = wst[blk % 2]
            o.dma(w.v.re("p (k n) -> p k n", k=KC), wada_in[l][:, :, blk * 512:(blk + 1) * 512])
            p = ps()
            for kc in range(KC):
                o.mm(p[0:1, :], condT[:, kc:kc + 1], w[:, kc * 512:(kc + 1) * 512], start=(kc == 0), stop=(kc == KC - 1))
            o.tt('dve', modrow[0:1, blk * 512:(blk + 1) * 512], p[0:1, :], bada[0:1, blk * 512:(blk + 1) * 512], ALU.add)
        p = ps()
        for j in range(48):
            o.mm(p[:, j:j + 1], modrow[0:1, j * 128:(j + 1) * 128], ones[0:1, 0:1])
        o.cp('dve', modT[l].v, p[:, 0:48])
        dr = derT[l]
        cl = colsT[l]
        o.ts('dve', dr[:, 0:8], modT[l][:, 8:16], 1.0, 32.0, ALU.add, ALU.mult)
        o.tt('dve', dr[:, 0:8], dr[:, 0:8], cl[:, C_N1G:C_N1G + 8], ALU.mult)
        o.ts('dve', dr[:, 8:16], modT[l][:, 32:40], 1.0, 32.0, ALU.add, ALU.mult)
        o.tt('dve', dr[:, 8:16], dr[:, 8:16], cl[:, C_N2G:C_N2G + 8], ALU.mult)
        o.ts('dve', dr[:, 17:21], cl[:, C_KA:C_KA + 4], -1.0, 1.0, ALU.mult, ALU.add)
        o.ts('dve', dr[:, 21:22], cl[:, C_QG:C_QG + 1], 8.0, None, ALU.mult)
        o.ts('dve', dr[:, 22:23], cl[:, C_KG:C_KG + 1], 8.0, None, ALU.mult)
        o.ts('dve', dr[:, 23:24], cl[:, C_OG:C_OG + 1], 8.0, None, ALU.mult)
        o.ts('dve', dr[:, 24:28], cl[:, C_MU + 0:C_MU + 4], -1.0, None, ALU.mult)
    o.act(ex.v, lbT.v, AF.Exp)
    o.cp('dve', sm.v, ex[:, 0:4])
    for l in range(1, L):
        o.tt('dve', sm.v, sm.v, ex[:, l * 4:(l + 1) * 4], ALU.add)
    o.recip(rs.v, sm.v)
    for l in range(L):
        o.tt('dve', ex[:, l * 4:(l + 1) * 4], ex[:, l * 4:(l + 1) * 4], rs.v, ALU.mult)
    o.memset('dve', lowT[:, 0:4], 0.0)
    for l in range(1, L):
        o.tt('dve', lowT[:, l * 4:(l + 1) * 4], lowT[:, (l - 1) * 4:l * 4], ex[:, l * 4:(l + 1) * 4], ALU.add)
    o.ts('dve', oml.v, lowT.v, -1.0, 1.0, ALU.mult, ALU.add)
    ctx = dict(nc=nc, S=S, o=o, sb=sb, ps=ps, st=st, L=L, NT=NT, NCK=NCK, S_LEN=S_LEN)
    NEG_C0 = -0.6065306597126334
    WK_N = 16
    sel_in = None

    def emit_layer(l):
        S.barrier()
        st['off'] = persist_mark
        cl = colsT[l]
        dr = derT[l]
        md = modT[l]
        Fcar = sb(2, 'Fcar')
        hsc = sb(3 * 4 * NCK, 'hsc')
        hscv = hsc.v.re("p (a c n) -> p a c n", a=3, c=4)
        gam = sb(4 * NCK, 'gam')
        gamv = gam.v.re("p (c n) -> p c n", c=4)
        ded = [sb(512, f'ded{i}') for i in range(7)]
        wkp = [sb(512, f'wk{i}') for i in range(WK_N)]
        wki = {'i': 0}

        def wk():
            t = wkp[wki['i'] % WK_N]
            wki['i'] += 1
            return t
        wkb = [sb(512, f'wkb{i}', BF16) for i in range(6)]
        wkbi = {'i': 0}

        def wkbf():
            t = wkb[wkbi['i'] % 6]
            wkbi['i'] += 1
            return t
        layer_mark = st['off']

        hts = [sb(KC * 512, f'ht{i}', BF16) for i in range(NT)]

        def norm_tile(xtile, Acol0, Bcol0, dst, sqt):
            sq = sqt
            o.act(sq.v, xtile.v, AF.Square)
            p = ps()
            for kc in range(KC):
                o.mm(p.v, onesb, sq[:, kc * 512:(kc + 1) * 512], start=(kc == 0), stop=(kc == KC - 1))
            rstd = wk()
            rtmp = wk()
            o.rsqrt(rstd[:, 0:512], p.v, 1024.0 * EPS, rtmp[:, 0:512])
            x3 = xtile.v.re("p (k t) -> p k t", k=KC)
            o.tt('dve', x3, x3, rstd[:, 0:512].us(1).bc([128, KC, 512]), ALU.mult)
            for kc in range(KC):
                o.act(dst[:, kc * 512:(kc + 1) * 512], xtile[:, kc * 512:(kc + 1) * 512], AF.Identity,
                      bias=md[:, Bcol0 + kc:Bcol0 + kc + 1], scale=dr[:, Acol0 + kc:Acol0 + kc + 1])

        nmark = st['off']
        xt = [sb(KC * 512, f'xt{i}') for i in range(2)]
        sqt = sb(KC * 512, 'sqt', BF16)
        for i in range(NT):
            x = xt[i % 2]
            o.dma(x.v.re("p (k t) -> p k t", k=KC), XT[:, :, i * 512:(i + 1) * 512])
            norm_tile(x, 0, 0, hts[i], sqt)
        S.barrier()
        st['off'] = nmark
        lorab = sb(512, 'lorab', BF16)
        g2b = sb(512, 'g2b', BF16)
        v2b = sb(512, 'v2b', BF16)
        wtmp = wk()
        o.dma(wtmp[:, 0:512], lora_in[l])
        o.cp('pool', lorab.v, wtmp[:, 0:512])
        wtmp = wk()
        o.dma(wtmp[:, 0:512], g2_in[l])
        o.cp('pool', g2b.v, wtmp[:, 0:512])
        wtmp = wk()
        o.dma(wtmp[0:64, 0:512], v2_in[l])
        o.cp('pool', v2b[0:64, :], wtmp[0:64, 0:512])
        twa = sb(S_LEN, 'twa', BF16)
        sgl = sb(S_LEN, 'sgl', BF16)
        vlo = sb(S_LEN, 'vlo', BF16)
        raws = [sb(514, f'raw{i}') for i in range(3)]
        Cx = sb(513, 'Cx')
        o.memset('dve', Cx[:, 0:1], 0.0)
        wf = [sb(KC * 128, 'wf0')] * 2
        wfi = {'i': 0}
        wbt = [[sb(KC * 128, f'wb{g}_{k}', BF16) for k in range(4)] for g in range(2)]
        grp = {'i': 0}

        def load_group(js):
            g = grp['i'] % 2
            grp['i'] += 1
            outs = []
            for k, j in enumerate(js):
                stg = wf[wfi['i'] % 2]
                wfi['i'] += 1
                o.dma(stg.v, win_in[l][j])
                o.cp('pool', wbt[g][k].v, stg.v)
                outs.append(wbt[g][k])
            return outs

        def proj(wt, i):
            p = ps()
            for kc in range(KC):
                o.mm(p.v, wt[:, kc * 128:(kc + 1) * 128], hts[i][:, kc * 512:(kc + 1) * 512], start=(kc == 0), stop=(kc == KC - 1))
            return p

        def shiftmix(p, raw, mu, r0=0, r1=128, dst=None):
            o.cp('act', raw[r0:r1, 1:513], p[r0:r1, :])
            d = wk()
            o.tt('dve', d[r0:r1, 0:512], raw[r0:r1, 0:512], raw[r0:r1, 1:513], ALU.subtract)
            m = dst if dst is not None else wk()
            o.stt('dve', m[r0:r1, 0:512], d[r0:r1, 0:512], mu, raw[r0:r1, 1:513], ALU.mult, ALU.add)
            o.cp('act', raw[r0:r1, 0:1], raw[r0:r1, 512:513])
            return m

        def reset_raws():
            for r in raws:
                o.memset('pool', r[:, 0:2], 0.0)

        def tsl(i):
            return slice(i * 512, (i + 1) * 512)

        (wm,) = load_group([J_MISC])
        reset_raws()
        for i in range(NT):
            p = proj(wm, i)
            t1 = wk()
            o.act(t1[0:8, 0:512], p[0:8, :], AF.Sigmoid, bias=cl[0:8, C_FBF:C_FBF + 1])
            o.act(t1[0:8, 0:512], t1[0:8, 0:512], AF.Ln)
            t2 = wk()
            o.scan(t2[0:8, 0:512], ones512[0:8, :], t1[0:8, 0:512], 0.0 if i == 0 else Fcar[0:8, 0:1])
            o.cp('dve', Fcar[0:8, 0:1], t2[0:8, 511:512])
            o.dma(FTD[0:8, tsl(i)], t2[0:8, 0:512])
            if l > 0:
                m = shiftmix(p, raws[0], cl[32:64, C_VMU:C_VMU + 1], 32, 64)
                o.cp('dve', vlo[32:64, tsl(i)], m[32:64, 0:512])
        wl, wg = load_group([J_LORA, J_GLO])
        reset_raws()
        for i in range(NT):
            p = proj(wl, i)
            m = shiftmix(p, raws[0], cl[:, C_MU + 12:C_MU + 13])
            o.act(twa[0:64, tsl(i)], m[0:64, 0:512], AF.Tanh)
            o.cp('dve', twa[64:128, tsl(i)], m[64:128, 0:512])
            p = proj(wg, i)
            m = shiftmix(p, raws[1], cl[:, C_MU + 13:C_MU + 14])
            o.act(sgl[:, tsl(i)], m[:, 0:512], AF.Sigmoid)
        for c in range(4):
            wr, wkk_, wv = load_group([J_RR + c, J_RK + c, J_RV + c])
            reset_raws()
            cs = slice(c * 128, (c + 1) * 128)
            for i in range(NT):
                pr = proj(wr, i)
                pk = proj(wkk_, i)
                pv = proj(wv, i)
                r_m = shiftmix(pr, raws[0], cl[:, C_MU + c:C_MU + c + 1], dst=ded[0])
                k_m = shiftmix(pk, raws[1], cl[:, C_MU + 4 + c:C_MU + 5 + c], dst=ded[1])
                v_m = shiftmix(pv, raws[2], cl[:, C_MU + 8 + c:C_MU + 9 + c], dst=ded[2])
                pa = ps()
                o.mm(pa.v, lorab[64:128, cs], twa[64:128, tsl(i)])
                a = ded[3]
                o.act(a[:, 0:512], pa.v, AF.Sigmoid, bias=cl[:, C_A0 + c:C_A0 + c + 1])
                pw = ps()
                o.mm(pw.v, lorab[0:64, cs], twa[0:64, tsl(i)])
                sgw = wk()
                o.act(sgw[:, 0:512], pw.v, AF.Sigmoid, bias=cl[:, C_W0 + c:C_W0 + c + 1])
                o.scan(Cx[:, 1:513], ones512.v, sgw[:, 0:512], 0.0)
                ref = Cx[:, 0:512].re("p (c t) -> p c t", t=64)[:, :, 0:1].bc([128, 8, 64])
                Dd = wk()
                o.tt('dve', Dd[:, 0:512].re("p (c t) -> p c t", t=64), Cx[:, 1:513].re("p (c t) -> p c t", t=64), ref, ALU.subtract)
                Dm = wk()
                o.tt('dve', Dm[:, 0:512].re("p (c t) -> p c t", t=64), Cx[:, 0:512].re("p (c t) -> p c t", t=64), ref, ALU.subtract)
                E1 = ded[4]
                E2 = wk()
                E3 = ded[5]
                o.act(E1[:, 0:512], Dd[:, 0:512], AF.Exp, scale=NEG_C0)
                o.act(E2[:, 0:512], Dm[:, 0:512], AF.Exp, scale=NEG_C0)
                o.act(E3[:, 0:512], Dd[:, 0:512], AF.Exp, scale=-NEG_C0)
                o.cp('pool', gamv[:, c, i * 8:(i + 1) * 8], E1[:, 0:512].re("p (c t) -> p c t", t=64)[:, :, 63])
                kk = wk()
                o.ts('pool', kk[:, 0:512], k_m[:, 0:512], cl[:, C_KK + c:C_KK + c + 1], None, ALU.mult)
                sq = wk()
                o.tt('pool', sq[:, 0:512], kk[:, 0:512], kk[:, 0:512], ALU.mult)
                pss = ps()
                o.mm(pss.v, onesbd, sq[:, 0:512])
                rinv = wk()
                rtmp = wk()
                o.act(rtmp[:, 0:512], pss.v, AF.Sqrt)
                o.ts('dve', rtmp[:, 0:512], rtmp[:, 0:512], 1e-12, None, ALU.max)
                o.recip(rinv[:, 0:512], rtmp[:, 0:512])
                kap = wk()
                o.tt('dve', kap[:, 0:512], kk[:, 0:512], rinv[:, 0:512], ALU.mult)
                tp = wk()
                o.ts('pool', tp[:, 0:512], a[:, 0:512], cl[:, C_KA + c:C_KA + c + 1], dr[:, 17 + c:18 + c], ALU.mult, ALU.add)
                kf = ded[6]
                o.tt('pool', kf[:, 0:512], k_m[:, 0:512], tp[:, 0:512], ALU.mult)
                bb = wk()
                o.tt('pool', bb[:, 0:512], a[:, 0:512], kap[:, 0:512], ALU.mult)
                ot = wk()
                o.tt('dve', ot[:, 0:512], r_m[:, 0:512], E1[:, 0:512], ALU.mult)
                o.dma(RKR[c][:, i * 8:(i + 1) * 8, 1, :], ot[:, 0:512].re("p (c t) -> p c t", t=64))
                ot = wk()
                o.tt('dve', ot[:, 0:512], kap[:, 0:512], E2[:, 0:512], ALU.mult)
                o.dma(RKR[c][:, i * 8:(i + 1) * 8, 0, :], ot[:, 0:512].re("p (c t) -> p c t", t=64))
                ot = wk()
                o.tt('pool', ot[:, 0:512], kf[:, 0:512], E3[:, 0:512], ALU.mult)
                o.dma(RKT[c][:, tsl(i)], ot[:, 0:512])
                ot = wk()
                o.tt('pool', ot[:, 0:512], bb[:, 0:512], E3[:, 0:512], ALU.mult)
                o.dma(RBT[c][:, tsl(i)], ot[:, 0:512])
                if l == 0:
                    o.dma(VF[c][:, tsl(i)], v_m[:, 0:512])
                    vv = v_m
                else:
                    vf = wk()
                    o.dma(vf[:, 0:512], VF[c][:, tsl(i)])
                    pg = ps()
                    o.mm(pg.v, v2b[32:64, cs], vlo[32:64, tsl(i)])
                    gt = wk()
                    o.act(gt[:, 0:512], pg.v, AF.Sigmoid, bias=cl[:, C_V0 + c:C_V0 + c + 1])
                    dv = wk()
                    o.tt('dve', dv[:, 0:512], vf[:, 0:512], v_m[:, 0:512], ALU.subtract)
                    o.tt('dve', dv[:, 0:512], dv[:, 0:512], gt[:, 0:512], ALU.mult)
                    vv = wk()
                    o.tt('dve', vv[:, 0:512], dv[:, 0:512], v_m[:, 0:512], ALU.add)
                o.dma(RVT[c][:, tsl(i)], vv[:, 0:512])
                rk = wk()
                o.stt('dve', rk[:, 0:512], r_m[:, 0:512], cl[:, C_RK + c:C_RK + c + 1], kf[:, 0:512], ALU.mult, ALU.mult)
                pb = ps()
                o.mm(pb.v, onesbd, rk[:, 0:512])
                bon = wk()
                o.tt('dve', bon[:, 0:512], pb.v, vv[:, 0:512], ALU.mult)
                o.dma(RBO[c][:, tsl(i)], bon[:, 0:512])
                pgg = ps()
                o.mm(pgg.v, g2b[:, cs], sgl[:, tsl(i)])
                gb = wkbf()
                o.cp('act', gb.v, pgg.v)
                o.dma(RG[c][:, tsl(i)], gb.v)
        for c in range(4):
            wq, wfg, wi, wo = load_group([J_HQ + c, J_HF + c, J_HI + c, J_HO + c])
            lcol = slice(l * 4 + c, l * 4 + c + 1)
            for i in range(NT):
                pq = proj(wq, i)
                pf = proj(wfg, i)
                pi_ = proj(wi, i)
                po = proj(wo, i)
                sq_ = wk()
                o.act(sq_[:, 0:512], pq.v, AF.Silu)
                sgf = wk()
                o.act(sgf[:, 0:512], pf.v, AF.Sigmoid)
                gate = wk()
                o.ts('dve', gate[:, 0:512], sgf[:, 0:512], oml[:, lcol], lowT[:, lcol], ALU.mult, ALU.add)
                lg = wk()
                o.act(lg[:, 0:512], gate[:, 0:512], AF.Ln)
                kx = wk()
                o.ts('pool', kx[:, 0:512], gate[:, 0:512], -1.0, 1.0, ALU.mult, ALU.add)
                o.scan(Cx[:, 1:513], ones512.v, lg[:, 0:512], 0.0)
                G3 = Cx[:, 1:513].re("p (c t) -> p c t", t=64)
                Dd = wk()
                o.tt('dve', Dd[:, 0:512].re("p (c t) -> p c t", t=64), G3, G3[:, :, 31:32].bc([128, 8, 64]), ALU.subtract)
                E1 = wk()
                E3 = wk()
                o.act(E1[:, 0:512], Dd[:, 0:512], AF.Exp)
                o.act(E3[:, 0:512], Dd[:, 0:512], AF.Exp, scale=-1.0)
                ot = wk()
                o.tt('dve', ot[:, 0:512], sq_[:, 0:512], E1[:, 0:512], ALU.mult)
                o.dma(HQ[c][:, tsl(i)], ot[:, 0:512])
                ot = wk()
                o.tt('pool', ot[:, 0:512], kx[:, 0:512], E3[:, 0:512], ALU.mult)
                o.dma(HK[c][:, tsl(i)], ot[:, 0:512])
                o.cp('pool', hscv[:, 0, c, i * 8:(i + 1) * 8], E1[:, 0:512].re("p (c t) -> p c t", t=64)[:, :, 63])
                G0 = Cx[:, 0:512].re("p (c t) -> p c t", t=64)
                d8 = wk()
                o.tt('dve', d8[:, 0:8], G3[:, :, 63], G0[:, :, 0], ALU.subtract)
                o.act(hscv[:, 1, c, i * 8:(i + 1) * 8], d8[:, 0:8], AF.Exp)
                d8 = wk()
                o.tt('dve', d8[:, 0:8], G3[:, :, 31], G0[:, :, 0], ALU.subtract)
                o.act(hscv[:, 2, c, i * 8:(i + 1) * 8], d8[:, 0:8], AF.Exp)
                it = wk()
                o.cp('act', it[:, 0:512], pi_.v)
                o.dma(HI[c][:, tsl(i)], it[:, 0:512])
                ob = wkbf()
                o.act(ob.v, po.v, AF.Silu)
                o.dma(HO[c][:, tsl(i)], ob.v)
        for c in range(4):
            wq, wk_ = load_group([J_FQ + c, J_FK + c])
            for i in range(NT):
                for which, wt in ((0, wq), (1, wk_)):
                    p = proj(wt, i)
                    sq = wk()
                    o.act(sq[:, 0:512], p.v, AF.Square)
                    pss = ps()
                    o.mm(pss.v, onesbd, sq[:, 0:512])
                    rstd = wk()
                    rtmp = wk()
                    o.rsqrt(rstd[:, 0:512], pss.v, 64.0 * EPS, rtmp[:, 0:512])
                    qb = wkbf()
                    o.stt('dve', qb.v, p.v, dr[:, 21 + which:22 + which], rstd[:, 0:512], ALU.mult, ALU.mult)
                    o.dma(QKT[which * 4 + c][:, tsl(i)], qb.v)
        wvs = load_group([J_FV + c for c in range(4)])
        for c in range(4):
            for i in range(NT):
                p = proj(wvs[c], i)
                vb = wkbf()
                o.cp('act', vb.v, p.v)
                o.dma(FVT[c][:, tsl(i)], vb.v)
        for c0 in range(0, 24, 4):
            wgs = load_group([J_GATE + c0 + k for k in range(4)])
            for k in range(4):
                for i in range(NT):
                    p = proj(wgs[k], i)
                    gb = wkbf()
                    o.act(gb.v, p.v, AF.Sigmoid)
                    o.dma(GT[c0 + k][:, tsl(i)], gb.v)
        if STOP <= 1:
            return
        NB = S_LEN // 128

        def r3(v, c):
            return v.re("p (c t) -> p c t", c=c)

        def par(v, bk):
            return v.re("p (c a t) -> p c a t", c=4, a=2)[:, :, bk, :]

        S.barrier()
        st['off'] = layer_mark
        Fsb = sb(S_LEN, 'Fsb')
        F8 = sb(S_LEN, 'F8', BF16)
        o.memset('pool', F8.v, 0.0)
        o.dma(Fsb[0:8, :], FTD.v)
        o.ts('dve', F8[0:8, :], Fsb[0:8, :], 8.0, None, ALU.mult)
        negF = sb(NB * 8, 'negF')
        for blk in range(NB):
            p = ps()
            o.tr(p[:, 0:8], Fsb[0:8, blk * 128:(blk + 1) * 128], ident[0:8, 0:8])
            o.ts('dve', negF[:, blk * 8:(blk + 1) * 8], p[:, 0:8], -1.0, None, ALU.mult)
        VTM = sb(NB * 512, 'VTM', BF16)
        V4 = VTM.v.re("p (n h x) -> p n h x", n=NB, h=4)
        o.memset('pool', VTM.v, 1.0)
        qz = [sb(S_LEN, f'qz{g}', BF16) for g in range(2)]
        kT = sb(S_LEN, 'kT', BF16)
        fv2 = [sb(S_LEN, f'fv{c}', BF16) for c in range(2)]
        Pt = [sb(512, f'P{i}', BF16) for i in range(6)]
        yt = [sb(512, f'y{i}', BF16) for i in range(3)]
        rd = [sb(512, f'rd{i}') for i in range(2)]
        pst['pc'] = 0
        ycount = 0
        for half in range(2):
            pst['n'] = 8
            for cc_ in range(2):
                o.dma(fv2[cc_].v, FVT[2 * half + cc_])
            for blk in range(NB):
                p = ps()
                pb = p.v.cast(BF16)
                for cc_ in range(2):
                    o.tr(pb[:, cc_ * 128:(cc_ + 1) * 128], fv2[cc_][:, blk * 128:(blk + 1) * 128], identb)
                o.cp('act' if blk % 2 else 'dve', V4[:, blk, :, 0:64], pb[:, 0:256].re("p (h x) -> p h x", h=4))
            pst['n'] = 4
            for cc_ in range(2):
                c = 2 * half + cc_
                o.dma(kT.v, QKT[4 + c])
                o.memset('pool', qz[0][64:128, :], 0.0)
                o.memset('pool', qz[1][0:64, :], 0.0)
                o.dma(qz[0][0:64, :], QKT[c][0:64, :])
                o.dma(qz[1][64:128, :], QKT[c][64:128, :])
                for h2 in range(2):
                    h = 2 * c + h2
                    hl = 2 * cc_ + h2
                    rows = slice(h2 * 64, (h2 + 1) * 64)
                    for qc in range(NT):
                        pn = psb[4 + ycount % 4]
                        nkb = 4 * qc + 4
                        pend = []

                        def stage_a(kb):
                            i_ = kb - 4 * qc
                            n0 = 128 * i_ if i_ > 0 else 0
                            qs = slice(qc * 512 + n0, (qc + 1) * 512)
                            psc = ps()
                            o.mm(psc[:, n0:512], kT[:, kb * 128:(kb + 1) * 128], qz[h2][:, qs], start=True, stop=False)
                            if i_ >= 0:
                                o.mm(psc[:, n0:n0 + 128], identb, trib, start=False, stop=False)
                            o.mm(psc[:, n0:512], selb[:, h * 128:(h + 1) * 128], F8[:, qs], start=False, stop=True)
                            P = Pt[pst['pc'] % 6]
                            pst['pc'] += 1
                            o.act(P[:, n0:512], psc[:, n0:512], AF.Exp, bias=negF[:, kb * 8 + h:kb * 8 + h + 1], scale=0.125)
                            return (kb, n0, P)

                        def stage_b(kb, n0, P):
                            o.mm(pn[:, n0:512], V4[:, kb, hl, :], P[:, n0:512], start=(kb == 0), stop=(kb == nkb - 1))
                        for kb in range(nkb):
                            pend.append(stage_a(kb))
                            if len(pend) > 2:
                                stage_b(*pend.pop(0))
                        while pend:
                            stage_b(*pend.pop(0))
                        r = rd[ycount % 2]
                        y = yt[ycount % 3]
                        ycount += 1
                        o.cp('act', r[0:64, :], pn[64:128, :])
                        o.recip(r[0:64, :], r[0:64, :])
                        o.tt('dve', y[0:64, :], pn[0:64, :], r[0:64, :], ALU.mult)
                        o.dma(YB[c][rows, tsl(qc)], y[0:64, :])
        pst['n'] = 8
        if STOP <= 2:
            return

        S.barrier()
        st['off'] = layer_mark
        Mst = sb(512, 'Mst')
        M3 = r3(Mst.v, 4)
        o.memset('dve', Mst.v, 0.0)
        vpad = sb(1024, 'vpad', BF16)
        o.memset('pool', vpad.v, 0.0)
        ktm = [sb(512, f'ktm{i}', BF16) for i in range(2)]
        vtm = [sb(512, f'vtm{i}', BF16) for i in range(2)]
        amt = [sb(512, f'amt{i}', BF16) for i in range(2)]
        m0t = [sb(512, f'm0t{i}', BF16) for i in range(2)]
        hqb = sb(2048, 'hqb', BF16)
        hkb = sb(2048, 'hkb', BF16)
        hib = sb(2048, 'hib', BF16)
        hq = [sb(2048, f'hq{i}') for i in range(2)]
        hk = [sb(2048, f'hk{i}') for i in range(2)]
        hi = [sb(2048, f'hi{i}') for i in range(2)]
        ho = [sb(2048, f'ho{i}', BF16) for i in range(2)]
        ot = sb(2048, 'ot')
        bd4 = bd.us(1).bc([128, 4, 128])
        if HGCUT <= -1:
            return
        for i in range(NT):
            b = i % 2
            o.dma(r3(hq[b].v, 4), HQ.v.re("c p t -> p c t")[:, :, tsl(i)])
            o.dma(r3(hk[b].v, 4), HK.v.re("c p t -> p c t")[:, :, tsl(i)])
            o.dma(r3(hi[b].v, 4), HI.v.re("c p t -> p c t")[:, :, tsl(i)])
            o.dma(r3(ho[b].v, 4), HO.v.re("c p t -> p c t")[:, :, tsl(i)])
            o.cp('pool', hqb.v, hq[b].v)
            o.cp('pool', hkb.v, hk[b].v)
            o.cp('pool', hib.v, hi[b].v)
            if HGCUT <= 0:
                continue
            for ck in range(8):
                g = i * 8 + ck

                def cc(c):
                    return slice(c * 512 + ck * 64, c * 512 + (ck + 1) * 64)
                pk_ = ps()
                pv_ = ps()
                for c in range(4):
                    o.mm(pk_[0:64, c * 128:(c + 1) * 128], hkb[:, cc(c)], identb)
                    o.mm(pv_[0:64, c * 128:(c + 1) * 128], hib[:, cc(c)], identb)
                kt = ktm[g % 2]
                vt = vtm[g % 2]
                if HGSUB <= 1:
                    continue
                o.cp('act', kt[0:64, :], pk_[0:64, :])
                o.cp('act', vt[0:64, :], pv_[0:64, :])
                if HGSUB <= 2:
                    continue
                o.tt('dve', vpad[0:64, 0:512], pv_[0:64, :], me8[0:64, :], ALU.mult)
                o.tt('dve', vpad[0:64, 512:1024], pv_[0:64, :], mo8[0:64, :], ALU.mult)
                if HGCUT <= 1:
                    continue
                M0s = m0t[g % 2]
                o.tt('dve', r3(M0s.v, 4), M3, hscv[:, 2, :, g:g + 1].bc([128, 4, 128]), ALU.mult)
                if HGSUB <= 3:
                    continue
                pA = [ps(), ps()]
                for h in range(8):
                    c = h // 2
                    rr = slice((h % 2) * 64, (h % 2) * 64 + 64)
                    b0 = c * 512 + ck * 64
                    o.mm(pA[h % 2][0:64, c * 64 + 32:(c + 1) * 64], hkb[rr, b0:b0 + 64], hqb[rr, b0 + 32:b0 + 64])
                    o.mm(pA[h % 2][0:32, c * 64:c * 64 + 32], hkb[rr, b0:b0 + 32], hqb[rr, b0:b0 + 32])
                Am = amt[g % 2]
                o.memset('pool', Am[32:64, :], 0.0)
                for bk in range(2):
                    o.tt('dve', par(Am[0:64, :], bk)[:, :, 32:64], r3(pA[bk][0:64, 0:256], 4)[:, :, 32:64], r3(ui4[0:64, :], 4)[:, :, 32:64], ALU.mult)
                    o.tt('dve', par(Am[0:32, :], bk)[:, :, 0:32], r3(pA[bk][0:32, 0:256], 4)[:, :, 0:32], r3(ui4[0:32, :], 4)[:, :, 0:32], ALU.mult)
                if HGCUT <= 2:
                    continue
                pO = ps()
                for c in range(4):
                    oc_ = pO[:, c * 64:(c + 1) * 64]
                    o.mm(oc_, M0s[:, c * 128:(c + 1) * 128], hqb[:, cc(c)], start=True, stop=False)
                    o.mm(oc_, vpad[0:64, c * 128:(c + 1) * 128], Am[0:64, (2 * c) * 64:(2 * c + 1) * 64], start=False, stop=False)
                    o.mm(oc_, vpad[0:64, 512 + c * 128:512 + (c + 1) * 128], Am[0:64, (2 * c + 1) * 64:(2 * c + 2) * 64], start=False, stop=True)
                o.cp('act', r3(ot.v, 4)[:, :, ck * 64:(ck + 1) * 64], r3(pO[:, 0:256], 4))
                if HGCUT <= 3:
                    continue
                pS = ps()
                for c in range(4):
                    o.mm(pS[:, c * 128:(c + 1) * 128], kt[0:64, c * 128:(c + 1) * 128], vt[0:64, c * 128:(c + 1) * 128])
                t1 = wk()
                o.tt('dve', r3(t1.v, 4), r3(pS.v, 4), bd4, ALU.mult)
                o.tt('dve', r3(t1.v, 4), r3(t1.v, 4), hscv[:, 0, :, g:g + 1].bc([128, 4, 128]), ALU.mult)
                o.tt('pool', M3, M3, hscv[:, 1, :, g:g + 1].bc([128, 4, 128]), ALU.mult)
                o.tt('pool', M3, M3, r3(t1.v, 4), ALU.add)
            for c in range(4):
                oc_ = ot[:, c * 512:(c + 1) * 512]
                sq = wk()
                o.act(sq.v, oc_, AF.Square)
                pss = ps()
                o.mm(pss.v, onesbd, sq.v)
                rstd = wk()
                rtmp = wk()
                o.rsqrt(rstd.v, pss.v, 64.0 * EPS, rtmp.v)
                y = wk()
                o.stt('dve', y.v, oc_, dr[:, 23:24], rstd.v, ALU.mult, ALU.mult)
                yb_ = wkbf()
                o.tt('dve', yb_.v, y.v, ho[b][:, c * 512:(c + 1) * 512], ALU.mult)
                o.dma(YB[4 + c][:, tsl(i)], yb_.v)
        if STOP <= 3:
            return

        S.barrier()
        st['off'] = layer_mark
        Mst = sb(512, 'MstR')
        M3 = r3(Mst.v, 4)
        o.memset('dve', Mst.v, 0.0)
        Mstb = sb(512, 'MstRb', BF16)
        o.memset('pool', Mstb.v, 0.0)
        vpad = sb(1024, 'vpadR', BF16)
        nupad = sb(1024, 'nupad', BF16)
        o.memset('pool', vpad.v, 0.0)
        o.memset('pool', nupad.v, 0.0)
        sm64 = [sb(512, f'sm{i}', BF16) for i in range(14)]
        (ktm_, btm_, vtm_, RBt, Qt, RKt, Wsb, nut, Y0, Y1, Yt0, Yt1, Tt0, Tt1) = sm64
        rkr = [sb(4096, 'rkr0')] * 2
        rkk = [sb(2048, 'rkk0')] * 2
        rbb = [sb(2048, 'rbb0')] * 2
        rvv = [sb(2048, 'rvv0')] * 2
        bo = [sb(2048, 'bo0')] * 2
        rg = [sb(2048, 'rg0', BF16)] * 2
        rkrb = sb(4096, 'rkrb', BF16)
        rkkb = sb(2048, 'rkkb', BF16)
        rbbb = sb(2048, 'rbbb', BF16)
        rvvb = sb(2048, 'rvvb', BF16)
        ot = sb(2048, 'otR')
        hsl = [slice(h * 64, (h + 1) * 64) for h in range(8)]
        for i in range(NT):
            b = i % 2
            o.dma(rkr[b].v.re("p (c n a t) -> p c n a t", c=4, n=8, a=2), RKR.v.re("c p n a t -> p c n a t")[:, :, i * 8:(i + 1) * 8, :, :])
            o.dma(r3(rkk[b].v, 4), RKT.v.re("c p t -> p c t")[:, :, tsl(i)])
            o.dma(r3(rbb[b].v, 4), RBT.v.re("c p t -> p c t")[:, :, tsl(i)])
            o.dma(r3(rvv[b].v, 4), RVT.v.re("c p t -> p c t")[:, :, tsl(i)])
            o.dma(r3(bo[b].v, 4), RBO.v.re("c p t -> p c t")[:, :, tsl(i)])
            o.dma(r3(rg[b].v, 4), RG.v.re("c p t -> p c t")[:, :, tsl(i)])
            o.cp('pool', rkrb.v, rkr[b].v)
            o.cp('pool', rkkb.v, rkk[b].v)
            o.cp('pool', rbbb.v, rbb[b].v)
            o.cp('pool', rvvb.v, rvv[b].v)
            for ck in range(8):
                g = i * 8 + ck

                def cc(c):
                    return slice(c * 512 + ck * 64, c * 512 + (ck + 1) * 64)

                def KR(c, a0, a1):
                    off = (c * 8 + ck) * 128
                    return slice(off + a0 * 64, off + a1 * 64)
                pk_ = ps()
                pb_ = ps()
                pv_ = ps()
                for c in range(4):
                    o.mm(pk_[0:64, c * 128:(c + 1) * 128], rkkb[:, cc(c)], identb)
                    o.mm(pb_[0:64, c * 128:(c + 1) * 128], rbbb[:, cc(c)], identb)
                    o.mm(pv_[0:64, c * 128:(c + 1) * 128], rvvb[:, cc(c)], identb)
                o.cp('act', ktm_[0:64, :], pk_[0:64, :])
                o.cp('act', btm_[0:64, :], pb_[0:64, :])
                o.cp('act', vtm_[0:64, :], pv_[0:64, :])
                o.tt('dve', vpad[0:64, 0:512], pv_[0:64, :], me8[0:64, :], ALU.mult)
                o.tt('dve', vpad[0:64, 512:1024], pv_[0:64, :], mo8[0:64, :], ALU.mult)
                pA1 = [ps(), ps()]
                pA2 = [ps(), ps()]
                for h in range(8):
                    c = h // 2
                    rr = slice((h % 2) * 64, (h % 2) * 64 + 64)
                    o.mm(pA1[h % 2][0:64, c * 128:(c + 1) * 128], rbbb[rr, cc(c)], rkrb[rr, KR(c, 0, 2)])
                    o.mm(pA2[h % 2][0:64, c * 128:(c + 1) * 128], rkkb[rr, cc(c)], rkrb[rr, KR(c, 0, 2)])
                for bk in range(2):
                    v1 = pA1[bk][0:64, :].re("p (h x) -> p h x", h=4)
                    v2 = pA2[bk][0:64, :].re("p (h x) -> p h x", h=4)
                    o.tt('dve', par(Yt0[0:64, :], bk), v1[:, :, 0:64], r3(nsu4[0:64, :], 4), ALU.mult)
                    o.tt('dve', par(RBt[0:64, :], bk), v1[:, :, 64:128], r3(ui4[0:64, :], 4), ALU.mult)
                    o.tt('dve', par(Qt[0:64, :], bk), v2[:, :, 0:64], r3(su4[0:64, :], 4), ALU.mult)
                    o.tt('dve', par(RKt[0:64, :], bk), v2[:, :, 64:128], r3(ui4[0:64, :], 4), ALU.mult)
                pA3 = [ps(), ps()]
                for h in range(8):
                    c = h // 2
                    rr = slice((h % 2) * 64, (h % 2) * 64 + 64)
                    o.mm(pA3[h % 2][0:64, c * 64:(c + 1) * 64], rkrb[rr, KR(c, 0, 1)], rbbb[rr, cc(c)])
                for bk in range(2):
                    o.tt('dve', par(Y0[0:64, :], bk), r3(pA3[bk][0:64, 0:256], 4), r3(nsl8[0:64, 0:256], 4), ALU.mult)
                o.tt('pool', Tt0[0:64, :], Yt0[0:64, :], id8[0:64, :], ALU.add)
                Ya, Yta, Tta = Y0, Yt0, Tt0
                Yb, Ytb, Ttb = Y1, Yt1, Tt1
                for j in range(5):
                    pY = ps()
                    pYt = ps()
                    for h in range(8):
                        o.mm(pY[0:64, hsl[h]], Yta[0:64, hsl[h]], Ya[0:64, hsl[h]])
                        o.mm(pYt[0:64, hsl[h]], Ya[0:64, hsl[h]], Yta[0:64, hsl[h]])
                    o.cp('act', Yb[0:64, :], pY[0:64, :])
                    o.cp('act', Ytb[0:64, :], pYt[0:64, :])
                    pT = ps()
                    for h in range(8):
                        o.mm(pT[0:64, hsl[h]], Yb[0:64, hsl[h]], Tta[0:64, hsl[h]])
                    o.tt('dve', Ttb[0:64, :], pT[0:64, :], Tta[0:64, :], ALU.add)
                    Ya, Yta, Tta, Yb, Ytb, Ttb = Yb, Ytb, Ttb, Ya, Yta, Tta
                pW = ps()
                for c in range(4):
                    o.mm(pW[0:64, c * 128:(c + 1) * 128], rkrb[:, KR(c, 0, 1)], Mstb[:, c * 128:(c + 1) * 128], start=True, stop=False)
                    for h2 in range(2):
                        h = 2 * c + h2
                        o.mm(pW[0:64, hsl[h]], Qt[0:64, hsl[h]], vtm_[0:64, hsl[h]], start=False, stop=True)
                o.cp('act', Wsb[0:64, :], pW[0:64, :])
                pU = ps()
                for h in range(8):
                    o.mm(pU[0:64, hsl[h]], Tta[0:64, hsl[h]], Wsb[0:64, hsl[h]])
                o.ts('dve', nut[0:64, :], pU[0:64, :], -1.0, None, ALU.mult)
                o.stt('dve', nupad[0:64, 0:512], pU[0:64, :], -1.0, me8[0:64, :], ALU.mult, ALU.mult)
                o.stt('dve', nupad[0:64, 512:1024], pU[0:64, :], -1.0, mo8[0:64, :], ALU.mult, ALU.mult)
                pO = ps()
                for c in range(4):
                    oc_ = pO[:, c * 64:(c + 1) * 64]
                    o.mm(oc_, Mstb[:, c * 128:(c + 1) * 128], rkrb[:, KR(c, 1, 2)], start=True, stop=False)
                    for h2 in range(2):
                        h = 2 * c + h2
                        o.mm(oc_, vpad[0:64, h2 * 512 + c * 128:h2 * 512 + (c + 1) * 128], RKt[0:64, hsl[h]], start=False, stop=False)
                        o.mm(oc_, nupad[0:64, h2 * 512 + c * 128:h2 * 512 + (c + 1) * 128], RBt[0:64, hsl[h]], start=False, stop=(h2 == 1))
                o.cp('act', r3(ot.v, 4)[:, :, ck * 64:(ck + 1) * 64], r3(pO[:, 0:256], 4))
                pS = ps()
                for c in range(4):
                    o.mm(pS[:, c * 128:(c + 1) * 128], ktm_[0:64, c * 128:(c + 1) * 128], vtm_[0:64, c * 128:(c + 1) * 128], start=True, stop=False)
                    o.mm(pS[:, c * 128:(c + 1) * 128], btm_[0:64, c * 128:(c + 1) * 128], nut[0:64, c * 128:(c + 1) * 128], start=False, stop=True)
                t1 = wk()
                o.tt('dve', r3(t1.v, 4), r3(pS.v, 4), bd4, ALU.mult)
                o.tt('dve', t1.v, t1.v, Mst.v, ALU.add)
                o.tt('pool', M3, r3(t1.v, 4), gamv[:, :, g:g + 1].bc([128, 4, 128]), ALU.mult)
                o.cp('act', Mstb.v, Mst.v)
            for c in range(4):
                oc_ = ot[:, c * 512:(c + 1) * 512]
                pm = ps()
                o.mm(pm.v, onesbd, oc_)
                sq = wk()
                o.act(sq.v, oc_, AF.Square)
                pvv = ps()
                o.mm(pvv.v, onesbd, sq.v)
                mean = wk()
                o.ts('dve', mean.v, pm.v, 1.0 / 64, None, ALU.mult)
                cen = wk()
                o.tt('dve', cen.v, oc_, mean.v, ALU.subtract)
                msq = wk()
                o.tt('pool', msq.v, mean.v, mean.v, ALU.mult)
                var = wk()
                o.stt('dve', var.v, pvv.v, 1.0 / 64, msq.v, ALU.mult, ALU.subtract)
                rstd = wk()
                rtmp = wk()
                o.rsqrt(rstd.v, var.v, LN_EPS, rtmp.v)
                y = wk()
                o.tt('dve', y.v, cen.v, rstd.v, ALU.mult)
                o.ts('dve', y.v, y.v, cl[:, C_LNW + c:C_LNW + c + 1], cl[:, C_LNB + c:C_LNB + c + 1], ALU.mult, ALU.add)
                o.tt('pool', y.v, y.v, bo[b][:, c * 512:(c + 1) * 512], ALU.add)
                yb_ = wkbf()
                o.tt('dve', yb_.v, y.v, rg[b][:, c * 512:(c + 1) * 512], ALU.mult)
                o.dma(YB[8 + c][:, tsl(i)], yb_.v)
        if STOP <= 4:
            return

        S.barrier()
        st['off'] = layer_mark
        wbrb = sb(12 * 1024, 'wbrb', BF16)
        woutb = sb(8 * 1024, 'woutb', BF16)
        stg = [sb(2048, f'stg{i}') for i in range(2)]
        for q in range(6):
            s_ = stg[q % 2]
            o.dma(r3(s_.v, 2), wbr_in[l][:, 2 * q:2 * q + 2, :])
            o.cp('pool', wbrb[:, 2 * q * 1024:(2 * q + 2) * 1024], s_.v)
        for q in range(4):
            s_ = stg[q % 2]
            o.dma(r3(s_.v, 2), wout_in[l][:, 2 * q:2 * q + 2, :])
            o.cp('pool', woutb[:, 2 * q * 1024:(2 * q + 2) * 1024], s_.v)
        ybt = [sb(12 * 512, 'ybt0', BF16)] * 2
        gtt = [sb(3 * 512, f'gtt{i}', BF16) for i in range(2)]
        mg = sb(8 * 512, 'mg', BF16)
        xt = [sb(KC * 512, 'xt30')] * 2
        sqt = sb(KC * 512, 'sqt3', BF16)
        h2o = [sb(KC * 512, 'h2o0', BF16)] * 2
        gn = 0
        for i in range(NT):
            yb_ = ybt[i % 2]
            o.dma(r3(yb_.v, 12), YB.v.re("c p t -> p c t")[:, :, tsl(i)])
            x = xt[i % 2]
            o.dma(r3(x.v, KC), XT[:, :, tsl(i)])
            for oc in range(8):
                g_ = gtt[gn % 2]
                gn += 1
                o.dma(r3(g_.v, 3), GT.v.re("(b o) p t -> o p b t", b=3)[oc][:, :, tsl(i)])
                acc = wk()
                for b_ in range(3):
                    p = ps()
                    for kc in range(4):
                        o.mm(p.v, wbrb[:, (b_ * 4 + kc) * 1024 + oc * 128:(b_ * 4 + kc) * 1024 + (oc + 1) * 128],
                             yb_[:, (b_ * 4 + kc) * 512:(b_ * 4 + kc + 1) * 512], start=(kc == 0), stop=(kc == 3))
                    if b_ == 0:
                        o.tt('dve', acc.v, p.v, g_[:, 0:512], ALU.mult)
                    else:
                        t = wk()
                        o.tt('dve', t.v, p.v, g_[:, b_ * 512:(b_ + 1) * 512], ALU.mult)
                        if b_ == 1:
                            o.tt('pool', acc.v, acc.v, t.v, ALU.add)
                        else:
                            o.tt('pool', mg[:, oc * 512:(oc + 1) * 512], acc.v, t.v, ALU.add)
            for oc in range(8):
                p = ps()
                for kc in range(KC):
                    o.mm(p.v, woutb[:, kc * 1024 + oc * 128:kc * 1024 + (oc + 1) * 128], mg[:, kc * 512:(kc + 1) * 512], start=(kc == 0), stop=(kc == KC - 1))
                xs_ = x[:, oc * 512:(oc + 1) * 512]
                o.stt('dve', xs_, p.v, md[:, 16 + oc:17 + oc], xs_, ALU.mult, ALU.add)
            o.dma(XT[:, :, tsl(i)], r3(x.v, KC))
            h2t = h2o[i % 2]
            norm_tile(x, 8, 24, h2t, sqt)
            o.dma(H2.v.re("k p t -> p k t")[:, :, tsl(i)], r3(h2t.v, KC))
        if STOP <= 5:
            return

        S.barrier()
        st['off'] = layer_mark
        hts = [sb(KC * 512, f'h2_{i}', BF16) for i in range(NT)]
        for i in range(NT):
            o.dma(r3(hts[i].v, KC), H2.v.re("k p t -> p k t")[:, :, tsl(i)])
        wst2 = [sb(2048, f'wst2_{i}') for i in range(2)]
        wpb = [sb(2048, f'wpb{i}', BF16) for i in range(2)]
        rawv = sb(514, 'rawv')
        rawg = sb(514, 'rawg')
        for j in range(NFF):
            s_ = wst2[j % 2]
            o.dma(s_.v, wup_in[l][j])
            wb_ = wpb[j % 2]
            o.cp('pool', wb_.v, s_.v)
            o.memset('pool', rawv[:, 0:2], 0.0)
            o.memset('pool', rawg[:, 0:2], 0.0)
            for i in range(NT):
                pv = ps()
                pg = ps()
                for kc in range(KC):
                    o.mm(pv.v, wb_[:, kc * 256:kc * 256 + 128], hts[i][:, kc * 512:(kc + 1) * 512], start=(kc == 0), stop=(kc == KC - 1))
                for kc in range(KC):
                    o.mm(pg.v, wb_[:, kc * 256 + 128:kc * 256 + 256], hts[i][:, kc * 512:(kc + 1) * 512], start=(kc == 0), stop=(kc == KC - 1))
                res = []
                for (p, raw, cc_) in ((pv, rawv, j), (pg, rawg, NFF + j)):
                    o.cp('act', raw[:, 2:514], p.v)
                    cv = wk()
                    o.ts('pool', cv.v, raw[:, 2:514], cl[:, C_CW + 88 + cc_:C_CW + 89 + cc_], cl[:, C_CB + cc_:C_CB + cc_ + 1], ALU.mult, ALU.add)
                    o.stt('dve', cv.v, raw[:, 1:513], cl[:, C_CW + 44 + cc_:C_CW + 45 + cc_], cv.v, ALU.mult, ALU.add)
                    o.stt('dve', cv.v, raw[:, 0:512], cl[:, C_CW + cc_:C_CW + cc_ + 1], cv.v, ALU.mult, ALU.add)
                    o.cp('act', raw[:, 0:2], raw[:, 512:514])
                    res.append(cv)
                sg = wk()
                o.act(sg.v, res[1].v, AF.Silu)
                ab = wkbf()
                o.tt('pool', ab.v, sg.v, res[0].v, ALU.mult)
                o.dma(ACTT[j][:, tsl(i)], ab.v)
        if STOP <= 6:
            return

        S.barrier()
        st['off'] = layer_mark
        wdnb = sb(NFF * 1024, 'wdnb', BF16)
        stg = [sb(2048, f'stgd{i}') for i in range(2)]
        for q in range(NFF // 2):
            s_ = stg[q % 2]
            o.dma(r3(s_.v, 2), wdn_in[l][:, 2 * q:2 * q + 2, :])
            o.cp('pool', wdnb[:, 2 * q * 1024:(2 * q + 2) * 1024], s_.v)
        att = [sb(NFF * 512, f'att{i}', BF16) for i in range(2)]
        xt = [sb(KC * 512, 'xt40')] * 2
        for i in range(NT):
            a_ = att[i % 2]
            o.dma(r3(a_.v, NFF), ACTT.v.re("c p t -> p c t")[:, :, tsl(i)])
            x = xt[i % 2]
            o.dma(r3(x.v, KC), XT[:, :, tsl(i)])
            for oc in range(8):
                p = ps()
                for kc in range(NFF):
                    o.mm(p.v, wdnb[:, kc * 1024 + oc * 128:kc * 1024 + (oc + 1) * 128], a_[:, kc * 512:(kc + 1) * 512], start=(kc == 0), stop=(kc == NFF - 1))
                xs_ = x[:, oc * 512:(oc + 1) * 512]
                o.stt('dve', xs_, p.v, md[:, 40 + oc:41 + oc], xs_, ALU.mult, ALU.add)
            o.dma(XT[:, :, tsl(i)], r3(x.v, KC))

    for l in range(L):
        emit_layer(l)

    S.barrier()
    st['off'] = persist_mark
    xl = [sb(KC * 512, f'xl{i}') for i in range(2)]
    yo = [sb(D, f'yo{i}') for i in range(2)]
    n = 0
    for i in range(NT):
        xs = xl[i % 2]
        o.dma(xs.v.re("p (k t) -> p k t", k=KC), XT[:, :, i * 512:(i + 1) * 512])
        for q in range(4):
            yy = yo[n % 2]
            n += 1
            for h in range(2):
                p = ps()
                for c in range(4):
                    kc = h * 4 + c
                    o.tr(p[:, c * 128:(c + 1) * 128], xs[:, kc * 512 + q * 128: kc * 512 + (q + 1) * 128], ident)
                o.cp('act' if h == 0 else 'dve', yy[:, h * 512:(h + 1) * 512], p.v)
            t0 = i * 512 + q * 128
            o.dma(y_out[t0:t0 + 128, :], yy.v)
    S.barrier()

    with contextlib.ExitStack() as es:
        sems = {}
        for e in COMPUTE:
            sems[e] = es.enter_context(nc.semaphore("s_" + e))
        for j in range(NDSEM):
            sems[('d', j)] = es.enter_context(nc.semaphore(f"d{j}"))
        block = es.enter_context(nc.Block())
        run = S.emit(sems)
        block.sync(lambda e: run('sp', e))
        block.tensor(lambda e: run('pe', e))
        block.scalar(lambda e: run('act', e))
        block.vector(lambda e: run('dve', e))
        block.gpsimd(lambda e: run('pool', e))
    return nc


def _fm(v):
    return np.ascontiguousarray(np.asarray(v, np.float32).reshape(-1, 128).T)


def make_consts():
    c = np.zeros((128, K_W), np.float32)
    c[:, K_ID:K_ID + 128] = np.eye(128)
    c[0:64, K_OBD:K_OBD + 64] = 1.0
    c[64:128, K_OBD + 64:K_OBD + 128] = 1.0
    c[:, K_ONE:K_ONE + 128] = 1.0
    r = np.arange(64)
    su = (r[:, None] < r[None, :]).astype(np.float32)
    ui = (r[:, None] <= r[None, :]).astype(np.float32)
    sl = (r[:, None] > r[None, :]).astype(np.float32)
    c[0:64, K_BD:K_BD + 64] = 1.0
    c[64:128, K_BD + 64:K_BD + 128] = 1.0
    c[0:64, K_UI8:K_UI8 + 512] = np.tile(ui, (1, 8))
    c[0:64, K_NSL8:K_NSL8 + 512] = -np.tile(sl, (1, 8))
    c[0:64, K_ID8:K_ID8 + 512] = np.tile(np.eye(64, dtype=np.float32), (1, 8))
    c[0:64, K_NSU4:K_NSU4 + 256] = -np.tile(su, (1, 4))
    c[0:64, K_SU4:K_SU4 + 256] = np.tile(su, (1, 4))
    me = np.concatenate([np.ones((64, 64), np.float32), np.zeros((64, 64), np.float32)], axis=1)
    c[0:64, K_ME:K_ME + 512] = np.tile(me, (1, 4))
    c[0:64, K_MO:K_MO + 512] = np.tile(1.0 - me, (1, 4))
    return c


def make_consts_b():
    c = np.zeros((128, KB_W), np.float32)
    c[:, KB_ID:KB_ID + 128] = np.eye(128)
    c[:, KB_ONE:KB_ONE + 128] = 1.0
    r = np.arange(128)
    c[:, KB_TRI:KB_TRI + 128] = np.where(r[:, None] <= r[None, :], 0.0, -98304.0)
    for h in range(8):
        c[h, KB_SEL + h * 128:KB_SEL + (h + 1) * 128] = 1.0
    return c


def prep_shared(inp, L):
    f = lambda k: np.asarray(inp[k], np.float32)
    cols = np.zeros((L, 128, NCOLS), np.float32)
    win = np.zeros((L, NCH, 128, KC * 128), np.float32)
    lora = np.zeros((L, 128, 512), np.float32)
    g2 = np.zeros((L, 128, 512), np.float32)
    v2 = np.zeros((L, 64, 512), np.float32)
    for l in range(L):
        cl = cols[l]
        cl[:, C_N1G:C_N1G + 8] = _fm(f('norm1_g')[l])
        cl[:, C_N2G:C_N2G + 8] = _fm(f('norm2_g')[l])
        cl[0:8, C_FBF] = f('fox_b_f')[l]
        cl[:, C_QG] = np.tile(f('fox_q_gain')[l], 2)
        cl[:, C_KG] = np.tile(f('fox_k_gain')[l], 2)
        cl[:, C_OG] = np.tile(f('hgrn_o_gain')[l], 2)
        cl[:, C_MU:C_MU + 14] = _fm(f('rwkv_mu')[l])
        cl[:, C_W0:C_W0 + 4] = _fm(f('rwkv_w0')[l])
        cl[:, C_A0:C_A0 + 4] = _fm(f('rwkv_a0')[l])
        cl[:, C_KK:C_KK + 4] = _fm(f('rwkv_k_k')[l])
        cl[:, C_KA:C_KA + 4] = _fm(f('rwkv_k_a')[l])
        cl[:, C_RK:C_RK + 4] = _fm(f('rwkv_r_k')[l].reshape(-1))
        cl[:, C_LNW:C_LNW + 4] = _fm(f('rwkv_ln_w')[l])
        cl[:, C_LNB:C_LNB + 4] = _fm(f('rwkv_ln_b')[l])
        if l > 0:
            cl[:, C_V0:C_V0 + 4] = _fm(f('rwkv_v0')[l - 1])
            cl[32:64, C_VMU] = f('rwkv_vres_mu')[l - 1]
            v2[l, 32:64] = f('rwkv_v2')[l - 1]
        cw = f('conv_w')[l]
        for j in range(3):
            cl[:, C_CW + j * 44:C_CW + (j + 1) * 44] = _fm(cw[j])
        cl[:, C_CB:C_CB + 44] = _fm(f('conv_b')[l])
        W = f('w_in')[l]
        main = np.concatenate([W[:, 0:1536], W[:, 1544:1544 + 2048 + 1792 + 3072]], axis=1)
        misc = np.zeros((D, 128), np.float32)
        misc[:, 0:8] = W[:, 1536:1544]
        if l > 0:
            misc[:, 32:64] = f('rwkv_vres_down')[l - 1]
        allc = np.concatenate([main, misc], axis=1)
        win[l] = allc.reshape(KC, 128, NCH, 128).transpose(2, 1, 0, 3).reshape(NCH, 128, KC * 128)
        lora[l, 0:64] = f('rwkv_w2')[l]
        lora[l, 64:128] = f('rwkv_a2')[l]
        g2[l] = f('rwkv_g2')[l]
    sh = {}
    sh['cols'] = cols
    sh['lbT'] = np.ascontiguousarray(f('hgrn_lb')[:4].reshape(-1, 4, 128).transpose(2, 0, 1).reshape(128, -1))
    if sh['lbT'].shape[1] < 16:
        sh['lbT'] = np.concatenate([sh['lbT'], np.zeros((128, 16 - sh['lbT'].shape[1]), np.float32)], axis=1)
    sh['wada'] = np.ascontiguousarray(f('w_ada')[:L].reshape(L, KC, 128, 6 * D).transpose(0, 2, 1, 3))
    sh['bada'] = np.ascontiguousarray(f('b_ada')[:L].reshape(L, 1, 6 * D))
    sh['win'] = win
    sh['lora'] = lora
    sh['g2'] = g2
    sh['v2'] = v2
    sh['wbr'] = np.ascontiguousarray(f('w_branch')[:L].reshape(L, 3, 4, 128, D).transpose(0, 3, 1, 2, 4).reshape(L, 128, 12, D))
    sh['wout'] = np.ascontiguousarray(f('w_out')[:L].reshape(L, KC, 128, D).transpose(0, 2, 1, 3))
    wu = f('w_up')[:L].reshape(L, KC, 128, 2, NFF, 128)
    sh['wup'] = np.ascontiguousarray(wu.transpose(0, 4, 2, 1, 3, 5).reshape(L, NFF, 128, KC * 256))
    sh['wdn'] = np.ascontiguousarray(f('w_down')[:L].reshape(L, NFF, 128, D).transpose(0, 2, 1, 3))
    sh['cst'] = make_consts()
    sh['cstb'] = make_consts_b()
    return sh


_CACHE = {}


def run(inp, S_LEN, L, dbg=(), STOP=99):
    key = (S_LEN, L, tuple(dbg), STOP)
    if key not in _CACHE:
        _CACHE[key] = build(S_LEN, L, dbg, STOP)
    nc = _CACHE[key]
    sh = prep_shared(inp, L)
    x = np.asarray(inp['x'], np.float32)
    c = np.asarray(inp['c'], np.float32)
    B = x.shape[0]
    in_maps = []
    for b in range(B):
        m = dict(sh)
        m['x'] = np.ascontiguousarray(x[b])
        m['cT'] = _fm(c[b])
        in_maps.append(m)
    res = run_bass_kernel_spmd(nc, in_maps, core_ids=list(range(B)))
    return res.results


def kernel(**inputs):
    res = run(inputs, 4096, 4)
    return np.stack([r['y'] for r in res]).astype(np.float32)
```

```python
import contextlib
import os
HGCUT = int(os.environ.get('HGCUT', '9'))
HGSUB = int(os.environ.get('HGSUB', '9'))
import numpy as np
import ml_dtypes
import concourse.bass as bass
import concourse.mybir as mybir
from concourse.bass_utils import run_bass_kernel_spmd

F32 = mybir.dt.float32
BF16 = mybir.dt.bfloat16
AF = mybir.ActivationFunctionType
ALU = mybir.AluOpType

NDSEM = 24
COMPUTE = ('pe', 'act', 'dve', 'pool')


class T:
    __slots__ = ('ap', 'name', 'w', 'r', 'ut', 'xr')

    def __init__(self, ap, name='', ut=False, xr=False):
        self.ap = ap
        self.name = name
        self.w = None
        self.r = {}
        self.ut = ut
        self.xr = xr

    def __getitem__(self, k):
        return Vw(self, self.ap[k])

    @property
    def v(self):
        return Vw(self, self.ap)


class Vw:
    __slots__ = ('t', 'ap')

    def __init__(self, t, ap):
        self.t = t
        self.ap = ap

    def __getitem__(self, k):
        return Vw(self.t, self.ap[k])

    def re(self, pat, **kw):
        return Vw(self.t, self.ap.rearrange(pat, **kw))

    def bc(self, shape):
        return Vw(self.t, self.ap.to_broadcast(list(shape)))

    def cast(self, dt):
        return Vw(self.t, self.ap.bitcast(dt))

    def us(self, axis):
        return Vw(self.t, self.ap.unsqueeze(axis))


class Ins:
    __slots__ = ('eng', 'idx', 'fn', 'waits', 'signal', 'vc', 'key', 'val', 'isdma')


class Sched:
    def __init__(self):
        self.q = {e: [] for e in COMPUTE + ('sp',)}
        self.clock = {e: {} for e in COMPUTE + ('sp',)}
        self.dcount = [0] * NDSEM
        self.dlast = [None] * NDSEM
        self.dnext = 0

    def _add(self, eng, fn, reads, writes, isdma):
        ins = Ins()
        ins.eng = eng
        ins.idx = len(self.q[eng])
        ins.fn = fn
        ins.waits = []
        ins.signal = False
        ins.isdma = isdma
        clock = self.clock[eng]
        deps = []
        if isdma:
            j = self.dnext
            self.dnext = (j + 1) % NDSEM
            self.dcount[j] += 1
            ins.key = ('d', j)
            ins.val = 16 * self.dcount[j]
            if self.dlast[j] is not None:
                deps.append((self.dlast[j], 'sem'))
            self.dlast[j] = ins
            ins.signal = True
        else:
            ins.key = eng
            ins.val = ins.idx + 1
        for t in reads:
            if t.w is not None:
                deps.append((t.w, 'raw'))
            if t.xr:
                for r in t.r.values():
                    if r.eng != eng:
                        deps.append((r, 'rar'))
        for t in writes:
            if t.w is not None:
                deps.append((t.w, 'waw'))
            for r in t.r.values():
                deps.append((r, 'war'))
        for p, kind in deps:
            if (not p.isdma) and (not isdma) and p.eng == eng:
                if eng == 'pe' or kind != 'raw':
                    continue
            if clock.get(p.key, 0) >= p.val:
                continue
            p.signal = True
            ins.waits.append(p)
            for k, v in p.vc.items():
                if clock.get(k, 0) < v:
                    clock[k] = v
        vc = dict(clock)
        vc[ins.key] = ins.val
        ins.vc = vc
        for t in reads:
            t.r[ins.key] = ins
        for t in writes:
            t.w = ins
            t.r = {}
        self.q[eng].append(ins)
        return ins

    def op(self, eng, fn, reads=(), writes=()):
        return self._add(eng, fn, reads, writes, False)

    def dma(self, fn, reads=(), writes=(), queue='sp'):
        return self._add(queue, fn, reads, writes, True)

    def barrier(self):
        lasts = []
        for e in COMPUTE:
            for ins in reversed(self.q[e]):
                if ins.fn is not None and not ins.isdma:
                    lasts.append(ins)
                    break
        for j in range(NDSEM):
            if self.dlast[j] is not None:
                lasts.append(self.dlast[j])
        for e in COMPUTE + ('sp',):
            ins = Ins()
            ins.eng = e
            ins.idx = len(self.q[e])
            ins.fn = None
            ins.waits = []
            ins.signal = False
            ins.isdma = False
            ins.key = e
            ins.val = ins.idx
            clock = self.clock[e]
            for p in lasts:
                if (not p.isdma) and p.eng == e:
                    continue
                if clock.get(p.key, 0) >= p.val:
                    continue
                p.signal = True
                ins.waits.append(p)
                for k, v in p.vc.items():
                    if clock.get(k, 0) < v:
                        clock[k] = v
            ins.vc = dict(clock)
            self.q[e].append(ins)

    def emit(self, sems):
        sigcount = {}
        for e in COMPUTE:
            c = 0
            for ins in self.q[e]:
                if ins.fn is not None and ins.signal and not ins.isdma:
                    c += 1
                    sigcount[id(ins)] = c

        def run(eng_name, e):
            for ins in self.q[eng_name]:
                w = {}
                for p in ins.waits:
                    v = p.val if p.isdma else sigcount[id(p)]
                    if w.get(p.key, 0) < v:
                        w[p.key] = v
                for k, v in w.items():
                    e.wait_ge(sems[k], v)
                if ins.fn is None:
                    continue
                r = ins.fn(e)
                if ins.isdma:
                    r.then_inc(sems[ins.key], 16)
                elif ins.signal:
                    r.then_inc(sems[ins.key], 1)
        return run


def _ap(x):
    return x.ap if isinstance(x, Vw) else x


def _ts(*xs):
    return [x.t for x in xs if isinstance(x, Vw) and not x.t.ut]


class Ops:
    def __init__(self, S):
        self.S = S

    def tt(self, eng, out, a, b, op):
        self.S.op(eng, lambda e: e.tensor_tensor(out=out.ap, in0=a.ap, in1=b.ap, op=op), _ts(a, b), _ts(out))

    def ts(self, eng, out, a, s1, s2, op0, op1=None):
        if op1 is None:
            self.S.op(eng, lambda e: e.tensor_scalar(out=out.ap, in0=a.ap, scalar1=_ap(s1), scalar2=None, op0=op0),
                      _ts(a, s1), _ts(out))
        else:
            self.S.op(eng, lambda e: e.tensor_scalar(out=out.ap, in0=a.ap, scalar1=_ap(s1), scalar2=_ap(s2), op0=op0, op1=op1),
                      _ts(a, s1, s2), _ts(out))

    def stt(self, eng, out, a, s, b, op0, op1):
        self.S.op(eng, lambda e: e.scalar_tensor_tensor(out=out.ap, in0=a.ap, scalar=_ap(s), in1=b.ap, op0=op0, op1=op1),
                  _ts(a, s, b), _ts(out))

    def cp(self, eng, out, a):
        if eng == 'act':
            self.S.op(eng, lambda e: e.copy(out=out.ap, in_=a.ap), _ts(a), _ts(out))
        else:
            self.S.op(eng, lambda e: e.tensor_copy(out=out.ap, in_=a.ap), _ts(a), _ts(out))

    def act(self, out, a, func, bias=None, scale=None):
        kw = {}
        if bias is not None:
            kw['bias'] = _ap(bias)
        if scale is not None:
            kw['scale'] = _ap(scale)
        self.S.op('act', lambda e: e.activation(out=out.ap, in_=a.ap, func=func, **kw), _ts(a, bias, scale), _ts(out))

    def recip(self, out, a):
        self.S.op('dve', lambda e: e.reciprocal(out=out.ap, in_=a.ap), _ts(a), _ts(out))

    def rsqrt(self, out, a, add, tmp):
        self.act(tmp, a, AF.Sqrt, bias=add)
        self.recip(out, tmp)

    def scan(self, out, d0, d1, init, op0=ALU.mult, op1=ALU.add):
        self.S.op('dve', lambda e: e.tensor_tensor_scan(out=out.ap, data0=d0.ap, data1=d1.ap, initial=_ap(init), op0=op0, op1=op1),
                  _ts(d0, d1, init), _ts(out))

    def memset(self, eng, out, val):
        self.S.op(eng, lambda e: e.memset(out.ap, val), [], _ts(out))

    def mm(self, out, lhsT, rhs, start=True, stop=True):
        self.S.op('pe', lambda e: e.matmul(out.ap, lhsT.ap, rhs.ap, start=start, stop=stop), _ts(lhsT, rhs), _ts(out))

    def tr(self, out, a, ident):
        self.S.op('pe', lambda e: e.transpose(out=out.ap, in_=a.ap, identity=ident.ap), _ts(a, ident), _ts(out))

    def dma(self, out, a, queue='sp'):
        self.S.dma(lambda e: e.dma_start(out=out.ap, in_=a.ap), _ts(a), _ts(out), queue=queue)


D = 1024
KC = 8
W_MIX = 512
DFF = 2816
NFF = 22
EPS = 1e-6
LN_EPS = 64e-5
NCOLS = 243
C_N1G, C_N2G, C_FBF, C_QG, C_KG, C_OG, C_MU, C_W0, C_A0, C_KK, C_KA, C_RK, C_LNW, C_LNB, C_V0, C_VMU, C_CW, C_CB = \
    0, 8, 16, 17, 18, 19, 20, 34, 38, 42, 46, 50, 54, 58, 62, 66, 67, 199
J_FQ, J_FK, J_FV, J_HQ, J_HF, J_HI, J_HO, J_RR, J_RK, J_RV, J_LORA, J_GLO, J_GATE, J_MISC = 0, 4, 8, 12, 16, 20, 24, 28, 32, 36, 40, 41, 42, 66
NCH = 67
K_ID, K_OBD, K_ONE, K_BD, K_UI8, K_NSL8, K_ID8, K_NSU4, K_SU4, K_ME, K_MO, K_W = 0, 128, 256, 384, 512, 1024, 1536, 2048, 2304, 2560, 3072, 3584
KB_ID, KB_ONE, KB_TRI, KB_SEL, KB_W = 0, 128, 256, 384, 1408


def build(S_LEN, L, dbg=(), STOP=99):
    NT = S_LEN // 512
    NCK = S_LEN // 64
    nc = bass.Bass("TRN2", target_bir_lowering=False)
    S = Sched()
    o = Ops(S)

    def dram(name, shape, dt, kind="Internal"):
        if name in dbg:
            kind = "ExternalOutput"
        return T(nc.dram_tensor(name, list(shape), dt, kind=kind).ap(), name, ut=True)

    x_in = dram("x", [S_LEN, D], F32, "ExternalInput")
    cT_in = dram("cT", [128, KC], F32, "ExternalInput")
    cols_in = dram("cols", [L, 128, NCOLS], F32, "ExternalInput")
    lbT_in = dram("lbT", [128, 16], F32, "ExternalInput")
    wada_in = dram("wada", [L, 128, KC, 6 * D], F32, "ExternalInput")
    bada_in = dram("bada", [L, 1, 6 * D], F32, "ExternalInput")
    win_in = dram("win", [L, NCH, 128, KC * 128], F32, "ExternalInput")
    lora_in = dram("lora", [L, 128, 512], F32, "ExternalInput")
    g2_in = dram("g2", [L, 128, 512], F32, "ExternalInput")
    v2_in = dram("v2", [L, 64, 512], F32, "ExternalInput")
    wbr_in = dram("wbr", [L, 128, 12, D], F32, "ExternalInput")
    wout_in = dram("wout", [L, 128, KC, D], F32, "ExternalInput")
    wup_in = dram("wup", [L, NFF, 128, KC * 256], F32, "ExternalInput")
    wdn_in = dram("wdn", [L, 128, NFF, D], F32, "ExternalInput")
    cst_in = dram("cst", [128, K_W], F32, "ExternalInput")
    cstb_in = dram("cstb", [128, KB_W], F32, "ExternalInput")
    y_out = dram("y", [S_LEN, D], F32, "ExternalOutput")

    XT = dram("XT", [128, KC, S_LEN], F32)
    QKT = dram("QKT", [8, 128, S_LEN], BF16)
    FVT = dram("FVT", [4, 128, S_LEN], BF16)
    GT = dram("GT", [24, 128, S_LEN], BF16)
    HQ = dram("HQ", [4, 128, S_LEN], F32)
    HK = dram("HK", [4, 128, S_LEN], F32)
    HI = dram("HI", [4, 128, S_LEN], F32)
    HO = dram("HO", [4, 128, S_LEN], BF16)
    RKR = dram("RKR", [4, 128, NCK, 2, 64], F32)
    RKT = dram("RKT", [4, 128, S_LEN], F32)
    RBT = dram("RBT", [4, 128, S_LEN], F32)
    RVT = dram("RVT", [4, 128, S_LEN], F32)
    RBO = dram("RBO", [4, 128, S_LEN], F32)
    RG = dram("RG", [4, 128, S_LEN], BF16)
    FTD = dram("FTD", [8, S_LEN], F32)
    VF = dram("VF", [4, 128, S_LEN], F32)
    YB = dram("YB", [12, 128, S_LEN], BF16)
    H2 = dram("H2", [KC, 128, S_LEN], BF16)
    ACTT = dram("ACTT", [NFF, 128, S_LEN], BF16)

    arena = nc.alloc_sbuf_tensor("arena", [128, 53200], F32).ap()
    st = {'off': 0, 'mark': 0}

    def sb(n, name='', dt=F32):
        words = n if dt == F32 else (n + 1) // 2
        assert st['off'] + words <= 53200, ("SBUF arena overflow", name, st['off'], words)
        a = arena[:, st['off']: st['off'] + words]
        st['off'] += words
        assert st['off'] <= 53200, ("SBUF arena overflow", name, st['off'])
        if dt != F32:
            a = a.bitcast(dt)
        return T(a, name)

    psb = [T(nc.alloc_psum_tensor(f"ps{i}", [128, 512], F32).ap(), f"ps{i}", xr=True) for i in range(8)]
    pst = {'i': 0}

    pst['n'] = 8

    def ps():
        p = psb[pst['i'] % pst['n']]
        pst['i'] += 1
        return p

    cst = sb(K_W, 'cst')
    o.dma(cst.v, cst_in.v)
    ident = cst[:, K_ID:K_ID + 128]
    onesbd = cst[:, K_OBD:K_OBD + 128]
    ones = cst[:, K_ONE:K_ONE + 128]
    bd = cst[:, K_BD:K_BD + 128]
    ui8 = cst[:, K_UI8:K_UI8 + 512]
    ui4 = cst[:, K_UI8:K_UI8 + 256]
    nsl8 = cst[:, K_NSL8:K_NSL8 + 512]
    id8 = cst[:, K_ID8:K_ID8 + 512]
    nsu4 = cst[:, K_NSU4:K_NSU4 + 256]
    su4 = cst[:, K_SU4:K_SU4 + 256]
    me8 = cst[:, K_ME:K_ME + 512]
    mo8 = cst[:, K_MO:K_MO + 512]
    cstb = sb(KB_W, 'cstb', BF16)
    identb = cstb[:, KB_ID:KB_ID + 128]
    onesb = cstb[:, KB_ONE:KB_ONE + 128]
    trib = cstb[:, KB_TRI:KB_TRI + 128]
    selb = cstb[:, KB_SEL:KB_SEL + 1024]
    ones512 = sb(512, 'ones512')
    o.memset('dve', ones512.v, 1.0)
    colsT = [sb(NCOLS, f'cols{l}') for l in range(L)]
    for l in range(L):
        o.dma(colsT[l].v, cols_in[l])
    modT = [sb(48, f'mod{l}') for l in range(L)]
    derT = [sb(64, f'der{l}') for l in range(L)]
    lbT = sb(16, 'lbT')
    lowT = sb(16, 'lowT')
    o.dma(lbT.v, lbT_in.v)
    ex = sb(16, 'lbexp')
    sm = sb(4, 'lbsum')
    rs = sb(4, 'lbrs')
    oml = sb(16, 'oml')
    cT = sb(KC, 'cT')
    condT = sb(KC, 'condT')
    persist_mark = st['off']

    st['mark'] = st['off']
    cbt = sb(KB_W, 'cbtmp')
    o.dma(cbt.v, cstb_in.v)
    o.cp('dve', cstb.v, cbt.v)
    xin = [sb(D, f'xin{i}') for i in range(2)]
    xst = [sb(KC * 512, f'xst{i}') for i in range(2)]
    for i in range(NT):
        xs = xst[i % 2]
        for q in range(4):
            xi = xin[q % 2]
            t0 = i * 512 + q * 128
            o.dma(xi.v, x_in[t0:t0 + 128, :])
            for h in range(2):
                p = ps()
                for c in range(4):
                    kc = h * 4 + c
                    o.tr(p[:, c * 128:(c + 1) * 128], xi[:, kc * 128:(kc + 1) * 128], ident)
                dst = xs.v.re("p (k t) -> p k t", k=KC)[:, h * 4:(h + 1) * 4, q * 128:(q + 1) * 128]
                o.cp('act' if h == 0 else 'dve', dst, p.v.re("p (c t) -> p c t", c=4))
        o.dma(XT[:, :, i * 512:(i + 1) * 512], xs.v.re("p (k t) -> p k t", k=KC))

    o.dma(cT.v, cT_in.v)
    o.act(condT.v, cT.v, AF.Silu)
    wst = [sb(KC * 512, f'wst{i}') for i in range(2)]
    bada = sb(6 * D, 'bada')
    modrow = sb(6 * D, 'modrow')
    for l in range(L):
        o.dma(bada[0:1, :], bada_in[l])
        for blk in range(12):
            w = wst[blk % 2]
            o.dma(w.v.re("p (k n) -> p k n", k=KC), wada_in[l][:, :, blk * 512:(blk + 1) * 512])
            p = ps()
            for kc in range(KC):
                o.mm(p[0:1, :], condT[:, kc:kc + 1], w[:, kc * 512:(kc + 1) * 512], start=(kc == 0), stop=(kc == KC - 1))
            o.tt('dve', modrow[0:1, blk * 512:(blk + 1) * 512], p[0:1, :], bada[0:1, blk * 512:(blk + 1) * 512], ALU.add)
        p = ps()
        for j in range(48):
            o.mm(p[:, j:j + 1], modrow[0:1, j * 128:(j + 1) * 128], ones[0:1, 0:1])
        o.cp('dve', modT[l].v, p[:, 0:48])
        dr = derT[l]
        cl = colsT[l]
        o.ts('dve', dr[:, 0:8], modT[l][:, 8:16], 1.0, 32.0, ALU.add, ALU.mult)
        o.tt('dve', dr[:, 0:8], dr[:, 0:8], cl[:, C_N1G:C_N1G + 8], ALU.mult)
        o.ts('dve', dr[:, 8:16], modT[l][:, 32:40], 1.0, 32.0, ALU.add, ALU.mult)
        o.tt('dve', dr[:, 8:16], dr[:, 8:16], cl[:, C_N2G:C_N2G + 8], ALU.mult)
        o.ts('dve', dr[:, 17:21], cl[:, C_KA:C_KA + 4], -1.0, 1.0, ALU.mult, ALU.add)
        o.ts('dve', dr[:, 21:22], cl[:, C_QG:C_QG + 1], 8.0, None, ALU.mult)
        o.ts('dve', dr[:, 22:23], cl[:, C_KG:C_KG + 1], 8.0, None, ALU.mult)
        o.ts('dve', dr[:, 23:24], cl[:, C_OG:C_OG + 1], 8.0, None, ALU.mult)
        o.ts('dve', dr[:, 24:28], cl[:, C_MU + 0:C_MU + 4], -1.0, None, ALU.mult)
    o.act(ex.v, lbT.v, AF.Exp)
    o.cp('dve', sm.v, ex[:, 0:4])
    for l in range(1, L):
        o.tt('dve', sm.v, sm.v, ex[:, l * 4:(l + 1) * 4], ALU.add)
    o.recip(rs.v, sm.v)
    for l in range(L):
        o.tt('dve', ex[:, l * 4:(l + 1) * 4], ex[:, l * 4:(l + 1) * 4], rs.v, ALU.mult)
    o.memset('dve', lowT[:, 0:4], 0.0)
    for l in range(1, L):
        o.tt('dve', lowT[:, l * 4:(l + 1) * 4], lowT[:, (l - 1) * 4:l * 4], ex[:, l * 4:(l + 1) * 4], ALU.add)
    o.ts('dve', oml.v, lowT.v, -1.0, 1.0, ALU.mult, ALU.add)
    ctx = dict(nc=nc, S=S, o=o, sb=sb, ps=ps, st=st, L=L, NT=NT, NCK=NCK, S_LEN=S_LEN)
    NEG_C0 = -0.6065306597126334
    WK_N = 16
    sel_in = None

    def emit_layer(l):
        S.barrier()
        st['off'] = persist_mark
        cl = colsT[l]
        dr = derT[l]
        md = modT[l]
        Fcar = sb(2, 'Fcar')
        hsc = sb(3 * 4 * NCK, 'hsc')
        hscv = hsc.v.re("p (a c n) -> p a c n", a=3, c=4)
        gam = sb(4 * NCK, 'gam')
        gamv = gam.v.re("p (c n) -> p c n", c=4)
        ded = [sb(512, f'ded{i}') for i in range(7)]
        wkp = [sb(512, f'wk{i}') for i in range(WK_N)]
        wki = {'i': 0}

        def wk():
            t = wkp[wki['i'] % WK_N]
            wki['i'] += 1
            return t
        wkb = [sb(512, f'wkb{i}', BF16) for i in range(6)]
        wkbi = {'i': 0}

        def wkbf():
            t = wkb[wkbi['i'] % 6]
            wkbi['i'] += 1
            return t
        layer_mark = st['off']

        hts = [sb(KC * 512, f'ht{i}', BF16) for i in range(NT)]

        def norm_tile(xtile, Acol0, Bcol0, dst, sqt):
            sq = sqt
            o.act(sq.v, xtile.v, AF.Square)
            p = ps()
            for kc in range(KC):
                o.mm(p.v, onesb, sq[:, kc * 512:(kc + 1) * 512], start=(kc == 0), stop=(kc == KC - 1))
            rstd = wk()
            rtmp = wk()
            o.rsqrt(rstd[:, 0:512], p.v, 1024.0 * EPS, rtmp[:, 0:512])
            x3 = xtile.v.re("p (k t) -> p k t", k=KC)
            o.tt('dve', x3, x3, rstd[:, 0:512].us(1).bc([128, KC, 512]), ALU.mult)
            for kc in range(KC):
                o.act(dst[:, kc * 512:(kc + 1) * 512], xtile[:, kc * 512:(kc + 1) * 512], AF.Identity,
                      bias=md[:, Bcol0 + kc:Bcol0 + kc + 1], scale=dr[:, Acol0 + kc:Acol0 + kc + 1])

        nmark = st['off']
        xt = [sb(KC * 512, f'xt{i}') for i in range(2)]
        sqt = sb(KC * 512, 'sqt', BF16)
        for i in range(NT):
            x = xt[i % 2]
            o.dma(x.v.re("p (k t) -> p k t", k=KC), XT[:, :, i * 512:(i + 1) * 512])
            norm_tile(x, 0, 0, hts[i], sqt)
        S.barrier()
        st['off'] = nmark
        lorab = sb(512, 'lorab', BF16)
        g2b = sb(512, 'g2b', BF16)
        v2b = sb(512, 'v2b', BF16)
        wtmp = wk()
        o.dma(wtmp[:, 0:512], lora_in[l])
        o.cp('pool', lorab.v, wtmp[:, 0:512])
        wtmp = wk()
        o.dma(wtmp[:, 0:512], g2_in[l])
        o.cp('pool', g2b.v, wtmp[:, 0:512])
        wtmp = wk()
        o.dma(wtmp[0:64, 0:512], v2_in[l])
        o.cp('pool', v2b[0:64, :], wtmp[0:64, 0:512])
        twa = sb(S_LEN, 'twa', BF16)
        sgl = sb(S_LEN, 'sgl', BF16)
        vlo = sb(S_LEN, 'vlo', BF16)
        raws = [sb(514, f'raw{i}') for i in range(3)]
        Cx = sb(513, 'Cx')
        o.memset('dve', Cx[:, 0:1], 0.0)
        wf = [sb(KC * 128, 'wf0')] * 2
        wfi = {'i': 0}
        wbt = [[sb(KC * 128, f'wb{g}_{k}', BF16) for k in range(4)] for g in range(2)]
        grp = {'i': 0}

        def load_group(js):
            g = grp['i'] % 2
            grp['i'] += 1
            outs = []
            for k, j in enumerate(js):
                stg = wf[wfi['i'] % 2]
                wfi['i'] += 1
                o.dma(stg.v, win_in[l][j])
                o.cp('pool', wbt[g][k].v, stg.v)
                outs.append(wbt[g][k])
            return outs

        def proj(wt, i):
            p = ps()
            for kc in range(KC):
                o.mm(p.v, wt[:, kc * 128:(kc + 1) * 128], hts[i][:, kc * 512:(kc + 1) * 512], start=(kc == 0), stop=(kc == KC - 1))
            return p

        def shiftmix(p, raw, mu, r0=0, r1=128, dst=None):
            o.cp('act', raw[r0:r1, 1:513], p[r0:r1, :])
            d = wk()
            o.tt('dve', d[r0:r1, 0:512], raw[r0:r1, 0:512], raw[r0:r1, 1:513], ALU.subtract)
            m = dst if dst is not None else wk()
            o.stt('dve', m[r0:r1, 0:512], d[r0:r1, 0:512], mu, raw[r0:r1, 1:513], ALU.mult, ALU.add)
            o.cp('act', raw[r0:r1, 0:1], raw[r0:r1, 512:513])
            return m

        def reset_raws():
            for r in raws:
                o.memset('pool', r[:, 0:2], 0.0)

        def tsl(i):
            return slice(i * 512, (i + 1) * 512)

        (wm,) = load_group([J_MISC])
        reset_raws()
        for i in range(NT):
            p = proj(wm, i)
            t1 = wk()
            o.act(t1[0:8, 0:512], p[0:8, :], AF.Sigmoid, bias=cl[0:8, C_FBF:C_FBF + 1])
            o.act(t1[0:8, 0:512], t1[0:8, 0:512], AF.Ln)
            t2 = wk()
            o.scan(t2[0:8, 0:512], ones512[0:8, :], t1[0:8, 0:512], 0.0 if i == 0 else Fcar[0:8, 0:1])
            o.cp('dve', Fcar[0:8, 0:1], t2[0:8, 511:512])
            o.dma(FTD[0:8, tsl(i)], t2[0:8, 0:512])
            if l > 0:
                m = shiftmix(p, raws[0], cl[32:64, C_VMU:C_VMU + 1], 32, 64)
                o.cp('dve', vlo[32:64, tsl(i)], m[32:64, 0:512])
        wl, wg = load_group([J_LORA, J_GLO])
        reset_raws()
        for i in range(NT):
            p = proj(wl, i)
            m = shiftmix(p, raws[0], cl[:, C_MU + 12:C_MU + 13])
            o.act(twa[0:64, tsl(i)], m[0:64, 0:512], AF.Tanh)
            o.cp('dve', twa[64:128, tsl(i)], m[64:128, 0:512])
            p = proj(wg, i)
            m = shiftmix(p, raws[1], cl[:, C_MU + 13:C_MU + 14])
            o.act(sgl[:, tsl(i)], m[:, 0:512], AF.Sigmoid)
        for c in range(4):
            wr, wkk_, wv = load_group([J_RR + c, J_RK + c, J_RV + c])
            reset_raws()
            cs = slice(c * 128, (c + 1) * 128)
            for i in range(NT):
                pr = proj(wr, i)
                pk = proj(wkk_, i)
                pv = proj(wv, i)
                r_m = shiftmix(pr, raws[0], cl[:, C_MU + c:C_MU + c + 1], dst=ded[0])
                k_m = shiftmix(pk, raws[1], cl[:, C_MU + 4 + c:C_MU + 5 + c], dst=ded[1])
                v_m = shiftmix(pv, raws[2], cl[:, C_MU + 8 + c:C_MU + 9 + c], dst=ded[2])
                pa = ps()
                o.mm(pa.v, lorab[64:128, cs], twa[64:128, tsl(i)])
                a = ded[3]
                o.act(a[:, 0:512], pa.v, AF.Sigmoid, bias=cl[:, C_A0 + c:C_A0 + c + 1])
                pw = ps()
                o.mm(pw.v, lorab[0:64, cs], twa[0:64, tsl(i)])
                sgw = wk()
                o.act(sgw[:, 0:512], pw.v, AF.Sigmoid, bias=cl[:, C_W0 + c:C_W0 + c + 1])
                o.scan(Cx[:, 1:513], ones512.v, sgw[:, 0:512], 0.0)
                ref = Cx[:, 0:512].re("p (c t) -> p c t", t=64)[:, :, 0:1].bc([128, 8, 64])
                Dd = wk()
                o.tt('dve', Dd[:, 0:512].re("p (c t) -> p c t", t=64), Cx[:, 1:513].re("p (c t) -> p c t", t=64), ref, ALU.subtract)
                Dm = wk()
                o.tt('dve', Dm[:, 0:512].re("p (c t) -> p c t", t=64), Cx[:, 0:512].re("p (c t) -> p c t", t=64), ref, ALU.subtract)
                E1 = ded[4]
                E2 = wk()
                E3 = ded[5]
                o.act(E1[:, 0:512], Dd[:, 0:512], AF.Exp, scale=NEG_C0)
                o.act(E2[:, 0:512], Dm[:, 0:512], AF.Exp, scale=NEG_C0)
                o.act(E3[:, 0:512], Dd[:, 0:512], AF.Exp, scale=-NEG_C0)
                o.cp('pool', gamv[:, c, i * 8:(i + 1) * 8], E1[:, 0:512].re("p (c t) -> p c t", t=64)[:, :, 63])
                kk = wk()
                o.ts('pool', kk[:, 0:512], k_m[:, 0:512], cl[:, C_KK + c:C_KK + c + 1], None, ALU.mult)
                sq = wk()
                o.tt('pool', sq[:, 0:512], kk[:, 0:512], kk[:, 0:512], ALU.mult)
                pss = ps()
                o.mm(pss.v, onesbd, sq[:, 0:512])
                rinv = wk()
                rtmp = wk()
                o.act(rtmp[:, 0:512], pss.v, AF.Sqrt)
                o.ts('dve', rtmp[:, 0:512], rtmp[:, 0:512], 1e-12, None, ALU.max)
                o.recip(rinv[:, 0:512], rtmp[:, 0:512])
                kap = wk()
                o.tt('dve', kap[:, 0:512], kk[:, 0:512], rinv[:, 0:512], ALU.mult)
                tp = wk()
                o.ts('pool', tp[:, 0:512], a[:, 0:512], cl[:, C_KA + c:C_KA + c + 1], dr[:, 17 + c:18 + c], ALU.mult, ALU.add)
                kf = ded[6]
                o.tt('pool', kf[:, 0:512], k_m[:, 0:512], tp[:, 0:512], ALU.mult)
                bb = wk()
                o.tt('pool', bb[:, 0:512], a[:, 0:512], kap[:, 0:512], ALU.mult)
                ot = wk()
                o.tt('dve', ot[:, 0:512], r_m[:, 0:512], E1[:, 0:512], ALU.mult)
                o.dma(RKR[c][:, i * 8:(i + 1) * 8, 1, :], ot[:, 0:512].re("p (c t) -> p c t", t=64))
                ot = wk()
                o.tt('dve', ot[:, 0:512], kap[:, 0:512], E2[:, 0:512], ALU.mult)
                o.dma(RKR[c][:, i * 8:(i + 1) * 8, 0, :], ot[:, 0:512].re("p (c t) -> p c t", t=64))
                ot = wk()
                o.tt('pool', ot[:, 0:512], kf[:, 0:512], E3[:, 0:512], ALU.mult)
                o.dma(RKT[c][:, tsl(i)], ot[:, 0:512])
                ot = wk()
                o.tt('pool', ot[:, 0:512], bb[:, 0:512], E3[:, 0:512], ALU.mult)
                o.dma(RBT[c][:, tsl(i)], ot[:, 0:512])
                if l == 0:
                    o.dma(VF[c][:, tsl(i)], v_m[:, 0:512])
                    vv = v_m
                else:
                    vf = wk()
                    o.dma(vf[:, 0:512], VF[c][:, tsl(i)])
                    pg = ps()
                    o.mm(pg.v, v2b[32:64, cs], vlo[32:64, tsl(i)])
                    gt = wk()
                    o.act(gt[:, 0:512], pg.v, AF.Sigmoid, bias=cl[:, C_V0 + c:C_V0 + c + 1])
                    dv = wk()
                    o.tt('dve', dv[:, 0:512], vf[:, 0:512], v_m[:, 0:512], ALU.subtract)
                    o.tt('dve', dv[:, 0:512], dv[:, 0:512], gt[:, 0:512], ALU.mult)
                    vv = wk()
                    o.tt('dve', vv[:, 0:512], dv[:, 0:512], v_m[:, 0:512], ALU.add)
                o.dma(RVT[c][:, tsl(i)], vv[:, 0:512])
                rk = wk()
                o.stt('dve', rk[:, 0:512], r_m[:, 0:512], cl[:, C_RK + c:C_RK + c + 1], kf[:, 0:512], ALU.mult, ALU.mult)
                pb = ps()
                o.mm(pb.v, onesbd, rk[:, 0:512])
                bon = wk()
                o.tt('dve', bon[:, 0:512], pb.v, vv[:, 0:512], ALU.mult)
                o.dma(RBO[c][:, tsl(i)], bon[:, 0:512])
                pgg = ps()
                o.mm(pgg.v, g2b[:, cs], sgl[:, tsl(i)])
                gb = wkbf()
                o.cp('act', gb.v, pgg.v)
                o.dma(RG[c][:, tsl(i)], gb.v)
        for c in range(4):
            wq, wfg, wi, wo = load_group([J_HQ + c, J_HF + c, J_HI + c, J_HO + c])
            lcol = slice(l * 4 + c, l * 4 + c + 1)
            for i in range(NT):
                pq = proj(wq, i)
                pf = proj(wfg, i)
                pi_ = proj(wi, i)
                po = proj(wo, i)
                sq_ = wk()
                o.act(sq_[:, 0:512], pq.v, AF.Silu)
                sgf = wk()
                o.act(sgf[:, 0:512], pf.v, AF.Sigmoid)
                gate = wk()
                o.ts('dve', gate[:, 0:512], sgf[:, 0:512], oml[:, lcol], lowT[:, lcol], ALU.mult, ALU.add)
                lg = wk()
                o.act(lg[:, 0:512], gate[:, 0:512], AF.Ln)
                kx = wk()
                o.ts('pool', kx[:, 0:512], gate[:, 0:512], -1.0, 1.0, ALU.mult, ALU.add)
                o.scan(Cx[:, 1:513], ones512.v, lg[:, 0:512], 0.0)
                G3 = Cx[:, 1:513].re("p (c t) -> p c t", t=64)
                Dd = wk()
                o.tt('dve', Dd[:, 0:512].re("p (c t) -> p c t", t=64), G3, G3[:, :, 31:32].bc([128, 8, 64]), ALU.subtract)
                E1 = wk()
                E3 = wk()
                o.act(E1[:, 0:512], Dd[:, 0:512], AF.Exp)
                o.act(E3[:, 0:512], Dd[:, 0:512], AF.Exp, scale=-1.0)
                ot = wk()
                o.tt('dve', ot[:, 0:512], sq_[:, 0:512], E1[:, 0:512], ALU.mult)
                o.dma(HQ[c][:, tsl(i)], ot[:, 0:512])
                ot = wk()
                o.tt('pool', ot[:, 0:512], kx[:, 0:512], E3[:, 0:512], ALU.mult)
                o.dma(HK[c][:, tsl(i)], ot[:, 0:512])
                o.cp('pool', hscv[:, 0, c, i * 8:(i + 1) * 8], E1[:, 0:512].re("p (c t) -> p c t", t=64)[:, :, 63])
                G0 = Cx[:, 0:512].re("p (c t) -> p c t", t=64)
                d8 = wk()
                o.tt('dve', d8[:, 0:8], G3[:, :, 63], G0[:, :, 0], ALU.subtract)
                o.act(hscv[:, 1, c, i * 8:(i + 1) * 8], d8[:, 0:8], AF.Exp)
                d8 = wk()
                o.tt('dve', d8[:, 0:8], G3[:, :, 31], G0[:, :, 0], ALU.subtract)
                o.act(hscv[:, 2, c, i * 8:(i + 1) * 8], d8[:, 0:8], AF.Exp)
                it = wk()
                o.cp('act', it[:, 0:512], pi_.v)
                o.dma(HI[c][:, tsl(i)], it[:, 0:512])
                ob = wkbf()
                o.act(ob.v, po.v, AF.Silu)
                o.dma(HO[c][:, tsl(i)], ob.v)
        for c in range(4):
            wq, wk_ = load_group([J_FQ + c, J_FK + c])
            for i in range(NT):
                for which, wt in ((0, wq), (1, wk_)):
                    p = proj(wt, i)
                    sq = wk()
                    o.act(sq[:, 0:512], p.v, AF.Square)
                    pss = ps()
                    o.mm(pss.v, onesbd, sq[:, 0:512])
                    rstd = wk()
                    rtmp = wk()
                    o.rsqrt(rstd[:, 0:512], pss.v, 64.0 * EPS, rtmp[:, 0:512])
                    qb = wkbf()
                    o.stt('dve', qb.v, p.v, dr[:, 21 + which:22 + which], rstd[:, 0:512], ALU.mult, ALU.mult)
                    o.dma(QKT[which * 4 + c][:, tsl(i)], qb.v)
        wvs = load_group([J_FV + c for c in range(4)])
        for c in range(4):
            for i in range(NT):
                p = proj(wvs[c], i)
                vb = wkbf()
                o.cp('act', vb.v, p.v)
                o.dma(FVT[c][:, tsl(i)], vb.v)
        for c0 in range(0, 24, 4):
            wgs = load_group([J_GATE + c0 + k for k in range(4)])
            for k in range(4):
                for i in range(NT):
                    p = proj(wgs[k], i)
                    gb = wkbf()
                    o.act(gb.v, p.v, AF.Sigmoid)
                    o.dma(GT[c0 + k][:, tsl(i)], gb.v)
        if STOP <= 1:
            return
        NB = S_LEN // 128

        def r3(v, c):
            return v.re("p (c t) -> p c t", c=c)

        def par(v, bk):
            return v.re("p (c a t) -> p c a t", c=4, a=2)[:, :, bk, :]

        S.barrier()
        st['off'] = layer_mark
        Fsb = sb(S_LEN, 'Fsb')
        F8 = sb(S_LEN, 'F8', BF16)
        o.memset('pool', F8.v, 0.0)
        o.dma(Fsb[0:8, :], FTD.v)
        o.ts('dve', F8[0:8, :], Fsb[0:8, :], 8.0, None, ALU.mult)
        negF = sb(NB * 8, 'negF')
        for blk in range(NB):
            p = ps()
            o.tr(p[:, 0:8], Fsb[0:8, blk * 128:(blk + 1) * 128], ident[0:8, 0:8])
            o.ts('dve', negF[:, blk * 8:(blk + 1) * 8], p[:, 0:8], -1.0, None, ALU.mult)
        VTM = sb(NB * 512, 'VTM', BF16)
        V4 = VTM.v.re("p (n h x) -> p n h x", n=NB, h=4)
        o.memset('pool', VTM.v, 1.0)
        qz = [sb(S_LEN, f'qz{g}', BF16) for g in range(2)]
        kT = sb(S_LEN, 'kT', BF16)
        fv2 = [sb(S_LEN, f'fv{c}', BF16) for c in range(2)]
        Pt = [sb(512, f'P{i}', BF16) for i in range(6)]
        yt = [sb(512, f'y{i}', BF16) for i in range(3)]
        rd = [sb(512, f'rd{i}') for i in range(2)]
        pst['pc'] = 0
        ycount = 0
        for half in range(2):
            pst['n'] = 8
            for cc_ in range(2):
                o.dma(fv2[cc_].v, FVT[2 * half + cc_])
            for blk in range(NB):
                p = ps()
                pb = p.v.cast(BF16)
                for cc_ in range(2):
                    o.tr(pb[:, cc_ * 128:(cc_ + 1) * 128], fv2[cc_][:, blk * 128:(blk + 1) * 128], identb)
                o.cp('act' if blk % 2 else 'dve', V4[:, blk, :, 0:64], pb[:, 0:256].re("p (h x) -> p h x", h=4))
            pst['n'] = 4
            for cc_ in range(2):
                c = 2 * half + cc_
                o.dma(kT.v, QKT[4 + c])
                o.memset('pool', qz[0][64:128, :], 0.0)
                o.memset('pool', qz[1][0:64, :], 0.0)
                o.dma(qz[0][0:64, :], QKT[c][0:64, :])
                o.dma(qz[1][64:128, :], QKT[c][64:128, :])
                for h2 in range(2):
                    h = 2 * c + h2
                    hl = 2 * cc_ + h2
                    rows = slice(h2 * 64, (h2 + 1) * 64)
                    for qc in range(NT):
                        pn = psb[4 + ycount % 4]
                        nkb = 4 * qc + 4
                        pend = []

                        def stage_a(kb):
                            i_ = kb - 4 * qc
                            n0 = 128 * i_ if i_ > 0 else 0
                            qs = slice(qc * 512 + n0, (qc + 1) * 512)
                            psc = ps()
                            o.mm(psc[:, n0:512], kT[:, kb * 128:(kb + 1) * 128], qz[h2][:, qs], start=True, stop=False)
                            if i_ >= 0:
                                o.mm(psc[:, n0:n0 + 128], identb, trib, start=False, stop=False)
                            o.mm(psc[:, n0:512], selb[:, h * 128:(h + 1) * 128], F8[:, qs], start=False, stop=True)
                            P = Pt[pst['pc'] % 6]
                            pst['pc'] += 1
                            o.act(P[:, n0:512], psc[:, n0:512], AF.Exp, bias=negF[:, kb * 8 + h:kb * 8 + h + 1], scale=0.125)
                            return (kb, n0, P)

                        def stage_b(kb, n0, P):
                            o.mm(pn[:, n0:512], V4[:, kb, hl, :], P[:, n0:512], start=(kb == 0), stop=(kb == nkb - 1))
                        for kb in range(nkb):
                            pend.append(stage_a(kb))
                            if len(pend) > 2:
                                stage_b(*pend.pop(0))
                        while pend:
                            stage_b(*pend.pop(0))
                        r = rd[ycount % 2]
                        y = yt[ycount % 3]
                        ycount += 1
                        o.cp('act', r[0:64, :], pn[64:128, :])
                        o.recip(r[0:64, :], r[0:64, :])
                        o.tt('dve', y[0:64, :], pn[0:64, :], r[0:64, :], ALU.mult)
                        o.dma(YB[c][rows, tsl(qc)], y[0:64, :])
        pst['n'] = 8
        if STOP <= 2:
            return

        S.barrier()
        st['off'] = layer_mark
        Mst = sb(512, 'Mst')
        M3 = r3(Mst.v, 4)
        o.memset('dve', Mst.v, 0.0)
        vpad = sb(1024, 'vpad', BF16)
        o.memset('pool', vpad.v, 0.0)
        ktm = [sb(512, f'ktm{i}', BF16) for i in range(2)]
        vtm = [sb(512, f'vtm{i}', BF16) for i in range(2)]
        amt = [sb(512, f'amt{i}', BF16) for i in range(2)]
        m0t = [sb(512, f'm0t{i}', BF16) for i in range(2)]
        hqb = sb(2048, 'hqb', BF16)
        hkb = sb(2048, 'hkb', BF16)
        hib = sb(2048, 'hib', BF16)
        hq = [sb(2048, f'hq{i}') for i in range(2)]
        hk = [sb(2048, f'hk{i}') for i in range(2)]
        hi = [sb(2048, f'hi{i}') for i in range(2)]
        ho = [sb(2048, f'ho{i}', BF16) for i in range(2)]
        ot = sb(2048, 'ot')
        bd4 = bd.us(1).bc([128, 4, 128])
        if HGCUT <= -1:
            return
        for i in range(NT):
            b = i % 2
            o.dma(r3(hq[b].v, 4), HQ.v.re("c p t -> p c t")[:, :, tsl(i)])
            o.dma(r3(hk[b].v, 4), HK.v.re("c p t -> p c t")[:, :, tsl(i)])
            o.dma(r3(hi[b].v, 4), HI.v.re("c p t -> p c t")[:, :, tsl(i)])
            o.dma(r3(ho[b].v, 4), HO.v.re("c p t -> p c t")[:, :, tsl(i)])
            o.cp('pool', hqb.v, hq[b].v)
            o.cp('pool', hkb.v, hk[b].v)
            o.cp('pool', hib.v, hi[b].v)
            if HGCUT <= 0:
                continue
            for ck in range(8):
                g = i * 8 + ck

                def cc(c):
                    return slice(c * 512 + ck * 64, c * 512 + (ck + 1) * 64)
                pk_ = ps()
                pv_ = ps()
                for c in range(4):
                    o.mm(pk_[0:64, c * 128:(c + 1) * 128], hkb[:, cc(c)], identb)
                    o.mm(pv_[0:64, c * 128:(c + 1) * 128], hib[:, cc(c)], identb)
                kt = ktm[g % 2]
                vt = vtm[g % 2]
                if HGSUB <= 1:
                    continue
                o.cp('act', kt[0:64, :], pk_[0:64, :])
                o.cp('act', vt[0:64, :], pv_[0:64, :])
                if HGSUB <= 2:
                    continue
                o.tt('dve', vpad[0:64, 0:512], pv_[0:64, :], me8[0:64, :], ALU.mult)
                o.tt('dve', vpad[0:64, 512:1024], pv_[0:64, :], mo8[0:64, :], ALU.mult)
                if HGCUT <= 1:
                    continue
                M0s = m0t[g % 2]
                o.tt('dve', r3(M0s.v, 4), M3, hscv[:, 2, :, g:g + 1].bc([128, 4, 128]), ALU.mult)
                if HGSUB <= 3:
                    continue
                pA = [ps(), ps()]
                for h in range(8):
                    c = h // 2
                    rr = slice((h % 2) * 64, (h % 2) * 64 + 64)
                    b0 = c * 512 + ck * 64
                    o.mm(pA[h % 2][0:64, c * 64 + 32:(c + 1) * 64], hkb[rr, b0:b0 + 64], hqb[rr, b0 + 32:b0 + 64])
                    o.mm(pA[h % 2][0:32, c * 64:c * 64 + 32], hkb[rr, b0:b0 + 32], hqb[rr, b0:b0 + 32])
                Am = amt[g % 2]
                o.memset('pool', Am[32:64, :], 0.0)
                for bk in range(2):
                    o.tt('dve', par(Am[0:64, :], bk)[:, :, 32:64], r3(pA[bk][0:64, 0:256], 4)[:, :, 32:64], r3(ui4[0:64, :], 4)[:, :, 32:64], ALU.mult)
                    o.tt('dve', par(Am[0:32, :], bk)[:, :, 0:32], r3(pA[bk][0:32, 0:256], 4)[:, :, 0:32], r3(ui4[0:32, :], 4)[:, :, 0:32], ALU.mult)
                if HGCUT <= 2:
                    continue
                pO = ps()
                for c in range(4):
                    oc_ = pO[:, c * 64:(c + 1) * 64]
                    o.mm(oc_, M0s[:, c * 128:(c + 1) * 128], hqb[:, cc(c)], start=True, stop=False)
                    o.mm(oc_, vpad[0:64, c * 128:(c + 1) * 128], Am[0:64, (2 * c) * 64:(2 * c + 1) * 64], start=False, stop=False)
                    o.mm(oc_, vpad[0:64, 512 + c * 128:512 + (c + 1) * 128], Am[0:64, (2 * c + 1) * 64:(2 * c + 2) * 64], start=False, stop=True)
                o.cp('act', r3(ot.v, 4)[:, :, ck * 64:(ck + 1) * 64], r3(pO[:, 0:256], 4))
                if HGCUT <= 3:
                    continue
                pS = ps()
                for c in range(4):
                    o.mm(pS[:, c * 128:(c + 1) * 128], kt[0:64, c * 128:(c + 1) * 128], vt[0:64, c * 128:(c + 1) * 128])
                t1 = wk()
                o.tt('dve', r3(t1.v, 4), r3(pS.v, 4), bd4, ALU.mult)
                o.tt('dve', r3(t1.v, 4), r3(t1.v, 4), hscv[:, 0, :, g:g + 1].bc([128, 4, 128]), ALU.mult)
                o.tt('pool', M3, M3, hscv[:, 1, :, g:g + 1].bc([128, 4, 128]), ALU.mult)
                o.tt('pool', M3, M3, r3(t1.v, 4), ALU.add)
            for c in range(4):
                oc_ = ot[:, c * 512:(c + 1) * 512]
                sq = wk()
                o.act(sq.v, oc_, AF.Square)
                pss = ps()
                o.mm(pss.v, onesbd, sq.v)
                rstd = wk()
                rtmp = wk()
                o.rsqrt(rstd.v, pss.v, 64.0 * EPS, rtmp.v)
                y = wk()
                o.stt('dve', y.v, oc_, dr[:, 23:24], rstd.v, ALU.mult, ALU.mult)
                yb_ = wkbf()
                o.tt('dve', yb_.v, y.v, ho[b][:, c * 512:(c + 1) * 512], ALU.mult)
                o.dma(YB[4 + c][:, tsl(i)], yb_.v)
        if STOP <= 3:
            return

        S.barrier()
        st['off'] = layer_mark
        Mst = sb(512, 'MstR')
        M3 = r3(Mst.v, 4)
        o.memset('dve', Mst.v, 0.0)
        Mstb = sb(512, 'MstRb', BF16)
        o.memset('pool', Mstb.v, 0.0)
        nupad = sb(1024, 'nupad', BF16)
        o.memset('pool', nupad.v, 0.0)
        sets = [{'tiles': [sb(512, f'sm{q}_{i}', BF16) for i in range(12)] + [sb(1024, f'vpadR{q}', BF16)]} for q in range(2)]
        Wsb = sb(512, 'Wsb', BF16)
        nut = sb(512, 'nut', BF16)
        rkr = [sb(4096, 'rkr0')] * 2
        rkk = [sb(2048, 'rkk0')] * 2
        rbb = [sb(2048, 'rbb0')] * 2
        rvv = [sb(2048, 'rvv0')] * 2
        bo = [sb(2048, 'bo0')] * 2
        rg = [sb(2048, 'rg0', BF16)] * 2
        rkrb = sb(4096, 'rkrb', BF16)
        rkkb = sb(2048, 'rkkb', BF16)
        rbbb = sb(2048, 'rbbb', BF16)
        rvvb = sb(2048, 'rvvb', BF16)
        ot = sb(2048, 'otR')
        hsl = [slice(h * 64, (h + 1) * 64) for h in range(8)]
        for i in range(NT):
            b = i % 2
            o.dma(rkr[b].v.re("p (c n a t) -> p c n a t", c=4, n=8, a=2), RKR.v.re("c p n a t -> p c n a t")[:, :, i * 8:(i + 1) * 8, :, :])
            o.dma(r3(rkk[b].v, 4), RKT.v.re("c p t -> p c t")[:, :, tsl(i)])
            o.dma(r3(rbb[b].v, 4), RBT.v.re("c p t -> p c t")[:, :, tsl(i)])
            o.dma(r3(rvv[b].v, 4), RVT.v.re("c p t -> p c t")[:, :, tsl(i)])
            o.dma(r3(bo[b].v, 4), RBO.v.re("c p t -> p c t")[:, :, tsl(i)])
            o.dma(r3(rg[b].v, 4), RG.v.re("c p t -> p c t")[:, :, tsl(i)])
            o.cp('pool', rkrb.v, rkr[b].v)
            o.cp('pool', rkkb.v, rkk[b].v)
            o.cp('pool', rbbb.v, rbb[b].v)
            o.cp('pool', rvvb.v, rvv[b].v)
            def partA(ck, T_):
                def cc(c):
                    return slice(c * 512 + ck * 64, c * 512 + (ck + 1) * 64)

                def KR(c, a0, a1):
                    off = (c * 8 + ck) * 128
                    return slice(off + a0 * 64, off + a1 * 64)
                ktm_, btm_, vtm_, RBt, Qt, RKt, Y0, Y1, Yt0, Yt1, Tt0, Tt1, vpad = T_['tiles']
                pk_ = ps()
                pb_ = ps()
                pv_ = ps()
                for c in range(4):
                    o.mm(pk_[0:64, c * 128:(c + 1) * 128], rkkb[:, cc(c)], identb)
                    o.mm(pb_[0:64, c * 128:(c + 1) * 128], rbbb[:, cc(c)], identb)
                    o.mm(pv_[0:64, c * 128:(c + 1) * 128], rvvb[:, cc(c)], identb)
                yield
                o.cp('act', ktm_[0:64, :], pk_[0:64, :])
                o.cp('act', btm_[0:64, :], pb_[0:64, :])
                o.cp('act', vtm_[0:64, :], pv_[0:64, :])
                o.tt('dve', vpad[0:64, 0:512], pv_[0:64, :], me8[0:64, :], ALU.mult)
                o.tt('dve', vpad[0:64, 512:1024], pv_[0:64, :], mo8[0:64, :], ALU.mult)
                pA1 = [ps(), ps()]
                pA2 = [ps(), ps()]
                for h in range(8):
                    c = h // 2
                    rr = slice((h % 2) * 64, (h % 2) * 64 + 64)
                    o.mm(pA1[h % 2][0:64, c * 128:(c + 1) * 128], rbbb[rr, cc(c)], rkrb[rr, KR(c, 0, 2)])
                    o.mm(pA2[h % 2][0:64, c * 128:(c + 1) * 128], rkkb[rr, cc(c)], rkrb[rr, KR(c, 0, 2)])
                yield
                for bk in range(2):
                    v1 = pA1[bk][0:64, :].re("p (h x) -> p h x", h=4)
                    v2 = pA2[bk][0:64, :].re("p (h x) -> p h x", h=4)
                    o.tt('dve', par(Yt0[0:64, :], bk), v1[:, :, 0:64], r3(nsu4[0:64, :], 4), ALU.mult)
                    o.tt('dve', par(RBt[0:64, :], bk), v1[:, :, 64:128], r3(ui4[0:64, :], 4), ALU.mult)
                    o.tt('dve', par(Qt[0:64, :], bk), v2[:, :, 0:64], r3(su4[0:64, :], 4), ALU.mult)
                    o.tt('dve', par(RKt[0:64, :], bk), v2[:, :, 64:128], r3(ui4[0:64, :], 4), ALU.mult)
                pA3 = [ps(), ps()]
                for h in range(8):
                    c = h // 2
                    rr = slice((h % 2) * 64, (h % 2) * 64 + 64)
                    o.mm(pA3[h % 2][0:64, c * 64:(c + 1) * 64], rkrb[rr, KR(c, 0, 1)], rbbb[rr, cc(c)])
                yield
                for bk in range(2):
                    o.tt('dve', par(Y0[0:64, :], bk), r3(pA3[bk][0:64, 0:256], 4), r3(nsl8[0:64, 0:256], 4), ALU.mult)
                o.tt('pool', Tt0[0:64, :], Yt0[0:64, :], id8[0:64, :], ALU.add)
                Ya, Yta, Tta = Y0, Yt0, Tt0
                Yb, Ytb, Ttb = Y1, Yt1, Tt1
                for j in range(5):
                    yield
                    pY = ps()
                    pYt = ps()
                    for h in range(8):
                        o.mm(pY[0:64, hsl[h]], Yta[0:64, hsl[h]], Ya[0:64, hsl[h]])
                        o.mm(pYt[0:64, hsl[h]], Ya[0:64, hsl[h]], Yta[0:64, hsl[h]])
                    yield
                    o.cp('act', Yb[0:64, :], pY[0:64, :])
                    o.cp('act', Ytb[0:64, :], pYt[0:64, :])
                    pT = ps()
                    for h in range(8):
                        o.mm(pT[0:64, hsl[h]], Yb[0:64, hsl[h]], Tta[0:64, hsl[h]])
                    yield
                    o.tt('dve', Ttb[0:64, :], pT[0:64, :], Tta[0:64, :], ALU.add)
                    Ya, Yta, Tta, Yb, Ytb, Ttb = Yb, Ytb, Ttb, Ya, Yta, Tta
                T_['T'] = Tta

            def partB(ck, T_):
                g = i * 8 + ck

                def KR(c, a0, a1):
                    off = (c * 8 + ck) * 128
                    return slice(off + a0 * 64, off + a1 * 64)
                ktm_, btm_, vtm_, RBt, Qt, RKt, Y0, Y1, Yt0, Yt1, Tt0, Tt1, vpad = T_['tiles']
                Tta = T_['T']
                pW = ps()
                for c in range(4):
                    o.mm(pW[0:64, c * 128:(c + 1) * 128], rkrb[:, KR(c, 0, 1)], Mstb[:, c * 128:(c + 1) * 128], start=True, stop=False)
                    for h2 in range(2):
                        h = 2 * c + h2
                        o.mm(pW[0:64, hsl[h]], Qt[0:64, hsl[h]], vtm_[0:64, hsl[h]], start=False, stop=True)
                yield
                o.cp('act', Wsb[0:64, :], pW[0:64, :])
                pU = ps()
                for h in range(8):
                    o.mm(pU[0:64, hsl[h]], Tta[0:64, hsl[h]], Wsb[0:64, hsl[h]])
                yield
                o.ts('dve', nut[0:64, :], pU[0:64, :], -1.0, None, ALU.mult)
                o.stt('dve', nupad[0:64, 0:512], pU[0:64, :], -1.0, me8[0:64, :], ALU.mult, ALU.mult)
                o.stt('dve', nupad[0:64, 512:1024], pU[0:64, :], -1.0, mo8[0:64, :], ALU.mult, ALU.mult)
                pO = ps()
                for c in range(4):
                    oc_ = pO[:, c * 64:(c + 1) * 64]
                    o.mm(oc_, Mstb[:, c * 128:(c + 1) * 128], rkrb[:, KR(c, 1, 2)], start=True, stop=False)
                    for h2 in range(2):
                        h = 2 * c + h2
                        o.mm(oc_, vpad[0:64, h2 * 512 + c * 128:h2 * 512 + (c + 1) * 128], RKt[0:64, hsl[h]], start=False, stop=False)
                        o.mm(oc_, nupad[0:64, h2 * 512 + c * 128:h2 * 512 + (c + 1) * 128], RBt[0:64, hsl[h]], start=False, stop=(h2 == 1))
                pS = ps()
                for c in range(4):
                    o.mm(pS[:, c * 128:(c + 1) * 128], ktm_[0:64, c * 128:(c + 1) * 128], vtm_[0:64, c * 128:(c + 1) * 128], start=True, stop=False)
                    o.mm(pS[:, c * 128:(c + 1) * 128], btm_[0:64, c * 128:(c + 1) * 128], nut[0:64, c * 128:(c + 1) * 128], start=False, stop=True)
                yield
                o.cp('act', r3(ot.v, 4)[:, :, ck * 64:(ck + 1) * 64], r3(pO[:, 0:256], 4))
                t1 = wk()
                o.tt('dve', r3(t1.v, 4), r3(pS.v, 4), bd4, ALU.mult)
                o.tt('dve', t1.v, t1.v, Mst.v, ALU.add)
                o.tt('pool', M3, r3(t1.v, 4), gamv[:, :, g:g + 1].bc([128, 4, 128]), ALU.mult)
                o.cp('act', Mstb.v, Mst.v)

            def drain(gen):
                for _ in gen:
                    pass
            drain(partA(0, sets[0]))
            for ck in range(8):
                gb = partB(ck, sets[ck % 2])
                ga = partA(ck + 1, sets[(ck + 1) % 2]) if ck < 7 else iter(())
                da = db = False
                while not (da and db):
                    if not da:
                        try:
                            next(ga)
                        except StopIteration:
                            da = True
                    if not db:
                        try:
                            next(gb)
                        except StopIteration:
                            db = True
            for c in range(4):
                oc_ = ot[:, c * 512:(c + 1) * 512]
                pm = ps()
                o.mm(pm.v, onesbd, oc_)
                sq = wk()
                o.act(sq.v, oc_, AF.Square)
                pvv = ps()
                o.mm(pvv.v, onesbd, sq.v)
                mean = wk()
                o.ts('dve', mean.v, pm.v, 1.0 / 64, None, ALU.mult)
                cen = wk()
                o.tt('dve', cen.v, oc_, mean.v, ALU.subtract)
                msq = wk()
                o.tt('pool', msq.v, mean.v, mean.v, ALU.mult)
                var = wk()
                o.stt('dve', var.v, pvv.v, 1.0 / 64, msq.v, ALU.mult, ALU.subtract)
                rstd = wk()
                rtmp = wk()
                o.rsqrt(rstd.v, var.v, LN_EPS, rtmp.v)
                y = wk()
                o.tt('dve', y.v, cen.v, rstd.v, ALU.mult)
                o.ts('dve', y.v, y.v, cl[:, C_LNW + c:C_LNW + c + 1], cl[:, C_LNB + c:C_LNB + c + 1], ALU.mult, ALU.add)
                o.tt('pool', y.v, y.v, bo[b][:, c * 512:(c + 1) * 512], ALU.add)
                yb_ = wkbf()
                o.tt('dve', yb_.v, y.v, rg[b][:, c * 512:(c + 1) * 512], ALU.mult)
                o.dma(YB[8 + c][:, tsl(i)], yb_.v)
        if STOP <= 4:
            return

        S.barrier()
        st['off'] = layer_mark
        wbrb = sb(12 * 1024, 'wbrb', BF16)
        woutb = sb(8 * 1024, 'woutb', BF16)
        stg = [sb(2048, f'stg{i}') for i in range(2)]
        for q in range(6):
            s_ = stg[q % 2]
            o.dma(r3(s_.v, 2), wbr_in[l][:, 2 * q:2 * q + 2, :])
            o.cp('pool', wbrb[:, 2 * q * 1024:(2 * q + 2) * 1024], s_.v)
        for q in range(4):
            s_ = stg[q % 2]
            o.dma(r3(s_.v, 2), wout_in[l][:, 2 * q:2 * q + 2, :])
            o.cp('pool', woutb[:, 2 * q * 1024:(2 * q + 2) * 1024], s_.v)
        ybt = [sb(12 * 512, 'ybt0', BF16)] * 2
        gtt = [sb(3 * 512, f'gtt{i}', BF16) for i in range(2)]
        mg = sb(8 * 512, 'mg', BF16)
        xt = [sb(KC * 512, 'xt30')] * 2
        sqt = sb(KC * 512, 'sqt3', BF16)
        h2o = [sb(KC * 512, 'h2o0', BF16)] * 2
        gn = 0
        for i in range(NT):
            yb_ = ybt[i % 2]
            o.dma(r3(yb_.v, 12), YB.v.re("c p t -> p c t")[:, :, tsl(i)])
            x = xt[i % 2]
            o.dma(r3(x.v, KC), XT[:, :, tsl(i)])
            for oc in range(8):
                g_ = gtt[gn % 2]
                gn += 1
                o.dma(r3(g_.v, 3), GT.v.re("(b o) p t -> o p b t", b=3)[oc][:, :, tsl(i)])
                acc = wk()
                for b_ in range(3):
                    p = ps()
                    for kc in range(4):
                        o.mm(p.v, wbrb[:, (b_ * 4 + kc) * 1024 + oc * 128:(b_ * 4 + kc) * 1024 + (oc + 1) * 128],
                             yb_[:, (b_ * 4 + kc) * 512:(b_ * 4 + kc + 1) * 512], start=(kc == 0), stop=(kc == 3))
                    if b_ == 0:
                        o.tt('dve', acc.v, p.v, g_[:, 0:512], ALU.mult)
                    else:
                        t = wk()
                        o.tt('dve', t.v, p.v, g_[:, b_ * 512:(b_ + 1) * 512], ALU.mult)
                        if b_ == 1:
                            o.tt('pool', acc.v, acc.v, t.v, ALU.add)
                        else:
                            o.tt('pool', mg[:, oc * 512:(oc + 1) * 512], acc.v, t.v, ALU.add)
            for oc in range(8):
                p = ps()
                for kc in range(KC):
                    o.mm(p.v, woutb[:, kc * 1024 + oc * 128:kc * 1024 + (oc + 1) * 128], mg[:, kc * 512:(kc + 1) * 512], start=(kc == 0), stop=(kc == KC - 1))
                xs_ = x[:, oc * 512:(oc + 1) * 512]
                o.stt('dve', xs_, p.v, md[:, 16 + oc:17 + oc], xs_, ALU.mult, ALU.add)
            o.dma(XT[:, :, tsl(i)], r3(x.v, KC))
            h2t = h2o[i % 2]
            norm_tile(x, 8, 24, h2t, sqt)
            o.dma(H2.v.re("k p t -> p k t")[:, :, tsl(i)], r3(h2t.v, KC))
        if STOP <= 5:
            return

        S.barrier()
        st['off'] = layer_mark
        hts = [sb(KC * 512, f'h2_{i}', BF16) for i in range(NT)]
        for i in range(NT):
            o.dma(r3(hts[i].v, KC), H2.v.re("k p t -> p k t")[:, :, tsl(i)])
        wst2 = [sb(2048, f'wst2_{i}') for i in range(2)]
        wpb = [sb(2048, f'wpb{i}', BF16) for i in range(2)]
        rawv = sb(514, 'rawv')
        rawg = sb(514, 'rawg')
        for j in range(NFF):
            s_ = wst2[j % 2]
            o.dma(s_.v, wup_in[l][j])
            wb_ = wpb[j % 2]
            o.cp('pool', wb_.v, s_.v)
            o.memset('pool', rawv[:, 0:2], 0.0)
            o.memset('pool', rawg[:, 0:2], 0.0)
            for i in range(NT):
                pv = ps()
                pg = ps()
                for kc in range(KC):
                    o.mm(pv.v, wb_[:, kc * 256:kc * 256 + 128], hts[i][:, kc * 512:(kc + 1) * 512], start=(kc == 0), stop=(kc == KC - 1))
                for kc in range(KC):
                    o.mm(pg.v, wb_[:, kc * 256 + 128:kc * 256 + 256], hts[i][:, kc * 512:(kc + 1) * 512], start=(kc == 0), stop=(kc == KC - 1))
                res = []
                for (p, raw, cc_) in ((pv, rawv, j), (pg, rawg, NFF + j)):
                    o.cp('act', raw[:, 2:514], p.v)
                    cv = wk()
                    o.ts('pool', cv.v, raw[:, 2:514], cl[:, C_CW + 88 + cc_:C_CW + 89 + cc_], cl[:, C_CB + cc_:C_CB + cc_ + 1], ALU.mult, ALU.add)
                    o.stt('dve', cv.v, raw[:, 1:513], cl[:, C_CW + 44 + cc_:C_CW + 45 + cc_], cv.v, ALU.mult, ALU.add)
                    o.stt('dve', cv.v, raw[:, 0:512], cl[:, C_CW + cc_:C_CW + cc_ + 1], cv.v, ALU.mult, ALU.add)
                    o.cp('act', raw[:, 0:2], raw[:, 512:514])
                    res.append(cv)
                sg = wk()
                o.act(sg.v, res[1].v, AF.Silu)
                ab = wkbf()
                o.tt('pool', ab.v, sg.v, res[0].v, ALU.mult)
                o.dma(ACTT[j][:, tsl(i)], ab.v)
        if STOP <= 6:
            return

        S.barrier()
        st['off'] = layer_mark
        wdnb = sb(NFF * 1024, 'wdnb', BF16)
        stg = [sb(2048, f'stgd{i}') for i in range(2)]
        for q in range(NFF // 2):
            s_ = stg[q % 2]
            o.dma(r3(s_.v, 2), wdn_in[l][:, 2 * q:2 * q + 2, :])
            o.cp('pool', wdnb[:, 2 * q * 1024:(2 * q + 2) * 1024], s_.v)
        att = [sb(NFF * 512, f'att{i}', BF16) for i in range(2)]
        xt = [sb(KC * 512, 'xt40')] * 2
        for i in range(NT):
            a_ = att[i % 2]
            o.dma(r3(a_.v, NFF), ACTT.v.re("c p t -> p c t")[:, :, tsl(i)])
            x = xt[i % 2]
            o.dma(r3(x.v, KC), XT[:, :, tsl(i)])
            for oc in range(8):
                p = ps()
                for kc in range(NFF):
                    o.mm(p.v, wdnb[:, kc * 1024 + oc * 128:kc * 1024 + (oc + 1) * 128], a_[:, kc * 512:(kc + 1) * 512], start=(kc == 0), stop=(kc == NFF - 1))
                xs_ = x[:, oc * 512:(oc + 1) * 512]
                o.stt('dve', xs_, p.v, md[:, 40 + oc:41 + oc], xs_, ALU.mult, ALU.add)
            o.dma(XT[:, :, tsl(i)], r3(x.v, KC))

    for l in range(L):
        emit_layer(l)

    S.barrier()
    st['off'] = persist_mark
    xl = [sb(KC * 512, f'xl{i}') for i in range(2)]
    yo = [sb(D, f'yo{i}') for i in range(2)]
    n = 0
    for i in range(NT):
        xs = xl[i % 2]
        o.dma(xs.v.re("p (k t) -> p k t", k=KC), XT[:, :, i * 512:(i + 1) * 512])
        for q in range(4):
            yy = yo[n % 2]
            n += 1
            for h in range(2):
                p = ps()
                for c in range(4):
                    kc = h * 4 + c
                    o.tr(p[:, c * 128:(c + 1) * 128], xs[:, kc * 512 + q * 128: kc * 512 + (q + 1) * 128], ident)
                o.cp('act' if h == 0 else 'dve', yy[:, h * 512:(h + 1) * 512], p.v)
            t0 = i * 512 + q * 128
            o.dma(y_out[t0:t0 + 128, :], yy.v)
    S.barrier()

    with contextlib.ExitStack() as es:
        sems = {}
        for e in COMPUTE:
            sems[e] = es.enter_context(nc.semaphore("s_" + e))
        for j in range(NDSEM):
            sems[('d', j)] = es.enter_context(nc.semaphore(f"d{j}"))
        block = es.enter_context(nc.Block())
        run = S.emit(sems)
        block.sync(lambda e: run('sp', e))
        block.tensor(lambda e: run('pe', e))
        block.scalar(lambda e: run('act', e))
        block.vector(lambda e: run('dve', e))
        block.gpsimd(lambda e: run('pool', e))
    return nc


def _fm(v):
    return np.ascontiguousarray(np.asarray(v, np.float32).reshape(-1, 128).T)


def make_consts():
    c = np.zeros((128, K_W), np.float32)
    c[:, K_ID:K_ID + 128] = np.eye(128)
    c[0:64, K_OBD:K_OBD + 64] = 1.0
    c[64:128, K_OBD + 64:K_OBD + 128] = 1.0
    c[:, K_ONE:K_ONE + 128] = 1.0
    r = np.arange(64)
    su = (r[:, None] < r[None, :]).astype(np.float32)
    ui = (r[:, None] <= r[None, :]).astype(np.float32)
    sl = (r[:, None] > r[None, :]).astype(np.float32)
    c[0:64, K_BD:K_BD + 64] = 1.0
    c[64:128, K_BD + 64:K_BD + 128] = 1.0
    c[0:64, K_UI8:K_UI8 + 512] = np.tile(ui, (1, 8))
    c[0:64, K_NSL8:K_NSL8 + 512] = -np.tile(sl, (1, 8))
    c[0:64, K_ID8:K_ID8 + 512] = np.tile(np.eye(64, dtype=np.float32), (1, 8))
    c[0:64, K_NSU4:K_NSU4 + 256] = -np.tile(su, (1, 4))
    c[0:64, K_SU4:K_SU4 + 256] = np.tile(su, (1, 4))
    me = np.concatenate([np.ones((64, 64), np.float32), np.zeros((64, 64), np.float32)], axis=1)
    c[0:64, K_ME:K_ME + 512] = np.tile(me, (1, 4))
    c[0:64, K_MO:K_MO + 512] = np.tile(1.0 - me, (1, 4))
    return c


def make_consts_b():
    c = np.zeros((128, KB_W), np.float32)
    c[:, KB_ID:KB_ID + 128] = np.eye(128)
    c[:, KB_ONE:KB_ONE + 128] = 1.0
    r = np.arange(128)
    c[:, KB_TRI:KB_TRI + 128] = np.where(r[:, None] <= r[None, :], 0.0, -98304.0)
    for h in range(8):
        c[h, KB_SEL + h * 128:KB_SEL + (h + 1) * 128] = 1.0
    return c


def prep_shared(inp, L):
    f = lambda k: np.asarray(inp[k], np.float32)
    cols = np.zeros((L, 128, NCOLS), np.float32)
    win = np.zeros((L, NCH, 128, KC * 128), np.float32)
    lora = np.zeros((L, 128, 512), np.float32)
    g2 = np.zeros((L, 128, 512), np.float32)
    v2 = np.zeros((L, 64, 512), np.float32)
    for l in range(L):
        cl = cols[l]
        cl[:, C_N1G:C_N1G + 8] = _fm(f('norm1_g')[l])
        cl[:, C_N2G:C_N2G + 8] = _fm(f('norm2_g')[l])
        cl[0:8, C_FBF] = f('fox_b_f')[l]
        cl[:, C_QG] = np.tile(f('fox_q_gain')[l], 2)
        cl[:, C_KG] = np.tile(f('fox_k_gain')[l], 2)
        cl[:, C_OG] = np.tile(f('hgrn_o_gain')[l], 2)
        cl[:, C_MU:C_MU + 14] = _fm(f('rwkv_mu')[l])
        cl[:, C_W0:C_W0 + 4] = _fm(f('rwkv_w0')[l])
        cl[:, C_A0:C_A0 + 4] = _fm(f('rwkv_a0')[l])
        cl[:, C_KK:C_KK + 4] = _fm(f('rwkv_k_k')[l])
        cl[:, C_KA:C_KA + 4] = _fm(f('rwkv_k_a')[l])
        cl[:, C_RK:C_RK + 4] = _fm(f('rwkv_r_k')[l].reshape(-1))
        cl[:, C_LNW:C_LNW + 4] = _fm(f('rwkv_ln_w')[l])
        cl[:, C_LNB:C_LNB + 4] = _fm(f('rwkv_ln_b')[l])
        if l > 0:
            cl[:, C_V0:C_V0 + 4] = _fm(f('rwkv_v0')[l - 1])
            cl[32:64, C_VMU] = f('rwkv_vres_mu')[l - 1]
            v2[l, 32:64] = f('rwkv_v2')[l - 1]
        cw = f('conv_w')[l]
        for j in range(3):
            cl[:, C_CW + j * 44:C_CW + (j + 1) * 44] = _fm(cw[j])
        cl[:, C_CB:C_CB + 44] = _fm(f('conv_b')[l])
        W = f('w_in')[l]
        main = np.concatenate([W[:, 0:1536], W[:, 1544:1544 + 2048 + 1792 + 3072]], axis=1)
        misc = np.zeros((D, 128), np.float32)
        misc[:, 0:8] = W[:, 1536:1544]
        if l > 0:
            misc[:, 32:64] = f('rwkv_vres_down')[l - 1]
        allc = np.concatenate([main, misc], axis=1)
        win[l] = allc.reshape(KC, 128, NCH, 128).transpose(2, 1, 0, 3).reshape(NCH, 128, KC * 128)
        lora[l, 0:64] = f('rwkv_w2')[l]
        lora[l, 64:128] = f('rwkv_a2')[l]
        g2[l] = f('rwkv_g2')[l]
    sh = {}
    sh['cols'] = cols
    sh['lbT'] = np.ascontiguousarray(f('hgrn_lb')[:4].reshape(-1, 4, 128).transpose(2, 0, 1).reshape(128, -1))
    if sh['lbT'].shape[1] < 16:
        sh['lbT'] = np.concatenate([sh['lbT'], np.zeros((128, 16 - sh['lbT'].shape[1]), np.float32)], axis=1)
    sh['wada'] = np.ascontiguousarray(f('w_ada')[:L].reshape(L, KC, 128, 6 * D).transpose(0, 2, 1, 3))
    sh['bada'] = np.ascontiguousarray(f('b_ada')[:L].reshape(L, 1, 6 * D))
    sh['win'] = win
    sh['lora'] = lora
    sh['g2'] = g2
    sh['v2'] = v2
    sh['wbr'] = np.ascontiguousarray(f('w_branch')[:L].reshape(L, 3, 4, 128, D).transpose(0, 3, 1, 2, 4).reshape(L, 128, 12, D))
    sh['wout'] = np.ascontiguousarray(f('w_out')[:L].reshape(L, KC, 128, D).transpose(0, 2, 1, 3))
    wu = f('w_up')[:L].reshape(L, KC, 128, 2, NFF, 128)
    sh['wup'] = np.ascontiguousarray(wu.transpose(0, 4, 2, 1, 3, 5).reshape(L, NFF, 128, KC * 256))
    sh['wdn'] = np.ascontiguousarray(f('w_down')[:L].reshape(L, NFF, 128, D).transpose(0, 2, 1, 3))
    sh['cst'] = make_consts()
    sh['cstb'] = make_consts_b()
    return sh


_CACHE = {}


def run(inp, S_LEN, L, dbg=(), STOP=99):
    key = (S_LEN, L, tuple(dbg), STOP)
    if key not in _CACHE:
        _CACHE[key] = build(S_LEN, L, dbg, STOP)
    nc = _CACHE[key]
    sh = prep_shared(inp, L)
    x = np.asarray(inp['x'], np.float32)
    c = np.asarray(inp['c'], np.float32)
    B = x.shape[0]
    in_maps = []
    for b in range(B):
        m = dict(sh)
        m['x'] = np.ascontiguousarray(x[b])
        m['cT'] = _fm(c[b])
        in_maps.append(m)
    res = run_bass_kernel_spmd(nc, in_maps, core_ids=list(range(B)))
    return res.results


def kernel(**inputs):
    res = run(inputs, 4096, 4)
    return np.stack([r['y'] for r in res]).astype(np.float32)
```
